# Optimizing a Trainium2 kernel written in Bass

```python
import jax, jax.numpy as jnp
from jax import lax
import numpy as np

D_MODEL = 1024
BATCH = 4
SEQ = 8192
DEPTH = 4

GRID_W = 64
HEAD_DIM = 64
N_RWKV_HEADS = 8
N_ATT_HEADS = 8
RWKV_WIDTH = N_RWKV_HEADS * HEAD_DIM
ATT_WIDTH = N_ATT_HEADS * HEAD_DIM
MIX_WIDTH = RWKV_WIDTH + ATT_WIDTH
DECAY_LORA = 64
AAA_LORA = 64
GATE_LORA = 128
RWKV_COLS = 3 * RWKV_WIDTH + 2 * DECAY_LORA + 2 * AAA_LORA + GATE_LORA
IN_COLS = RWKV_COLS + 3 * ATT_WIDTH
WIN_ROWS_MAX = 8
WIN_COLS = 16
D_FF = ((8 * D_MODEL + 3 * 256 - 1) // (3 * 256)) * 256
NORM_EPS = 1e-6
GN_EPS = 64e-5
N_DIRS = 2

kernel_name = "hybrid_rwkv7_natten_adaln_encoder"


def rms_norm(x, g):
    xf = x.astype(jnp.float32)
    y = xf * lax.rsqrt(jnp.mean(xf * xf, axis=-1, keepdims=True) + NORM_EPS)
    return (y * g).astype(x.dtype)


def centred_shift(z, mu):
    z_prev = jnp.pad(z[:, :-1], ((0, 0), (1, 0), (0, 0)))
    z_next = jnp.pad(z[:, 1:], ((0, 0), (0, 1), (0, 0)))
    return z + mu[0] * (z_prev - z) + mu[1] * (z_next - z)


def wkv7_scan(r, w, k, v, a_neg, b):
    def step(state, inp):
        r_t, w_t, k_t, v_t, an_t, b_t = inp
        sa = jnp.einsum('dbhij,dbhj->dbhi', state, an_t)
        state = (state * w_t[..., None, :] + sa[..., :, None] * b_t[..., None, :]
                 + v_t[..., :, None] * k_t[..., None, :])
        y = jnp.einsum('dbhij,dbhj->dbhi', state, r_t)
        return state, y
    state0 = jnp.zeros(r.shape[1:] + (r.shape[-1],), jnp.float32)
    _, y = lax.scan(step, state0, (r, w, k, v, a_neg, b))
    return y


def rwkv7_bidir(z, w0, w2, a0, a2, g2, k_k, k_a, r_k, lnx_g, lnx_b):
    B_, S_, _ = z.shape
    H, N, RW = N_RWKV_HEADS, HEAD_DIM, RWKV_WIDTH
    z = z.astype(jnp.float32)
    r = z[..., :RW]
    k = z[..., RW:2 * RW]
    v = z[..., 2 * RW:3 * RW]
    o = 3 * RW
    xw = z[..., o:o + 2 * DECAY_LORA].reshape(B_, S_, N_DIRS, DECAY_LORA)
    o += 2 * DECAY_LORA
    xa = z[..., o:o + 2 * AAA_LORA].reshape(B_, S_, N_DIRS, AAA_LORA)
    o += 2 * AAA_LORA
    xg = z[..., o:o + GATE_LORA]
    w_log = -jax.nn.softplus(-(w0 + jnp.einsum('bsdr,drc->bsdc', jnp.tanh(xw), w2))) - 0.5
    decay = jnp.exp(-jnp.exp(w_log))
    a = jax.nn.sigmoid(a0 + jnp.einsum('bsdr,drc->bsdc', xa, a2))
    g = jnp.einsum('bsr,rc->bsc', jax.nn.sigmoid(xg), g2)
    kk = (k * k_k).reshape(B_, S_, H, N)
    kk = kk / jnp.maximum(jnp.sqrt(jnp.sum(kk * kk, axis=-1, keepdims=True)), 1e-12)
    kk = kk.reshape(B_, S_, RW)
    k_dir = k[:, :, None, :] * (1.0 + (a - 1.0) * k_a)

    def both(t):
        return jnp.broadcast_to(t[:, :, None, :], (B_, S_, N_DIRS, RW))

    def to_scan(t):
        t = t.reshape(B_, S_, N_DIRS, H, N).transpose(1, 2, 0, 3, 4)
        return jnp.stack([t[:, 0], t[::-1, 1]], axis=1)

    y = wkv7_scan(to_scan(both(r)), to_scan(decay), to_scan(k_dir), to_scan(both(v)),
                  to_scan(both(-kk)), to_scan(both(kk) * a))
    y = (y[:, 0] + y[::-1, 1]).transpose(1, 0, 2, 3)
    mu = jnp.mean(y, axis=-1, keepdims=True)
    var = jnp.mean(jnp.square(y - mu), axis=-1, keepdims=True)
    yn = ((y - mu) * lax.rsqrt(var + GN_EPS)).reshape(B_, S_, RW) * lnx_g + lnx_b
    r_h = r.reshape(B_, S_, H, N)
    k_sum = jnp.sum(k_dir, axis=2).reshape(B_, S_, H, N)
    coef = jnp.sum(r_h * k_sum * r_k.reshape(H, N), axis=-1, keepdims=True)
    bonus = (coef * v.reshape(B_, S_, H, N)).reshape(B_, S_, RW)
    return (yn + bonus) * g


def head_rms(t, g):
    tf = t.astype(jnp.float32)
    y = tf * lax.rsqrt(jnp.mean(tf * tf, axis=-1, keepdims=True) + NORM_EPS)
    return (y * g).astype(t.dtype)


def neighbourhood_attention(q, k, v, rpb):
    B_, S_, H, Dh = q.shape
    rows = S_ // GRID_W
    kh = min(WIN_ROWS_MAX, rows)
    kw = WIN_COLS
    scale = Dh ** -0.5
    qg = q.reshape(B_, rows, GRID_W, H, Dh).transpose(1, 0, 2, 3, 4)
    kg = k.reshape(B_, rows, GRID_W, H, Dh)
    vg = v.reshape(B_, rows, GRID_W, H, Dh)
    col = jnp.arange(GRID_W)
    col_start = jnp.clip(col - kw // 2, 0, GRID_W - kw)
    col_idx = col_start[:, None] + jnp.arange(kw)[None, :]
    col_off = col_idx - col[:, None] + (WIN_COLS - 1)

    def one_row(args):
        q_row, i = args
        r0 = jnp.clip(i - kh // 2, 0, rows - kh)
        k_nb = lax.dynamic_slice_in_dim(kg, r0, kh, axis=1)[:, :, col_idx]
        v_nb = lax.dynamic_slice_in_dim(vg, r0, kh, axis=1)[:, :, col_idx]
        row_off = r0 + jnp.arange(kh) - i + (WIN_ROWS_MAX - 1)
        bias = rpb[:, row_off][:, :, col_off].transpose(0, 2, 1, 3)
        s = jnp.einsum('bwhd,bkwlhd->bhwkl', q_row, k_nb).astype(jnp.float32) * scale + bias[None]
        p = jax.nn.softmax(s.reshape(B_, H, GRID_W, kh * kw), axis=-1)
        p = p.reshape(B_, H, GRID_W, kh, kw).astype(v.dtype)
        return jnp.einsum('bhwkl,bkwlhd->bwhd', p, v_nb)

    out = lax.map(one_row, (qg, jnp.arange(rows)))
    return out.transpose(1, 0, 2, 3, 4).reshape(B_, S_, H * Dh)


def setup_inputs(seed: int = 0) -> dict:
    key = jax.random.key(seed)
    ks = jax.random.split(key, 26)
    L, D, RW = DEPTH, D_MODEL, RWKV_WIDTH
    nrm = jax.random.normal
    f32 = jnp.float32
    return {
        "x": nrm(ks[0], (BATCH, SEQ, D), f32),
        "c": nrm(ks[1], (BATCH, D), f32),
        "ada_w": nrm(ks[2], (L, D, 6 * D), f32) * (0.5 * D ** -0.5),
        "ada_b": nrm(ks[3], (L, 6 * D), f32) * 0.01,
        "norm1_g": 1.0 + 0.05 * nrm(ks[4], (L, D), f32),
        "norm2_g": 1.0 + 0.05 * nrm(ks[5], (L, D), f32),
        "w_in": nrm(ks[6], (L, D, IN_COLS), f32) * D ** -0.5,
        "shift_mu": jax.random.uniform(ks[7], (L, 2, RWKV_COLS), f32, 0.0, 0.5),
        "w0": jax.random.uniform(ks[8], (L, N_DIRS, RW), f32, -6.0, 1.0),
        "w2": nrm(ks[9], (L, N_DIRS, DECAY_LORA, RW), f32) * (0.5 * DECAY_LORA ** -0.5),
        "a0": 0.5 * nrm(ks[10], (L, N_DIRS, RW), f32),
        "a2": nrm(ks[11], (L, N_DIRS, AAA_LORA, RW), f32) * (0.5 * AAA_LORA ** -0.5),
        "g2": nrm(ks[12], (L, GATE_LORA, RW), f32) * GATE_LORA ** -0.5,
        "k_k": 0.85 + 0.05 * nrm(ks[13], (L, RW), f32),
        "k_a": 1.0 + 0.05 * nrm(ks[14], (L, RW), f32),
        "r_k": 0.1 * nrm(ks[15], (L, RW), f32),
        "lnx_g": 1.0 + 0.05 * nrm(ks[16], (L, RW), f32),
        "lnx_b": 0.01 * nrm(ks[17], (L, RW), f32),
        "q_norm_g": 1.0 + 0.05 * nrm(ks[18], (L, HEAD_DIM), f32),
        "k_norm_g": 1.0 + 0.05 * nrm(ks[19], (L, HEAD_DIM), f32),
        "rpb": 0.1 * nrm(ks[20], (L, N_ATT_HEADS, 2 * WIN_ROWS_MAX - 1, 2 * WIN_COLS - 1), f32),
        "w_out": nrm(ks[21], (L, MIX_WIDTH, D), f32) * MIX_WIDTH ** -0.5,
        "ffn_w_in": nrm(ks[22], (L, D, 2 * D_FF), f32) * D ** -0.5,
        "ffn_w_out": nrm(ks[23], (L, D_FF, D), f32) * D_FF ** -0.5,
    }


def reference(x, c, ada_w, ada_b, norm1_g, norm2_g, w_in, shift_mu, w0, w2, a0, a2, g2,
              k_k, k_a, r_k, lnx_g, lnx_b, q_norm_g, k_norm_g, rpb, w_out, ffn_w_in, ffn_w_out):
    B_, S_, D = x.shape
    c_act = jax.nn.silu(c)
    h = x
    for l in range(DEPTH):
        mod = (c_act @ ada_w[l] + ada_b[l])[:, None, :]
        sh1, sc1, gt1, sh2, sc2, gt2 = jnp.split(mod, 6, axis=-1)
        u = rms_norm(h, norm1_g[l]) * (1.0 + sc1) + sh1
        z = u @ w_in[l]
        z_rwkv = centred_shift(z[..., :RWKV_COLS], shift_mu[l])
        y_rwkv = rwkv7_bidir(z_rwkv, w0[l], w2[l], a0[l], a2[l], g2[l], k_k[l], k_a[l],
                             r_k[l], lnx_g[l], lnx_b[l])
        za = z[..., RWKV_COLS:]
        q = head_rms(za[..., :ATT_WIDTH].reshape(B_, S_, N_ATT_HEADS, HEAD_DIM), q_norm_g[l])
        k = head_rms(za[..., ATT_WIDTH:2 * ATT_WIDTH].reshape(B_, S_, N_ATT_HEADS, HEAD_DIM), k_norm_g[l])
        v = za[..., 2 * ATT_WIDTH:].reshape(B_, S_, N_ATT_HEADS, HEAD_DIM)
        y_att = neighbourhood_attention(q, k, v, rpb[l])
        y = jnp.concatenate([y_rwkv.astype(h.dtype), y_att.astype(h.dtype)], axis=-1) @ w_out[l]
        h = h + gt1 * y
        u = rms_norm(h, norm2_g[l]) * (1.0 + sc2) + sh2
        gu = u @ ffn_w_in[l]
        h = h + gt2 * ((jax.nn.silu(gu[..., :D_FF]) * gu[..., D_FF:]) @ ffn_w_out[l])
    return h
```

```python
import numpy as np
import concourse.bass as bass
import concourse.mybir as mybir
from concourse.bass_utils import run_bass_kernel_spmd

F32 = mybir.dt.float32
BF16 = mybir.dt.bfloat16
AF = mybir.ActivationFunctionType
ALU = mybir.AluOpType
AX = mybir.AxisListType

D = 1024
GW = 64
RW = 512
RWKV_COLS = 1920
IN_COLS = 3456
DFF = 2816
NEG = -30000.0
DEBUG = False
PHASES = (1, 2, 3, 4)


class Buf:
    def __init__(self, name, t):
        self.name = name
        self.t = t
        self.st = {}
        self.dsem = None

    def __getitem__(self, k):
        return self.t[k]


class Sched:
    ENG = ["pe", "dve", "act", "pool", "sp"]

    def __init__(self, nc, stack):
        self.nc = nc
        self.stack = stack
        self.sems = {}
        self.count = {}
        self.known = {e: {} for e in self.ENG}
        self.lists = {e: [] for e in self.ENG}
        for e in self.ENG:
            self.sems[e] = stack.enter_context(nc.semaphore("s_" + e))
            self.count[e] = 0
        self.dsems = []
        for i in range(24):
            nm = "d%d" % i
            self.sems[nm] = stack.enter_context(nc.semaphore("s_" + nm))
            self.count[nm] = 0
            self.dsems.append(nm)
        self.dnext = 0
        self.nbuf = 0

    def sb(self, stack, shape, dt, name=None):
        self.nbuf += 1
        name = (name or "b") + "_%d" % self.nbuf
        return Buf(name, stack.enter_context(self.nc.sbuf_tensor(name, list(shape), dt)))

    def ps(self, stack, name=None):
        self.nbuf += 1
        name = (name or "p") + "_%d" % self.nbuf
        return Buf(name, stack.enter_context(self.nc.psum_tensor(name, [128, 512], F32)))

    def dsem_for(self, buf):
        if buf.dsem is None:
            buf.dsem = self.dsems[self.dnext % len(self.dsems)]
            self.dnext += 1
        return buf.dsem

    @staticmethod
    def _norm(x):
        return x if isinstance(x, tuple) else (x, None)

    def _deps(self, reads, writes):
        deps = {}

        def add(tok):
            if tok is None:
                return
            s, v = tok
            if deps.get(s, 0) < v:
                deps[s] = v

        for item in reads:
            b, p = self._norm(item)
            for q, st in b.st.items():
                if p is None or q is None or p == q:
                    add(st[0])
        for item in writes:
            b, p = self._norm(item)
            for q, st in b.st.items():
                if p is None or q is None or p == q:
                    add(st[0])
                    for s, v in st[1].items():
                        add((s, v))
        return deps

    def _commit(self, tok, reads, writes):
        for item in reads:
            b, p = self._norm(item)
            st = b.st.setdefault(p, [None, {}])
            if st[1].get(tok[0], 0) < tok[1]:
                st[1][tok[0]] = tok[1]
        for item in writes:
            b, p = self._norm(item)
            if p is None:
                b.st = {None: [tok, {}]}
            else:
                b.st[p] = [tok, {}]

    def _waits(self, eng, deps):
        w = []
        kn = self.known[eng]
        for s, v in deps.items():
            if s == eng and eng == "pe":
                continue
            if kn.get(s, 0) < v:
                kn[s] = v
                w.append((s, v))
        return w

    def op(self, eng, fn, reads=(), writes=()):
        deps = self._deps(reads, writes)
        w = self._waits(eng, deps)
        self.count[eng] += 1
        tok = (eng, self.count[eng])
        self.lists[eng].append((w, fn, eng, 1))
        self._commit(tok, reads, writes)

    def dma(self, q, out_ap, in_ap, reads=(), writes=(), sbuf=None, **kw):
        ds = self.dsem_for(sbuf)
        deps = self._deps(reads, writes)
        deps[ds] = max(deps.get(ds, 0), self.count[ds])
        w = self._waits(q, deps)
        self.count[ds] += 16
        tok = (ds, self.count[ds])
        self.lists[q].append((w, lambda e: e.dma_start(out=out_ap, in_=in_ap, **kw), ds, 16))
        self._commit(tok, reads, writes)

    def barrier(self):
        for e in self.ENG:
            deps = {s: c for s, c in self.count.items() if c > 0 and s != e}
            w = self._waits(e, deps)
            if w:
                self.lists[e].append((w, None, None, 0))

    def emit(self):
        self.barrier()
        nc = self.nc
        sems = self.sems
        lists = self.lists

        def run(e, items):
            for w, fn, s, inc in items:
                for ws, wv in w:
                    e.wait_ge(sems[ws], wv)
                if fn is not None:
                    fn(e).then_inc(sems[s], inc)

        with nc.Block() as block:
            @block.tensor
            def _(e):
                run(e, lists["pe"])

            @block.vector
            def _(e):
                run(e, lists["dve"])

            @block.scalar
            def _(e):
                run(e, lists["act"])

            @block.gpsimd
            def _(e):
                run(e, lists["pool"])

            @block.sync
            def _(e):
                run(e, lists["sp"])


class Rot:
    def __init__(self, bufs):
        self.bufs = bufs
        self.i = 0

    def next(self):
        b = self.bufs[self.i % len(self.bufs)]
        self.i += 1
        return b


def att_tiles(rows):
    sigs = []
    tiles = []
    for j in range(rows // 2):
        kb = min(max(2 * j - 4, 0), rows - 9)
        i0 = 2 * j
        r00 = min(max(i0 - 4, 0), rows - 8)
        r01 = min(max(i0 + 1 - 4, 0), rows - 8)
        sig = (i0 - kb, r00 - kb, r01 - kb)
        if sig not in sigs:
            sigs.append(sig)
        tiles.append((kb, sigs.index(sig)))
    return tiles, sigs


def build_bias(rpb_l, sigs):
    nv = len(sigs)
    out = np.full((nv, 5 * 128, 8, 128), NEG, np.float32)
    qc = np.arange(64)
    cs = np.clip(qc - 8, 0, GW - 16)
    for vi, (di, d0, d1) in enumerate(sigs):
        for ri in range(2):
            irel = di + ri
            r0rel = d0 if ri == 0 else d1
            for kr in range(r0rel, r0rel + 8):
                ro = kr - irel + 7
                for q in range(64):
                    kc = np.arange(cs[q], cs[q] + 16)
                    co = kc - q + 15
                    out[vi, kr * 64 + kc, :, ri * 64 + q] = rpb_l[:, ro, co].T
    return out.reshape(nv, 5, 128, 8, 128).transpose(0, 2, 1, 3, 4).copy()


def build_nc(S, NL, NV):
    from contextlib import ExitStack
    nc = bass.Bass("TRN2", target_bir_lowering=False)
    rows = S // GW
    NT = S // 128
    tiles, sigs = att_tiles(rows)
    assert len(sigs) == NV

    def din(name, shape, dt=F32):
        return nc.dram_tensor(name, list(shape), dt, kind="ExternalInput").ap()

    def dscr(name, shape, dt):
        if DEBUG:
            return nc.dram_tensor(name, list(shape), dt, kind="ExternalOutput").ap()
        return nc.dram_tensor(name, list(shape), dt).ap()

    xT = din("xT", [D, S])
    cT = din("cT", [128, 8])
    ada_w = din("ada_w", [NL, D, 6 * D])
    ada_bT = din("ada_bT", [NL, 128, 48])
    n1g = din("n1g", [NL, 128, 8])
    n2g = din("n2g", [NL, 128, 8])
    w_in = din("w_in", [NL, D, IN_COLS])
    muT = din("muT", [NL, 128, 2, 15])
    w0T = din("w0T", [NL, 128, 2, 4])
    a0T = din("a0T", [NL, 128, 2, 4])
    w2Z = din("w2Z", [NL, 128, 2, RW])
    a2Z = din("a2Z", [NL, 128, 2, RW])
    g2 = din("g2", [NL, 128, RW])
    vecT = din("vecT", [NL, 128, 5, 4])
    qkg = din("qkg", [NL, 128, 2])
    biasT = din("biasT", [NL, NV, 128, 5, 8, 128])
    w_out = din("w_out", [NL, D, D])
    f_in = din("f_in", [NL, D, 2 * DFF])
    f_out = din("f_out", [NL, DFF, D])
    outT = nc.dram_tensor("outT", [D, S], F32, kind="ExternalOutput").ap()

    hT = dscr("hT", [D, S], F32)
    zT = dscr("zT", [RWKV_COLS, S], BF16)
    QT = dscr("QT", [RW, S], BF16)
    KT = dscr("KT", [RW, S], BF16)
    Vtm = dscr("Vtm", [S, 520], BF16)
    ymT = dscr("ymT", [D, S], BF16)
    yfw = dscr("yfw", [S, RW], F32)
    class DB:
        pass
    dram = {n: Buf(n, None) for n in ["hT", "zT", "QT", "KT", "Vtm", "ymT", "yfw", "outT"]}

    with ExitStack() as gs:
        Sx = Sched(nc, gs)
        psb = [Sx.ps(gs) for _ in range(8)]
        PS = Rot(psb)
        PS6 = Rot(psb[2:])
        identf = Sx.sb(gs, [128, 128], F32, "identf")
        identb = Sx.sb(gs, [128, 128], BF16, "identb")
        onesb = Sx.sb(gs, [128, 128], BF16, "onesb")
        blkf = Sx.sb(gs, [128, 128], F32, "blkf")
        blkb = Sx.sb(gs, [128, 128], BF16, "blkb")
        mLs = Sx.sb(gs, [128, 128], F32, "mLs")
        mLi = Sx.sb(gs, [128, 128], F32, "mLi")
        mUs = Sx.sb(gs, [128, 128], F32, "mUs")
        mUi = Sx.sb(gs, [128, 128], F32, "mUi")
        modT = Sx.sb(gs, [128, NL, 48], F32, "modT")
        cact = Sx.sb(gs, [128, 8], F32, "cact")
        lay = Sx.sb(gs, [128, 64], F32, "lay")
        tmpc = Sx.sb(gs, [128, 16], F32, "tmpc")

        Sx.op("pool", lambda e: e.memset(identf[:], 0.0), writes=[identf])
        Sx.op("pool", lambda e: e.affine_select(out=identf[:], in_=identf[:], pattern=[[-1, 128]],
                                                compare_op=ALU.not_equal, fill=1.0, base=0, channel_multiplier=1),
              reads=[identf], writes=[identf])
        Sx.op("pool", lambda e: e.tensor_copy(identb[:], identf[:]), reads=[identf], writes=[identb])
        Sx.op("pool", lambda e: e.memset(onesb[:], 1.0), writes=[onesb])
        Sx.op("pool", lambda e: e.memset(blkf[:], 0.0), writes=[blkf])
        Sx.op("pool", lambda e: e.memset(blkf[0:64, 0:64], 1.0), reads=[blkf], writes=[blkf])
        Sx.op("pool", lambda e: e.memset(blkf[64:128, 64:128], 1.0), reads=[blkf], writes=[blkf])
        Sx.op("pool", lambda e: e.tensor_copy(blkb[:], blkf[:]), reads=[blkf], writes=[blkb])
        for mb, cmp, stp, cm in [(mLs, ALU.is_gt, -1, 1), (mLi, ALU.is_ge, -1, 1), (mUs, ALU.is_gt, 1, -1), (mUi, ALU.is_ge, 1, -1)]:
            Sx.op("pool", lambda e, mb=mb: e.memset(mb[:], 1.0), writes=[mb])
            Sx.op("pool", lambda e, mb=mb, cmp=cmp, stp=stp, cm=cm: e.affine_select(
                out=mb[:], in_=mb[:], pattern=[[stp, 128]], compare_op=cmp, fill=0.0, base=0, channel_multiplier=cm),
                reads=[mb], writes=[mb])

        Sx.dma("sp", cact[:], cT[:, :], writes=[cact], sbuf=cact)
        Sx.op("act", lambda e: e.activation(out=cact[:], in_=cact[:], func=AF.Silu), reads=[cact], writes=[cact])
        with ExitStack() as ls:
            awp = Rot([Sx.sb(ls, [128, 8, 512], F32, "aw") for _ in range(2)])
            adb = Sx.sb(ls, [128, NL, 48], F32, "adb")
            Sx.dma("sp", adb[:], ada_bT.rearrange("l p c -> p l c"), writes=[adb], sbuf=adb)
            for l in range(NL):
                pm = PS.next()
                for cb in range(12):
                    aw = awp.next()
                    Sx.dma("sp", aw[:], ada_w[l, :, cb * 512:(cb + 1) * 512].rearrange("(kc p) f -> p kc f", p=128),
                           writes=[aw], sbuf=aw)
                    for f4 in range(4):
                        fc = cb * 4 + f4
                        for kc in range(8):
                            Sx.op("pe", lambda e, aw=aw, kc=kc, f4=f4, fc=fc, pm=pm: e.matmul(
                                pm[:, fc:fc + 1], aw[:, kc, f4 * 128:(f4 + 1) * 128], cact[:, kc:kc + 1],
                                start=(kc == 0), stop=(kc == 7)), reads=[aw, cact], writes=[pm])
                Sx.op("dve", lambda e, pm=pm, l=l: e.tensor_tensor(out=modT[:, l, :], in0=pm[:, 0:48], in1=adb[:, l, :],
                                                                   op=ALU.add), reads=[pm, adb], writes=[modT])
            Sx.barrier()

        def layer_cols(l):
            Sx.dma("sp", tmpc[:, 0:8], n1g[l], writes=[tmpc], sbuf=tmpc)
            Sx.dma("sp", tmpc[:, 8:16], n2g[l], writes=[tmpc], sbuf=tmpc)
            Sx.op("dve", lambda e: e.scalar_tensor_tensor(out=lay[:, 0:8], in0=modT[:, l, 8:16], scalar=1.0,
                                                          in1=tmpc[:, 0:8], op0=ALU.add, op1=ALU.mult),
                  reads=[modT, tmpc], writes=[lay])
            Sx.op("dve", lambda e: e.scalar_tensor_tensor(out=lay[:, 24:32], in0=modT[:, l, 32:40], scalar=1.0,
                                                          in1=tmpc[:, 8:16], op0=ALU.add, op1=ALU.mult),
                  reads=[modT, tmpc, lay], writes=[lay])
            for dst, src in [(8, 0), (16, 16), (32, 24), (40, 40)]:
                Sx.op("dve", lambda e, dst=dst, src=src: e.tensor_copy(lay[:, dst:dst + 8], modT[:, l, src:src + 8]),
                      reads=[modT, lay], writes=[lay])

        def rmsnorm(stk, hb, ub, n, gcol, shcol, sq, rstd, tmpr):
            Sx.op("act", lambda e: e.activation(out=sq[:], in_=hb[:], func=AF.Square), reads=[hb], writes=[sq])
            pm = PS.next()
            for kc in range(8):
                Sx.op("pe", lambda e, kc=kc: e.matmul(pm[:, 0:n], onesb[:], sq[:, kc, :], start=(kc == 0), stop=(kc == 7)),
                      reads=[sq, onesb], writes=[pm])
            Sx.op("act", lambda e: e.activation(out=rstd[:], in_=pm[:, 0:n], func=AF.Sqrt, scale=1.0 / D, bias=epsc[:, 0:1]),
                  reads=[pm, epsc], writes=[rstd])
            Sx.op("dve", lambda e: e.reciprocal(rstd[:], rstd[:]), reads=[rstd], writes=[rstd])
            for kc in range(8):
                tm = tmpr.next()
                Sx.op("dve", lambda e, kc=kc, tm=tm: e.scalar_tensor_tensor(
                    out=tm[:], in0=hb[:, kc, :], scalar=lay[:, gcol + kc:gcol + kc + 1], in1=rstd[:],
                    op0=ALU.mult, op1=ALU.mult), reads=[hb, lay, rstd], writes=[tm])
                Sx.op("act", lambda e, kc=kc, tm=tm: e.activation(out=ub[:, kc, :], in_=tm[:], func=AF.Identity,
                                                                   bias=lay[:, shcol + kc:shcol + kc + 1]),
                      reads=[tm, lay], writes=[(ub, kc)])

        epsc = Sx.sb(gs, [128, 4], F32, "epsc")
        Sx.op("pool", lambda e: e.memset(epsc[:, 0:1], 1e-6), writes=[epsc])
        Sx.op("pool", lambda e: e.memset(epsc[:, 1:2], 64e-5), reads=[epsc], writes=[epsc])
        Sx.op("pool", lambda e: e.memset(epsc[:, 2:3], 1e-24), reads=[epsc], writes=[epsc])

        evac_flip = [0]

        def evac(out_ap, in_ap, reads, writes):
            evac_flip[0] ^= 1
            if evac_flip[0]:
                Sx.op("dve", lambda e: e.tensor_copy(out_ap, in_ap), reads=reads, writes=writes)
            else:
                Sx.op("act", lambda e: e.activation(out=out_ap, in_=in_ap, func=AF.Copy), reads=reads, writes=writes)

        def P1(l):
            hsrc, hbuf = (xT, None) if l == 0 else (hT, dram["hT"])
            with ExitStack() as ls:
                win = Sx.sb(ls, [128, 8, IN_COLS], BF16, "win")
                hp = Rot([Sx.sb(ls, [128, 8, 512], F32, "h") for _ in range(2)])
                sq = Sx.sb(ls, [128, 8, 512], BF16, "sq")
                ub = Sx.sb(ls, [128, 8, 512], BF16, "u")
                rstd = Sx.sb(ls, [128, 512], F32, "rstd")
                tmpr = Rot([Sx.sb(ls, [128, 512], F32, "tm") for _ in range(2)])
                zo = Rot([Sx.sb(ls, [128, 512], BF16, "zo") for _ in range(4)])
                sqz = Rot([Sx.sb(ls, [128, 512], BF16, "sqz") for _ in range(2)])
                rsq = Rot([Sx.sb(ls, [128, 512], F32, "rsq") for _ in range(2)])
                gq = Sx.sb(ls, [128, 2], F32, "gq")
                vop = Rot([Sx.sb(ls, [128, 8, 65], BF16, "vo") for _ in range(2)])
                for vo in vop.bufs:
                    Sx.op("pool", lambda e, vo=vo: e.memset(vo[:], 1.0), writes=[vo])
                Sx.dma("sp", gq[:], qkg[l], writes=[gq], sbuf=gq)
                for kc in range(8):
                    Sx.dma("pool", win[:, kc, :], w_in[l, kc * 128:(kc + 1) * 128, :], writes=[(win, kc)], sbuf=win,
                           max_dma_last_dim=4096)
                nb = S // 512

                def load(b):
                    hb = hp.next()
                    Sx.dma("sp", hb[:], hsrc[:, b * 512:(b + 1) * 512].rearrange("(kc p) t -> p kc t", p=128),
                           reads=[hbuf] if hbuf else [], writes=[hb], sbuf=hb)
                    return hb
                nxt = load(0)
                for b in range(nb):
                    hb = nxt
                    if b + 1 < nb:
                        nxt = load(b + 1)
                    if l == 0:
                        Sx.dma("pool", hT[:, b * 512:(b + 1) * 512].rearrange("(kc p) t -> p kc t", p=128), hb[:],
                               reads=[hb], writes=[(dram["hT"], b)], sbuf=hb)
                    rmsnorm(ls, hb, ub, 512, 0, 8, sq, rstd, tmpr)
                    tsl = slice(b * 512, (b + 1) * 512)
                    for fc in range(23):
                        pm = PS.next()
                        for kc in range(8):
                            Sx.op("pe", lambda e, kc=kc, fc=fc, pm=pm: e.matmul(
                                pm[:, :], win[:, kc, fc * 128:(fc + 1) * 128], ub[:, kc, :], start=(kc == 0), stop=(kc == 7)),
                                reads=[win, ub], writes=[pm])
                        z = zo.next()
                        if fc < 15:
                            evac(z[:], pm[:, :], [pm], [z])
                            Sx.dma("pool", zT[fc * 128:(fc + 1) * 128, tsl], z[:], reads=[z], writes=[(dram["zT"], (fc, b))], sbuf=z)
                        else:
                            isq = fc < 19
                            s2 = sqz.next()
                            r2 = rsq.next()
                            Sx.op("act", lambda e, s2=s2, pm=pm: e.activation(out=s2[:], in_=pm[:, :], func=AF.Square),
                                  reads=[pm], writes=[s2])
                            p2 = PS.next()
                            Sx.op("pe", lambda e, p2=p2, s2=s2: e.matmul(p2[:, :], blkb[:], s2[:], start=True, stop=True),
                                  reads=[s2, blkb], writes=[p2])
                            Sx.op("act", lambda e, p2=p2, r2=r2: e.activation(out=r2[:], in_=p2[:, :], func=AF.Sqrt,
                                                                             scale=1.0 / 64, bias=epsc[:, 0:1]),
                                  reads=[p2, epsc], writes=[r2])
                            Sx.op("dve", lambda e, r2=r2: e.reciprocal(r2[:], r2[:]), reads=[r2], writes=[r2])
                            gi = 0 if isq else 1
                            Sx.op("dve", lambda e, z=z, pm=pm, r2=r2, gi=gi: e.scalar_tensor_tensor(
                                out=z[:], in0=pm[:, :], scalar=gq[:, gi:gi + 1], in1=r2[:], op0=ALU.mult, op1=ALU.mult),
                                reads=[pm, gq, r2], writes=[z])
                            if isq:
                                Sx.dma("pool", QT[(fc - 15) * 128:(fc - 14) * 128, tsl], z[:], reads=[z],
                                       writes=[(dram["QT"], (fc, b))], sbuf=z)
                            else:
                                Sx.dma("pool", KT[(fc - 19) * 128:(fc - 18) * 128, tsl], z[:], reads=[z],
                                       writes=[(dram["KT"], (fc, b))], sbuf=z)
                    for sub in range(4):
                        pm = PS.next()
                        for kc in range(8):
                            Sx.op("pe", lambda e, kc=kc, sub=sub, pm=pm: e.matmul(
                                pm[:, :], ub[:, kc, sub * 128:(sub + 1) * 128], win[:, kc, 2944:3456],
                                start=(kc == 0), stop=(kc == 7)), reads=[win, ub], writes=[pm])
                        z = vop.next()
                        evac(z[:, :, 0:64], pm[:, :].rearrange("p (h d) -> p h d", d=64), [pm], [z])
                        Sx.dma("pool", Vtm[b * 512 + sub * 128: b * 512 + (sub + 1) * 128, :], z[:].rearrange("p h d -> p (h d)"), reads=[z],
                               writes=[(dram["Vtm"], (b, sub))], sbuf=z)
                Sx.barrier()

        def P2(l):
            with ExitStack() as ls:
                ktp = Rot([Sx.sb(ls, [128, 4, 576], BF16, "kt") for _ in range(2)])
                qzp = Rot([Sx.sb(ls, [128, 8, 128], BF16, "qz") for _ in range(2)])
                vwp = Rot([Sx.sb(ls, [128, 5, 8, 65], BF16, "vw") for _ in range(2)])
                bias = Sx.sb(ls, [128, 5, 8, 128], F32, "bias")
                sTp = Rot([Sx.sb(ls, [128, 5, 128], F32, "sT") for _ in range(2)])
                pTp = Rot([Sx.sb(ls, [128, 5, 128], BF16, "pT") for _ in range(2)])
                yap = Rot([Sx.sb(ls, [128, 512], BF16, "ya") for _ in range(2)])
                yTp = Rot([Sx.sb(ls, [128, 4, 128], BF16, "yT") for _ in range(2)])
                rs = Sx.sb(ls, [128, 8], F32, "rs")
                for qz in qzp.bufs:
                    Sx.op("pool", lambda e, qz=qz: e.memset(qz[:], 0.0), writes=[qz])
                QTv = QT.rearrange("(pr par d) t -> par d pr t", par=2, d=64)

                def load(j):
                    kb, var = tiles[j]
                    kt, qz, vw = ktp.next(), qzp.next(), vwp.next()
                    Sx.dma("sp", kt[:], KT[:, kb * 64:kb * 64 + 576].rearrange("(pr p) t -> p pr t", p=128),
                           reads=[dram["KT"]], writes=[kt], sbuf=kt)
                    qzv = qz[:].rearrange("p (pr par) q -> p par pr q", par=2)
                    for par in range(2):
                        Sx.dma("sp", qzv[par * 64:(par + 1) * 64, par, :, :], QTv[par, :, :, j * 128:(j + 1) * 128],
                               reads=[dram["QT"]], writes=[qz], sbuf=qz)
                    Sx.dma("sp", vw[:, 0:4, :, :].rearrange("p c h d -> p c (h d)"),
                           Vtm[kb * 64:kb * 64 + 512, :].rearrange("(c p) f -> p c f", p=128),
                           reads=[dram["Vtm"]], writes=[vw], sbuf=vw)
                    Sx.dma("sp", vw[0:64, 4, :, :].rearrange("p h d -> p (h d)"),
                           Vtm[kb * 64 + 512:kb * 64 + 576, :],
                           reads=[dram["Vtm"]], writes=[vw], sbuf=vw)
                    return kt, qz, vw
                cur_var = -1
                nxt = load(0)
                for j in range(NT):
                    kt, qz, vw = nxt
                    if j + 1 < NT:
                        nxt = load(j + 1)
                    kb, var = tiles[j]
                    if var != cur_var:
                        cur_var = var
                        Sx.dma("sp", bias[:], biasT[l, var], writes=[bias], sbuf=bias)
                    poA, poB = psb[0], psb[1]
                    for h in range(8):
                        pr = h // 2
                        pA, pB = PS6.next(), PS6.next()
                        for c in range(4):
                            Sx.op("pe", lambda e, c=c, pA=pA, kt=kt, qz=qz, pr=pr, h=h: e.matmul(
                                pA[:, c * 128:(c + 1) * 128], kt[:, pr, c * 128:(c + 1) * 128], qz[:, h, :],
                                start=True, stop=True), reads=[kt, qz], writes=[pA])
                        Sx.op("pe", lambda e, pB=pB, kt=kt, qz=qz, pr=pr, h=h: e.matmul(
                            pB[0:64, 0:128], kt[:, pr, 512:576], qz[:, h, :], start=True, stop=True),
                            reads=[kt, qz], writes=[pB])
                        sT, pT = sTp.next(), pTp.next()
                        Sx.op("dve", lambda e, sT=sT, pA=pA, h=h: e.scalar_tensor_tensor(
                            out=sT[:, 0:4, :], in0=pA[:, :].rearrange("p (c q) -> p c q", c=4), scalar=0.125,
                            in1=bias[:, 0:4, h, :], op0=ALU.mult, op1=ALU.add), reads=[pA, bias], writes=[(sT, 0)])
                        Sx.op("dve", lambda e, sT=sT, pB=pB, h=h: e.scalar_tensor_tensor(
                            out=sT[0:64, 4, :], in0=pB[0:64, 0:128], scalar=0.125,
                            in1=bias[0:64, 4, h, :], op0=ALU.mult, op1=ALU.add), reads=[pB, bias], writes=[(sT, 1)])
                        Sx.op("act", lambda e, sT=sT, pT=pT: e.activation(out=pT[:, 0:4, :], in_=sT[:, 0:4, :], func=AF.Exp),
                              reads=[(sT, 0)], writes=[(pT, 0)])
                        Sx.op("act", lambda e, sT=sT, pT=pT: e.activation(out=pT[0:64, 4, :], in_=sT[0:64, 4, :], func=AF.Exp),
                              reads=[(sT, 1)], writes=[(pT, 1)])
                        po = poA if h < 4 else poB
                        hh = h % 4
                        for c in range(4):
                            Sx.op("pe", lambda e, c=c, po=po, pT=pT, vw=vw, h=h, hh=hh: e.matmul(
                                po[:, hh * 65:(hh + 1) * 65], pT[:, c, :], vw[:, c, h, :], start=(c == 0), stop=False),
                                reads=[(pT, 0), vw], writes=[po])
                        Sx.op("pe", lambda e, po=po, pT=pT, vw=vw, h=h, hh=hh: e.matmul(
                            po[:, hh * 65:(hh + 1) * 65], pT[0:64, 4, :], vw[0:64, 4, h, :], start=False, stop=True),
                            reads=[(pT, 1), vw], writes=[po])
                    ya = yap.next()
                    for hf, po in enumerate([poA, poB]):
                        pv = po[:, 0:260].rearrange("p (h d) -> p h d", d=65)
                        Sx.op("dve", lambda e, pv=pv, hf=hf: e.reciprocal(rs[:, hf * 4:(hf + 1) * 4].unsqueeze(2), pv[:, :, 64:65]),
                              reads=[po], writes=[rs])
                        Sx.op("dve", lambda e, pv=pv, hf=hf, ya=ya: e.tensor_tensor(
                            out=ya[:, hf * 256:(hf + 1) * 256].rearrange("p (h d) -> p h d", d=64), in0=pv[:, :, 0:64],
                            in1=rs[:, hf * 4:(hf + 1) * 4].unsqueeze(2).to_broadcast([128, 4, 64]), op=ALU.mult),
                            reads=[po, rs], writes=[ya])
                    pt = PS6.next()
                    ptb = pt[:].bitcast(BF16)
                    for pr in range(4):
                        Sx.op("pe", lambda e, pr=pr, ptb=ptb, ya=ya: e.transpose(ptb[:, pr * 128:(pr + 1) * 128],
                                                                                ya[:, pr * 128:(pr + 1) * 128], identb[:]),
                              reads=[ya, identb], writes=[pt])
                    yT = yTp.next()
                    evac(yT[:].rearrange("p a q -> p (a q)"), ptb[:, 0:512], [pt], [yT])
                    Sx.dma("pool", ymT[512:1024, j * 128:(j + 1) * 128].rearrange("(pr p) q -> p pr q", p=128), yT[:],
                           reads=[yT], writes=[(dram["ymT"], ("a", j))], sbuf=yT)
                Sx.barrier()

        def P4(l, last):
            NB = 256
            hdst, hdb = (outT, dram["outT"]) if last else (hT, dram["hT"])
            with ExitStack() as ls:
                wo = Sx.sb(ls, [128, 8, D], BF16, "wo")
                wf1 = Sx.sb(ls, [128, 8, 2 * DFF], BF16, "wf1")
                wf2 = Sx.sb(ls, [128, 22, D], BF16, "wf2")
                hp = Rot([Sx.sb(ls, [128, 8, NB], F32, "h") for _ in range(2)])
                ymp = Rot([Sx.sb(ls, [128, 8, NB], BF16, "ym") for _ in range(2)])
                sq = Sx.sb(ls, [128, 8, NB], BF16, "sq")
                ub = Sx.sb(ls, [128, 8, NB], BF16, "u")
                hid = Sx.sb(ls, [128, 22, NB], BF16, "hid")
                rstd = Sx.sb(ls, [128, NB], F32, "rstd")
                tmpr = Rot([Sx.sb(ls, [128, NB], F32, "tm") for _ in range(2)])
                silp = Rot([Sx.sb(ls, [128, NB], F32, "sil") for _ in range(2)])
                for kc in range(8):
                    Sx.dma("pool", wo[:, kc, :], w_out[l, kc * 128:(kc + 1) * 128, :], writes=[(wo, kc)], sbuf=wo)
                for kc in range(8):
                    Sx.dma("pool", wf1[:, kc, :], f_in[l, kc * 128:(kc + 1) * 128, :], writes=[(wf1, kc)], sbuf=wf1,
                           max_dma_last_dim=4096)
                for j in range(22):
                    Sx.dma("pool", wf2[:, j, :], f_out[l, j * 128:(j + 1) * 128, :], writes=[(wf2, j)], sbuf=wf2)
                nb = S // NB

                def load(b):
                    hb, ym = hp.next(), ymp.next()
                    Sx.dma("sp", hb[:], hT[:, b * NB:(b + 1) * NB].rearrange("(kc p) t -> p kc t", p=128),
                           reads=[(dram["hT"], b)], writes=[hb], sbuf=hb)
                    Sx.dma("sp", ym[:], ymT[:, b * NB:(b + 1) * NB].rearrange("(kc p) t -> p kc t", p=128),
                           reads=[dram["ymT"]], writes=[ym], sbuf=ym)
                    return hb, ym
                nxt = load(0)
                for b in range(nb):
                    hb, ym = nxt
                    if b + 1 < nb:
                        nxt = load(b + 1)
                    for fc in range(8):
                        pm = PS.next()
                        for kc in range(8):
                            Sx.op("pe", lambda e, kc=kc, fc=fc, pm=pm, ym=ym: e.matmul(
                                pm[:, 0:NB], wo[:, kc, fc * 128:(fc + 1) * 128], ym[:, kc, :], start=(kc == 0), stop=(kc == 7)),
                                reads=[wo, ym], writes=[pm])
                        Sx.op("dve", lambda e, fc=fc, pm=pm, hb=hb: e.scalar_tensor_tensor(
                            out=hb[:, fc, :], in0=pm[:, 0:NB], scalar=lay[:, 16 + fc:17 + fc], in1=hb[:, fc, :],
                            op0=ALU.mult, op1=ALU.add), reads=[pm, lay, hb], writes=[hb])
                    rmsnorm(ls, hb, ub, NB, 24, 32, sq, rstd, tmpr)
                    for j in range(22):
                        pg, pu = PS.next(), PS.next()
                        for kc in range(8):
                            Sx.op("pe", lambda e, kc=kc, j=j, pg=pg: e.matmul(
                                pg[:, 0:NB], wf1[:, kc, j * 128:(j + 1) * 128], ub[:, kc, :], start=(kc == 0), stop=(kc == 7)),
                                reads=[wf1, ub], writes=[pg])
                        for kc in range(8):
                            Sx.op("pe", lambda e, kc=kc, j=j, pu=pu: e.matmul(
                                pu[:, 0:NB], wf1[:, kc, DFF + j * 128:DFF + (j + 1) * 128], ub[:, kc, :],
                                start=(kc == 0), stop=(kc == 7)), reads=[wf1, ub], writes=[pu])
                        sl = silp.next()
                        Sx.op("act", lambda e, sl=sl, pg=pg: e.activation(out=sl[:], in_=pg[:, 0:NB], func=AF.Silu),
                              reads=[pg], writes=[sl])
                        Sx.op("dve", lambda e, sl=sl, pu=pu, j=j: e.tensor_tensor(out=hid[:, j, :], in0=sl[:], in1=pu[:, 0:NB],
                                                                              op=ALU.mult), reads=[sl, pu], writes=[(hid, j)])
                    for fc in range(8):
                        pm = PS.next()
                        for j in range(22):
                            Sx.op("pe", lambda e, j=j, fc=fc, pm=pm: e.matmul(
                                pm[:, 0:NB], wf2[:, j, fc * 128:(fc + 1) * 128], hid[:, j, :], start=(j == 0), stop=(j == 21)),
                                reads=[wf2, hid], writes=[pm])
                        Sx.op("dve", lambda e, fc=fc, pm=pm, hb=hb: e.scalar_tensor_tensor(
                            out=hb[:, fc, :], in0=pm[:, 0:NB], scalar=lay[:, 40 + fc:41 + fc], in1=hb[:, fc, :],
                            op0=ALU.mult, op1=ALU.add), reads=[pm, lay, hb], writes=[hb])
                    Sx.dma("pool", hdst[:, b * NB:(b + 1) * NB].rearrange("(kc p) t -> p kc t", p=128), hb[:],
                           reads=[hb], writes=[(hdb, b)], sbuf=hb)
                Sx.barrier()

        P3 = make_P3(locals())

        for l in range(NL):
            layer_cols(l)
            if 1 in PHASES:
                P1(l)
            if 2 in PHASES:
                P2(l)
            if 3 in PHASES:
                P3(l)
            if 4 in PHASES:
                P4(l, l == NL - 1)
        Sx.emit()
    return nc


def make_P3(env):
    Sx = env["Sx"]; PS = env["PS"]; nc = env["nc"]; S = env["S"]; NT = env["NT"]; dram = env["dram"]
    zT = env["zT"]; Vtm = env["Vtm"]; ymT = env["ymT"]; yfw = env["yfw"]
    muT = env["muT"]; w0T = env["w0T"]; a0T = env["a0T"]; w2Z = env["w2Z"]; a2Z = env["a2Z"]; g2 = env["g2"]
    vecT = env["vecT"]
    identf = env["identf"]; identb = env["identb"]; blkf = env["blkf"]; blkb = env["blkb"]
    mLs = env["mLs"]; mLi = env["mLi"]; mUs = env["mUs"]; mUi = env["mUi"]; epsc = env["epsc"]
    evac = env["evac"]
    from contextlib import ExitStack
    MID = 63
    C1 = float(np.exp(-0.5))

    def P3(l):
        with ExitStack() as ls:
            sb = lambda shape, dt, name: Sx.sb(ls, shape, dt, name)
            mu = sb([128, 3, 15], F32, "mu")
            w0c = sb([128, 2, 4], F32, "w0c"); a0c = sb([128, 2, 4], F32, "a0c")
            w2s = sb([128, 2, RW], BF16, "w2s"); a2s = sb([128, 2, RW], BF16, "a2s"); g2s = sb([128, RW], BF16, "g2s")
            vec = sb([128, 5, 4], F32, "vec")
            oneka = sb([128, 4], F32, "oneka")
            ones128 = sb([128, 128], F32, "ones128")
            Sx.dma("sp", mu[:, 0:2, :], muT[l], writes=[mu], sbuf=mu)
            Sx.dma("sp", w0c[:], w0T[l], writes=[w0c], sbuf=w0c)
            Sx.dma("sp", a0c[:], a0T[l], writes=[a0c], sbuf=a0c)
            Sx.dma("sp", vec[:], vecT[l], writes=[vec], sbuf=vec)
            Sx.dma("pool", w2s[:], w2Z[l], writes=[w2s], sbuf=w2s)
            Sx.dma("pool", a2s[:], a2Z[l], writes=[a2s], sbuf=a2s)
            Sx.dma("pool", g2s[:], g2[l], writes=[g2s], sbuf=g2s)
            Sx.op("pool", lambda e: e.memset(ones128[:], 1.0), writes=[ones128])
            Sx.op("dve", lambda e: e.tensor_tensor(out=mu[:, 2, :], in0=mu[:, 0, :], in1=mu[:, 1, :], op=ALU.add),
                  reads=[mu], writes=[mu])
            Sx.op("dve", lambda e: e.tensor_scalar(out=mu[:, 2, :], in0=mu[:, 2, :], scalar1=-1.0, scalar2=1.0,
                                                   op0=ALU.mult, op1=ALU.add), reads=[mu], writes=[mu])
            Sx.op("dve", lambda e: e.tensor_scalar(out=oneka[:], in0=vec[:, 1, :], scalar1=-1.0, scalar2=1.0,
                                                   op0=ALU.mult, op1=ALU.add), reads=[vec], writes=[oneka])
            kark = sb([128, 8], F32, "kark")
            Sx.op("dve", lambda e: e.tensor_tensor(out=kark[:, 0:4], in0=vec[:, 1, :], in1=vec[:, 2, :], op=ALU.mult),
                  reads=[vec], writes=[kark])
            Sx.op("dve", lambda e: e.scalar_tensor_tensor(out=kark[:, 4:8], in0=oneka[:], scalar=2.0, in1=vec[:, 2, :],
                                                          op0=ALU.mult, op1=ALU.mult), reads=[vec, oneka, kark], writes=[kark])
            tot = sb([128, 4], F32, "tot")
            zcp = Rot([sb([128, 15, 130], BF16, "zc") for _ in range(2)])
            zs = sb([128, 15, 128], F32, "zs")
            actb = sb([128, 3, 128], BF16, "actb")
            sg = sb([128, 4, 128], F32, "sg")
            aa = sb([128, 4, 128], F32, "aa")
            aa2 = sb([128, 4, 128], F32, "aa2")
            kk = sb([128, 4, 128], F32, "kk")
            t1 = sb([128, 4, 128], F32, "t1")
            t2 = sb([128, 4, 128], F32, "t2")
            kd = sb([128, 4, 128], F32, "kd")
            bb = sb([128, 4, 128], F32, "bb")
            Lc = sb([128, 4, 128], F32, "Lc")
            Lm = sb([128, 4, 128], F32, "Lm")
            eR = sb([128, 4, 128], F32, "eR"); eA = sb([128, 4, 128], F32, "eA")
            eB = sb([128, 4, 128], F32, "eB"); eE = sb([128, 4, 128], F32, "eE")
            ARp = Rot([sb([128, 4, 2, 128], BF16, "AR") for _ in range(2)])
            BTu = sb([128, 4, 128], BF16, "BTu")
            AZ = sb([128, 4, 2, 128], BF16, "AZ"); BZ = sb([128, 4, 2, 128], BF16, "BZ"); KZ = sb([128, 4, 2, 128], BF16, "KZ")
            bpf = sb([128, 4, 128], BF16, "bpf"); kpf = sb([128, 4, 128], BF16, "kpf"); vTf = sb([128, 4, 128], BF16, "vTf")
            Bpp = Rot([sb([128, 4, 128], BF16, "Bp") for _ in range(2)])
            Kpp = Rot([sb([128, 4, 128], BF16, "Kp") for _ in range(2)])
            Vp = Rot([sb([128, 512], BF16, "V") for _ in range(2)])
            PCf = Rot([sb([128, 4, 128], F32, "PCf") for _ in range(2)])
            PMf = Rot([sb([128, 4, 128], F32, "PMf") for _ in range(2)])
            pcol = sb([128, 8], F32, "pcol")
            M = [sb([128, 8, 128], F32, "M%d" % i) for i in range(2)]
            Mt = [sb([128, 8, 128], F32, "Mt%d" % i) for i in range(2)]
            St = [sb([128, 8, 128], F32, "St%d" % i) for i in range(2)]
            Tp = Rot([sb([128, 8, 128], BF16, "T") for _ in range(2)])
            Akp = Rot([sb([128, 8, 128], BF16, "Ak") for _ in range(2)])
            Arbp = Rot([sb([128, 8, 128], BF16, "Arb") for _ in range(2)])
            Arkp = Rot([sb([128, 8, 128], BF16, "Ark") for _ in range(2)])
            H = sb([128, 4, 128], F32, "H")
            Hs = Rot([sb([128, 4, 128], BF16, "Hs") for _ in range(2)])
            Xb = sb([128, 512], BF16, "Xb"); Ub = sb([128, 512], BF16, "Ub")
            Yp = Rot([sb([128, 512], F32, "Y") for _ in range(2)])
            gT = sb([128, 4, 128], F32, "gT")
            bon = sb([128, 4, 128], F32, "bon")
            yn = sb([128, 8, 64], F32, "yn"); ynb = sb([128, 512], BF16, "ynb")
            st8 = sb([128, 32], F32, "st8")
            oT = Rot([sb([128, 4, 128], BF16, "oT") for _ in range(2)])
            for zb in (AZ, BZ, KZ):
                Sx.op("pool", lambda e, zb=zb: e.memset(zb[:], 0.0), writes=[zb])

            def pair_ops(eng, fn_name, out_b, out_ap, in_b, in_ap, col_b, col_ap, op):
                pass

            def loadz(c):
                zc = zcp.next()
                t0 = c * 128
                lo, hi = max(t0 - 1, 0), min(t0 + 129, S)
                if t0 == 0:
                    Sx.op("pool", lambda e, zc=zc: e.memset(zc[:, :, 0:1], 0.0), writes=[zc])
                if t0 + 129 > S:
                    Sx.op("pool", lambda e, zc=zc: e.memset(zc[:, :, 129:130], 0.0), writes=[zc])
                Sx.dma("sp", zc[:, :, lo - (t0 - 1):hi - (t0 - 1)], zT[:, lo:hi].rearrange("(c p) t -> p c t", p=128),
                       reads=[dram["zT"]], writes=[zc], sbuf=zc)
                return zc

            def prep(c, d, zc):
                post = (d == 1)
                for j in range(15):
                    Sx.op("dve", lambda e, j=j: e.tensor_scalar(out=zs[:, j, :], in0=zc[:, j, 1:129], scalar1=mu[:, 2, j:j + 1],
                                                              scalar2=None, op0=ALU.mult), reads=[zc, mu], writes=[(zs, j)])
                    Sx.op("dve", lambda e, j=j: e.scalar_tensor_tensor(out=zs[:, j, :], in0=zc[:, j, 0:128], scalar=mu[:, 0, j:j + 1],
                                                                     in1=zs[:, j, :], op0=ALU.mult, op1=ALU.add),
                          reads=[zc, mu, (zs, j)], writes=[(zs, j)])
                    Sx.op("dve", lambda e, j=j: e.scalar_tensor_tensor(out=zs[:, j, :], in0=zc[:, j, 2:130], scalar=mu[:, 1, j:j + 1],
                                                                     in1=zs[:, j, :], op0=ALU.mult, op1=ALU.add),
                          reads=[zc, mu, (zs, j)], writes=[(zs, j)])
                r_ = lambda: zs[:, 0:4, :]
                k_ = lambda: zs[:, 4:8, :]
                v_ = lambda: zs[:, 8:12, :]
                Sx.op("act", lambda e: e.activation(out=actb[:, 0, :], in_=zs[:, 12, :], func=AF.Tanh), reads=[zs], writes=[(actb, 0)])
                Sx.op("act", lambda e: e.activation(out=actb[:, 1, :], in_=zs[:, 13, :], func=AF.Copy), reads=[zs], writes=[(actb, 1)])
                if post:
                    Sx.op("act", lambda e: e.activation(out=actb[:, 2, :], in_=zs[:, 14, :], func=AF.Sigmoid), reads=[zs], writes=[(actb, 2)])

                def lora(wz, dd, idx, outb, bias_b, scale_out=None):
                    pm = PS.next()
                    for pr in range(4):
                        Sx.op("pe", lambda e, pr=pr, pm=pm: e.matmul(pm[:, pr * 128:(pr + 1) * 128], wz[:, dd, pr * 128:(pr + 1) * 128],
                                                                    actb[:, idx, :], start=True, stop=True),
                              reads=[wz, (actb, idx)], writes=[pm])
                    for pr in range(4):
                        Sx.op("act", lambda e, pr=pr, pm=pm: e.activation(out=outb[:, pr, :], in_=pm[:, pr * 128:(pr + 1) * 128],
                                                                         func=AF.Sigmoid, bias=bias_b[:, dd, pr:pr + 1]),
                              reads=[pm, bias_b], writes=[outb])
                lora(w2s, d, 0, sg, w0c)
                Sx.op("dve", lambda e: e.tensor_scalar(out=sg[:], in0=sg[:], scalar1=-C1, scalar2=None, op0=ALU.mult),
                      reads=[sg], writes=[sg])
                lora(a2s, d, 1, aa, a0c)
                if post:
                    lora(a2s, 0, 1, aa2, a0c)
                    pm = PS.next()
                    for pr in range(4):
                        Sx.op("pe", lambda e, pr=pr, pm=pm: e.matmul(pm[:, pr * 128:(pr + 1) * 128], g2s[:, pr * 128:(pr + 1) * 128],
                                                                    actb[:, 2, :], start=True, stop=True),
                              reads=[g2s, (actb, 2)], writes=[pm])
                    evac(gT[:].rearrange("p a t -> p (a t)"), pm[:, :], [pm], [gT])
                for pr in range(4):
                    Sx.op("dve", lambda e, pr=pr: e.tensor_scalar(out=kk[:, pr, :], in0=zs[:, 4 + pr, :], scalar1=vec[:, 0, pr:pr + 1],
                                                                scalar2=None, op0=ALU.mult), reads=[zs, vec], writes=[kk])
                Sx.op("dve", lambda e: e.tensor_tensor(out=t1[:], in0=kk[:], in1=kk[:], op=ALU.mult), reads=[kk], writes=[t1])
                pm = PS.next()
                for pr in range(4):
                    Sx.op("pe", lambda e, pr=pr, pm=pm: e.matmul(pm[:, pr * 128:(pr + 1) * 128], blkf[:], t1[:, pr, :], start=True, stop=True),
                          reads=[blkf, t1], writes=[pm])
                Sx.op("act", lambda e, pm=pm: e.activation(out=t2[:].rearrange("p a t -> p (a t)"), in_=pm[:, :], func=AF.Sqrt,
                                                          bias=epsc[:, 2:3]), reads=[pm, epsc], writes=[t2])
                Sx.op("dve", lambda e: e.reciprocal(t2[:], t2[:]), reads=[t2], writes=[t2])
                Sx.op("dve", lambda e: e.tensor_tensor(out=kk[:], in0=kk[:], in1=t2[:], op=ALU.mult), reads=[kk, t2], writes=[kk])
                for pr in range(4):
                    Sx.op("dve", lambda e, pr=pr: e.tensor_scalar(out=t1[:, pr, :], in0=aa[:, pr, :], scalar1=vec[:, 1, pr:pr + 1],
                                                                scalar2=oneka[:, pr:pr + 1], op0=ALU.mult, op1=ALU.add),
                          reads=[aa, vec, oneka], writes=[t1])
                Sx.op("dve", lambda e: e.tensor_tensor(out=kd[:], in0=zs[:, 4:8, :], in1=t1[:], op=ALU.mult), reads=[zs, t1], writes=[kd])
                Sx.op("dve", lambda e: e.tensor_tensor(out=bb[:], in0=kk[:], in1=aa[:], op=ALU.mult), reads=[kk, aa], writes=[bb])
                if post:
                    Sx.op("dve", lambda e: e.tensor_tensor(out=t2[:], in0=aa[:], in1=aa2[:], op=ALU.add), reads=[aa, aa2], writes=[t2])
                    for pr in range(4):
                        Sx.op("dve", lambda e, pr=pr: e.tensor_scalar(out=t2[:, pr, :], in0=t2[:, pr, :], scalar1=kark[:, pr:pr + 1],
                                                                    scalar2=kark[:, 4 + pr:5 + pr], op0=ALU.mult, op1=ALU.add),
                              reads=[t2, kark], writes=[t2])
                    Sx.op("dve", lambda e: e.tensor_tensor(out=t2[:], in0=t2[:], in1=zs[:, 4:8, :], op=ALU.mult), reads=[t2, zs], writes=[t2])
                    Sx.op("dve", lambda e: e.tensor_tensor(out=t2[:], in0=t2[:], in1=zs[:, 0:4, :], op=ALU.mult), reads=[t2, zs], writes=[t2])
                    pm = PS.next()
                    for pr in range(4):
                        Sx.op("pe", lambda e, pr=pr, pm=pm: e.matmul(pm[:, pr * 128:(pr + 1) * 128], blkf[:], t2[:, pr, :], start=True, stop=True),
                              reads=[blkf, t2], writes=[pm])
                    Sx.op("dve", lambda e, pm=pm: e.tensor_tensor(out=bon[:].rearrange("p a t -> p (a t)"), in0=pm[:, :],
                                                                 in1=zs[:, 8:12, :].rearrange("p a t -> p (a t)"), op=ALU.mult),
                          reads=[pm, zs], writes=[bon])
                for pr in range(4):
                    Sx.op("dve", lambda e, pr=pr: e.tensor_tensor_scan(out=Lc[:, pr, :], data0=ones128[:], data1=sg[:, pr, :], initial=0.0,
                                                                      op0=ALU.mult, op1=ALU.add), reads=[ones128, sg], writes=[Lc])
                last = 127
                if d == 1:
                    Sx.op("dve", lambda e: e.tensor_tensor(out=t1[:], in0=sg[:], in1=Lc[:], op=ALU.subtract), reads=[sg, Lc], writes=[t1])
                    Sx.op("dve", lambda e: e.tensor_copy(tot[:].unsqueeze(2), Lc[:, :, 127:128]), reads=[Lc], writes=[tot])
                    Sx.op("dve", lambda e: e.tensor_tensor(out=Lc[:], in0=t1[:], in1=tot[:].unsqueeze(2).to_broadcast([128, 4, 128]),
                                                           op=ALU.add), reads=[t1, tot], writes=[Lc])
                    last = 0
                Sx.op("dve", lambda e: e.tensor_tensor(out=Lm[:], in0=Lc[:], in1=Lc[:, :, MID:MID + 1].to_broadcast([128, 4, 128]),
                                                       op=ALU.subtract), reads=[Lc], writes=[Lm])
                Sx.op("act", lambda e: e.activation(out=eR[:], in_=Lm[:], func=AF.Exp), reads=[Lm], writes=[eR])
                Sx.op("act", lambda e: e.activation(out=eB[:], in_=Lm[:], func=AF.Exp, scale=-1.0), reads=[Lm], writes=[eB])
                Sx.op("dve", lambda e: e.tensor_tensor(out=t1[:], in0=Lm[:], in1=sg[:], op=ALU.subtract), reads=[Lm, sg], writes=[t1])
                Sx.op("act", lambda e: e.activation(out=eA[:], in_=t1[:], func=AF.Exp), reads=[t1], writes=[eA])
                Sx.op("dve", lambda e: e.tensor_tensor(out=t2[:], in0=Lc[:], in1=Lc[:, :, last:last + 1].to_broadcast([128, 4, 128]),
                                                       op=ALU.subtract), reads=[Lc], writes=[t2])
                Sx.op("act", lambda e: e.activation(out=eE[:], in_=t2[:], func=AF.Exp, scale=-1.0), reads=[t2], writes=[eE])
                PC, PM = PCf.next(), PMf.next()
                Sx.op("act", lambda e: e.activation(out=pcol[:, 0:4].unsqueeze(2), in_=Lc[:, :, last:last + 1], func=AF.Exp),
                      reads=[Lc], writes=[pcol])
                Sx.op("act", lambda e: e.activation(out=pcol[:, 4:8].unsqueeze(2), in_=Lc[:, :, MID:MID + 1], func=AF.Exp),
                      reads=[Lc, pcol], writes=[pcol])
                Sx.op("dve", lambda e, PC=PC: e.tensor_copy(PC[:], pcol[:, 0:4].unsqueeze(2).to_broadcast([128, 4, 128])),
                      reads=[pcol], writes=[PC])
                for pr in range(4):
                    Sx.op("dve", lambda e, PM=PM, pr=pr: e.tensor_scalar(out=PM[:, pr, :], in0=blkf[:], scalar1=pcol[:, 4 + pr:5 + pr],
                                                                      scalar2=None, op0=ALU.mult), reads=[pcol, blkf], writes=[PM])
                AR = ARp.next()
                Sx.op("dve", lambda e, AR=AR: e.scalar_tensor_tensor(out=AR[:, :, 0, :], in0=kk[:], scalar=-1.0, in1=eA[:],
                                                                   op0=ALU.mult, op1=ALU.mult), reads=[kk, eA], writes=[AR])
                Sx.op("dve", lambda e, AR=AR: e.tensor_tensor(out=AR[:, :, 1, :], in0=zs[:, 0:4, :], in1=eR[:], op=ALU.mult),
                      reads=[zs, eR, AR], writes=[AR])
                Sx.op("dve", lambda e: e.tensor_tensor(out=BTu[:], in0=bb[:], in1=eB[:], op=ALU.mult), reads=[bb, eB], writes=[BTu])
                Sx.op("dve", lambda e: e.tensor_tensor(out=t1[:], in0=kd[:], in1=eB[:], op=ALU.mult), reads=[kd, eB], writes=[t1])
                for par in range(2):
                    ps_ = slice(par * 64, (par + 1) * 64)
                    Sx.op("pool", lambda e, ps_=ps_, par=par, AR=AR: e.tensor_copy(AZ[ps_, :, par, :], AR[ps_, :, 0, :]), reads=[AR], writes=[AZ])
                    Sx.op("pool", lambda e, ps_=ps_, par=par: e.tensor_copy(BZ[ps_, :, par, :], BTu[ps_, :, :]), reads=[BTu], writes=[BZ])
                    Sx.op("pool", lambda e, ps_=ps_, par=par: e.tensor_copy(KZ[ps_, :, par, :], t1[ps_, :, :]), reads=[t1], writes=[KZ])
                Sx.op("dve", lambda e: e.tensor_tensor(out=bpf[:], in0=bb[:], in1=eE[:], op=ALU.mult), reads=[bb, eE], writes=[bpf])
                Sx.op("dve", lambda e: e.tensor_tensor(out=kpf[:], in0=kd[:], in1=eE[:], op=ALU.mult), reads=[kd, eE], writes=[kpf])
                Sx.op("act", lambda e: e.activation(out=vTf[:], in_=zs[:, 8:12, :], func=AF.Copy), reads=[zs], writes=[vTf])
                Bp, Kp, V = Bpp.next(), Kpp.next(), Vp.next()
                for src, dst in [(bpf, Bp), (kpf, Kp), (vTf, V)]:
                    pt = PS.next()
                    ptb = pt[:].bitcast(BF16)
                    for pr in range(4):
                        Sx.op("pe", lambda e, pr=pr, ptb=ptb, src=src: e.transpose(ptb[:, pr * 128:(pr + 1) * 128], src[:, pr, :], identb[:]),
                              reads=[src, identb], writes=[pt])
                    dap = dst[:] if dst is V else dst[:].rearrange("p a t -> p (a t)")
                    evac(dap, ptb[:, 0:512], [pt], [dst])
                m_ab = mLs if d == 0 else mUs
                m_abT = mUs if d == 0 else mLs
                m_inT = mUi if d == 0 else mLi
                Ak, Arb, Ark = Akp.next(), Arbp.next(), Arkp.next()
                for hg in range(2):
                    p1 = PS.next()
                    for hh in range(4):
                        h = hg * 4 + hh
                        pr, par = h // 2, h % 2
                        Sx.op("pe", lambda e, p1=p1, hh=hh, pr=pr, par=par: e.matmul(p1[:, hh * 128:(hh + 1) * 128], AZ[:, pr, par, :],
                                                                                      BTu[:, pr, :], start=True, stop=True),
                              reads=[AZ, BTu], writes=[p1])
                    Sx.op("dve", lambda e, p1=p1, hg=hg: e.tensor_tensor(out=M[0][:, hg * 4:(hg + 1) * 4, :],
                                                                        in0=p1[:, :].rearrange("p (a t) -> p a t", a=4),
                                                                        in1=m_ab[:].unsqueeze(1).to_broadcast([128, 4, 128]), op=ALU.mult),
                          reads=[p1, m_ab], writes=[(M[0], hg)])
                    for (LZ, o1, m1, o2, m2) in [(BZ, Mt[0], m_abT, Arb, m_inT), (KZ, Ak, m_abT, Ark, m_inT)]:
                        for h2 in range(2):
                            pass
                        for half in range(2):
                            p2 = PS.next()
                            for q in range(2):
                                h = hg * 4 + half * 2 + q
                                pr, par = h // 2, h % 2
                                Sx.op("pe", lambda e, p2=p2, q=q, pr=pr, par=par, LZ=LZ, AR=AR: e.matmul(
                                    p2[:, q * 256:(q + 1) * 256], LZ[:, pr, par, :], AR[:, pr, :, :].rearrange("p a t -> p (a t)"),
                                    start=True, stop=True), reads=[LZ, AR], writes=[p2])
                            h0 = hg * 4 + half * 2
                            pv = p2[:, :].rearrange("p (q a t) -> p q a t", q=2, a=2)
                            Sx.op("dve", lambda e, pv=pv, o1=o1, m1=m1, h0=h0: e.tensor_tensor(
                                out=o1[:, h0:h0 + 2, :], in0=pv[:, :, 0, :], in1=m1[:].unsqueeze(1).to_broadcast([128, 2, 128]), op=ALU.mult),
                                reads=[p2, m1], writes=[(o1, h0)])
                            Sx.op("dve", lambda e, pv=pv, o2=o2, m2=m2, h0=h0: e.tensor_tensor(
                                out=o2[:, h0:h0 + 2, :], in0=pv[:, :, 1, :], in1=m2[:].unsqueeze(1).to_broadcast([128, 2, 128]), op=ALU.mult),
                                reads=[p2, m2], writes=[(o2, h0)])
                Sx.op("dve", lambda e: e.tensor_tensor(out=St[0][:], in0=Mt[0][:], in1=identf[:].unsqueeze(1).to_broadcast([128, 8, 128]),
                                                       op=ALU.add), reads=[Mt[0], identf], writes=[St[0]])
                cur = 0
                NLV = 6
                for lv in range(NLV):
                    nx = 1 - cur
                    lastlv = (lv == NLV - 1)
                    for hg in range(2):
                        pM, pMt, pS = PS.next(), (None if lastlv else PS.next()), PS.next()
                        for hh in range(4):
                            h = hg * 4 + hh
                            cs = slice(hh * 128, (hh + 1) * 128)
                            Sx.op("pe", lambda e, pM=pM, cs=cs, h=h, cur=cur: e.matmul(pM[:, cs], Mt[cur][:, h, :], M[cur][:, h, :], start=True, stop=True),
                                  reads=[Mt[cur], M[cur]], writes=[pM])
                        hsl = slice(hg * 4, (hg + 1) * 4)
                        evac(M[nx][:, hsl, :].rearrange("p a t -> p (a t)"), pM[:, :], [pM], [(M[nx], hg)])
                        if not lastlv:
                            for hh in range(4):
                                h = hg * 4 + hh
                                cs = slice(hh * 128, (hh + 1) * 128)
                                Sx.op("pe", lambda e, pMt=pMt, cs=cs, h=h, cur=cur: e.matmul(pMt[:, cs], M[cur][:, h, :], Mt[cur][:, h, :], start=True, stop=True),
                                      reads=[Mt[cur], M[cur]], writes=[pMt])
                            evac(Mt[nx][:, hsl, :].rearrange("p a t -> p (a t)"), pMt[:, :], [pMt], [(Mt[nx], hg)])
                        for hh in range(4):
                            h = hg * 4 + hh
                            cs = slice(hh * 128, (hh + 1) * 128)
                            Sx.op("pe", lambda e, pS=pS, cs=cs, h=h, cur=cur, nx=nx: e.matmul(pS[:, cs], M[nx][:, h, :], St[cur][:, h, :], start=True, stop=True),
                                  reads=[(M[nx], hg), St[cur]], writes=[pS])
                        Sx.op("dve", lambda e, pS=pS, hsl=hsl, cur=cur, nx=nx: e.tensor_tensor(
                            out=St[nx][:, hsl, :].rearrange("p a t -> p (a t)"), in0=pS[:, :],
                            in1=St[cur][:, hsl, :].rearrange("p a t -> p (a t)"), op=ALU.add),
                            reads=[pS, St[cur]], writes=[(St[nx], hg)])
                    cur = nx
                T = Tp.next()
                Sx.op("act", lambda e, T=T, cur=cur: e.activation(out=T[:], in_=St[cur][:], func=AF.Copy), reads=[St[cur]], writes=[T])
                return dict(AR=AR, Bp=Bp, Kp=Kp, V=V, PC=PC, PM=PM, T=T, Ak=Ak, Arb=Arb, Ark=Ark)

            def serial(c, d, P, Hs_cur, PMnext):
                AR, Bp, Kp, V, PC, T, Ak, Arb, Ark = (P[k] for k in ["AR", "Bp", "Kp", "V", "PC", "T", "Ak", "Arb", "Ark"])
                pX = PS.next()
                for h in range(8):
                    pr, par = h // 2, h % 2
                    cs = slice(h * 64, (h + 1) * 64)
                    Sx.op("pe", lambda e, pX=pX, cs=cs, pr=pr, par=par: e.matmul(pX[:, cs], AR[:, pr, 0, :], Hs_cur[:, pr, par * 64:(par + 1) * 64],
                                                                                  start=True, stop=False), reads=[AR, Hs_cur], writes=[pX])
                    Sx.op("pe", lambda e, pX=pX, cs=cs, h=h: e.matmul(pX[:, cs], Ak[:, h, :], V[:, cs], start=False, stop=True),
                          reads=[Ak, V], writes=[pX])
                Sx.op("dve", lambda e, pX=pX: e.tensor_copy(Xb[:], pX[:, :]), reads=[pX], writes=[Xb])
                pU = PS.next()
                for h in range(8):
                    cs = slice(h * 64, (h + 1) * 64)
                    Sx.op("pe", lambda e, pU=pU, cs=cs, h=h: e.matmul(pU[:, cs], T[:, h, :], Xb[:, cs], start=True, stop=True),
                          reads=[T, Xb], writes=[pU])
                Sx.op("act", lambda e, pU=pU: e.activation(out=Ub[:], in_=pU[:, :], func=AF.Copy), reads=[pU], writes=[Ub])
                pY = PS.next()
                for h in range(8):
                    pr, par = h // 2, h % 2
                    cs = slice(h * 64, (h + 1) * 64)
                    Sx.op("pe", lambda e, pY=pY, cs=cs, pr=pr, par=par: e.matmul(pY[:, cs], AR[:, pr, 1, :], Hs_cur[:, pr, par * 64:(par + 1) * 64],
                                                                                  start=True, stop=False), reads=[AR, Hs_cur], writes=[pY])
                    Sx.op("pe", lambda e, pY=pY, cs=cs, h=h: e.matmul(pY[:, cs], Arb[:, h, :], Ub[:, cs], start=False, stop=False),
                          reads=[Arb, Ub], writes=[pY])
                    Sx.op("pe", lambda e, pY=pY, cs=cs, h=h: e.matmul(pY[:, cs], Ark[:, h, :], V[:, cs], start=False, stop=True),
                          reads=[Ark, V], writes=[pY])
                pH = PS.next()
                for pr in range(4):
                    cs = slice(pr * 128, (pr + 1) * 128)
                    Sx.op("pe", lambda e, pH=pH, cs=cs, pr=pr: e.matmul(pH[:, cs], Bp[:, pr, :], Ub[:, cs], start=True, stop=False),
                          reads=[Bp, Ub], writes=[pH])
                    Sx.op("pe", lambda e, pH=pH, cs=cs, pr=pr: e.matmul(pH[:, cs], Kp[:, pr, :], V[:, cs], start=False, stop=True),
                          reads=[Kp, V], writes=[pH])
                Sx.op("dve", lambda e: e.tensor_tensor(out=H[:], in0=H[:], in1=PC[:], op=ALU.mult), reads=[H, PC], writes=[H])
                Sx.op("dve", lambda e, pH=pH: e.tensor_tensor(out=H[:].rearrange("p a t -> p (a t)"), in0=H[:].rearrange("p a t -> p (a t)"),
                                                             in1=pH[:, :], op=ALU.add), reads=[H, pH], writes=[H])
                Hn = None
                if PMnext is not None:
                    Hn = Hs.next()
                    Sx.op("dve", lambda e, Hn=Hn: e.tensor_tensor(out=Hn[:], in0=H[:], in1=PMnext[:], op=ALU.mult),
                          reads=[H, PMnext], writes=[Hn])
                return pY, Hn

            def post(c, pY):
                Y = Yp.next()
                Sx.dma("sp", Y[:], yfw[c * 128:(c + 1) * 128, :], reads=[(dram["yfw"], c)], writes=[Y], sbuf=Y)
                Yv = Y[:].rearrange("p (h d) -> p h d", d=64)
                Sx.op("dve", lambda e: e.tensor_tensor(out=Y[:], in0=Y[:], in1=pY[:, :], op=ALU.add), reads=[Y, pY], writes=[Y])
                Sx.op("dve", lambda e: e.tensor_reduce(out=st8[:, 0:8], in_=Yv, axis=AX.X, op=ALU.add), reads=[Y], writes=[st8])
                Sx.op("dve", lambda e: e.tensor_scalar(out=st8[:, 0:8], in0=st8[:, 0:8], scalar1=1.0 / 64, scalar2=None, op0=ALU.mult),
                      reads=[st8], writes=[st8])
                Sx.op("dve", lambda e: e.tensor_tensor(out=yn[:], in0=Yv, in1=st8[:, 0:8].unsqueeze(2).to_broadcast([128, 8, 64]),
                                                       op=ALU.subtract), reads=[Y, st8], writes=[yn])
                Sx.op("dve", lambda e: e.tensor_tensor(out=Yv, in0=yn[:], in1=yn[:], op=ALU.mult), reads=[yn, Y], writes=[Y])
                Sx.op("dve", lambda e: e.tensor_reduce(out=st8[:, 8:16], in_=Yv, axis=AX.X, op=ALU.add), reads=[Y, st8], writes=[st8])
                Sx.op("act", lambda e: e.activation(out=st8[:, 16:24], in_=st8[:, 8:16], func=AF.Sqrt, scale=1.0 / 64, bias=epsc[:, 1:2]),
                      reads=[st8, epsc], writes=[st8])
                Sx.op("dve", lambda e: e.reciprocal(st8[:, 24:32], st8[:, 16:24]), reads=[st8], writes=[st8])
                Sx.op("dve", lambda e: e.tensor_tensor(out=ynb[:].rearrange("p (h d) -> p h d", d=64), in0=yn[:],
                                                       in1=st8[:, 24:32].unsqueeze(2).to_broadcast([128, 8, 64]), op=ALU.mult),
                      reads=[yn, st8], writes=[ynb])
                pt = PS.next()
                ptb = pt[:].bitcast(BF16)
                for pr in range(4):
                    Sx.op("pe", lambda e, pr=pr, ptb=ptb: e.transpose(ptb[:, pr * 128:(pr + 1) * 128], ynb[:, pr * 128:(pr + 1) * 128], identb[:]),
                          reads=[ynb, identb], writes=[pt])
                o = oT.next()
                for pr in range(4):
                    Sx.op("dve", lambda e, pr=pr, ptb=ptb: e.tensor_scalar(out=t1[:, pr, :], in0=ptb[:, pr * 128:(pr + 1) * 128],
                                                                        scalar1=vec[:, 3, pr:pr + 1], scalar2=vec[:, 4, pr:pr + 1],
                                                                        op0=ALU.mult, op1=ALU.add), reads=[pt, vec], writes=[t1])
                Sx.op("dve", lambda e: e.tensor_tensor(out=t1[:], in0=t1[:], in1=bon[:], op=ALU.add), reads=[t1, bon], writes=[t1])
                Sx.op("dve", lambda e, o=o: e.tensor_tensor(out=o[:], in0=t1[:], in1=gT[:], op=ALU.mult), reads=[t1, gT], writes=[o])
                Sx.dma("pool", ymT[0:512, c * 128:(c + 1) * 128].rearrange("(pr p) t -> p pr t", p=128), o[:], reads=[o],
                       writes=[(dram["ymT"], ("r", c))], sbuf=o)

            for d in range(2):
                order = list(range(NT)) if d == 0 else list(range(NT - 1, -1, -1))
                Sx.op("pool", lambda e: e.memset(H[:], 0.0), writes=[H])
                Hcur = Hs.next()
                Sx.op("pool", lambda e, Hcur=Hcur: e.memset(Hcur[:], 0.0), writes=[Hcur])
                zc = loadz(order[0])
                Pn = prep(order[0], d, zc)
                for i, c in enumerate(order):
                    Pc = Pn
                    if i + 1 < len(order):
                        zc = loadz(order[i + 1])
                    if d == 1:
                        pass
                    if d == 0:
                        if i + 1 < len(order):
                            Pn = prep(order[i + 1], d, zc)
                            pY, Hcur = serial(c, d, Pc, Hcur, Pn["PM"])
                        else:
                            pY, Hcur = serial(c, d, Pc, Hcur, None)
                        Y = Yp.next()
                        Sx.op("act", lambda e, Y=Y, pY=pY: e.activation(out=Y[:], in_=pY[:, :], func=AF.Copy), reads=[pY], writes=[Y])
                        Sx.dma("pool", yfw[c * 128:(c + 1) * 128, :], Y[:], reads=[Y], writes=[(dram["yfw"], c)], sbuf=Y)
                    else:
                        pY, _ = serial(c, d, Pc, Hcur, None)
                        post(c, pY)
                        if i + 1 < len(order):
                            Pn = prep(order[i + 1], d, zc)
                            Hn = Hs.next()
                            Sx.op("dve", lambda e, Hn=Hn, PMn=Pn["PM"]: e.tensor_tensor(out=Hn[:], in0=H[:], in1=PMn[:], op=ALU.mult),
                                  reads=[H, Pn["PM"]], writes=[Hn])
                            Hcur = Hn
            Sx.barrier()
    return P3


_NC_CACHE = {}


def prep_inputs(inputs, S, NL):
    f = lambda a: np.ascontiguousarray(a, dtype=np.float32)
    rows = S // GW
    tiles, sigs = att_tiles(rows)
    common = {}
    common["ada_w"] = f(inputs["ada_w"][:NL])
    common["ada_bT"] = f(inputs["ada_b"][:NL].reshape(NL, 48, 128).transpose(0, 2, 1))
    common["n1g"] = f(inputs["norm1_g"][:NL].reshape(NL, 8, 128).transpose(0, 2, 1))
    common["n2g"] = f(inputs["norm2_g"][:NL].reshape(NL, 8, 128).transpose(0, 2, 1))
    common["w_in"] = f(inputs["w_in"][:NL])
    common["muT"] = f(inputs["shift_mu"][:NL].reshape(NL, 2, 15, 128).transpose(0, 3, 1, 2))
    common["w0T"] = f(inputs["w0"][:NL].reshape(NL, 2, 4, 128).transpose(0, 3, 1, 2))
    common["a0T"] = f(inputs["a0"][:NL].reshape(NL, 2, 4, 128).transpose(0, 3, 1, 2))
    w2Z = np.zeros((NL, 128, 2, RW), np.float32)
    a2Z = np.zeros((NL, 128, 2, RW), np.float32)
    for d in range(2):
        w2Z[:, d * 64:(d + 1) * 64, d, :] = inputs["w2"][:NL, d]
        a2Z[:, d * 64:(d + 1) * 64, d, :] = inputs["a2"][:NL, d]
    common["w2Z"] = w2Z
    common["a2Z"] = a2Z
    common["g2"] = f(inputs["g2"][:NL])
    vec = np.stack([inputs[k][:NL].reshape(NL, 4, 128).transpose(0, 2, 1) for k in ["k_k", "k_a", "r_k", "lnx_g", "lnx_b"]], axis=2)
    common["vecT"] = f(vec)
    qk = np.stack([np.tile(inputs["q_norm_g"][:NL], (1, 2)), np.tile(inputs["k_norm_g"][:NL], (1, 2))], axis=2)
    common["qkg"] = f(qk)
    common["biasT"] = f(np.stack([build_bias(np.asarray(inputs["rpb"][l]), sigs) for l in range(NL)]))
    common["w_out"] = f(inputs["w_out"][:NL])
    common["f_in"] = f(inputs["ffn_w_in"][:NL])
    common["f_out"] = f(inputs["ffn_w_out"][:NL])
    return common, len(sigs)


def run(inputs, S, NL, ncores=8, trace=False):
    inputs = {k: np.asarray(v) for k, v in inputs.items()}
    B = inputs["x"].shape[0]
    common, NV = prep_inputs(inputs, S, NL)
    key = (S, NL, NV)
    if key not in _NC_CACHE:
        _NC_CACHE[key] = build_nc(S, NL, NV)
    nc = _NC_CACHE[key]
    in_maps = []
    for cidx in range(ncores):
        b = cidx % B
        m = dict(common)
        m["xT"] = np.ascontiguousarray(inputs["x"][b, :S].T, dtype=np.float32)
        m["cT"] = np.ascontiguousarray(inputs["c"][b].reshape(8, 128).T, dtype=np.float32)
        in_maps.append(m)
    res = run_bass_kernel_spmd(nc, in_maps, core_ids=list(range(ncores)))
    nb = min(B, ncores)
    out = np.stack([np.ascontiguousarray(res.results[b]["outT"].T) for b in range(nb)], axis=0)
    if DEBUG:
        return out.astype(np.float32), res.results
    return out.astype(np.float32)


def kernel(**inputs):
    return run(inputs, 8192, 4)
```

```python
import numpy as np
import concourse.bass as bass
import concourse.mybir as mybir
from concourse.bass_utils import run_bass_kernel_spmd

F32 = mybir.dt.float32
BF16 = mybir.dt.bfloat16
AF = mybir.ActivationFunctionType
ALU = mybir.AluOpType
AX = mybir.AxisListType

D = 1024
GW = 64
RW = 512
RWKV_COLS = 1920
IN_COLS = 3456
DFF = 2816
NEG = -30000.0
DEBUG = False
PHASES = (1, 2, 3, 4)


class Buf:
    def __init__(self, name, t):
        self.name = name
        self.t = t
        self.st = {}
        self.dsem = None

    def __getitem__(self, k):
        return self.t[k]


class Sched:
    ENG = ["pe", "dve", "act", "pool", "sp"]

    def __init__(self, nc, stack):
        self.nc = nc
        self.stack = stack
        self.sems = {}
        self.count = {}
        self.known = {e: {} for e in self.ENG}
        self.lists = {e: [] for e in self.ENG}
        for e in self.ENG:
            self.sems[e] = stack.enter_context(nc.semaphore("s_" + e))
            self.count[e] = 0
        self.dsems = []
        for i in range(24):
            nm = "d%d" % i
            self.sems[nm] = stack.enter_context(nc.semaphore("s_" + nm))
            self.count[nm] = 0
            self.dsems.append(nm)
        self.dnext = 0
        self.nbuf = 0

    def sb(self, stack, shape, dt, name=None):
        self.nbuf += 1
        name = (name or "b") + "_%d" % self.nbuf
        return Buf(name, stack.enter_context(self.nc.sbuf_tensor(name, list(shape), dt)))

    def ps(self, stack, name=None):
        self.nbuf += 1
        name = (name or "p") + "_%d" % self.nbuf
        return Buf(name, stack.enter_context(self.nc.psum_tensor(name, [128, 512], F32)))

    def dsem_for(self, buf):
        if buf.dsem is None:
            buf.dsem = self.dsems[self.dnext % len(self.dsems)]
            self.dnext += 1
        return buf.dsem

    @staticmethod
    def _norm(x):
        return x if isinstance(x, tuple) else (x, None)

    def _deps(self, reads, writes):
        deps = {}

        def add(tok):
            if tok is None:
                return
            s, v = tok
            if deps.get(s, 0) < v:
                deps[s] = v

        for item in reads:
            b, p = self._norm(item)
            for q, st in b.st.items():
                if p is None or q is None or p == q:
                    add(st[0])
        for item in writes:
            b, p = self._norm(item)
            for q, st in b.st.items():
                if p is None or q is None or p == q:
                    add(st[0])
                    for s, v in st[1].items():
                        add((s, v))
        return deps

    def _commit(self, tok, reads, writes):
        for item in reads:
            b, p = self._norm(item)
            st = b.st.setdefault(p, [None, {}])
            if st[1].get(tok[0], 0) < tok[1]:
                st[1][tok[0]] = tok[1]
        for item in writes:
            b, p = self._norm(item)
            if p is None:
                b.st = {None: [tok, {}]}
            else:
                b.st[p] = [tok, {}]

    def _waits(self, eng, deps):
        w = []
        kn = self.known[eng]
        for s, v in deps.items():
            if s == eng and eng == "pe":
                continue
            if kn.get(s, 0) < v:
                kn[s] = v
                w.append((s, v))
        return w

    def op(self, eng, fn, reads=(), writes=()):
        deps = self._deps(reads, writes)
        w = self._waits(eng, deps)
        self.count[eng] += 1
        tok = (eng, self.count[eng])
        self.lists[eng].append((w, fn, eng, 1))
        self._commit(tok, reads, writes)

    def dma(self, q, out_ap, in_ap, reads=(), writes=(), sbuf=None, **kw):
        ds = self.dsem_for(sbuf)
        deps = self._deps(reads, writes)
        deps[ds] = max(deps.get(ds, 0), self.count[ds])
        w = self._waits(q, deps)
        self.count[ds] += 16
        tok = (ds, self.count[ds])
        self.lists[q].append((w, lambda e: e.dma_start(out=out_ap, in_=in_ap, **kw), ds, 16))
        self._commit(tok, reads, writes)

    def barrier(self):
        for e in self.ENG:
            deps = {s: c for s, c in self.count.items() if c > 0 and s != e}
            w = self._waits(e, deps)
            if w:
                self.lists[e].append((w, None, None, 0))

    def emit(self):
        self.barrier()
        nc = self.nc
        sems = self.sems
        lists = self.lists

        def run(e, items):
            for w, fn, s, inc in items:
                for ws, wv in w:
                    e.wait_ge(sems[ws], wv)
                if fn is not None:
                    fn(e).then_inc(sems[s], inc)

        with nc.Block() as block:
            @block.tensor
            def _(e):
                run(e, lists["pe"])

            @block.vector
            def _(e):
                run(e, lists["dve"])

            @block.scalar
            def _(e):
                run(e, lists["act"])

            @block.gpsimd
            def _(e):
                run(e, lists["pool"])

            @block.sync
            def _(e):
                run(e, lists["sp"])


class Rot:
    def __init__(self, bufs):
        self.bufs = bufs
        self.i = 0

    def next(self):
        b = self.bufs[self.i % len(self.bufs)]
        self.i += 1
        return b


def att_tiles(rows):
    sigs = []
    tiles = []
    for j in range(rows // 2):
        kb = min(max(2 * j - 4, 0), rows - 9)
        i0 = 2 * j
        r00 = min(max(i0 - 4, 0), rows - 8)
        r01 = min(max(i0 + 1 - 4, 0), rows - 8)
        sig = (i0 - kb, r00 - kb, r01 - kb)
        if sig not in sigs:
            sigs.append(sig)
        tiles.append((kb, sigs.index(sig)))
    return tiles, sigs


def build_bias(rpb_l, sigs):
    nv = len(sigs)
    out = np.full((nv, 5 * 128, 8, 128), NEG, np.float32)
    qc = np.arange(64)
    cs = np.clip(qc - 8, 0, GW - 16)
    for vi, (di, d0, d1) in enumerate(sigs):
        for ri in range(2):
            irel = di + ri
            r0rel = d0 if ri == 0 else d1
            for kr in range(r0rel, r0rel + 8):
                ro = kr - irel + 7
                for q in range(64):
                    kc = np.arange(cs[q], cs[q] + 16)
                    co = kc - q + 15
                    out[vi, kr * 64 + kc, :, ri * 64 + q] = rpb_l[:, ro, co].T
    return out.reshape(nv, 5, 128, 8, 128).transpose(0, 2, 1, 3, 4).copy()


def build_nc(S, NL, NV):
    from contextlib import ExitStack
    nc = bass.Bass("TRN2", target_bir_lowering=False)
    rows = S // GW
    NT = S // 128
    tiles, sigs = att_tiles(rows)
    assert len(sigs) == NV

    def din(name, shape, dt=F32):
        return nc.dram_tensor(name, list(shape), dt, kind="ExternalInput").ap()

    def dscr(name, shape, dt):
        if DEBUG:
            return nc.dram_tensor(name, list(shape), dt, kind="ExternalOutput").ap()
        return nc.dram_tensor(name, list(shape), dt).ap()

    xT = din("xT", [D, S])
    cT = din("cT", [128, 8])
    ada_w = din("ada_w", [NL, D, 6 * D])
    ada_bT = din("ada_bT", [NL, 128, 48])
    n1g = din("n1g", [NL, 128, 8])
    n2g = din("n2g", [NL, 128, 8])
    w_in = din("w_in", [NL, D, IN_COLS])
    muT = din("muT", [NL, 128, 2, 15])
    w0T = din("w0T", [NL, 128, 2, 4])
    a0T = din("a0T", [NL, 128, 2, 4])
    w2Z = din("w2Z", [NL, 128, 2, RW])
    a2Z = din("a2Z", [NL, 128, 2, RW])
    g2 = din("g2", [NL, 128, RW])
    vecT = din("vecT", [NL, 128, 5, 4])
    qkg = din("qkg", [NL, 128, 2])
    biasT = din("biasT", [NL, NV, 128, 5, 8, 128])
    w_out = din("w_out", [NL, D, D])
    f_in = din("f_in", [NL, D, 2 * DFF])
    f_out = din("f_out", [NL, DFF, D])
    outT = nc.dram_tensor("outT", [D, S], F32, kind="ExternalOutput").ap()

    hT = dscr("hT", [D, S], F32)
    zT = dscr("zT", [RWKV_COLS, S], BF16)
    zsT = dscr("zsT", [RWKV_COLS, S], BF16)
    QT = dscr("QT", [RW, S], BF16)
    KT = dscr("KT", [RW, S], BF16)
    Vtm = dscr("Vtm", [S, 520], BF16)
    ymT = dscr("ymT", [D, S], BF16)
    yfw = dscr("yfw", [S, RW], F32)
    class DB:
        pass
    dram = {n: Buf(n, None) for n in ["hT", "zT", "zsT", "QT", "KT", "Vtm", "ymT", "yfw", "outT"]}

    with ExitStack() as gs:
        Sx = Sched(nc, gs)
        psb = [Sx.ps(gs) for _ in range(8)]
        PS = Rot(psb)
        PS6 = Rot(psb[2:])
        identf = Sx.sb(gs, [128, 128], F32, "identf")
        identb = Sx.sb(gs, [128, 128], BF16, "identb")
        onesb = Sx.sb(gs, [128, 128], BF16, "onesb")
        blkf = Sx.sb(gs, [128, 128], F32, "blkf")
        blkb = Sx.sb(gs, [128, 128], BF16, "blkb")
        mLs = Sx.sb(gs, [128, 128], F32, "mLs")
        mLi = Sx.sb(gs, [128, 128], F32, "mLi")
        mUs = Sx.sb(gs, [128, 128], F32, "mUs")
        mUi = Sx.sb(gs, [128, 128], F32, "mUi")
        modT = Sx.sb(gs, [128, NL, 48], F32, "modT")
        cact = Sx.sb(gs, [128, 8], F32, "cact")
        lay = Sx.sb(gs, [128, 64], F32, "lay")
        tmpc = Sx.sb(gs, [128, 16], F32, "tmpc")

        Sx.op("pool", lambda e: e.memset(identf[:], 0.0), writes=[identf])
        Sx.op("pool", lambda e: e.affine_select(out=identf[:], in_=identf[:], pattern=[[-1, 128]],
                                                compare_op=ALU.not_equal, fill=1.0, base=0, channel_multiplier=1),
              reads=[identf], writes=[identf])
        Sx.op("pool", lambda e: e.tensor_copy(identb[:], identf[:]), reads=[identf], writes=[identb])
        Sx.op("pool", lambda e: e.memset(onesb[:], 1.0), writes=[onesb])
        Sx.op("pool", lambda e: e.memset(blkf[:], 0.0), writes=[blkf])
        Sx.op("pool", lambda e: e.memset(blkf[0:64, 0:64], 1.0), reads=[blkf], writes=[blkf])
        Sx.op("pool", lambda e: e.memset(blkf[64:128, 64:128], 1.0), reads=[blkf], writes=[blkf])
        Sx.op("pool", lambda e: e.tensor_copy(blkb[:], blkf[:]), reads=[blkf], writes=[blkb])
        for mb, cmp, stp, cm in [(mLs, ALU.is_gt, -1, 1), (mLi, ALU.is_ge, -1, 1), (mUs, ALU.is_gt, 1, -1), (mUi, ALU.is_ge, 1, -1)]:
            Sx.op("pool", lambda e, mb=mb: e.memset(mb[:], 1.0), writes=[mb])
            Sx.op("pool", lambda e, mb=mb, cmp=cmp, stp=stp, cm=cm: e.affine_select(
                out=mb[:], in_=mb[:], pattern=[[stp, 128]], compare_op=cmp, fill=0.0, base=0, channel_multiplier=cm),
                reads=[mb], writes=[mb])

        Sx.dma("sp", cact[:], cT[:, :], writes=[cact], sbuf=cact)
        Sx.op("act", lambda e: e.activation(out=cact[:], in_=cact[:], func=AF.Silu), reads=[cact], writes=[cact])
        with ExitStack() as ls:
            awp = Rot([Sx.sb(ls, [128, 8, 512], F32, "aw") for _ in range(2)])
            adb = Sx.sb(ls, [128, NL, 48], F32, "adb")
            Sx.dma("sp", adb[:], ada_bT.rearrange("l p c -> p l c"), writes=[adb], sbuf=adb)
            for l in range(NL):
                pm = PS.next()
                for cb in range(12):
                    aw = awp.next()
                    Sx.dma("sp", aw[:], ada_w[l, :, cb * 512:(cb + 1) * 512].rearrange("(kc p) f -> p kc f", p=128),
                           writes=[aw], sbuf=aw)
                    for f4 in range(4):
                        fc = cb * 4 + f4
                        for kc in range(8):
                            Sx.op("pe", lambda e, aw=aw, kc=kc, f4=f4, fc=fc, pm=pm: e.matmul(
                                pm[:, fc:fc + 1], aw[:, kc, f4 * 128:(f4 + 1) * 128], cact[:, kc:kc + 1],
                                start=(kc == 0), stop=(kc == 7)), reads=[aw, cact], writes=[pm])
                Sx.op("dve", lambda e, pm=pm, l=l: e.tensor_tensor(out=modT[:, l, :], in0=pm[:, 0:48], in1=adb[:, l, :],
                                                                   op=ALU.add), reads=[pm, adb], writes=[modT])
            Sx.barrier()

        def layer_cols(l):
            Sx.dma("sp", tmpc[:, 0:8], n1g[l], writes=[tmpc], sbuf=tmpc)
            Sx.dma("sp", tmpc[:, 8:16], n2g[l], writes=[tmpc], sbuf=tmpc)
            Sx.op("dve", lambda e: e.scalar_tensor_tensor(out=lay[:, 0:8], in0=modT[:, l, 8:16], scalar=1.0,
                                                          in1=tmpc[:, 0:8], op0=ALU.add, op1=ALU.mult),
                  reads=[modT, tmpc], writes=[lay])
            Sx.op("dve", lambda e: e.scalar_tensor_tensor(out=lay[:, 24:32], in0=modT[:, l, 32:40], scalar=1.0,
                                                          in1=tmpc[:, 8:16], op0=ALU.add, op1=ALU.mult),
                  reads=[modT, tmpc, lay], writes=[lay])
            for dst, src in [(8, 0), (16, 16), (32, 24), (40, 40)]:
                Sx.op("dve", lambda e, dst=dst, src=src: e.tensor_copy(lay[:, dst:dst + 8], modT[:, l, src:src + 8]),
                      reads=[modT, lay], writes=[lay])

        def rmsnorm(stk, hb, ub, n, gcol, shcol, sq, rstd, tmpr):
            Sx.op("act", lambda e: e.activation(out=sq[:], in_=hb[:], func=AF.Square), reads=[hb], writes=[sq])
            pm = PS.next()
            for kc in range(8):
                Sx.op("pe", lambda e, kc=kc: e.matmul(pm[:, 0:n], onesb[:], sq[:, kc, :], start=(kc == 0), stop=(kc == 7)),
                      reads=[sq, onesb], writes=[pm])
            Sx.op("act", lambda e: e.activation(out=rstd[:], in_=pm[:, 0:n], func=AF.Sqrt, scale=1.0 / D, bias=epsc[:, 0:1]),
                  reads=[pm, epsc], writes=[rstd])
            Sx.op("dve", lambda e: e.reciprocal(rstd[:], rstd[:]), reads=[rstd], writes=[rstd])
            for kc in range(8):
                tm = tmpr.next()
                Sx.op("dve", lambda e, kc=kc, tm=tm: e.scalar_tensor_tensor(
                    out=tm[:], in0=hb[:, kc, :], scalar=lay[:, gcol + kc:gcol + kc + 1], in1=rstd[:],
                    op0=ALU.mult, op1=ALU.mult), reads=[hb, lay, rstd], writes=[tm])
                Sx.op("act", lambda e, kc=kc, tm=tm: e.activation(out=ub[:, kc, :], in_=tm[:], func=AF.Identity,
                                                                   bias=lay[:, shcol + kc:shcol + kc + 1]),
                      reads=[tm, lay], writes=[(ub, kc)])

        epsc = Sx.sb(gs, [128, 4], F32, "epsc")
        Sx.op("pool", lambda e: e.memset(epsc[:, 0:1], 1e-6), writes=[epsc])
        Sx.op("pool", lambda e: e.memset(epsc[:, 1:2], 64e-5), reads=[epsc], writes=[epsc])
        Sx.op("pool", lambda e: e.memset(epsc[:, 2:3], 1e-24), reads=[epsc], writes=[epsc])

        evac_flip = [0]

        def evac(out_ap, in_ap, reads, writes, eng=None):
            evac_flip[0] ^= 1
            if eng == "dve" or (eng is None and evac_flip[0]):
                Sx.op("dve", lambda e: e.tensor_copy(out_ap, in_ap), reads=reads, writes=writes)
            else:
                Sx.op("act", lambda e: e.activation(out=out_ap, in_=in_ap, func=AF.Copy), reads=reads, writes=writes)

        def P1(l):
            hsrc, hbuf = (xT, None) if l == 0 else (hT, dram["hT"])
            with ExitStack() as ls:
                win = Sx.sb(ls, [128, 8, IN_COLS], BF16, "win")
                hp = Rot([Sx.sb(ls, [128, 8, 512], F32, "h") for _ in range(2)])
                sq = Sx.sb(ls, [128, 8, 512], BF16, "sq")
                ub = Sx.sb(ls, [128, 8, 512], BF16, "u")
                rstd = Sx.sb(ls, [128, 512], F32, "rstd")
                tmpr = Rot([Sx.sb(ls, [128, 512], F32, "tm") for _ in range(2)])
                zo = Rot([Sx.sb(ls, [128, 512], BF16, "zo") for _ in range(4)])
                sqz = Rot([Sx.sb(ls, [128, 512], BF16, "sqz") for _ in range(2)])
                rsq = Rot([Sx.sb(ls, [128, 512], F32, "rsq") for _ in range(2)])
                gq = Sx.sb(ls, [128, 2], F32, "gq")
                vop = Rot([Sx.sb(ls, [128, 8, 65], BF16, "vo") for _ in range(2)])
                for vo in vop.bufs:
                    Sx.op("pool", lambda e, vo=vo: e.memset(vo[:], 1.0), writes=[vo])
                Sx.dma("sp", gq[:], qkg[l], writes=[gq], sbuf=gq)
                for kc in range(8):
                    Sx.dma("pool", win[:, kc, :], w_in[l, kc * 128:(kc + 1) * 128, :], writes=[(win, kc)], sbuf=win,
                           max_dma_last_dim=4096)
                nb = S // 512

                def load(b):
                    hb = hp.next()
                    Sx.dma("sp", hb[:], hsrc[:, b * 512:(b + 1) * 512].rearrange("(kc p) t -> p kc t", p=128),
                           reads=[hbuf] if hbuf else [], writes=[hb], sbuf=hb)
                    return hb
                nxt = load(0)
                for b in range(nb):
                    hb = nxt
                    if b + 1 < nb:
                        nxt = load(b + 1)
                    if l == 0:
                        Sx.dma("pool", hT[:, b * 512:(b + 1) * 512].rearrange("(kc p) t -> p kc t", p=128), hb[:],
                               reads=[hb], writes=[(dram["hT"], b)], sbuf=hb)
                    rmsnorm(ls, hb, ub, 512, 0, 8, sq, rstd, tmpr)
                    tsl = slice(b * 512, (b + 1) * 512)
                    for fc in range(23):
                        pm = PS.next()
                        for kc in range(8):
                            Sx.op("pe", lambda e, kc=kc, fc=fc, pm=pm: e.matmul(
                                pm[:, :], win[:, kc, fc * 128:(fc + 1) * 128], ub[:, kc, :], start=(kc == 0), stop=(kc == 7)),
                                reads=[win, ub], writes=[pm])
                        z = zo.next()
                        if fc < 15:
                            evac(z[:], pm[:, :], [pm], [z])
                            Sx.dma("pool", zT[fc * 128:(fc + 1) * 128, tsl], z[:], reads=[z], writes=[(dram["zT"], (fc, b))], sbuf=z)
                        else:
                            isq = fc < 19
                            s2 = sqz.next()
                            r2 = rsq.next()
                            Sx.op("act", lambda e, s2=s2, pm=pm: e.activation(out=s2[:], in_=pm[:, :], func=AF.Square),
                                  reads=[pm], writes=[s2])
                            p2 = PS.next()
                            Sx.op("pe", lambda e, p2=p2, s2=s2: e.matmul(p2[:, :], blkb[:], s2[:], start=True, stop=True),
                                  reads=[s2, blkb], writes=[p2])
                            Sx.op("act", lambda e, p2=p2, r2=r2: e.activation(out=r2[:], in_=p2[:, :], func=AF.Sqrt,
                                                                             scale=1.0 / 64, bias=epsc[:, 0:1]),
                                  reads=[p2, epsc], writes=[r2])
                            Sx.op("dve", lambda e, r2=r2: e.reciprocal(r2[:], r2[:]), reads=[r2], writes=[r2])
                            gi = 0 if isq else 1
                            Sx.op("dve", lambda e, z=z, pm=pm, r2=r2, gi=gi: e.scalar_tensor_tensor(
                                out=z[:], in0=pm[:, :], scalar=gq[:, gi:gi + 1], in1=r2[:], op0=ALU.mult, op1=ALU.mult),
                                reads=[pm, gq, r2], writes=[z])
                            if isq:
                                Sx.dma("pool", QT[(fc - 15) * 128:(fc - 14) * 128, tsl], z[:], reads=[z],
                                       writes=[(dram["QT"], (fc, b))], sbuf=z)
                            else:
                                Sx.dma("pool", KT[(fc - 19) * 128:(fc - 18) * 128, tsl], z[:], reads=[z],
                                       writes=[(dram["KT"], (fc, b))], sbuf=z)
                    for sub in range(4):
                        pm = PS.next()
                        for kc in range(8):
                            Sx.op("pe", lambda e, kc=kc, sub=sub, pm=pm: e.matmul(
                                pm[:, :], ub[:, kc, sub * 128:(sub + 1) * 128], win[:, kc, 2944:3456],
                                start=(kc == 0), stop=(kc == 7)), reads=[win, ub], writes=[pm])
                        z = vop.next()
                        evac(z[:, :, 0:64], pm[:, :].rearrange("p (h d) -> p h d", d=64), [pm], [z])
                        Sx.dma("pool", Vtm[b * 512 + sub * 128: b * 512 + (sub + 1) * 128, :], z[:].rearrange("p h d -> p (h d)"), reads=[z],
                               writes=[(dram["Vtm"], (b, sub))], sbuf=z)
                Sx.barrier()

        def P2(l):
            with ExitStack() as ls:
                ktp = Rot([Sx.sb(ls, [128, 4, 576], BF16, "kt") for _ in range(2)])
                qzp = Rot([Sx.sb(ls, [128, 8, 128], BF16, "qz") for _ in range(2)])
                vwp = Rot([Sx.sb(ls, [128, 5, 8, 65], BF16, "vw") for _ in range(2)])
                bias = Sx.sb(ls, [128, 5, 8, 128], F32, "bias")
                sTp = Rot([Sx.sb(ls, [128, 5, 128], F32, "sT") for _ in range(2)])
                pTp = Rot([Sx.sb(ls, [128, 5, 128], BF16, "pT") for _ in range(2)])
                yap = Rot([Sx.sb(ls, [128, 512], BF16, "ya") for _ in range(2)])
                yTp = Rot([Sx.sb(ls, [128, 4, 128], BF16, "yT") for _ in range(2)])
                rs = Sx.sb(ls, [128, 8], F32, "rs")
                for qz in qzp.bufs:
                    Sx.op("pool", lambda e, qz=qz: e.memset(qz[:], 0.0), writes=[qz])
                QTv = QT.rearrange("(pr par d) t -> par d pr t", par=2, d=64)

                def load(j):
                    kb, var = tiles[j]
                    kt, qz, vw = ktp.next(), qzp.next(), vwp.next()
                    Sx.dma("sp", kt[:], KT[:, kb * 64:kb * 64 + 576].rearrange("(pr p) t -> p pr t", p=128),
                           reads=[dram["KT"]], writes=[kt], sbuf=kt)
                    qzv = qz[:].rearrange("p (pr par) q -> p par pr q", par=2)
                    for par in range(2):
                        Sx.dma("sp", qzv[par * 64:(par + 1) * 64, par, :, :], QTv[par, :, :, j * 128:(j + 1) * 128],
                               reads=[dram["QT"]], writes=[qz], sbuf=qz)
                    Sx.dma("sp", vw[:, 0:4, :, :].rearrange("p c h d -> p c (h d)"),
                           Vtm[kb * 64:kb * 64 + 512, :].rearrange("(c p) f -> p c f", p=128),
                           reads=[dram["Vtm"]], writes=[vw], sbuf=vw)
                    Sx.dma("sp", vw[0:64, 4, :, :].rearrange("p h d -> p (h d)"),
                           Vtm[kb * 64 + 512:kb * 64 + 576, :],
                           reads=[dram["Vtm"]], writes=[vw], sbuf=vw)
                    return kt, qz, vw
                cur_var = -1
                nxt = load(0)
                for j in range(NT):
                    kt, qz, vw = nxt
                    if j + 1 < NT:
                        nxt = load(j + 1)
                    kb, var = tiles[j]
                    if var != cur_var:
                        cur_var = var
                        Sx.dma("sp", bias[:], biasT[l, var], writes=[bias], sbuf=bias)
                    poA, poB = psb[0], psb[1]
                    for h in range(8):
                        pr = h // 2
                        pA, pB = PS6.next(), PS6.next()
                        for c in range(4):
                            Sx.op("pe", lambda e, c=c, pA=pA, kt=kt, qz=qz, pr=pr, h=h: e.matmul(
                                pA[:, c * 128:(c + 1) * 128], kt[:, pr, c * 128:(c + 1) * 128], qz[:, h, :],
                                start=True, stop=True), reads=[kt, qz], writes=[pA])
                        Sx.op("pe", lambda e, pB=pB, kt=kt, qz=qz, pr=pr, h=h: e.matmul(
                            pB[0:64, 0:128], kt[:, pr, 512:576], qz[:, h, :], start=True, stop=True),
                            reads=[kt, qz], writes=[pB])
                        sT, pT = sTp.next(), pTp.next()
                        Sx.op("dve", lambda e, sT=sT, pA=pA, h=h: e.scalar_tensor_tensor(
                            out=sT[:, 0:4, :], in0=pA[:, :].rearrange("p (c q) -> p c q", c=4), scalar=0.125,
                            in1=bias[:, 0:4, h, :], op0=ALU.mult, op1=ALU.add), reads=[pA, bias], writes=[(sT, 0)])
                        Sx.op("dve", lambda e, sT=sT, pB=pB, h=h: e.scalar_tensor_tensor(
                            out=sT[0:64, 4, :], in0=pB[0:64, 0:128], scalar=0.125,
                            in1=bias[0:64, 4, h, :], op0=ALU.mult, op1=ALU.add), reads=[pB, bias], writes=[(sT, 1)])
                        Sx.op("act", lambda e, sT=sT, pT=pT: e.activation(out=pT[:, 0:4, :], in_=sT[:, 0:4, :], func=AF.Exp),
                              reads=[(sT, 0)], writes=[(pT, 0)])
                        Sx.op("act", lambda e, sT=sT, pT=pT: e.activation(out=pT[0:64, 4, :], in_=sT[0:64, 4, :], func=AF.Exp),
                              reads=[(sT, 1)], writes=[(pT, 1)])
                        po = poA if h < 4 else poB
                        hh = h % 4
                        for c in range(4):
                            Sx.op("pe", lambda e, c=c, po=po, pT=pT, vw=vw, h=h, hh=hh: e.matmul(
                                po[:, hh * 65:(hh + 1) * 65], pT[:, c, :], vw[:, c, h, :], start=(c == 0), stop=False),
                                reads=[(pT, 0), vw], writes=[po])
                        Sx.op("pe", lambda e, po=po, pT=pT, vw=vw, h=h, hh=hh: e.matmul(
                            po[:, hh * 65:(hh + 1) * 65], pT[0:64, 4, :], vw[0:64, 4, h, :], start=False, stop=True),
                            reads=[(pT, 1), vw], writes=[po])
                    ya = yap.next()
                    for hf, po in enumerate([poA, poB]):
                        pv = po[:, 0:260].rearrange("p (h d) -> p h d", d=65)
                        Sx.op("dve", lambda e, pv=pv, hf=hf: e.reciprocal(rs[:, hf * 4:(hf + 1) * 4].unsqueeze(2), pv[:, :, 64:65]),
                              reads=[po], writes=[rs])
                        Sx.op("dve", lambda e, pv=pv, hf=hf, ya=ya: e.tensor_tensor(
                            out=ya[:, hf * 256:(hf + 1) * 256].rearrange("p (h d) -> p h d", d=64), in0=pv[:, :, 0:64],
                            in1=rs[:, hf * 4:(hf + 1) * 4].unsqueeze(2).to_broadcast([128, 4, 64]), op=ALU.mult),
                            reads=[po, rs], writes=[ya])
                    pt = PS6.next()
                    ptb = pt[:].bitcast(BF16)
                    for pr in range(4):
                        Sx.op("pe", lambda e, pr=pr, ptb=ptb, ya=ya: e.transpose(ptb[:, pr * 128:(pr + 1) * 128],
                                                                                ya[:, pr * 128:(pr + 1) * 128], identb[:]),
                              reads=[ya, identb], writes=[pt])
                    yT = yTp.next()
                    evac(yT[:].rearrange("p a q -> p (a q)"), ptb[:, 0:512], [pt], [yT])
                    Sx.dma("pool", ymT[512:1024, j * 128:(j + 1) * 128].rearrange("(pr p) q -> p pr q", p=128), yT[:],
                           reads=[yT], writes=[(dram["ymT"], ("a", j))], sbuf=yT)
                Sx.barrier()

        def P4(l, last):
            NB = 256
            hdst, hdb = (outT, dram["outT"]) if last else (hT, dram["hT"])
            with ExitStack() as ls:
                wo = Sx.sb(ls, [128, 8, D], BF16, "wo")
                wf1 = Sx.sb(ls, [128, 8, 2 * DFF], BF16, "wf1")
                wf2 = Sx.sb(ls, [128, 22, D], BF16, "wf2")
                hp = Rot([Sx.sb(ls, [128, 8, NB], F32, "h") for _ in range(2)])
                ymp = Rot([Sx.sb(ls, [128, 8, NB], BF16, "ym") for _ in range(2)])
                sq = Sx.sb(ls, [128, 8, NB], BF16, "sq")
                ub = Sx.sb(ls, [128, 8, NB], BF16, "u")
                hid = Sx.sb(ls, [128, 22, NB], BF16, "hid")
                rstd = Sx.sb(ls, [128, NB], F32, "rstd")
                tmpr = Rot([Sx.sb(ls, [128, NB], F32, "tm") for _ in range(2)])
                silp = Rot([Sx.sb(ls, [128, NB], F32, "sil") for _ in range(2)])
                for kc in range(8):
                    Sx.dma("pool", wo[:, kc, :], w_out[l, kc * 128:(kc + 1) * 128, :], writes=[(wo, kc)], sbuf=wo)
                for kc in range(8):
                    Sx.dma("pool", wf1[:, kc, :], f_in[l, kc * 128:(kc + 1) * 128, :], writes=[(wf1, kc)], sbuf=wf1,
                           max_dma_last_dim=4096)
                for j in range(22):
                    Sx.dma("pool", wf2[:, j, :], f_out[l, j * 128:(j + 1) * 128, :], writes=[(wf2, j)], sbuf=wf2)
                nb = S // NB

                def load(b):
                    hb, ym = hp.next(), ymp.next()
                    Sx.dma("sp", hb[:], hT[:, b * NB:(b + 1) * NB].rearrange("(kc p) t -> p kc t", p=128),
                           reads=[(dram["hT"], b)], writes=[hb], sbuf=hb)
                    Sx.dma("sp", ym[:], ymT[:, b * NB:(b + 1) * NB].rearrange("(kc p) t -> p kc t", p=128),
                           reads=[dram["ymT"]], writes=[ym], sbuf=ym)
                    return hb, ym
                nxt = load(0)
                for b in range(nb):
                    hb, ym = nxt
                    if b + 1 < nb:
                        nxt = load(b + 1)
                    for fc in range(8):
                        pm = PS.next()
                        for kc in range(8):
                            Sx.op("pe", lambda e, kc=kc, fc=fc, pm=pm, ym=ym: e.matmul(
                                pm[:, 0:NB], wo[:, kc, fc * 128:(fc + 1) * 128], ym[:, kc, :], start=(kc == 0), stop=(kc == 7)),
                                reads=[wo, ym], writes=[pm])
                        Sx.op("dve", lambda e, fc=fc, pm=pm, hb=hb: e.scalar_tensor_tensor(
                            out=hb[:, fc, :], in0=pm[:, 0:NB], scalar=lay[:, 16 + fc:17 + fc], in1=hb[:, fc, :],
                            op0=ALU.mult, op1=ALU.add), reads=[pm, lay, hb], writes=[hb])
                    rmsnorm(ls, hb, ub, NB, 24, 32, sq, rstd, tmpr)
                    for j in range(22):
                        pg, pu = PS.next(), PS.next()
                        for kc in range(8):
                            Sx.op("pe", lambda e, kc=kc, j=j, pg=pg: e.matmul(
                                pg[:, 0:NB], wf1[:, kc, j * 128:(j + 1) * 128], ub[:, kc, :], start=(kc == 0), stop=(kc == 7)),
                                reads=[wf1, ub], writes=[pg])
                        for kc in range(8):
                            Sx.op("pe", lambda e, kc=kc, j=j, pu=pu: e.matmul(
                                pu[:, 0:NB], wf1[:, kc, DFF + j * 128:DFF + (j + 1) * 128], ub[:, kc, :],
                                start=(kc == 0), stop=(kc == 7)), reads=[wf1, ub], writes=[pu])
                        sl = silp.next()
                        Sx.op("act", lambda e, sl=sl, pg=pg: e.activation(out=sl[:], in_=pg[:, 0:NB], func=AF.Silu),
                              reads=[pg], writes=[sl])
                        Sx.op("dve", lambda e, sl=sl, pu=pu, j=j: e.tensor_tensor(out=hid[:, j, :], in0=sl[:], in1=pu[:, 0:NB],
                                                                              op=ALU.mult), reads=[sl, pu], writes=[(hid, j)])
                    for fc in range(8):
                        pm = PS.next()
                        for j in range(22):
                            Sx.op("pe", lambda e, j=j, fc=fc, pm=pm: e.matmul(
                                pm[:, 0:NB], wf2[:, j, fc * 128:(fc + 1) * 128], hid[:, j, :], start=(j == 0), stop=(j == 21)),
                                reads=[wf2, hid], writes=[pm])
                        Sx.op("dve", lambda e, fc=fc, pm=pm, hb=hb: e.scalar_tensor_tensor(
                            out=hb[:, fc, :], in0=pm[:, 0:NB], scalar=lay[:, 40 + fc:41 + fc], in1=hb[:, fc, :],
                            op0=ALU.mult, op1=ALU.add), reads=[pm, lay, hb], writes=[hb])
                    Sx.dma("pool", hdst[:, b * NB:(b + 1) * NB].rearrange("(kc p) t -> p kc t", p=128), hb[:],
                           reads=[hb], writes=[(hdb, b)], sbuf=hb)
                Sx.barrier()

        P3 = make_P3(locals())

        for l in range(NL):
            layer_cols(l)
            if 1 in PHASES:
                P1(l)
            if 2 in PHASES:
                P2(l)
            if 3 in PHASES:
                P3(l)
            if 4 in PHASES:
                P4(l, l == NL - 1)
        Sx.emit()
    return nc


def make_P3(env):
    Sx = env["Sx"]; PS = env["PS"]; nc = env["nc"]; S = env["S"]; NT = env["NT"]; dram = env["dram"]
    zT = env["zT"]; zsT = env["zsT"]; Vtm = env["Vtm"]; ymT = env["ymT"]; yfw = env["yfw"]
    muT = env["muT"]; w0T = env["w0T"]; a0T = env["a0T"]; w2Z = env["w2Z"]; a2Z = env["a2Z"]; g2 = env["g2"]
    vecT = env["vecT"]
    identf = env["identf"]; identb = env["identb"]; blkf = env["blkf"]; blkb = env["blkb"]
    mLs = env["mLs"]; mLi = env["mLi"]; mUs = env["mUs"]; mUi = env["mUi"]; epsc = env["epsc"]
    evac = env["evac"]; psb = env["psb"]
    from contextlib import ExitStack
    MID = 63
    C1 = float(np.exp(-0.5))

    def P3pre(l):
        with ExitStack() as ls:
            PW = min(2048, S)
            mu = Sx.sb(ls, [128, 3, 15], F32, "mu")
            Sx.dma("sp", mu[:, 0:2, :], muT[l], writes=[mu], sbuf=mu)
            Sx.op("dve", lambda e: e.tensor_tensor(out=mu[:, 2, :], in0=mu[:, 0, :], in1=mu[:, 1, :], op=ALU.add),
                  reads=[mu], writes=[mu])
            Sx.op("dve", lambda e: e.tensor_scalar(out=mu[:, 2, :], in0=mu[:, 2, :], scalar1=-1.0, scalar2=1.0,
                                                   op0=ALU.mult, op1=ALU.add), reads=[mu], writes=[mu])
            zrp = Rot([Sx.sb(ls, [128, S + 2], BF16, "zr") for _ in range(2)])
            tmp = Rot([Sx.sb(ls, [128, PW], F32, "ztmp") for _ in range(2)])
            zop = Rot([Sx.sb(ls, [128, PW], BF16, "zso") for _ in range(3)])
            for zr in zrp.bufs:
                Sx.op("pool", lambda e, zr=zr: e.memset(zr[:, 0:1], 0.0), writes=[zr])
                Sx.op("pool", lambda e, zr=zr: e.memset(zr[:, S + 1:S + 2], 0.0), reads=[zr], writes=[zr])

            def loadrow(j):
                zr = zrp.next()
                Sx.dma("sp", zr[:, 1:S + 1], zT[j * 128:(j + 1) * 128, :], reads=[dram["zT"]], writes=[zr], sbuf=zr)
                return zr
            nxt = loadrow(0)
            for j in range(15):
                zr = nxt
                if j + 1 < 15:
                    nxt = loadrow(j + 1)
                for pc in range(S // PW):
                    a = pc * PW
                    tm, zo = tmp.next(), zop.next()
                    Sx.op("act", lambda e, tm=tm, zr=zr, a=a, j=j: e.activation(out=tm[:], in_=zr[:, 1 + a:1 + a + PW], func=AF.Identity,
                                                                              scale=mu[:, 2, j:j + 1]), reads=[zr, mu], writes=[tm])
                    Sx.op("dve", lambda e, tm=tm, zr=zr, a=a, j=j: e.scalar_tensor_tensor(out=tm[:], in0=zr[:, a:a + PW], scalar=mu[:, 0, j:j + 1],
                                                                                        in1=tm[:], op0=ALU.mult, op1=ALU.add),
                          reads=[zr, mu, tm], writes=[tm])
                    Sx.op("dve", lambda e, tm=tm, zr=zr, a=a, j=j, zo=zo: e.scalar_tensor_tensor(out=zo[:], in0=zr[:, a + 2:a + 2 + PW],
                                                                                               scalar=mu[:, 1, j:j + 1], in1=tm[:],
                                                                                               op0=ALU.mult, op1=ALU.add),
                          reads=[zr, mu, tm], writes=[zo])
                    Sx.dma("pool", zsT[j * 128:(j + 1) * 128, a:a + PW], zo[:], reads=[zo], writes=[(dram["zsT"], (j, pc))], sbuf=zo)
            Sx.barrier()

    def P3(l):
        P3pre(l)
        with ExitStack() as ls:
            sb = lambda shape, dt, name: Sx.sb(ls, shape, dt, name)
            w0c = sb([128, 2, 4], F32, "w0c"); a0c = sb([128, 2, 4], F32, "a0c")
            w2s = sb([128, 2, RW], BF16, "w2s"); a2s = sb([128, 2, RW], BF16, "a2s"); g2s = sb([128, RW], BF16, "g2s")
            vec = sb([128, 5, 4], F32, "vec")
            oneka = sb([128, 4], F32, "oneka")
            ones128 = sb([128, 128], F32, "ones128")
            Sx.dma("sp", w0c[:], w0T[l], writes=[w0c], sbuf=w0c)
            Sx.dma("sp", a0c[:], a0T[l], writes=[a0c], sbuf=a0c)
            Sx.dma("sp", vec[:], vecT[l], writes=[vec], sbuf=vec)
            Sx.dma("pool", w2s[:], w2Z[l], writes=[w2s], sbuf=w2s)
            Sx.dma("pool", a2s[:], a2Z[l], writes=[a2s], sbuf=a2s)
            Sx.dma("pool", g2s[:], g2[l], writes=[g2s], sbuf=g2s)
            Sx.op("pool", lambda e: e.memset(ones128[:], 1.0), writes=[ones128])
            Sx.op("dve", lambda e: e.tensor_scalar(out=oneka[:], in0=vec[:, 1, :], scalar1=-1.0, scalar2=1.0,
                                                   op0=ALU.mult, op1=ALU.add), reads=[vec], writes=[oneka])
            kark = sb([128, 8], F32, "kark")
            Sx.op("dve", lambda e: e.tensor_tensor(out=kark[:, 0:4], in0=vec[:, 1, :], in1=vec[:, 2, :], op=ALU.mult),
                  reads=[vec], writes=[kark])
            Sx.op("dve", lambda e: e.scalar_tensor_tensor(out=kark[:, 4:8], in0=oneka[:], scalar=2.0, in1=vec[:, 2, :],
                                                          op0=ALU.mult, op1=ALU.mult), reads=[vec, oneka, kark], writes=[kark])
            tot = sb([128, 4], F32, "tot")
            zcp = Rot([sb([128, 15, 128], BF16, "zc") for _ in range(2)])
            actb = sb([128, 3, 128], BF16, "actb")
            sg = sb([128, 4, 128], F32, "sg")
            aa = sb([128, 4, 128], F32, "aa")
            aa2 = sb([128, 4, 128], F32, "aa2")
            kk = sb([128, 4, 128], F32, "kk")
            t1 = sb([128, 4, 128], F32, "t1")
            t2 = sb([128, 4, 128], F32, "t2")
            kd = sb([128, 4, 128], F32, "kd")
            bb = sb([128, 4, 128], F32, "bb")
            Lc = sb([128, 4, 128], F32, "Lc")
            Lm = sb([128, 4, 128], F32, "Lm")
            eR = sb([128, 4, 128], F32, "eR"); eA = sb([128, 4, 128], F32, "eA")
            eB = sb([128, 4, 128], F32, "eB"); eE = sb([128, 4, 128], F32, "eE")
            ARp = Rot([sb([128, 4, 2, 128], BF16, "AR") for _ in range(3)])
            BTu = sb([128, 4, 128], BF16, "BTu")
            AZ = sb([128, 4, 2, 128], BF16, "AZ"); BZ = sb([128, 4, 2, 128], BF16, "BZ"); KZ = sb([128, 4, 2, 128], BF16, "KZ")
            bpf = sb([128, 4, 128], BF16, "bpf"); kpf = sb([128, 4, 128], BF16, "kpf"); vTf = sb([128, 4, 128], BF16, "vTf")
            Bpp = Rot([sb([128, 4, 128], BF16, "Bp") for _ in range(3)])
            Kpp = Rot([sb([128, 4, 128], BF16, "Kp") for _ in range(3)])
            Vp = Rot([sb([128, 512], BF16, "V") for _ in range(3)])
            PCf = Rot([sb([128, 4, 128], F32, "PCf") for _ in range(3)])
            PMf = Rot([sb([128, 4, 128], F32, "PMf") for _ in range(3)])
            pcol = sb([128, 8], F32, "pcol")
            M = [sb([128, 8, 128], F32, "M%d" % i) for i in range(2)]
            Mt = [sb([128, 8, 128], F32, "Mt%d" % i) for i in range(2)]
            St = [sb([128, 8, 128], F32, "St%d" % i) for i in range(2)]
            Tp = Rot([sb([128, 8, 128], BF16, "T") for _ in range(2)])
            Akp = Rot([sb([128, 8, 128], BF16, "Ak") for _ in range(2)])
            Arbp = Rot([sb([128, 8, 128], BF16, "Arb") for _ in range(2)])
            Arkp = Rot([sb([128, 8, 128], BF16, "Ark") for _ in range(2)])
            H = sb([128, 4, 128], F32, "H")
            Hs = Rot([sb([128, 4, 128], BF16, "Hs") for _ in range(2)])
            Xb = sb([128, 512], BF16, "Xb"); Ub = sb([128, 512], BF16, "Ub")
            Yp = Rot([sb([128, 512], F32, "Y") for _ in range(2)])
            gTp = Rot([sb([128, 4, 128], F32, "gT") for _ in range(3)])
            bonp = Rot([sb([128, 4, 128], F32, "bon") for _ in range(3)])
            PSA, PSB, PSS = Rot(psb[0:2]), Rot(psb[2:6]), Rot(psb[6:8])
            yn = sb([128, 8, 64], F32, "yn"); ynb = sb([128, 512], BF16, "ynb")
            st8 = sb([128, 32], F32, "st8")
            oT = Rot([sb([128, 4, 128], BF16, "oT") for _ in range(2)])
            for zb in (AZ, BZ, KZ):
                Sx.op("pool", lambda e, zb=zb: e.memset(zb[:], 0.0), writes=[zb])

            def pair_ops(eng, fn_name, out_b, out_ap, in_b, in_ap, col_b, col_ap, op):
                pass

            def loadz(c):
                zc = zcp.next()
                Sx.dma("sp", zc[:], zsT[:, c * 128:(c + 1) * 128].rearrange("(c p) t -> p c t", p=128),
                       reads=[dram["zsT"]], writes=[zc], sbuf=zc)
                return zc

            def prepA(c, d, zc, out):
                post = (d == 1)
                gT, bon = (gTp.next(), bonp.next()) if post else (None, None)
                zs = zc
                r_ = lambda: zs[:, 0:4, :]
                k_ = lambda: zs[:, 4:8, :]
                v_ = lambda: zs[:, 8:12, :]
                Sx.op("act", lambda e: e.activation(out=actb[:, 0, :], in_=zs[:, 12, :], func=AF.Tanh), reads=[zs], writes=[(actb, 0)])
                Sx.op("act", lambda e: e.activation(out=actb[:, 1, :], in_=zs[:, 13, :], func=AF.Copy), reads=[zs], writes=[(actb, 1)])
                if post:
                    Sx.op("act", lambda e: e.activation(out=actb[:, 2, :], in_=zs[:, 14, :], func=AF.Sigmoid), reads=[zs], writes=[(actb, 2)])

                def lora(wz, dd, idx, outb, bias_b, scale_out=None):
                    pm = PSA.next()
                    for pr in range(4):
                        Sx.op("pe", lambda e, pr=pr, pm=pm: e.matmul(pm[:, pr * 128:(pr + 1) * 128], wz[:, dd, pr * 128:(pr + 1) * 128],
                                                                    actb[:, idx, :], start=True, stop=True),
                              reads=[wz, (actb, idx)], writes=[pm])
                    for pr in range(4):
                        Sx.op("act", lambda e, pr=pr, pm=pm: e.activation(out=outb[:, pr, :], in_=pm[:, pr * 128:(pr + 1) * 128],
                                                                         func=AF.Sigmoid, bias=bias_b[:, dd, pr:pr + 1]),
                              reads=[pm, bias_b], writes=[outb])
                yield
                lora(w2s, d, 0, sg, w0c)
                yield
                lora(a2s, d, 1, aa, a0c)
                yield
                if post:
                    lora(a2s, 0, 1, aa2, a0c)
                    pm = PSA.next()
                    for pr in range(4):
                        Sx.op("pe", lambda e, pr=pr, pm=pm: e.matmul(pm[:, pr * 128:(pr + 1) * 128], g2s[:, pr * 128:(pr + 1) * 128],
                                                                    actb[:, 2, :], start=True, stop=True),
                              reads=[g2s, (actb, 2)], writes=[pm])
                    evac(gT[:].rearrange("p a t -> p (a t)"), pm[:, :], [pm], [gT])
                yield
                for pr in range(4):
                    Sx.op("dve", lambda e, pr=pr: e.tensor_scalar(out=kk[:, pr, :], in0=zs[:, 4 + pr, :], scalar1=vec[:, 0, pr:pr + 1],
                                                                scalar2=None, op0=ALU.mult), reads=[zs, vec], writes=[kk])
                Sx.op("pool", lambda e: e.tensor_tensor(out=t1[:], in0=kk[:], in1=kk[:], op=ALU.mult), reads=[kk], writes=[t1])
                pm = PSA.next()
                for pr in range(4):
                    Sx.op("pe", lambda e, pr=pr, pm=pm: e.matmul(pm[:, pr * 128:(pr + 1) * 128], blkf[:], t1[:, pr, :], start=True, stop=True),
                          reads=[blkf, t1], writes=[pm])
                Sx.op("act", lambda e, pm=pm: e.activation(out=t2[:].rearrange("p a t -> p (a t)"), in_=pm[:, :], func=AF.Sqrt,
                                                          bias=epsc[:, 2:3]), reads=[pm, epsc], writes=[t2])
                Sx.op("dve", lambda e: e.reciprocal(t2[:], t2[:]), reads=[t2], writes=[t2])
                Sx.op("dve", lambda e: e.tensor_tensor(out=kk[:], in0=kk[:], in1=t2[:], op=ALU.mult), reads=[kk, t2], writes=[kk])
                yield
                for pr in range(4):
                    Sx.op("dve", lambda e, pr=pr: e.tensor_scalar(out=t1[:, pr, :], in0=aa[:, pr, :], scalar1=vec[:, 1, pr:pr + 1],
                                                                scalar2=oneka[:, pr:pr + 1], op0=ALU.mult, op1=ALU.add),
                          reads=[aa, vec, oneka], writes=[t1])
                Sx.op("pool", lambda e: e.tensor_tensor(out=kd[:], in0=zs[:, 4:8, :], in1=t1[:], op=ALU.mult), reads=[zs, t1], writes=[kd])
                Sx.op("pool", lambda e: e.tensor_tensor(out=bb[:], in0=kk[:], in1=aa[:], op=ALU.mult), reads=[kk, aa], writes=[bb])
                yield
                if post:
                    Sx.op("dve", lambda e: e.tensor_tensor(out=t2[:], in0=aa[:], in1=aa2[:], op=ALU.add), reads=[aa, aa2], writes=[t2])
                    for pr in range(4):
                        Sx.op("dve", lambda e, pr=pr: e.tensor_scalar(out=t2[:, pr, :], in0=t2[:, pr, :], scalar1=kark[:, pr:pr + 1],
                                                                    scalar2=kark[:, 4 + pr:5 + pr], op0=ALU.mult, op1=ALU.add),
                              reads=[t2, kark], writes=[t2])
                    Sx.op("dve", lambda e: e.tensor_tensor(out=t2[:], in0=t2[:], in1=zs[:, 4:8, :], op=ALU.mult), reads=[t2, zs], writes=[t2])
                    Sx.op("dve", lambda e: e.tensor_tensor(out=t2[:], in0=t2[:], in1=zs[:, 0:4, :], op=ALU.mult), reads=[t2, zs], writes=[t2])
                    pm = PSA.next()
                    for pr in range(4):
                        Sx.op("pe", lambda e, pr=pr, pm=pm: e.matmul(pm[:, pr * 128:(pr + 1) * 128], blkf[:], t2[:, pr, :], start=True, stop=True),
                              reads=[blkf, t2], writes=[pm])
                    Sx.op("dve", lambda e, pm=pm: e.tensor_tensor(out=bon[:].rearrange("p a t -> p (a t)"), in0=pm[:, :],
                                                                 in1=zs[:, 8:12, :].rearrange("p a t -> p (a t)"), op=ALU.mult),
                          reads=[pm, zs], writes=[bon])
                yield
                for pr in range(4):
                    Sx.op("dve", lambda e, pr=pr: e.tensor_tensor_scan(out=Lc[:, pr, :], data0=ones128[:], data1=sg[:, pr, :], initial=0.0,
                                                                      op0=ALU.mult, op1=ALU.add), reads=[ones128, sg], writes=[Lc])
                last = 127
                if d == 1:
                    Sx.op("dve", lambda e: e.tensor_tensor(out=t1[:], in0=sg[:], in1=Lc[:], op=ALU.subtract), reads=[sg, Lc], writes=[t1])
                    Sx.op("dve", lambda e: e.tensor_copy(tot[:].unsqueeze(2), Lc[:, :, 127:128]), reads=[Lc], writes=[tot])
                    Sx.op("dve", lambda e: e.tensor_tensor(out=Lc[:], in0=t1[:], in1=tot[:].unsqueeze(2).to_broadcast([128, 4, 128]),
                                                           op=ALU.add), reads=[t1, tot], writes=[Lc])
                    last = 0
                yield
                Sx.op("dve", lambda e: e.tensor_tensor(out=Lm[:], in0=Lc[:], in1=Lc[:, :, MID:MID + 1].to_broadcast([128, 4, 128]),
                                                       op=ALU.subtract), reads=[Lc], writes=[Lm])
                Sx.op("act", lambda e: e.activation(out=eR[:], in_=Lm[:], func=AF.Exp, scale=-C1), reads=[Lm], writes=[eR])
                Sx.op("act", lambda e: e.activation(out=eB[:], in_=Lm[:], func=AF.Exp, scale=C1), reads=[Lm], writes=[eB])
                Sx.op("pool", lambda e: e.tensor_tensor(out=t1[:], in0=Lm[:], in1=sg[:], op=ALU.subtract), reads=[Lm, sg], writes=[t1])
                Sx.op("act", lambda e: e.activation(out=eA[:], in_=t1[:], func=AF.Exp, scale=-C1), reads=[t1], writes=[eA])
                Sx.op("dve", lambda e: e.tensor_tensor(out=t2[:], in0=Lc[:], in1=Lc[:, :, last:last + 1].to_broadcast([128, 4, 128]),
                                                       op=ALU.subtract), reads=[Lc], writes=[t2])
                Sx.op("act", lambda e: e.activation(out=eE[:], in_=t2[:], func=AF.Exp, scale=C1), reads=[t2], writes=[eE])
                yield
                PC, PM = PCf.next(), PMf.next()
                Sx.op("act", lambda e: e.activation(out=pcol[:, 0:4].unsqueeze(2), in_=Lc[:, :, last:last + 1], func=AF.Exp, scale=-C1),
                      reads=[Lc], writes=[pcol])
                Sx.op("act", lambda e: e.activation(out=pcol[:, 4:8].unsqueeze(2), in_=Lc[:, :, MID:MID + 1], func=AF.Exp, scale=-C1),
                      reads=[Lc, pcol], writes=[pcol])
                Sx.op("dve", lambda e, PC=PC: e.tensor_copy(PC[:], pcol[:, 0:4].unsqueeze(2).to_broadcast([128, 4, 128])),
                      reads=[pcol], writes=[PC])
                for pr in range(4):
                    Sx.op("act", lambda e, PM=PM, pr=pr: e.activation(out=PM[:, pr, :], in_=blkf[:], func=AF.Identity,
                                                                   scale=pcol[:, 4 + pr:5 + pr]), reads=[pcol, blkf], writes=[PM])
                yield
                AR = ARp.next()
                Sx.op("dve", lambda e, AR=AR: e.scalar_tensor_tensor(out=AR[:, :, 0, :], in0=kk[:], scalar=-1.0, in1=eA[:],
                                                                   op0=ALU.mult, op1=ALU.mult), reads=[kk, eA], writes=[AR])
                Sx.op("pool", lambda e, AR=AR: e.tensor_tensor(out=AR[:, :, 1, :], in0=zs[:, 0:4, :], in1=eR[:], op=ALU.mult),
                      reads=[zs, eR, AR], writes=[AR])
                Sx.op("dve", lambda e: e.tensor_tensor(out=BTu[:], in0=bb[:], in1=eB[:], op=ALU.mult), reads=[bb, eB], writes=[BTu])
                Sx.op("dve", lambda e: e.tensor_tensor(out=t1[:], in0=kd[:], in1=eB[:], op=ALU.mult), reads=[kd, eB], writes=[t1])
                yield
                for par in range(2):
                    ps_ = slice(par * 64, (par + 1) * 64)
                    Sx.op("act", lambda e, ps_=ps_, par=par, AR=AR: e.activation(out=AZ[ps_, :, par, :], in_=AR[ps_, :, 0, :], func=AF.Copy), reads=[AR], writes=[AZ])
                    Sx.op("act", lambda e, ps_=ps_, par=par: e.activation(out=BZ[ps_, :, par, :], in_=BTu[ps_, :, :], func=AF.Copy), reads=[BTu], writes=[BZ])
                    Sx.op("act", lambda e, ps_=ps_, par=par: e.activation(out=KZ[ps_, :, par, :], in_=t1[ps_, :, :], func=AF.Copy), reads=[t1], writes=[KZ])
                Sx.op("pool", lambda e: e.tensor_tensor(out=bpf[:], in0=bb[:], in1=eE[:], op=ALU.mult), reads=[bb, eE], writes=[bpf])
                Sx.op("pool", lambda e: e.tensor_tensor(out=kpf[:], in0=kd[:], in1=eE[:], op=ALU.mult), reads=[kd, eE], writes=[kpf])
                Sx.op("act", lambda e: e.activation(out=vTf[:], in_=zs[:, 8:12, :], func=AF.Copy), reads=[zs], writes=[vTf])
                yield
                Bp, Kp, V = Bpp.next(), Kpp.next(), Vp.next()
                for src, dst in [(bpf, Bp), (kpf, Kp), (vTf, V)]:
                    pt = PSA.next()
                    ptb = pt[:].bitcast(BF16)
                    for pr in range(4):
                        Sx.op("pe", lambda e, pr=pr, ptb=ptb, src=src: e.transpose(ptb[:, pr * 128:(pr + 1) * 128], src[:, pr, :], identb[:]),
                              reads=[src, identb], writes=[pt])
                    dap = dst[:] if dst is V else dst[:].rearrange("p a t -> p (a t)")
                    evac(dap, ptb[:, 0:512], [pt], [dst])
                    yield
                out.update(dict(AR=AR, Bp=Bp, Kp=Kp, V=V, PC=PC, PM=PM, gT=gT, bon=bon))

            def prepB(d, P):
                AR = P["AR"]
                m_ab = mLs if d == 0 else mUs
                m_abT = mUs if d == 0 else mLs
                m_inT = mUi if d == 0 else mLi
                Ak, Arb, Ark = Akp.next(), Arbp.next(), Arkp.next()
                for hg in range(2):
                    p1 = PSB.next()
                    for hh in range(4):
                        h = hg * 4 + hh
                        pr, par = h // 2, h % 2
                        Sx.op("pe", lambda e, p1=p1, hh=hh, pr=pr, par=par: e.matmul(p1[:, hh * 128:(hh + 1) * 128], AZ[:, pr, par, :],
                                                                                      BTu[:, pr, :], start=True, stop=True),
                              reads=[AZ, BTu], writes=[p1])
                    Sx.op("dve", lambda e, p1=p1, hg=hg: e.tensor_tensor(out=M[0][:, hg * 4:(hg + 1) * 4, :],
                                                                        in0=p1[:, :].rearrange("p (a t) -> p a t", a=4),
                                                                        in1=m_ab[:].unsqueeze(1).to_broadcast([128, 4, 128]), op=ALU.mult),
                          reads=[p1, m_ab], writes=[(M[0], hg)])
                    for (LZ, o1, m1, o2, m2) in [(BZ, Mt[0], m_abT, Arb, m_inT), (KZ, Ak, m_abT, Ark, m_inT)]:
                        for h2 in range(2):
                            pass
                        for half in range(2):
                            p2 = PSB.next()
                            for q in range(2):
                                h = hg * 4 + half * 2 + q
                                pr, par = h // 2, h % 2
                                Sx.op("pe", lambda e, p2=p2, q=q, pr=pr, par=par, LZ=LZ, AR=AR: e.matmul(
                                    p2[:, q * 256:(q + 1) * 256], LZ[:, pr, par, :], AR[:, pr, :, :].rearrange("p a t -> p (a t)"),
                                    start=True, stop=True), reads=[LZ, AR], writes=[p2])
                            h0 = hg * 4 + half * 2
                            pv = p2[:, :].rearrange("p (q a t) -> p q a t", q=2, a=2)
                            Sx.op("dve", lambda e, pv=pv, o1=o1, m1=m1, h0=h0: e.tensor_tensor(
                                out=o1[:, h0:h0 + 2, :], in0=pv[:, :, 0, :], in1=m1[:].unsqueeze(1).to_broadcast([128, 2, 128]), op=ALU.mult),
                                reads=[p2, m1], writes=[(o1, h0)])
                            Sx.op("dve", lambda e, pv=pv, o2=o2, m2=m2, h0=h0: e.tensor_tensor(
                                out=o2[:, h0:h0 + 2, :], in0=pv[:, :, 1, :], in1=m2[:].unsqueeze(1).to_broadcast([128, 2, 128]), op=ALU.mult),
                                reads=[p2, m2], writes=[(o2, h0)])
                            yield
                Sx.op("dve", lambda e: e.tensor_tensor(out=St[0][:], in0=Mt[0][:], in1=identf[:].unsqueeze(1).to_broadcast([128, 8, 128]),
                                                       op=ALU.add), reads=[Mt[0], identf], writes=[St[0]])
                yield
                cur = 0
                NLV = 6
                for lv in range(NLV):
                    nx = 1 - cur
                    lastlv = (lv == NLV - 1)
                    for hg in range(2):
                        hsl = slice(hg * 4, (hg + 1) * 4)
                        pM = PSB.next()
                        for hh in range(4):
                            h = hg * 4 + hh
                            cs = slice(hh * 128, (hh + 1) * 128)
                            Sx.op("pe", lambda e, pM=pM, cs=cs, h=h, cur=cur: e.matmul(pM[:, cs], Mt[cur][:, h, :], M[cur][:, h, :], start=True, stop=True),
                                  reads=[Mt[cur], M[cur]], writes=[pM])
                        evac(M[nx][:, hsl, :].rearrange("p a t -> p (a t)"), pM[:, :], [pM], [(M[nx], hg)], eng="act")
                        if not lastlv:
                            pMt = PSB.next()
                            for hh in range(4):
                                h = hg * 4 + hh
                                cs = slice(hh * 128, (hh + 1) * 128)
                                Sx.op("pe", lambda e, pMt=pMt, cs=cs, h=h, cur=cur: e.matmul(pMt[:, cs], M[cur][:, h, :], Mt[cur][:, h, :], start=True, stop=True),
                                      reads=[Mt[cur], M[cur]], writes=[pMt])
                            evac(Mt[nx][:, hsl, :].rearrange("p a t -> p (a t)"), pMt[:, :], [pMt], [(Mt[nx], hg)], eng="act")
                        yield
                    for hg in range(2):
                        hsl = slice(hg * 4, (hg + 1) * 4)
                        pS = PSB.next()
                        for hh in range(4):
                            h = hg * 4 + hh
                            cs = slice(hh * 128, (hh + 1) * 128)
                            Sx.op("pe", lambda e, pS=pS, cs=cs, h=h, cur=cur, nx=nx: e.matmul(pS[:, cs], M[nx][:, h, :], St[cur][:, h, :], start=True, stop=True),
                                  reads=[(M[nx], hg), St[cur]], writes=[pS])
                        Sx.op("dve", lambda e, pS=pS, hsl=hsl, cur=cur, nx=nx: e.tensor_tensor(
                            out=St[nx][:, hsl, :].rearrange("p a t -> p (a t)"), in0=pS[:, :],
                            in1=St[cur][:, hsl, :].rearrange("p a t -> p (a t)"), op=ALU.add),
                            reads=[pS, St[cur]], writes=[(St[nx], hg)])
                        yield
                    cur = nx
                T = Tp.next()
                Sx.op("act", lambda e, T=T, cur=cur: e.activation(out=T[:], in_=St[cur][:], func=AF.Copy), reads=[St[cur]], writes=[T])
                P.update(dict(T=T, Ak=Ak, Arb=Arb, Ark=Ark))

            def serial(c, d, P, Hs_cur, PMnext, Hn):
                AR, Bp, Kp, V, PC, T, Ak, Arb, Ark = (P[k] for k in ["AR", "Bp", "Kp", "V", "PC", "T", "Ak", "Arb", "Ark"])
                pX = PSS.next()
                for h in range(8):
                    pr, par = h // 2, h % 2
                    cs = slice(h * 64, (h + 1) * 64)
                    Sx.op("pe", lambda e, pX=pX, cs=cs, pr=pr, par=par: e.matmul(pX[:, cs], AR[:, pr, 0, :], Hs_cur[:, pr, par * 64:(par + 1) * 64],
                                                                                  start=True, stop=False), reads=[AR, Hs_cur], writes=[pX])
                    Sx.op("pe", lambda e, pX=pX, cs=cs, h=h: e.matmul(pX[:, cs], Ak[:, h, :], V[:, cs], start=False, stop=True),
                          reads=[Ak, V], writes=[pX])
                Sx.op("dve", lambda e, pX=pX: e.tensor_copy(Xb[:], pX[:, :]), reads=[pX], writes=[Xb])
                yield
                pU = PSS.next()
                for h in range(8):
                    cs = slice(h * 64, (h + 1) * 64)
                    Sx.op("pe", lambda e, pU=pU, cs=cs, h=h: e.matmul(pU[:, cs], T[:, h, :], Xb[:, cs], start=True, stop=True),
                          reads=[T, Xb], writes=[pU])
                Sx.op("act", lambda e, pU=pU: e.activation(out=Ub[:], in_=pU[:, :], func=AF.Copy), reads=[pU], writes=[Ub])
                yield
                pY = PSS.next()
                for h in range(8):
                    pr, par = h // 2, h % 2
                    cs = slice(h * 64, (h + 1) * 64)
                    Sx.op("pe", lambda e, pY=pY, cs=cs, pr=pr, par=par: e.matmul(pY[:, cs], AR[:, pr, 1, :], Hs_cur[:, pr, par * 64:(par + 1) * 64],
                                                                                  start=True, stop=False), reads=[AR, Hs_cur], writes=[pY])
                    Sx.op("pe", lambda e, pY=pY, cs=cs, h=h: e.matmul(pY[:, cs], Arb[:, h, :], Ub[:, cs], start=False, stop=False),
                          reads=[Arb, Ub], writes=[pY])
                    Sx.op("pe", lambda e, pY=pY, cs=cs, h=h: e.matmul(pY[:, cs], Ark[:, h, :], V[:, cs], start=False, stop=True),
                          reads=[Ark, V], writes=[pY])
                pH = PSS.next()
                for pr in range(4):
                    cs = slice(pr * 128, (pr + 1) * 128)
                    Sx.op("pe", lambda e, pH=pH, cs=cs, pr=pr: e.matmul(pH[:, cs], Bp[:, pr, :], Ub[:, cs], start=True, stop=False),
                          reads=[Bp, Ub], writes=[pH])
                    Sx.op("pe", lambda e, pH=pH, cs=cs, pr=pr: e.matmul(pH[:, cs], Kp[:, pr, :], V[:, cs], start=False, stop=True),
                          reads=[Kp, V], writes=[pH])
                Sx.op("dve", lambda e: e.tensor_tensor(out=H[:], in0=H[:], in1=PC[:], op=ALU.mult), reads=[H, PC], writes=[H])
                Sx.op("dve", lambda e, pH=pH: e.tensor_tensor(out=H[:].rearrange("p a t -> p (a t)"), in0=H[:].rearrange("p a t -> p (a t)"),
                                                             in1=pH[:, :], op=ALU.add), reads=[H, pH], writes=[H])
                if PMnext is not None:
                    Sx.op("dve", lambda e, Hn=Hn: e.tensor_tensor(out=Hn[:], in0=H[:], in1=PMnext[:], op=ALU.mult),
                          reads=[H, PMnext], writes=[Hn])
                yield
                if d == 0:
                    Y = Yp.next()
                    Sx.op("act", lambda e, Y=Y, pY=pY: e.activation(out=Y[:], in_=pY[:, :], func=AF.Copy), reads=[pY], writes=[Y])
                    Sx.dma("pool", yfw[c * 128:(c + 1) * 128, :], Y[:], reads=[Y], writes=[(dram["yfw"], c)], sbuf=Y)
                else:
                    post(c, pY, P["gT"], P["bon"])

            def post(c, pY, gT, bon):
                Y = Yp.next()
                Sx.dma("sp", Y[:], yfw[c * 128:(c + 1) * 128, :], reads=[(dram["yfw"], c)], writes=[Y], sbuf=Y)
                Yv = Y[:].rearrange("p (h d) -> p h d", d=64)
                Sx.op("dve", lambda e: e.tensor_tensor(out=Y[:], in0=Y[:], in1=pY[:, :], op=ALU.add), reads=[Y, pY], writes=[Y])
                Sx.op("dve", lambda e: e.tensor_reduce(out=st8[:, 0:8], in_=Yv, axis=AX.X, op=ALU.add), reads=[Y], writes=[st8])
                Sx.op("dve", lambda e: e.tensor_scalar(out=st8[:, 0:8], in0=st8[:, 0:8], scalar1=1.0 / 64, scalar2=None, op0=ALU.mult),
                      reads=[st8], writes=[st8])
                Sx.op("dve", lambda e: e.tensor_tensor(out=yn[:], in0=Yv, in1=st8[:, 0:8].unsqueeze(2).to_broadcast([128, 8, 64]),
                                                       op=ALU.subtract), reads=[Y, st8], writes=[yn])
                Sx.op("pool", lambda e: e.tensor_tensor(out=Yv, in0=yn[:], in1=yn[:], op=ALU.mult), reads=[yn, Y], writes=[Y])
                Sx.op("dve", lambda e: e.tensor_reduce(out=st8[:, 8:16], in_=Yv, axis=AX.X, op=ALU.add), reads=[Y, st8], writes=[st8])
                Sx.op("act", lambda e: e.activation(out=st8[:, 16:24], in_=st8[:, 8:16], func=AF.Sqrt, scale=1.0 / 64, bias=epsc[:, 1:2]),
                      reads=[st8, epsc], writes=[st8])
                Sx.op("dve", lambda e: e.reciprocal(st8[:, 24:32], st8[:, 16:24]), reads=[st8], writes=[st8])
                Sx.op("dve", lambda e: e.tensor_tensor(out=ynb[:].rearrange("p (h d) -> p h d", d=64), in0=yn[:],
                                                       in1=st8[:, 24:32].unsqueeze(2).to_broadcast([128, 8, 64]), op=ALU.mult),
                      reads=[yn, st8], writes=[ynb])
                pt = PSS.next()
                ptb = pt[:].bitcast(BF16)
                for pr in range(4):
                    Sx.op("pe", lambda e, pr=pr, ptb=ptb: e.transpose(ptb[:, pr * 128:(pr + 1) * 128], ynb[:, pr * 128:(pr + 1) * 128], identb[:]),
                          reads=[ynb, identb], writes=[pt])
                o = oT.next()
                for pr in range(4):
                    Sx.op("dve", lambda e, pr=pr, ptb=ptb: e.tensor_scalar(out=t1[:, pr, :], in0=ptb[:, pr * 128:(pr + 1) * 128],
                                                                        scalar1=vec[:, 3, pr:pr + 1], scalar2=vec[:, 4, pr:pr + 1],
                                                                        op0=ALU.mult, op1=ALU.add), reads=[pt, vec], writes=[t1])
                Sx.op("pool", lambda e: e.tensor_tensor(out=t1[:], in0=t1[:], in1=bon[:], op=ALU.add), reads=[t1, bon], writes=[t1])
                Sx.op("pool", lambda e, o=o: e.tensor_tensor(out=o[:], in0=t1[:], in1=gT[:], op=ALU.mult), reads=[t1, gT], writes=[o])
                Sx.dma("pool", ymT[0:512, c * 128:(c + 1) * 128].rearrange("(pr p) t -> p pr t", p=128), o[:], reads=[o],
                       writes=[(dram["ymT"], ("r", c))], sbuf=o)

            def interleave(gens):
                st = [[g, n, 0] for g, n in gens]
                while st:
                    it = min(st, key=lambda x: (x[2] + 1.0) / x[1])
                    try:
                        next(it[0])
                        it[2] += 1
                    except StopIteration:
                        st.remove(it)

            for d in range(2):
                order = list(range(NT)) if d == 0 else list(range(NT - 1, -1, -1))
                n = len(order)
                Sx.op("pool", lambda e: e.memset(H[:], 0.0), writes=[H])
                Hlist = [Hs.next()]
                Sx.op("pool", lambda e, Hc=Hlist[0]: e.memset(Hc[:], 0.0), writes=[Hlist[0]])
                Ps = {}

                def genA(i, d=d, order=order, Ps=Ps):
                    zc = loadz(order[i])
                    Ps[i] = {}
                    yield from prepA(order[i], d, zc, Ps[i])

                def genB(i, d=d, Ps=Ps):
                    yield from prepB(d, Ps[i])

                def genS(i, d=d, order=order, Ps=Ps, Hlist=Hlist, n=n):
                    Hn = Hs.next() if i + 1 < n else None
                    Hc = Hlist[0]
                    Hlist[0] = Hn
                    yield from serial(order[i], d, Ps[i], Hc, Ps[i + 1]["PM"] if i + 1 < n else None, Hn)
                    del Ps[i]

                for r in range(n + 2):
                    act = []
                    if 2 <= r <= n + 1:
                        act.append((genS(r - 2), 5))
                    if 1 <= r <= n:
                        act.append((genB(r - 1), 34))
                    if r < n:
                        act.append((genA(r), 17))
                    interleave(act)
            Sx.barrier()
    return P3


_NC_CACHE = {}


def prep_inputs(inputs, S, NL):
    f = lambda a: np.ascontiguousarray(a, dtype=np.float32)
    rows = S // GW
    tiles, sigs = att_tiles(rows)
    common = {}
    common["ada_w"] = f(inputs["ada_w"][:NL])
    common["ada_bT"] = f(inputs["ada_b"][:NL].reshape(NL, 48, 128).transpose(0, 2, 1))
    common["n1g"] = f(inputs["norm1_g"][:NL].reshape(NL, 8, 128).transpose(0, 2, 1))
    common["n2g"] = f(inputs["norm2_g"][:NL].reshape(NL, 8, 128).transpose(0, 2, 1))
    common["w_in"] = f(inputs["w_in"][:NL])
    common["muT"] = f(inputs["shift_mu"][:NL].reshape(NL, 2, 15, 128).transpose(0, 3, 1, 2))
    common["w0T"] = f(inputs["w0"][:NL].reshape(NL, 2, 4, 128).transpose(0, 3, 1, 2))
    common["a0T"] = f(inputs["a0"][:NL].reshape(NL, 2, 4, 128).transpose(0, 3, 1, 2))
    w2Z = np.zeros((NL, 128, 2, RW), np.float32)
    a2Z = np.zeros((NL, 128, 2, RW), np.float32)
    for d in range(2):
        w2Z[:, d * 64:(d + 1) * 64, d, :] = inputs["w2"][:NL, d]
        a2Z[:, d * 64:(d + 1) * 64, d, :] = inputs["a2"][:NL, d]
    common["w2Z"] = w2Z
    common["a2Z"] = a2Z
    common["g2"] = f(inputs["g2"][:NL])
    vec = np.stack([inputs[k][:NL].reshape(NL, 4, 128).transpose(0, 2, 1) for k in ["k_k", "k_a", "r_k", "lnx_g", "lnx_b"]], axis=2)
    common["vecT"] = f(vec)
    qk = np.stack([np.tile(inputs["q_norm_g"][:NL], (1, 2)), np.tile(inputs["k_norm_g"][:NL], (1, 2))], axis=2)
    common["qkg"] = f(qk)
    common["biasT"] = f(np.stack([build_bias(np.asarray(inputs["rpb"][l]), sigs) for l in range(NL)]))
    common["w_out"] = f(inputs["w_out"][:NL])
    common["f_in"] = f(inputs["ffn_w_in"][:NL])
    common["f_out"] = f(inputs["ffn_w_out"][:NL])
    return common, len(sigs)


def run(inputs, S, NL, ncores=8, trace=False):
    inputs = {k: np.asarray(v) for k, v in inputs.items()}
    B = inputs["x"].shape[0]
    common, NV = prep_inputs(inputs, S, NL)
    key = (S, NL, NV)
    if key not in _NC_CACHE:
        _NC_CACHE[key] = build_nc(S, NL, NV)
    nc = _NC_CACHE[key]
    in_maps = []
    for cidx in range(ncores):
        b = cidx % B
        m = dict(common)
        m["xT"] = np.ascontiguousarray(inputs["x"][b, :S].T, dtype=np.float32)
        m["cT"] = np.ascontiguousarray(inputs["c"][b].reshape(8, 128).T, dtype=np.float32)
        in_maps.append(m)
    res = run_bass_kernel_spmd(nc, in_maps, core_ids=list(range(ncores)), trace=trace)
    if trace:
        print("EXEC_TIME_NS", res.exec_time_ns)
    nb = min(B, ncores)
    out = np.stack([np.ascontiguousarray(res.results[b]["outT"].T) for b in range(nb)], axis=0)
    if DEBUG:
        return out.astype(np.float32), res.results
    return out.astype(np.float32)


def kernel(**inputs):
    return run(inputs, 8192, 4)
```

```python
import numpy as np
import concourse.bass as bass
import concourse.mybir as mybir
from concourse.bass_utils import run_bass_kernel_spmd

F32 = mybir.dt.float32
BF16 = mybir.dt.bfloat16
AF = mybir.ActivationFunctionType
ALU = mybir.AluOpType
AX = mybir.AxisListType

D = 1024
GW = 64
RW = 512
RWKV_COLS = 1920
IN_COLS = 3456
DFF = 2816
NEG = -30000.0
DEBUG = False
PHASES = (1, 2, 3, 4)


class Buf:
    def __init__(self, name, t):
        self.name = name
        self.t = t
        self.st = {}
        self.dsem = None

    def __getitem__(self, k):
        return self.t[k]


class Sched:
    ENG = ["pe", "dve", "act", "pool", "sp"]

    def __init__(self, nc, stack):
        self.nc = nc
        self.stack = stack
        self.sems = {}
        self.count = {}
        self.known = {e: {} for e in self.ENG}
        self.lists = {e: [] for e in self.ENG}
        for e in self.ENG:
            self.sems[e] = stack.enter_context(nc.semaphore("s_" + e))
            self.count[e] = 0
        self.dsems = []
        for i in range(24):
            nm = "d%d" % i
            self.sems[nm] = stack.enter_context(nc.semaphore("s_" + nm))
            self.count[nm] = 0
            self.dsems.append(nm)
        self.dnext = 0
        self.nbuf = 0

    def sb(self, stack, shape, dt, name=None):
        self.nbuf += 1
        name = (name or "b") + "_%d" % self.nbuf
        return Buf(name, stack.enter_context(self.nc.sbuf_tensor(name, list(shape), dt)))

    def ps(self, stack, name=None):
        self.nbuf += 1
        name = (name or "p") + "_%d" % self.nbuf
        return Buf(name, stack.enter_context(self.nc.psum_tensor(name, [128, 512], F32)))

    def dsem_for(self, buf):
        if buf.dsem is None:
            buf.dsem = self.dsems[self.dnext % len(self.dsems)]
            self.dnext += 1
        return buf.dsem

    @staticmethod
    def _norm(x):
        return x if isinstance(x, tuple) else (x, None)

    def _deps(self, reads, writes):
        deps = {}

        def add(tok):
            if tok is None:
                return
            s, v = tok
            if deps.get(s, 0) < v:
                deps[s] = v

        for item in reads:
            b, p = self._norm(item)
            for q, st in b.st.items():
                if p is None or q is None or p == q:
                    add(st[0])
        for item in writes:
            b, p = self._norm(item)
            for q, st in b.st.items():
                if p is None or q is None or p == q:
                    add(st[0])
                    for s, v in st[1].items():
                        add((s, v))
        return deps

    def _commit(self, tok, reads, writes):
        for item in reads:
            b, p = self._norm(item)
            st = b.st.setdefault(p, [None, {}])
            if st[1].get(tok[0], 0) < tok[1]:
                st[1][tok[0]] = tok[1]
        for item in writes:
            b, p = self._norm(item)
            if p is None:
                b.st = {None: [tok, {}]}
            else:
                b.st[p] = [tok, {}]

    def _waits(self, eng, deps):
        w = []
        kn = self.known[eng]
        for s, v in deps.items():
            if s == eng and eng == "pe":
                continue
            if kn.get(s, 0) < v:
                kn[s] = v
                w.append((s, v))
        return w

    def op(self, eng, fn, reads=(), writes=()):
        deps = self._deps(reads, writes)
        w = self._waits(eng, deps)
        self.count[eng] += 1
        tok = (eng, self.count[eng])
        self.lists[eng].append((w, fn, eng, 1))
        self._commit(tok, reads, writes)

    def dma(self, q, out_ap, in_ap, reads=(), writes=(), sbuf=None, **kw):
        ds = self.dsem_for(sbuf)
        deps = self._deps(reads, writes)
        deps[ds] = max(deps.get(ds, 0), self.count[ds])
        w = self._waits(q, deps)
        self.count[ds] += 16
        tok = (ds, self.count[ds])
        self.lists[q].append((w, lambda e: e.dma_start(out=out_ap, in_=in_ap, **kw), ds, 16))
        self._commit(tok, reads, writes)

    def barrier(self):
        for e in self.ENG:
            deps = {s: c for s, c in self.count.items() if c > 0 and s != e}
            w = self._waits(e, deps)
            if w:
                self.lists[e].append((w, None, None, 0))

    def emit(self):
        self.barrier()
        nc = self.nc
        sems = self.sems
        lists = self.lists

        def run(e, items):
            for w, fn, s, inc in items:
                for ws, wv in w:
                    e.wait_ge(sems[ws], wv)
                if fn is not None:
                    fn(e).then_inc(sems[s], inc)

        with nc.Block() as block:
            @block.tensor
            def _(e):
                run(e, lists["pe"])

            @block.vector
            def _(e):
                run(e, lists["dve"])

            @block.scalar
            def _(e):
                run(e, lists["act"])

            @block.gpsimd
            def _(e):
                run(e, lists["pool"])

            @block.sync
            def _(e):
                run(e, lists["sp"])


class Rot:
    def __init__(self, bufs):
        self.bufs = bufs
        self.i = 0

    def next(self):
        b = self.bufs[self.i % len(self.bufs)]
        self.i += 1
        return b


def att_tiles(rows):
    sigs = []
    tiles = []
    for j in range(rows // 2):
        kb = min(max(2 * j - 4, 0), rows - 9)
        i0 = 2 * j
        r00 = min(max(i0 - 4, 0), rows - 8)
        r01 = min(max(i0 + 1 - 4, 0), rows - 8)
        sig = (i0 - kb, r00 - kb, r01 - kb)
        if sig not in sigs:
            sigs.append(sig)
        tiles.append((kb, sigs.index(sig)))
    return tiles, sigs


def build_bias(rpb_l, sigs):
    nv = len(sigs)
    out = np.full((nv, 5 * 128, 8, 128), NEG, np.float32)
    qc = np.arange(64)
    cs = np.clip(qc - 8, 0, GW - 16)
    for vi, (di, d0, d1) in enumerate(sigs):
        for ri in range(2):
            irel = di + ri
            r0rel = d0 if ri == 0 else d1
            for kr in range(r0rel, r0rel + 8):
                ro = kr - irel + 7
                for q in range(64):
                    kc = np.arange(cs[q], cs[q] + 16)
                    co = kc - q + 15
                    out[vi, kr * 64 + kc, :, ri * 64 + q] = rpb_l[:, ro, co].T
    return out.reshape(nv, 5, 128, 8, 128).transpose(0, 2, 1, 3, 4).copy()


def build_nc(S, NL, NV):
    from contextlib import ExitStack
    nc = bass.Bass("TRN2", target_bir_lowering=False)
    rows = S // GW
    NT = S // 128
    tiles, sigs = att_tiles(rows)
    assert len(sigs) == NV

    def din(name, shape, dt=F32):
        return nc.dram_tensor(name, list(shape), dt, kind="ExternalInput").ap()

    def dscr(name, shape, dt):
        if DEBUG:
            return nc.dram_tensor(name, list(shape), dt, kind="ExternalOutput").ap()
        return nc.dram_tensor(name, list(shape), dt).ap()

    xT = din("xT", [D, S])
    cT = din("cT", [128, 8])
    ada_w = din("ada_w", [NL, D, 6 * D])
    ada_bT = din("ada_bT", [NL, 128, 48])
    n1g = din("n1g", [NL, 128, 8])
    n2g = din("n2g", [NL, 128, 8])
    w_in = din("w_in", [NL, D, IN_COLS])
    muT = din("muT", [NL, 128, 2, 15])
    w0T = din("w0T", [NL, 128, 2, 4])
    a0T = din("a0T", [NL, 128, 2, 4])
    w2Z = din("w2Z", [NL, 128, 2, RW])
    a2Z = din("a2Z", [NL, 128, 2, RW])
    g2 = din("g2", [NL, 128, RW])
    vecT = din("vecT", [NL, 128, 5, 4])
    qkg = din("qkg", [NL, 128, 2])
    biasT = din("biasT", [NL, NV, 128, 5, 8, 128])
    w_out = din("w_out", [NL, D, D])
    f_in = din("f_in", [NL, D, 2 * DFF])
    f_out = din("f_out", [NL, DFF, D])
    outT = nc.dram_tensor("outT", [D, S], F32, kind="ExternalOutput").ap()

    hT = dscr("hT", [D, S], F32)
    zT = dscr("zT", [RWKV_COLS, S], BF16)
    zsT = dscr("zsT", [RWKV_COLS, S], BF16)
    QT = dscr("QT", [RW, S], BF16)
    KT = dscr("KT", [RW, S], BF16)
    Vtm = dscr("Vtm", [S, 520], BF16)
    ymT = dscr("ymT", [D, S], BF16)
    yfw = dscr("yfw", [S, RW], F32)
    class DB:
        pass
    dram = {n: Buf(n, None) for n in ["hT", "zT", "zsT", "QT", "KT", "Vtm", "ymT", "yfw", "outT"]}

    with ExitStack() as gs:
        Sx = Sched(nc, gs)
        psb = [Sx.ps(gs) for _ in range(8)]
        PS = Rot(psb)
        PS6 = Rot(psb[2:])
        identf = Sx.sb(gs, [128, 128], F32, "identf")
        identb = Sx.sb(gs, [128, 128], BF16, "identb")
        onesb = Sx.sb(gs, [128, 128], BF16, "onesb")
        blkf = Sx.sb(gs, [128, 128], F32, "blkf")
        blkb = Sx.sb(gs, [128, 128], BF16, "blkb")
        mLs = Sx.sb(gs, [128, 128], F32, "mLs")
        mLi = Sx.sb(gs, [128, 128], F32, "mLi")
        mUs = Sx.sb(gs, [128, 128], F32, "mUs")
        mUi = Sx.sb(gs, [128, 128], F32, "mUi")
        modT = Sx.sb(gs, [128, NL, 48], F32, "modT")
        cact = Sx.sb(gs, [128, 8], F32, "cact")
        lay = Sx.sb(gs, [128, 64], F32, "lay")
        tmpc = Sx.sb(gs, [128, 16], F32, "tmpc")

        Sx.op("pool", lambda e: e.memset(identf[:], 0.0), writes=[identf])
        Sx.op("pool", lambda e: e.affine_select(out=identf[:], in_=identf[:], pattern=[[-1, 128]],
                                                compare_op=ALU.not_equal, fill=1.0, base=0, channel_multiplier=1),
              reads=[identf], writes=[identf])
        Sx.op("pool", lambda e: e.tensor_copy(identb[:], identf[:]), reads=[identf], writes=[identb])
        Sx.op("pool", lambda e: e.memset(onesb[:], 1.0), writes=[onesb])
        Sx.op("pool", lambda e: e.memset(blkf[:], 0.0), writes=[blkf])
        Sx.op("pool", lambda e: e.memset(blkf[0:64, 0:64], 1.0), reads=[blkf], writes=[blkf])
        Sx.op("pool", lambda e: e.memset(blkf[64:128, 64:128], 1.0), reads=[blkf], writes=[blkf])
        Sx.op("pool", lambda e: e.tensor_copy(blkb[:], blkf[:]), reads=[blkf], writes=[blkb])
        for mb, cmp, stp, cm in [(mLs, ALU.is_gt, -1, 1), (mLi, ALU.is_ge, -1, 1), (mUs, ALU.is_gt, 1, -1), (mUi, ALU.is_ge, 1, -1)]:
            Sx.op("pool", lambda e, mb=mb: e.memset(mb[:], 1.0), writes=[mb])
            Sx.op("pool", lambda e, mb=mb, cmp=cmp, stp=stp, cm=cm: e.affine_select(
                out=mb[:], in_=mb[:], pattern=[[stp, 128]], compare_op=cmp, fill=0.0, base=0, channel_multiplier=cm),
                reads=[mb], writes=[mb])

        Sx.dma("sp", cact[:], cT[:, :], writes=[cact], sbuf=cact)
        Sx.op("act", lambda e: e.activation(out=cact[:], in_=cact[:], func=AF.Silu), reads=[cact], writes=[cact])
        with ExitStack() as ls:
            awp = Rot([Sx.sb(ls, [128, 8, 512], F32, "aw") for _ in range(2)])
            adb = Sx.sb(ls, [128, NL, 48], F32, "adb")
            Sx.dma("sp", adb[:], ada_bT.rearrange("l p c -> p l c"), writes=[adb], sbuf=adb)
            for l in range(NL):
                pm = PS.next()
                for cb in range(12):
                    aw = awp.next()
                    Sx.dma("sp", aw[:], ada_w[l, :, cb * 512:(cb + 1) * 512].rearrange("(kc p) f -> p kc f", p=128),
                           writes=[aw], sbuf=aw)
                    for f4 in range(4):
                        fc = cb * 4 + f4
                        for kc in range(8):
                            Sx.op("pe", lambda e, aw=aw, kc=kc, f4=f4, fc=fc, pm=pm: e.matmul(
                                pm[:, fc:fc + 1], aw[:, kc, f4 * 128:(f4 + 1) * 128], cact[:, kc:kc + 1],
                                start=(kc == 0), stop=(kc == 7)), reads=[aw, cact], writes=[pm])
                Sx.op("dve", lambda e, pm=pm, l=l: e.tensor_tensor(out=modT[:, l, :], in0=pm[:, 0:48], in1=adb[:, l, :],
                                                                   op=ALU.add), reads=[pm, adb], writes=[modT])
            Sx.barrier()

        def layer_cols(l):
            Sx.dma("sp", tmpc[:, 0:8], n1g[l], writes=[tmpc], sbuf=tmpc)
            Sx.dma("sp", tmpc[:, 8:16], n2g[l], writes=[tmpc], sbuf=tmpc)
            Sx.op("dve", lambda e: e.scalar_tensor_tensor(out=lay[:, 0:8], in0=modT[:, l, 8:16], scalar=1.0,
                                                          in1=tmpc[:, 0:8], op0=ALU.add, op1=ALU.mult),
                  reads=[modT, tmpc], writes=[lay])
            Sx.op("dve", lambda e: e.scalar_tensor_tensor(out=lay[:, 24:32], in0=modT[:, l, 32:40], scalar=1.0,
                                                          in1=tmpc[:, 8:16], op0=ALU.add, op1=ALU.mult),
                  reads=[modT, tmpc, lay], writes=[lay])
            for dst, src in [(8, 0), (16, 16), (32, 24), (40, 40)]:
                Sx.op("dve", lambda e, dst=dst, src=src: e.tensor_copy(lay[:, dst:dst + 8], modT[:, l, src:src + 8]),
                      reads=[modT, lay], writes=[lay])

        def rmsnorm(stk, hb, ub, n, gcol, shcol, sq, rstd, tmpr):
            Sx.op("act", lambda e: e.activation(out=sq[:], in_=hb[:], func=AF.Square), reads=[hb], writes=[sq])
            pm = PS.next()
            for kc in range(8):
                Sx.op("pe", lambda e, kc=kc: e.matmul(pm[:, 0:n], onesb[:], sq[:, kc, :], start=(kc == 0), stop=(kc == 7)),
                      reads=[sq, onesb], writes=[pm])
            Sx.op("act", lambda e: e.activation(out=rstd[:], in_=pm[:, 0:n], func=AF.Ln, scale=1.0 / D, bias=epsc[:, 0:1]),
                  reads=[pm, epsc], writes=[rstd])
            Sx.op("act", lambda e: e.activation(out=rstd[:], in_=rstd[:], func=AF.Exp, scale=-0.5), reads=[rstd], writes=[rstd])
            for kc in range(8):
                tm = tmpr.next()
                Sx.op("dve", lambda e, kc=kc, tm=tm: e.scalar_tensor_tensor(
                    out=tm[:], in0=hb[:, kc, :], scalar=lay[:, gcol + kc:gcol + kc + 1], in1=rstd[:],
                    op0=ALU.mult, op1=ALU.mult), reads=[hb, lay, rstd], writes=[tm])
                Sx.op("act", lambda e, kc=kc, tm=tm: e.activation(out=ub[:, kc, :], in_=tm[:], func=AF.Identity,
                                                                   bias=lay[:, shcol + kc:shcol + kc + 1]),
                      reads=[tm, lay], writes=[(ub, kc)])

        epsc = Sx.sb(gs, [128, 4], F32, "epsc")
        Sx.op("pool", lambda e: e.memset(epsc[:, 0:1], 1e-6), writes=[epsc])
        Sx.op("pool", lambda e: e.memset(epsc[:, 1:2], 64e-5), reads=[epsc], writes=[epsc])
        Sx.op("pool", lambda e: e.memset(epsc[:, 2:3], 1e-19), reads=[epsc], writes=[epsc])

        evac_flip = [0]

        def evac(out_ap, in_ap, reads, writes, eng=None):
            evac_flip[0] ^= 1
            if eng == "dve" or (eng is None and evac_flip[0]):
                Sx.op("dve", lambda e: e.tensor_copy(out_ap, in_ap), reads=reads, writes=writes)
            else:
                Sx.op("act", lambda e: e.activation(out=out_ap, in_=in_ap, func=AF.Copy), reads=reads, writes=writes)

        def P1(l):
            hsrc, hbuf = (xT, None) if l == 0 else (hT, dram["hT"])
            with ExitStack() as ls:
                win = Sx.sb(ls, [128, 8, IN_COLS], BF16, "win")
                hp = Rot([Sx.sb(ls, [128, 8, 512], F32, "h") for _ in range(2)])
                sq = Sx.sb(ls, [128, 8, 512], BF16, "sq")
                ub = Sx.sb(ls, [128, 8, 512], BF16, "u")
                rstd = Sx.sb(ls, [128, 512], F32, "rstd")
                tmpr = Rot([Sx.sb(ls, [128, 512], F32, "tm") for _ in range(2)])
                zo = Rot([Sx.sb(ls, [128, 512], BF16, "zo") for _ in range(4)])
                sqz = Rot([Sx.sb(ls, [128, 512], BF16, "sqz") for _ in range(2)])
                rsq = Rot([Sx.sb(ls, [128, 512], F32, "rsq") for _ in range(2)])
                gq = Sx.sb(ls, [128, 2], F32, "gq")
                vop = Rot([Sx.sb(ls, [128, 8, 65], BF16, "vo") for _ in range(2)])
                for vo in vop.bufs:
                    Sx.op("pool", lambda e, vo=vo: e.memset(vo[:], 1.0), writes=[vo])
                Sx.dma("sp", gq[:], qkg[l], writes=[gq], sbuf=gq)
                for kc in range(8):
                    Sx.dma("pool", win[:, kc, :], w_in[l, kc * 128:(kc + 1) * 128, :], writes=[(win, kc)], sbuf=win,
                           max_dma_last_dim=4096)
                nb = S // 512

                def load(b):
                    hb = hp.next()
                    Sx.dma("sp", hb[:], hsrc[:, b * 512:(b + 1) * 512].rearrange("(kc p) t -> p kc t", p=128),
                           reads=[hbuf] if hbuf else [], writes=[hb], sbuf=hb)
                    return hb
                nxt = load(0)
                for b in range(nb):
                    hb = nxt
                    if b + 1 < nb:
                        nxt = load(b + 1)
                    if l == 0:
                        Sx.dma("pool", hT[:, b * 512:(b + 1) * 512].rearrange("(kc p) t -> p kc t", p=128), hb[:],
                               reads=[hb], writes=[(dram["hT"], b)], sbuf=hb)
                    rmsnorm(ls, hb, ub, 512, 0, 8, sq, rstd, tmpr)
                    tsl = slice(b * 512, (b + 1) * 512)
                    for fc in range(23):
                        pm = PS.next()
                        for kc in range(8):
                            Sx.op("pe", lambda e, kc=kc, fc=fc, pm=pm: e.matmul(
                                pm[:, :], win[:, kc, fc * 128:(fc + 1) * 128], ub[:, kc, :], start=(kc == 0), stop=(kc == 7)),
                                reads=[win, ub], writes=[pm])
                        z = zo.next()
                        if fc < 15:
                            evac(z[:], pm[:, :], [pm], [z])
                            Sx.dma("pool", zT[fc * 128:(fc + 1) * 128, tsl], z[:], reads=[z], writes=[(dram["zT"], (fc, b))], sbuf=z)
                        else:
                            isq = fc < 19
                            s2 = sqz.next()
                            r2 = rsq.next()
                            Sx.op("act", lambda e, s2=s2, pm=pm: e.activation(out=s2[:], in_=pm[:, :], func=AF.Square),
                                  reads=[pm], writes=[s2])
                            p2 = PS.next()
                            Sx.op("pe", lambda e, p2=p2, s2=s2: e.matmul(p2[:, :], blkb[:], s2[:], start=True, stop=True),
                                  reads=[s2, blkb], writes=[p2])
                            Sx.op("act", lambda e, p2=p2, r2=r2: e.activation(out=r2[:], in_=p2[:, :], func=AF.Ln,
                                                                             scale=1.0 / 64, bias=epsc[:, 0:1]),
                                  reads=[p2, epsc], writes=[r2])
                            Sx.op("act", lambda e, r2=r2: e.activation(out=r2[:], in_=r2[:], func=AF.Exp, scale=-0.5), reads=[r2], writes=[r2])
                            gi = 0 if isq else 1
                            Sx.op("dve", lambda e, z=z, pm=pm, r2=r2, gi=gi: e.scalar_tensor_tensor(
                                out=z[:], in0=pm[:, :], scalar=gq[:, gi:gi + 1], in1=r2[:], op0=ALU.mult, op1=ALU.mult),
                                reads=[pm, gq, r2], writes=[z])
                            if isq:
                                Sx.dma("pool", QT[(fc - 15) * 128:(fc - 14) * 128, tsl], z[:], reads=[z],
                                       writes=[(dram["QT"], (fc, b))], sbuf=z)
                            else:
                                Sx.dma("pool", KT[(fc - 19) * 128:(fc - 18) * 128, tsl], z[:], reads=[z],
                                       writes=[(dram["KT"], (fc, b))], sbuf=z)
                    for sub in range(4):
                        pm = PS.next()
                        for kc in range(8):
                            Sx.op("pe", lambda e, kc=kc, sub=sub, pm=pm: e.matmul(
                                pm[:, :], ub[:, kc, sub * 128:(sub + 1) * 128], win[:, kc, 2944:3456],
                                start=(kc == 0), stop=(kc == 7)), reads=[win, ub], writes=[pm])
                        z = vop.next()
                        evac(z[:, :, 0:64], pm[:, :].rearrange("p (h d) -> p h d", d=64), [pm], [z])
                        Sx.dma("pool", Vtm[b * 512 + sub * 128: b * 512 + (sub + 1) * 128, :], z[:].rearrange("p h d -> p (h d)"), reads=[z],
                               writes=[(dram["Vtm"], (b, sub))], sbuf=z)
                Sx.barrier()

        def P2(l):
            with ExitStack() as ls:
                ktp = Rot([Sx.sb(ls, [128, 4, 576], BF16, "kt") for _ in range(2)])
                qzp = Rot([Sx.sb(ls, [128, 8, 128], BF16, "qz") for _ in range(2)])
                vwp = Rot([Sx.sb(ls, [128, 5, 8, 65], BF16, "vw") for _ in range(2)])
                bias = Sx.sb(ls, [128, 5, 8, 128], F32, "bias")
                sTp = Rot([Sx.sb(ls, [128, 5, 128], F32, "sT") for _ in range(3)])
                pTp = Rot([Sx.sb(ls, [128, 5, 128], BF16, "pT") for _ in range(3)])
                yap = Rot([Sx.sb(ls, [128, 512], BF16, "ya") for _ in range(2)])
                yTp = Rot([Sx.sb(ls, [128, 4, 128], BF16, "yT") for _ in range(2)])
                rs = Sx.sb(ls, [128, 8], F32, "rs")
                for qz in qzp.bufs:
                    Sx.op("pool", lambda e, qz=qz: e.memset(qz[:], 0.0), writes=[qz])
                QTv = QT.rearrange("(pr par d) t -> par d pr t", par=2, d=64)

                def load(j):
                    kb, var = tiles[j]
                    kt, qz, vw = ktp.next(), qzp.next(), vwp.next()
                    Sx.dma("sp", kt[:], KT[:, kb * 64:kb * 64 + 576].rearrange("(pr p) t -> p pr t", p=128),
                           reads=[dram["KT"]], writes=[kt], sbuf=kt)
                    qzv = qz[:].rearrange("p (pr par) q -> p par pr q", par=2)
                    for par in range(2):
                        Sx.dma("sp", qzv[par * 64:(par + 1) * 64, par, :, :], QTv[par, :, :, j * 128:(j + 1) * 128],
                               reads=[dram["QT"]], writes=[qz], sbuf=qz)
                    Sx.dma("sp", vw[:, 0:4, :, :].rearrange("p c h d -> p c (h d)"),
                           Vtm[kb * 64:kb * 64 + 512, :].rearrange("(c p) f -> p c f", p=128),
                           reads=[dram["Vtm"]], writes=[vw], sbuf=vw)
                    Sx.dma("sp", vw[0:64, 4, :, :].rearrange("p h d -> p (h d)"),
                           Vtm[kb * 64 + 512:kb * 64 + 576, :],
                           reads=[dram["Vtm"]], writes=[vw], sbuf=vw)
                    return kt, qz, vw
                cur_var = -1
                nxt = load(0)
                for j in range(NT):
                    kt, qz, vw = nxt
                    if j + 1 < NT:
                        nxt = load(j + 1)
                    kb, var = tiles[j]
                    if var != cur_var:
                        cur_var = var
                        Sx.dma("sp", bias[:], biasT[l, var], writes=[bias], sbuf=bias)
                    poA, poB = psb[0], psb[1]

                    def stage1(h):
                        pr = h // 2
                        pA, pB = PS6.next(), PS6.next()
                        for c in range(4):
                            Sx.op("pe", lambda e, c=c, pA=pA, kt=kt, qz=qz, pr=pr, h=h: e.matmul(
                                pA[:, c * 128:(c + 1) * 128], kt[:, pr, c * 128:(c + 1) * 128], qz[:, h, :],
                                start=True, stop=True), reads=[kt, qz], writes=[pA])
                        Sx.op("pe", lambda e, pB=pB, kt=kt, qz=qz, pr=pr, h=h: e.matmul(
                            pB[0:64, 0:128], kt[:, pr, 512:576], qz[:, h, :], start=True, stop=True),
                            reads=[kt, qz], writes=[pB])
                        sT, pT = sTp.next(), pTp.next()
                        Sx.op("dve", lambda e, sT=sT, pA=pA, h=h: e.scalar_tensor_tensor(
                            out=sT[:, 0:4, :], in0=pA[:, :].rearrange("p (c q) -> p c q", c=4), scalar=0.125,
                            in1=bias[:, 0:4, h, :], op0=ALU.mult, op1=ALU.add), reads=[pA, bias], writes=[(sT, 0)])
                        Sx.op("dve", lambda e, sT=sT, pB=pB, h=h: e.scalar_tensor_tensor(
                            out=sT[0:64, 4, :], in0=pB[0:64, 0:128], scalar=0.125,
                            in1=bias[0:64, 4, h, :], op0=ALU.mult, op1=ALU.add), reads=[pB, bias], writes=[(sT, 1)])
                        Sx.op("act", lambda e, sT=sT, pT=pT: e.activation(out=pT[:, 0:4, :], in_=sT[:, 0:4, :], func=AF.Exp),
                              reads=[(sT, 0)], writes=[(pT, 0)])
                        Sx.op("act", lambda e, sT=sT, pT=pT: e.activation(out=pT[0:64, 4, :], in_=sT[0:64, 4, :], func=AF.Exp),
                              reads=[(sT, 1)], writes=[(pT, 1)])
                        return pT

                    def stage2(h, pT):
                        po = poA if h < 4 else poB
                        hh = h % 4
                        for c in range(4):
                            Sx.op("pe", lambda e, c=c, po=po, pT=pT, vw=vw, h=h, hh=hh: e.matmul(
                                po[:, hh * 65:(hh + 1) * 65], pT[:, c, :], vw[:, c, h, :], start=(c == 0), stop=False),
                                reads=[(pT, 0), vw], writes=[po])
                        Sx.op("pe", lambda e, po=po, pT=pT, vw=vw, h=h, hh=hh: e.matmul(
                            po[:, hh * 65:(hh + 1) * 65], pT[0:64, 4, :], vw[0:64, 4, h, :], start=False, stop=True),
                            reads=[(pT, 1), vw], writes=[po])
                    pTs = {}
                    pTs[0] = stage1(0)
                    for h in range(8):
                        if h + 1 < 8:
                            pTs[h + 1] = stage1(h + 1)
                        stage2(h, pTs.pop(h))
                    ya = yap.next()
                    for hf, po in enumerate([poA, poB]):
                        pv = po[:, 0:260].rearrange("p (h d) -> p h d", d=65)
                        Sx.op("dve", lambda e, pv=pv, hf=hf: e.reciprocal(rs[:, hf * 4:(hf + 1) * 4].unsqueeze(2), pv[:, :, 64:65]),
                              reads=[po], writes=[rs])
                        Sx.op("dve", lambda e, pv=pv, hf=hf, ya=ya: e.tensor_tensor(
                            out=ya[:, hf * 256:(hf + 1) * 256].rearrange("p (h d) -> p h d", d=64), in0=pv[:, :, 0:64],
                            in1=rs[:, hf * 4:(hf + 1) * 4].unsqueeze(2).to_broadcast([128, 4, 64]), op=ALU.mult),
                            reads=[po, rs], writes=[ya])
                    pt = PS6.next()
                    ptb = pt[:].bitcast(BF16)
                    for pr in range(4):
                        Sx.op("pe", lambda e, pr=pr, ptb=ptb, ya=ya: e.transpose(ptb[:, pr * 128:(pr + 1) * 128],
                                                                                ya[:, pr * 128:(pr + 1) * 128], identb[:]),
                              reads=[ya, identb], writes=[pt])
                    yT = yTp.next()
                    evac(yT[:].rearrange("p a q -> p (a q)"), ptb[:, 0:512], [pt], [yT])
                    Sx.dma("pool", ymT[512:1024, j * 128:(j + 1) * 128].rearrange("(pr p) q -> p pr q", p=128), yT[:],
                           reads=[yT], writes=[(dram["ymT"], ("a", j))], sbuf=yT)
                Sx.barrier()

        def P4(l, last):
            NB = 256
            hdst, hdb = (outT, dram["outT"]) if last else (hT, dram["hT"])
            with ExitStack() as ls:
                wo = Sx.sb(ls, [128, 8, D], BF16, "wo")
                wf1 = Sx.sb(ls, [128, 8, 2 * DFF], BF16, "wf1")
                wf2 = Sx.sb(ls, [128, 22, D], BF16, "wf2")
                hp = Rot([Sx.sb(ls, [128, 8, NB], F32, "h") for _ in range(2)])
                ymp = Rot([Sx.sb(ls, [128, 8, NB], BF16, "ym") for _ in range(2)])
                sq = Sx.sb(ls, [128, 8, NB], BF16, "sq")
                ub = Sx.sb(ls, [128, 8, NB], BF16, "u")
                hid = Sx.sb(ls, [128, 22, NB], BF16, "hid")
                rstd = Sx.sb(ls, [128, NB], F32, "rstd")
                tmpr = Rot([Sx.sb(ls, [128, NB], F32, "tm") for _ in range(2)])
                silp = Rot([Sx.sb(ls, [128, NB], F32, "sil") for _ in range(2)])
                for kc in range(8):
                    Sx.dma("pool", wo[:, kc, :], w_out[l, kc * 128:(kc + 1) * 128, :], writes=[(wo, kc)], sbuf=wo)
                for kc in range(8):
                    Sx.dma("pool", wf1[:, kc, :], f_in[l, kc * 128:(kc + 1) * 128, :], writes=[(wf1, kc)], sbuf=wf1,
                           max_dma_last_dim=4096)
                for j in range(22):
                    Sx.dma("pool", wf2[:, j, :], f_out[l, j * 128:(j + 1) * 128, :], writes=[(wf2, j)], sbuf=wf2)
                nb = S // NB

                def load(b):
                    hb, ym = hp.next(), ymp.next()
                    Sx.dma("sp", hb[:], hT[:, b * NB:(b + 1) * NB].rearrange("(kc p) t -> p kc t", p=128),
                           reads=[(dram["hT"], b)], writes=[hb], sbuf=hb)
                    Sx.dma("sp", ym[:], ymT[:, b * NB:(b + 1) * NB].rearrange("(kc p) t -> p kc t", p=128),
                           reads=[dram["ymT"]], writes=[ym], sbuf=ym)
                    return hb, ym
                nxt = load(0)
                for b in range(nb):
                    hb, ym = nxt
                    if b + 1 < nb:
                        nxt = load(b + 1)
                    for fc in range(8):
                        pm = PS.next()
                        for kc in range(8):
                            Sx.op("pe", lambda e, kc=kc, fc=fc, pm=pm, ym=ym: e.matmul(
                                pm[:, 0:NB], wo[:, kc, fc * 128:(fc + 1) * 128], ym[:, kc, :], start=(kc == 0), stop=(kc == 7)),
                                reads=[wo, ym], writes=[pm])
                        Sx.op("dve", lambda e, fc=fc, pm=pm, hb=hb: e.scalar_tensor_tensor(
                            out=hb[:, fc, :], in0=pm[:, 0:NB], scalar=lay[:, 16 + fc:17 + fc], in1=hb[:, fc, :],
                            op0=ALU.mult, op1=ALU.add), reads=[pm, lay, hb], writes=[hb])
                    rmsnorm(ls, hb, ub, NB, 24, 32, sq, rstd, tmpr)
                    for j in range(22):
                        pg, pu = PS.next(), PS.next()
                        for kc in range(8):
                            Sx.op("pe", lambda e, kc=kc, j=j, pg=pg: e.matmul(
                                pg[:, 0:NB], wf1[:, kc, j * 128:(j + 1) * 128], ub[:, kc, :], start=(kc == 0), stop=(kc == 7)),
                                reads=[wf1, ub], writes=[pg])
                        for kc in range(8):
                            Sx.op("pe", lambda e, kc=kc, j=j, pu=pu: e.matmul(
                                pu[:, 0:NB], wf1[:, kc, DFF + j * 128:DFF + (j + 1) * 128], ub[:, kc, :],
                                start=(kc == 0), stop=(kc == 7)), reads=[wf1, ub], writes=[pu])
                        sl = silp.next()
                        Sx.op("act", lambda e, sl=sl, pg=pg: e.activation(out=sl[:], in_=pg[:, 0:NB], func=AF.Silu),
                              reads=[pg], writes=[sl])
                        Sx.op("dve", lambda e, sl=sl, pu=pu, j=j: e.tensor_tensor(out=hid[:, j, :], in0=sl[:], in1=pu[:, 0:NB],
                                                                              op=ALU.mult), reads=[sl, pu], writes=[(hid, j)])
                    for fc in range(8):
                        pm = PS.next()
                        for j in range(22):
                            Sx.op("pe", lambda e, j=j, fc=fc, pm=pm: e.matmul(
                                pm[:, 0:NB], wf2[:, j, fc * 128:(fc + 1) * 128], hid[:, j, :], start=(j == 0), stop=(j == 21)),
                                reads=[wf2, hid], writes=[pm])
                        Sx.op("dve", lambda e, fc=fc, pm=pm, hb=hb: e.scalar_tensor_tensor(
                            out=hb[:, fc, :], in0=pm[:, 0:NB], scalar=lay[:, 40 + fc:41 + fc], in1=hb[:, fc, :],
                            op0=ALU.mult, op1=ALU.add), reads=[pm, lay, hb], writes=[hb])
                    Sx.dma("pool", hdst[:, b * NB:(b + 1) * NB].rearrange("(kc p) t -> p kc t", p=128), hb[:],
                           reads=[hb], writes=[(hdb, b)], sbuf=hb)
                Sx.barrier()

        P3 = make_P3(locals())

        for l in range(NL):
            layer_cols(l)
            if 1 in PHASES:
                P1(l)
            if 2 in PHASES:
                P2(l)
            if 3 in PHASES:
                P3(l)
            if 4 in PHASES:
                P4(l, l == NL - 1)
        Sx.emit()
    return nc


def make_P3(env):
    Sx = env["Sx"]; PS = env["PS"]; nc = env["nc"]; S = env["S"]; NT = env["NT"]; dram = env["dram"]
    zT = env["zT"]; zsT = env["zsT"]; Vtm = env["Vtm"]; ymT = env["ymT"]; yfw = env["yfw"]
    muT = env["muT"]; w0T = env["w0T"]; a0T = env["a0T"]; w2Z = env["w2Z"]; a2Z = env["a2Z"]; g2 = env["g2"]
    vecT = env["vecT"]
    identf = env["identf"]; identb = env["identb"]; blkf = env["blkf"]; blkb = env["blkb"]
    mLs = env["mLs"]; mLi = env["mLi"]; mUs = env["mUs"]; mUi = env["mUi"]; epsc = env["epsc"]
    evac = env["evac"]; psb = env["psb"]
    from contextlib import ExitStack
    MID = 63
    C1 = float(np.exp(-0.5))

    def P3pre(l):
        with ExitStack() as ls:
            PW = min(2048, S)
            mu = Sx.sb(ls, [128, 3, 15], F32, "mu")
            Sx.dma("sp", mu[:, 0:2, :], muT[l], writes=[mu], sbuf=mu)
            Sx.op("dve", lambda e: e.tensor_tensor(out=mu[:, 2, :], in0=mu[:, 0, :], in1=mu[:, 1, :], op=ALU.add),
                  reads=[mu], writes=[mu])
            Sx.op("dve", lambda e: e.tensor_scalar(out=mu[:, 2, :], in0=mu[:, 2, :], scalar1=-1.0, scalar2=1.0,
                                                   op0=ALU.mult, op1=ALU.add), reads=[mu], writes=[mu])
            zrp = Rot([Sx.sb(ls, [128, S + 2], BF16, "zr") for _ in range(2)])
            tmp = Rot([Sx.sb(ls, [128, PW], F32, "ztmp") for _ in range(2)])
            zop = Rot([Sx.sb(ls, [128, PW], BF16, "zso") for _ in range(3)])
            for zr in zrp.bufs:
                Sx.op("pool", lambda e, zr=zr: e.memset(zr[:, 0:1], 0.0), writes=[zr])
                Sx.op("pool", lambda e, zr=zr: e.memset(zr[:, S + 1:S + 2], 0.0), reads=[zr], writes=[zr])

            def loadrow(j):
                zr = zrp.next()
                Sx.dma("sp", zr[:, 1:S + 1], zT[j * 128:(j + 1) * 128, :], reads=[dram["zT"]], writes=[zr], sbuf=zr)
                return zr
            nxt = loadrow(0)
            for j in range(15):
                zr = nxt
                if j + 1 < 15:
                    nxt = loadrow(j + 1)
                for pc in range(S // PW):
                    a = pc * PW
                    tm, zo = tmp.next(), zop.next()
                    Sx.op("act", lambda e, tm=tm, zr=zr, a=a, j=j: e.activation(out=tm[:], in_=zr[:, 1 + a:1 + a + PW], func=AF.Identity,
                                                                              scale=mu[:, 2, j:j + 1]), reads=[zr, mu], writes=[tm])
                    Sx.op("dve", lambda e, tm=tm, zr=zr, a=a, j=j: e.scalar_tensor_tensor(out=tm[:], in0=zr[:, a:a + PW], scalar=mu[:, 0, j:j + 1],
                                                                                        in1=tm[:], op0=ALU.mult, op1=ALU.add),
                          reads=[zr, mu, tm], writes=[tm])
                    Sx.op("dve", lambda e, tm=tm, zr=zr, a=a, j=j, zo=zo: e.scalar_tensor_tensor(out=zo[:], in0=zr[:, a + 2:a + 2 + PW],
                                                                                               scalar=mu[:, 1, j:j + 1], in1=tm[:],
                                                                                               op0=ALU.mult, op1=ALU.add),
                          reads=[zr, mu, tm], writes=[zo])
                    Sx.dma("pool", zsT[j * 128:(j + 1) * 128, a:a + PW], zo[:], reads=[zo], writes=[(dram["zsT"], (j, pc))], sbuf=zo)
            Sx.barrier()

    def P3(l):
        P3pre(l)
        with ExitStack() as ls:
            sb = lambda shape, dt, name: Sx.sb(ls, shape, dt, name)
            w0c = sb([128, 2, 4], F32, "w0c"); a0c = sb([128, 2, 4], F32, "a0c")
            w2s = sb([128, 2, RW], BF16, "w2s"); a2s = sb([128, 2, RW], BF16, "a2s"); g2s = sb([128, RW], BF16, "g2s")
            vec = sb([128, 5, 4], F32, "vec")
            oneka = sb([128, 4], F32, "oneka")
            ones128 = sb([128, 128], F32, "ones128")
            Sx.dma("sp", w0c[:], w0T[l], writes=[w0c], sbuf=w0c)
            Sx.dma("sp", a0c[:], a0T[l], writes=[a0c], sbuf=a0c)
            Sx.dma("sp", vec[:], vecT[l], writes=[vec], sbuf=vec)
            Sx.dma("pool", w2s[:], w2Z[l], writes=[w2s], sbuf=w2s)
            Sx.dma("pool", a2s[:], a2Z[l], writes=[a2s], sbuf=a2s)
            Sx.dma("pool", g2s[:], g2[l], writes=[g2s], sbuf=g2s)
            Sx.op("pool", lambda e: e.memset(ones128[:], 1.0), writes=[ones128])
            Sx.op("dve", lambda e: e.tensor_scalar(out=oneka[:], in0=vec[:, 1, :], scalar1=-1.0, scalar2=1.0,
                                                   op0=ALU.mult, op1=ALU.add), reads=[vec], writes=[oneka])
            kark = sb([128, 8], F32, "kark")
            Sx.op("dve", lambda e: e.tensor_tensor(out=kark[:, 0:4], in0=vec[:, 1, :], in1=vec[:, 2, :], op=ALU.mult),
                  reads=[vec], writes=[kark])
            Sx.op("dve", lambda e: e.scalar_tensor_tensor(out=kark[:, 4:8], in0=oneka[:], scalar=2.0, in1=vec[:, 2, :],
                                                          op0=ALU.mult, op1=ALU.mult), reads=[vec, oneka, kark], writes=[kark])
            tot = sb([128, 4], F32, "tot")
            bm32 = sb([128, 128], F32, "bm32")
            mDL = sb([128, 128], F32, "mDL"); mNL = sb([128, 128], F32, "mNL")
            mDU = sb([128, 128], F32, "mDU"); mNU = sb([128, 128], F32, "mNU")
            Sx.op("pool", lambda e: e.memset(bm32[:], 0.0), writes=[bm32])
            for q4 in range(4):
                Sx.op("pool", lambda e, q4=q4: e.memset(bm32[q4 * 32:(q4 + 1) * 32, q4 * 32:(q4 + 1) * 32], 1.0), reads=[bm32], writes=[bm32])
            for (mm_, mD_, mN_) in [(mLs, mDL, mNL), (mUs, mDU, mNU)]:
                Sx.op("pool", lambda e, mm_=mm_, mD_=mD_: e.tensor_tensor(out=mD_[:], in0=mm_[:], in1=bm32[:], op=ALU.mult),
                      reads=[mm_, bm32], writes=[mD_])
                Sx.op("pool", lambda e, mm_=mm_, mD_=mD_, mN_=mN_: e.tensor_tensor(out=mN_[:], in0=mm_[:], in1=mD_[:], op=ALU.subtract),
                      reads=[mm_, mD_], writes=[mN_])
            zcp = Rot([sb([128, 15, 128], BF16, "zc") for _ in range(2)])
            actb = sb([128, 3, 128], BF16, "actb")
            sg = sb([128, 4, 128], F32, "sg")
            aa = sb([128, 4, 128], F32, "aa")
            aa2 = sb([128, 4, 128], F32, "aa2")
            kk = sb([128, 4, 128], F32, "kk")
            t1 = sb([128, 4, 128], F32, "t1")
            t2 = sb([128, 4, 128], F32, "t2")
            kd = sb([128, 4, 128], F32, "kd")
            bb = sb([128, 4, 128], F32, "bb")
            Lc = sb([128, 4, 128], F32, "Lc")
            Lm = sb([128, 4, 128], F32, "Lm")
            eR = sb([128, 4, 128], F32, "eR"); eA = sb([128, 4, 128], F32, "eA")
            eB = sb([128, 4, 128], F32, "eB"); eE = sb([128, 4, 128], F32, "eE")
            ARp = Rot([sb([128, 4, 2, 128], BF16, "AR") for _ in range(3)])
            BTu = sb([128, 4, 128], BF16, "BTu")
            AZ = sb([128, 4, 2, 128], BF16, "AZ"); BZ = sb([128, 4, 2, 128], BF16, "BZ"); KZ = sb([128, 4, 2, 128], BF16, "KZ")
            bpf = sb([128, 4, 128], BF16, "bpf"); kpf = sb([128, 4, 128], BF16, "kpf"); vTf = sb([128, 4, 128], BF16, "vTf")
            Bpp = Rot([sb([128, 4, 128], BF16, "Bp") for _ in range(3)])
            Kpp = Rot([sb([128, 4, 128], BF16, "Kp") for _ in range(3)])
            Vp = Rot([sb([128, 512], BF16, "V") for _ in range(3)])
            PCf = Rot([sb([128, 4, 128], F32, "PCf") for _ in range(3)])
            PMf = Rot([sb([128, 4, 128], F32, "PMf") for _ in range(3)])
            pcol = sb([128, 8], F32, "pcol")
            Dk = [sb([128, 8, 128], BF16, "Dk%d" % i) for i in range(2)]
            Dtk = [sb([128, 8, 128], BF16, "Dtk%d" % i) for i in range(2)]
            Sk = [sb([128, 8, 128], BF16, "Sk%d" % i) for i in range(2)]
            Stk = [sb([128, 8, 128], BF16, "Stk%d" % i) for i in range(2)]
            Nn = sb([128, 8, 128], BF16, "Nn"); Eb = sb([128, 8, 128], BF16, "Eb"); Etb = sb([128, 8, 128], BF16, "Etb")
            Gp = sb([128, 8, 128], BF16, "Gp"); Fp = sb([128, 8, 128], BF16, "Fp")
            Tp = Rot([sb([128, 8, 128], BF16, "T") for _ in range(2)])
            Akp = Rot([sb([128, 8, 128], BF16, "Ak") for _ in range(2)])
            Arbp = Rot([sb([128, 8, 128], BF16, "Arb") for _ in range(2)])
            Arkp = Rot([sb([128, 8, 128], BF16, "Ark") for _ in range(2)])
            H = sb([128, 4, 128], F32, "H")
            Hs = Rot([sb([128, 4, 128], BF16, "Hs") for _ in range(2)])
            Xb = sb([128, 512], BF16, "Xb"); Ub = sb([128, 512], BF16, "Ub")
            Yp = Rot([sb([128, 512], F32, "Y") for _ in range(2)])
            gTp = Rot([sb([128, 4, 128], F32, "gT") for _ in range(3)])
            bonp = Rot([sb([128, 4, 128], F32, "bon") for _ in range(3)])
            PSA, PSB, PSS = Rot(psb[0:2]), Rot(psb[2:6]), Rot(psb[6:8])
            yn = sb([128, 8, 64], F32, "yn"); ynb = sb([128, 512], BF16, "ynb")
            st8 = sb([128, 32], F32, "st8")
            oT = Rot([sb([128, 4, 128], BF16, "oT") for _ in range(2)])
            if DEBUG:
                print("P3 sbuf bytes remaining", nc.sbuf_bytes_remaining)
            for zb in (AZ, BZ, KZ):
                Sx.op("pool", lambda e, zb=zb: e.memset(zb[:], 0.0), writes=[zb])

            def pair_ops(eng, fn_name, out_b, out_ap, in_b, in_ap, col_b, col_ap, op):
                pass

            def loadz(c):
                zc = zcp.next()
                Sx.dma("sp", zc[:], zsT[:, c * 128:(c + 1) * 128].rearrange("(c p) t -> p c t", p=128),
                       reads=[dram["zsT"]], writes=[zc], sbuf=zc)
                return zc

            def prepA(c, d, zc, out):
                post = (d == 1)
                gT, bon = (gTp.next(), bonp.next()) if post else (None, None)
                zs = zc
                r_ = lambda: zs[:, 0:4, :]
                k_ = lambda: zs[:, 4:8, :]
                v_ = lambda: zs[:, 8:12, :]
                Sx.op("act", lambda e: e.activation(out=actb[:, 0, :], in_=zs[:, 12, :], func=AF.Tanh), reads=[zs], writes=[(actb, 0)])
                Sx.op("act", lambda e: e.activation(out=actb[:, 1, :], in_=zs[:, 13, :], func=AF.Copy), reads=[zs], writes=[(actb, 1)])
                if post:
                    Sx.op("act", lambda e: e.activation(out=actb[:, 2, :], in_=zs[:, 14, :], func=AF.Sigmoid), reads=[zs], writes=[(actb, 2)])

                def lora(wz, dd, idx, outb, bias_b, scale_out=None):
                    pm = PSA.next()
                    for pr in range(4):
                        Sx.op("pe", lambda e, pr=pr, pm=pm: e.matmul(pm[:, pr * 128:(pr + 1) * 128], wz[:, dd, pr * 128:(pr + 1) * 128],
                                                                    actb[:, idx, :], start=True, stop=True),
                              reads=[wz, (actb, idx)], writes=[pm])
                    for pr in range(4):
                        Sx.op("act", lambda e, pr=pr, pm=pm: e.activation(out=outb[:, pr, :], in_=pm[:, pr * 128:(pr + 1) * 128],
                                                                         func=AF.Sigmoid, bias=bias_b[:, dd, pr:pr + 1]),
                              reads=[pm, bias_b], writes=[outb])
                yield
                lora(w2s, d, 0, sg, w0c)
                yield
                lora(a2s, d, 1, aa, a0c)
                yield
                if post:
                    lora(a2s, 0, 1, aa2, a0c)
                    pm = PSA.next()
                    for pr in range(4):
                        Sx.op("pe", lambda e, pr=pr, pm=pm: e.matmul(pm[:, pr * 128:(pr + 1) * 128], g2s[:, pr * 128:(pr + 1) * 128],
                                                                    actb[:, 2, :], start=True, stop=True),
                              reads=[g2s, (actb, 2)], writes=[pm])
                    evac(gT[:].rearrange("p a t -> p (a t)"), pm[:, :], [pm], [gT])
                yield
                for pr in range(4):
                    Sx.op("dve", lambda e, pr=pr: e.tensor_scalar(out=kk[:, pr, :], in0=zs[:, 4 + pr, :], scalar1=vec[:, 0, pr:pr + 1],
                                                                scalar2=None, op0=ALU.mult), reads=[zs, vec], writes=[kk])
                Sx.op("pool", lambda e: e.tensor_tensor(out=t1[:], in0=kk[:], in1=kk[:], op=ALU.mult), reads=[kk], writes=[t1])
                pm = PSA.next()
                for pr in range(4):
                    Sx.op("pe", lambda e, pr=pr, pm=pm: e.matmul(pm[:, pr * 128:(pr + 1) * 128], blkf[:], t1[:, pr, :], start=True, stop=True),
                          reads=[blkf, t1], writes=[pm])
                Sx.op("act", lambda e, pm=pm: e.activation(out=t2[:].rearrange("p a t -> p (a t)"), in_=pm[:, :], func=AF.Ln,
                                                          bias=epsc[:, 2:3]), reads=[pm, epsc], writes=[t2])
                Sx.op("act", lambda e: e.activation(out=t2[:], in_=t2[:], func=AF.Exp, scale=-0.5), reads=[t2], writes=[t2])
                Sx.op("dve", lambda e: e.tensor_tensor(out=kk[:], in0=kk[:], in1=t2[:], op=ALU.mult), reads=[kk, t2], writes=[kk])
                yield
                for pr in range(4):
                    Sx.op("dve", lambda e, pr=pr: e.tensor_scalar(out=t1[:, pr, :], in0=aa[:, pr, :], scalar1=vec[:, 1, pr:pr + 1],
                                                                scalar2=oneka[:, pr:pr + 1], op0=ALU.mult, op1=ALU.add),
                          reads=[aa, vec, oneka], writes=[t1])
                Sx.op("pool", lambda e: e.tensor_tensor(out=kd[:], in0=zs[:, 4:8, :], in1=t1[:], op=ALU.mult), reads=[zs, t1], writes=[kd])
                Sx.op("pool", lambda e: e.tensor_tensor(out=bb[:], in0=kk[:], in1=aa[:], op=ALU.mult), reads=[kk, aa], writes=[bb])
                yield
                if post:
                    Sx.op("dve", lambda e: e.tensor_tensor(out=t2[:], in0=aa[:], in1=aa2[:], op=ALU.add), reads=[aa, aa2], writes=[t2])
                    for pr in range(4):
                        Sx.op("dve", lambda e, pr=pr: e.tensor_scalar(out=t2[:, pr, :], in0=t2[:, pr, :], scalar1=kark[:, pr:pr + 1],
                                                                    scalar2=kark[:, 4 + pr:5 + pr], op0=ALU.mult, op1=ALU.add),
                              reads=[t2, kark], writes=[t2])
                    Sx.op("dve", lambda e: e.tensor_tensor(out=t2[:], in0=t2[:], in1=zs[:, 4:8, :], op=ALU.mult), reads=[t2, zs], writes=[t2])
                    Sx.op("dve", lambda e: e.tensor_tensor(out=t2[:], in0=t2[:], in1=zs[:, 0:4, :], op=ALU.mult), reads=[t2, zs], writes=[t2])
                    pm = PSA.next()
                    for pr in range(4):
                        Sx.op("pe", lambda e, pr=pr, pm=pm: e.matmul(pm[:, pr * 128:(pr + 1) * 128], blkf[:], t2[:, pr, :], start=True, stop=True),
                              reads=[blkf, t2], writes=[pm])
                    Sx.op("dve", lambda e, pm=pm: e.tensor_tensor(out=bon[:].rearrange("p a t -> p (a t)"), in0=pm[:, :],
                                                                 in1=zs[:, 8:12, :].rearrange("p a t -> p (a t)"), op=ALU.mult),
                          reads=[pm, zs], writes=[bon])
                yield
                for pr in range(4):
                    Sx.op("dve", lambda e, pr=pr: e.tensor_tensor_scan(out=Lc[:, pr, :], data0=ones128[:], data1=sg[:, pr, :], initial=0.0,
                                                                      op0=ALU.mult, op1=ALU.add), reads=[ones128, sg], writes=[Lc])
                last = 127
                if d == 1:
                    Sx.op("dve", lambda e: e.tensor_tensor(out=t1[:], in0=sg[:], in1=Lc[:], op=ALU.subtract), reads=[sg, Lc], writes=[t1])
                    Sx.op("dve", lambda e: e.tensor_copy(tot[:].unsqueeze(2), Lc[:, :, 127:128]), reads=[Lc], writes=[tot])
                    Sx.op("dve", lambda e: e.tensor_tensor(out=Lc[:], in0=t1[:], in1=tot[:].unsqueeze(2).to_broadcast([128, 4, 128]),
                                                           op=ALU.add), reads=[t1, tot], writes=[Lc])
                    last = 0
                yield
                Sx.op("dve", lambda e: e.tensor_tensor(out=Lm[:], in0=Lc[:], in1=Lc[:, :, MID:MID + 1].to_broadcast([128, 4, 128]),
                                                       op=ALU.subtract), reads=[Lc], writes=[Lm])
                Sx.op("act", lambda e: e.activation(out=eR[:], in_=Lm[:], func=AF.Exp, scale=-C1), reads=[Lm], writes=[eR])
                Sx.op("act", lambda e: e.activation(out=eB[:], in_=Lm[:], func=AF.Exp, scale=C1), reads=[Lm], writes=[eB])
                Sx.op("pool", lambda e: e.tensor_tensor(out=t1[:], in0=Lm[:], in1=sg[:], op=ALU.subtract), reads=[Lm, sg], writes=[t1])
                Sx.op("act", lambda e: e.activation(out=eA[:], in_=t1[:], func=AF.Exp, scale=-C1), reads=[t1], writes=[eA])
                Sx.op("dve", lambda e: e.tensor_tensor(out=t2[:], in0=Lc[:], in1=Lc[:, :, last:last + 1].to_broadcast([128, 4, 128]),
                                                       op=ALU.subtract), reads=[Lc], writes=[t2])
                Sx.op("act", lambda e: e.activation(out=eE[:], in_=t2[:], func=AF.Exp, scale=C1), reads=[t2], writes=[eE])
                yield
                PC, PM = PCf.next(), PMf.next()
                Sx.op("act", lambda e: e.activation(out=pcol[:, 0:4].unsqueeze(2), in_=Lc[:, :, last:last + 1], func=AF.Exp, scale=-C1),
                      reads=[Lc], writes=[pcol])
                Sx.op("act", lambda e: e.activation(out=pcol[:, 4:8].unsqueeze(2), in_=Lc[:, :, MID:MID + 1], func=AF.Exp, scale=-C1),
                      reads=[Lc, pcol], writes=[pcol])
                Sx.op("dve", lambda e, PC=PC: e.tensor_copy(PC[:], pcol[:, 0:4].unsqueeze(2).to_broadcast([128, 4, 128])),
                      reads=[pcol], writes=[PC])
                for pr in range(4):
                    Sx.op("act", lambda e, PM=PM, pr=pr: e.activation(out=PM[:, pr, :], in_=blkf[:], func=AF.Identity,
                                                                   scale=pcol[:, 4 + pr:5 + pr]), reads=[pcol, blkf], writes=[PM])
                yield
                AR = ARp.next()
                Sx.op("dve", lambda e, AR=AR: e.scalar_tensor_tensor(out=AR[:, :, 0, :], in0=kk[:], scalar=-1.0, in1=eA[:],
                                                                   op0=ALU.mult, op1=ALU.mult), reads=[kk, eA], writes=[AR])
                Sx.op("pool", lambda e, AR=AR: e.tensor_tensor(out=AR[:, :, 1, :], in0=zs[:, 0:4, :], in1=eR[:], op=ALU.mult),
                      reads=[zs, eR, AR], writes=[AR])
                Sx.op("dve", lambda e: e.tensor_tensor(out=BTu[:], in0=bb[:], in1=eB[:], op=ALU.mult), reads=[bb, eB], writes=[BTu])
                Sx.op("dve", lambda e: e.tensor_tensor(out=t1[:], in0=kd[:], in1=eB[:], op=ALU.mult), reads=[kd, eB], writes=[t1])
                yield
                for par in range(2):
                    ps_ = slice(par * 64, (par + 1) * 64)
                    Sx.op("act", lambda e, ps_=ps_, par=par, AR=AR: e.activation(out=AZ[ps_, :, par, :], in_=AR[ps_, :, 0, :], func=AF.Copy), reads=[AR], writes=[AZ])
                    Sx.op("act", lambda e, ps_=ps_, par=par: e.activation(out=BZ[ps_, :, par, :], in_=BTu[ps_, :, :], func=AF.Copy), reads=[BTu], writes=[BZ])
                    Sx.op("act", lambda e, ps_=ps_, par=par: e.activation(out=KZ[ps_, :, par, :], in_=t1[ps_, :, :], func=AF.Copy), reads=[t1], writes=[KZ])
                Sx.op("pool", lambda e: e.tensor_tensor(out=bpf[:], in0=bb[:], in1=eE[:], op=ALU.mult), reads=[bb, eE], writes=[bpf])
                Sx.op("pool", lambda e: e.tensor_tensor(out=kpf[:], in0=kd[:], in1=eE[:], op=ALU.mult), reads=[kd, eE], writes=[kpf])
                Sx.op("act", lambda e: e.activation(out=vTf[:], in_=zs[:, 8:12, :], func=AF.Copy), reads=[zs], writes=[vTf])
                yield
                Bp, Kp, V = Bpp.next(), Kpp.next(), Vp.next()
                for src, dst in [(bpf, Bp), (kpf, Kp), (vTf, V)]:
                    pt = PSA.next()
                    ptb = pt[:].bitcast(BF16)
                    for pr in range(4):
                        Sx.op("pe", lambda e, pr=pr, ptb=ptb, src=src: e.transpose(ptb[:, pr * 128:(pr + 1) * 128], src[:, pr, :], identb[:]),
                              reads=[src, identb], writes=[pt])
                    dap = dst[:] if dst is V else dst[:].rearrange("p a t -> p (a t)")
                    evac(dap, ptb[:, 0:512], [pt], [dst])
                    yield
                out.update(dict(AR=AR, Bp=Bp, Kp=Kp, V=V, PC=PC, PM=PM, gT=gT, bon=bon))

            def prepB(d, P):
                AR = P["AR"]
                m_abT = mUs if d == 0 else mLs
                mD_ab, mN_ab = (mDL, mNL) if d == 0 else (mDU, mNU)
                mD_abT = mDU if d == 0 else mDL
                m_inT = mUi if d == 0 else mLi
                Ak, Arb, Ark = Akp.next(), Arbp.next(), Arkp.next()
                for hg in range(2):
                    p1 = PSB.next()
                    for hh in range(4):
                        h = hg * 4 + hh
                        pr, par = h // 2, h % 2
                        Sx.op("pe", lambda e, p1=p1, hh=hh, pr=pr, par=par: e.matmul(p1[:, hh * 128:(hh + 1) * 128], AZ[:, pr, par, :],
                                                                                      BTu[:, pr, :], start=True, stop=True),
                              reads=[AZ, BTu], writes=[p1])
                    Sx.op("dve", lambda e, p1=p1, hg=hg: e.tensor_tensor(out=Dk[0][:, hg * 4:(hg + 1) * 4, :],
                                                                        in0=p1[:, :].rearrange("p (a t) -> p a t", a=4),
                                                                        in1=mD_ab[:].unsqueeze(1).to_broadcast([128, 4, 128]), op=ALU.mult),
                          reads=[p1, mD_ab], writes=[(Dk[0], hg)])
                    Sx.op("dve", lambda e, p1=p1, hg=hg: e.tensor_tensor(out=Nn[:, hg * 4:(hg + 1) * 4, :],
                                                                        in0=p1[:, :].rearrange("p (a t) -> p a t", a=4),
                                                                        in1=mN_ab[:].unsqueeze(1).to_broadcast([128, 4, 128]), op=ALU.mult),
                          reads=[p1, mN_ab], writes=[(Nn, hg)])
                    for (LZ, o1, m1, o2, m2) in [(BZ, Dtk[0], mD_abT, Arb, m_inT), (KZ, Ak, m_abT, Ark, m_inT)]:
                        for h2 in range(2):
                            pass
                        for half in range(2):
                            p2 = PSB.next()
                            for q in range(2):
                                h = hg * 4 + half * 2 + q
                                pr, par = h // 2, h % 2
                                Sx.op("pe", lambda e, p2=p2, q=q, pr=pr, par=par, LZ=LZ, AR=AR: e.matmul(
                                    p2[:, q * 256:(q + 1) * 256], LZ[:, pr, par, :], AR[:, pr, :, :].rearrange("p a t -> p (a t)"),
                                    start=True, stop=True), reads=[LZ, AR], writes=[p2])
                            h0 = hg * 4 + half * 2
                            pv = p2[:, :].rearrange("p (q a t) -> p q a t", q=2, a=2)
                            Sx.op("dve", lambda e, pv=pv, o1=o1, m1=m1, h0=h0: e.tensor_tensor(
                                out=o1[:, h0:h0 + 2, :], in0=pv[:, :, 0, :], in1=m1[:].unsqueeze(1).to_broadcast([128, 2, 128]), op=ALU.mult),
                                reads=[p2, m1], writes=[(o1, h0)])
                            Sx.op("dve", lambda e, pv=pv, o2=o2, m2=m2, h0=h0: e.tensor_tensor(
                                out=o2[:, h0:h0 + 2, :], in0=pv[:, :, 1, :], in1=m2[:].unsqueeze(1).to_broadcast([128, 2, 128]), op=ALU.mult),
                                reads=[p2, m2], writes=[(o2, h0)])
                            yield
                idb3 = lambda n: identb[:].unsqueeze(1).to_broadcast([128, n, 128])
                Sx.op("dve", lambda e: e.tensor_tensor(out=Sk[0][:], in0=Dk[0][:], in1=idb3(8), op=ALU.add), reads=[Dk[0], identb], writes=[Sk[0]])
                Sx.op("pool", lambda e: e.tensor_tensor(out=Stk[0][:], in0=Dtk[0][:], in1=idb3(8), op=ALU.add), reads=[Dtk[0], identb], writes=[Stk[0]])
                yield

                def mm4(ps_, lhsb, rhsb, hg, rd):
                    for hh in range(4):
                        h = hg * 4 + hh
                        cs = slice(hh * 128, (hh + 1) * 128)
                        Sx.op("pe", lambda e, ps_=ps_, cs=cs, h=h: e.matmul(ps_[:, cs], lhsb[:, h, :], rhsb[:, h, :], start=True, stop=True),
                              reads=rd, writes=[ps_])

                def hv(b, hg):
                    return b[:, hg * 4:(hg + 1) * 4, :].rearrange("p a t -> p (a t)")
                cur = 0
                for lv in range(4):
                    nx = 1 - cur
                    for hg in range(2):
                        pD = PSB.next()
                        mm4(pD, Dtk[cur], Dk[cur], hg, [Dtk[cur], Dk[cur]])
                        evac(hv(Dk[nx], hg), pD[:, :], [pD], [(Dk[nx], hg)], eng="act")
                        pDt = PSB.next()
                        mm4(pDt, Dk[cur], Dtk[cur], hg, [Dtk[cur], Dk[cur]])
                        evac(hv(Dtk[nx], hg), pDt[:, :], [pDt], [(Dtk[nx], hg)], eng="act")
                        yield
                    for hg in range(2):
                        pS = PSB.next()
                        mm4(pS, Dtk[nx], Sk[cur], hg, [(Dtk[nx], hg), Sk[cur]])
                        Sx.op("dve", lambda e, pS=pS, hg=hg, cur=cur, nx=nx: e.tensor_tensor(out=hv(Sk[nx], hg), in0=pS[:, :], in1=hv(Sk[cur], hg), op=ALU.add),
                              reads=[pS, Sk[cur]], writes=[(Sk[nx], hg)])
                        pS2 = PSB.next()
                        mm4(pS2, Dk[nx], Stk[cur], hg, [(Dk[nx], hg), Stk[cur]])
                        Sx.op("dve", lambda e, pS2=pS2, hg=hg, cur=cur, nx=nx: e.tensor_tensor(out=hv(Stk[nx], hg), in0=pS2[:, :], in1=hv(Stk[cur], hg), op=ALU.add),
                              reads=[pS2, Stk[cur]], writes=[(Stk[nx], hg)])
                        yield
                    cur = nx
                Dinv, Dip = Sk[cur], Stk[cur]
                for hg in range(2):
                    pE = PSB.next()
                    mm4(pE, Nn, Dip, hg, [Nn, Dip])
                    evac(hv(Etb, hg), pE[:, :], [pE], [(Etb, hg)], eng="act")
                    pE2 = PSB.next()
                    mm4(pE2, Dip, Nn, hg, [Nn, Dip])
                    evac(hv(Eb, hg), pE2[:, :], [pE2], [(Eb, hg)], eng="act")
                    yield
                for hg in range(2):
                    pG = PSB.next()
                    mm4(pG, Eb, Etb, hg, [(Eb, hg), (Etb, hg)])
                    Sx.op("dve", lambda e, pG=pG, hg=hg: e.tensor_tensor(out=Gp[:, hg * 4:(hg + 1) * 4, :], in0=pG[:, :].rearrange("p (a t) -> p a t", a=4),
                                                                        in1=idb3(4), op=ALU.add), reads=[pG, identb], writes=[(Gp, hg)])
                    yield
                for hg in range(2):
                    pF = PSB.next()
                    mm4(pF, Eb, Gp, hg, [(Eb, hg), (Gp, hg)])
                    Sx.op("dve", lambda e, pF=pF, hg=hg: e.tensor_tensor(out=hv(Fp, hg), in0=pF[:, :], in1=hv(Gp, hg), op=ALU.add),
                          reads=[pF, (Gp, hg)], writes=[(Fp, hg)])
                    yield
                T = Tp.next()
                for hg in range(2):
                    pT = PSB.next()
                    mm4(pT, Dinv, Fp, hg, [Dinv, (Fp, hg)])
                    evac(hv(T, hg), pT[:, :], [pT], [(T, hg)], eng="act")
                    yield
                P.update(dict(T=T, Ak=Ak, Arb=Arb, Ark=Ark))

            def serial(c, d, P, Hs_cur, PMnext, Hn):
                AR, Bp, Kp, V, PC, T, Ak, Arb, Ark = (P[k] for k in ["AR", "Bp", "Kp", "V", "PC", "T", "Ak", "Arb", "Ark"])
                pX = PSS.next()
                for h in range(8):
                    pr, par = h // 2, h % 2
                    cs = slice(h * 64, (h + 1) * 64)
                    Sx.op("pe", lambda e, pX=pX, cs=cs, pr=pr, par=par: e.matmul(pX[:, cs], AR[:, pr, 0, :], Hs_cur[:, pr, par * 64:(par + 1) * 64],
                                                                                  start=True, stop=False), reads=[AR, Hs_cur], writes=[pX])
                    Sx.op("pe", lambda e, pX=pX, cs=cs, h=h: e.matmul(pX[:, cs], Ak[:, h, :], V[:, cs], start=False, stop=True),
                          reads=[Ak, V], writes=[pX])
                Sx.op("dve", lambda e, pX=pX: e.tensor_copy(Xb[:], pX[:, :]), reads=[pX], writes=[Xb])
                yield
                pU = PSS.next()
                for h in range(8):
                    cs = slice(h * 64, (h + 1) * 64)
                    Sx.op("pe", lambda e, pU=pU, cs=cs, h=h: e.matmul(pU[:, cs], T[:, h, :], Xb[:, cs], start=True, stop=True),
                          reads=[T, Xb], writes=[pU])
                Sx.op("act", lambda e, pU=pU: e.activation(out=Ub[:], in_=pU[:, :], func=AF.Copy), reads=[pU], writes=[Ub])
                yield
                pY = PSS.next()
                for h in range(8):
                    pr, par = h // 2, h % 2
                    cs = slice(h * 64, (h + 1) * 64)
                    Sx.op("pe", lambda e, pY=pY, cs=cs, pr=pr, par=par: e.matmul(pY[:, cs], AR[:, pr, 1, :], Hs_cur[:, pr, par * 64:(par + 1) * 64],
                                                                                  start=True, stop=False), reads=[AR, Hs_cur], writes=[pY])
                    Sx.op("pe", lambda e, pY=pY, cs=cs, h=h: e.matmul(pY[:, cs], Arb[:, h, :], Ub[:, cs], start=False, stop=False),
                          reads=[Arb, Ub], writes=[pY])
                    Sx.op("pe", lambda e, pY=pY, cs=cs, h=h: e.matmul(pY[:, cs], Ark[:, h, :], V[:, cs], start=False, stop=True),
                          reads=[Ark, V], writes=[pY])
                pH = PSS.next()
                for pr in range(4):
                    cs = slice(pr * 128, (pr + 1) * 128)
                    Sx.op("pe", lambda e, pH=pH, cs=cs, pr=pr: e.matmul(pH[:, cs], Bp[:, pr, :], Ub[:, cs], start=True, stop=False),
                          reads=[Bp, Ub], writes=[pH])
                    Sx.op("pe", lambda e, pH=pH, cs=cs, pr=pr: e.matmul(pH[:, cs], Kp[:, pr, :], V[:, cs], start=False, stop=True),
                          reads=[Kp, V], writes=[pH])
                Sx.op("dve", lambda e: e.tensor_tensor(out=H[:], in0=H[:], in1=PC[:], op=ALU.mult), reads=[H, PC], writes=[H])
                Sx.op("dve", lambda e, pH=pH: e.tensor_tensor(out=H[:].rearrange("p a t -> p (a t)"), in0=H[:].rearrange("p a t -> p (a t)"),
                                                             in1=pH[:, :], op=ALU.add), reads=[H, pH], writes=[H])
                if PMnext is not None:
                    Sx.op("dve", lambda e, Hn=Hn: e.tensor_tensor(out=Hn[:], in0=H[:], in1=PMnext[:], op=ALU.mult),
                          reads=[H, PMnext], writes=[Hn])
                yield
                if d == 0:
                    Y = Yp.next()
                    Sx.op("act", lambda e, Y=Y, pY=pY: e.activation(out=Y[:], in_=pY[:, :], func=AF.Copy), reads=[pY], writes=[Y])
                    Sx.dma("pool", yfw[c * 128:(c + 1) * 128, :], Y[:], reads=[Y], writes=[(dram["yfw"], c)], sbuf=Y)
                else:
                    post(c, pY, P["gT"], P["bon"])

            def post(c, pY, gT, bon):
                Y = Yp.next()
                Sx.dma("sp", Y[:], yfw[c * 128:(c + 1) * 128, :], reads=[(dram["yfw"], c)], writes=[Y], sbuf=Y)
                Yv = Y[:].rearrange("p (h d) -> p h d", d=64)
                Sx.op("dve", lambda e: e.tensor_tensor(out=Y[:], in0=Y[:], in1=pY[:, :], op=ALU.add), reads=[Y, pY], writes=[Y])
                Sx.op("dve", lambda e: e.tensor_reduce(out=st8[:, 0:8], in_=Yv, axis=AX.X, op=ALU.add), reads=[Y], writes=[st8])
                Sx.op("dve", lambda e: e.tensor_scalar(out=st8[:, 0:8], in0=st8[:, 0:8], scalar1=1.0 / 64, scalar2=None, op0=ALU.mult),
                      reads=[st8], writes=[st8])
                Sx.op("dve", lambda e: e.tensor_tensor(out=yn[:], in0=Yv, in1=st8[:, 0:8].unsqueeze(2).to_broadcast([128, 8, 64]),
                                                       op=ALU.subtract), reads=[Y, st8], writes=[yn])
                Sx.op("pool", lambda e: e.tensor_tensor(out=Yv, in0=yn[:], in1=yn[:], op=ALU.mult), reads=[yn, Y], writes=[Y])
                Sx.op("dve", lambda e: e.tensor_reduce(out=st8[:, 8:16], in_=Yv, axis=AX.X, op=ALU.add), reads=[Y, st8], writes=[st8])
                Sx.op("act", lambda e: e.activation(out=st8[:, 16:24], in_=st8[:, 8:16], func=AF.Sqrt, scale=1.0 / 64, bias=epsc[:, 1:2]),
                      reads=[st8, epsc], writes=[st8])
                Sx.op("dve", lambda e: e.reciprocal(st8[:, 24:32], st8[:, 16:24]), reads=[st8], writes=[st8])
                Sx.op("dve", lambda e: e.tensor_tensor(out=ynb[:].rearrange("p (h d) -> p h d", d=64), in0=yn[:],
                                                       in1=st8[:, 24:32].unsqueeze(2).to_broadcast([128, 8, 64]), op=ALU.mult),
                      reads=[yn, st8], writes=[ynb])
                pt = PSS.next()
                ptb = pt[:].bitcast(BF16)
                for pr in range(4):
                    Sx.op("pe", lambda e, pr=pr, ptb=ptb: e.transpose(ptb[:, pr * 128:(pr + 1) * 128], ynb[:, pr * 128:(pr + 1) * 128], identb[:]),
                          reads=[ynb, identb], writes=[pt])
                o = oT.next()
                for pr in range(4):
                    Sx.op("dve", lambda e, pr=pr, ptb=ptb: e.tensor_scalar(out=t1[:, pr, :], in0=ptb[:, pr * 128:(pr + 1) * 128],
                                                                        scalar1=vec[:, 3, pr:pr + 1], scalar2=vec[:, 4, pr:pr + 1],
                                                                        op0=ALU.mult, op1=ALU.add), reads=[pt, vec], writes=[t1])
                Sx.op("pool", lambda e: e.tensor_tensor(out=t1[:], in0=t1[:], in1=bon[:], op=ALU.add), reads=[t1, bon], writes=[t1])
                Sx.op("pool", lambda e, o=o: e.tensor_tensor(out=o[:], in0=t1[:], in1=gT[:], op=ALU.mult), reads=[t1, gT], writes=[o])
                Sx.dma("pool", ymT[0:512, c * 128:(c + 1) * 128].rearrange("(pr p) t -> p pr t", p=128), o[:], reads=[o],
                       writes=[(dram["ymT"], ("r", c))], sbuf=o)

            def interleave(gens):
                st = [[g, n, 0] for g, n in gens]
                while st:
                    it = min(st, key=lambda x: (x[2] + 1.0) / x[1])
                    try:
                        next(it[0])
                        it[2] += 1
                    except StopIteration:
                        st.remove(it)

            for d in range(2):
                order = list(range(NT)) if d == 0 else list(range(NT - 1, -1, -1))
                n = len(order)
                Sx.op("pool", lambda e: e.memset(H[:], 0.0), writes=[H])
                Hlist = [Hs.next()]
                Sx.op("pool", lambda e, Hc=Hlist[0]: e.memset(Hc[:], 0.0), writes=[Hlist[0]])
                Ps = {}

                def genA(i, d=d, order=order, Ps=Ps):
                    zc = loadz(order[i])
                    Ps[i] = {}
                    yield from prepA(order[i], d, zc, Ps[i])

                def genB(i, d=d, Ps=Ps):
                    yield from prepB(d, Ps[i])

                def genS(i, d=d, order=order, Ps=Ps, Hlist=Hlist, n=n):
                    Hn = Hs.next() if i + 1 < n else None
                    Hc = Hlist[0]
                    Hlist[0] = Hn
                    yield from serial(order[i], d, Ps[i], Hc, Ps[i + 1]["PM"] if i + 1 < n else None, Hn)
                    del Ps[i]

                for r in range(n + 2):
                    act = []
                    if 2 <= r <= n + 1:
                        act.append((genS(r - 2), 12))
                    if 1 <= r <= n:
                        act.append((genB(r - 1), 50))
                    if r < n:
                        act.append((genA(r), 17))
                    interleave(act)
            Sx.barrier()
    return P3


_NC_CACHE = {}


def prep_inputs(inputs, S, NL):
    f = lambda a: np.ascontiguousarray(a, dtype=np.float32)
    rows = S // GW
    tiles, sigs = att_tiles(rows)
    common = {}
    common["ada_w"] = f(inputs["ada_w"][:NL])
    common["ada_bT"] = f(inputs["ada_b"][:NL].reshape(NL, 48, 128).transpose(0, 2, 1))
    common["n1g"] = f(inputs["norm1_g"][:NL].reshape(NL, 8, 128).transpose(0, 2, 1))
    common["n2g"] = f(inputs["norm2_g"][:NL].reshape(NL, 8, 128).transpose(0, 2, 1))
    common["w_in"] = f(inputs["w_in"][:NL])
    common["muT"] = f(inputs["shift_mu"][:NL].reshape(NL, 2, 15, 128).transpose(0, 3, 1, 2))
    common["w0T"] = f(inputs["w0"][:NL].reshape(NL, 2, 4, 128).transpose(0, 3, 1, 2))
    common["a0T"] = f(inputs["a0"][:NL].reshape(NL, 2, 4, 128).transpose(0, 3, 1, 2))
    w2Z = np.zeros((NL, 128, 2, RW), np.float32)
    a2Z = np.zeros((NL, 128, 2, RW), np.float32)
    for d in range(2):
        w2Z[:, d * 64:(d + 1) * 64, d, :] = inputs["w2"][:NL, d]
        a2Z[:, d * 64:(d + 1) * 64, d, :] = inputs["a2"][:NL, d]
    common["w2Z"] = w2Z
    common["a2Z"] = a2Z
    common["g2"] = f(inputs["g2"][:NL])
    vec = np.stack([inputs[k][:NL].reshape(NL, 4, 128).transpose(0, 2, 1) for k in ["k_k", "k_a", "r_k", "lnx_g", "lnx_b"]], axis=2)
    common["vecT"] = f(vec)
    qk = np.stack([np.tile(inputs["q_norm_g"][:NL], (1, 2)), np.tile(inputs["k_norm_g"][:NL], (1, 2))], axis=2)
    common["qkg"] = f(qk)
    common["biasT"] = f(np.stack([build_bias(np.asarray(inputs["rpb"][l]), sigs) for l in range(NL)]))
    common["w_out"] = f(inputs["w_out"][:NL])
    common["f_in"] = f(inputs["ffn_w_in"][:NL])
    common["f_out"] = f(inputs["ffn_w_out"][:NL])
    return common, len(sigs)


def run(inputs, S, NL, ncores=8, trace=False):
    inputs = {k: np.asarray(v) for k, v in inputs.items()}
    B = inputs["x"].shape[0]
    common, NV = prep_inputs(inputs, S, NL)
    key = (S, NL, NV)
    if key not in _NC_CACHE:
        _NC_CACHE[key] = build_nc(S, NL, NV)
    nc = _NC_CACHE[key]
    in_maps = []
    for cidx in range(ncores):
        b = cidx % B
        m = dict(common)
        m["xT"] = np.ascontiguousarray(inputs["x"][b, :S].T, dtype=np.float32)
        m["cT"] = np.ascontiguousarray(inputs["c"][b].reshape(8, 128).T, dtype=np.float32)
        in_maps.append(m)
    res = run_bass_kernel_spmd(nc, in_maps, core_ids=list(range(ncores)), trace=trace)
    if trace:
        print("EXEC_TIME_NS", res.exec_time_ns)
    nb = min(B, ncores)
    out = np.stack([np.ascontiguousarray(res.results[b]["outT"].T) for b in range(nb)], axis=0)
    if DEBUG:
        return out.astype(np.float32), res.results
    return out.astype(np.float32)


def kernel(**inputs):
    return run(inputs, 8192, 4)
```

```python
import numpy as np
import concourse.bass as bass
import concourse.mybir as mybir
from concourse.bass_utils import run_bass_kernel_spmd

F32 = mybir.dt.float32
BF16 = mybir.dt.bfloat16
AF = mybir.ActivationFunctionType
ALU = mybir.AluOpType
AX = mybir.AxisListType

D = 1024
GW = 64
RW = 512
RWKV_COLS = 1920
IN_COLS = 3456
DFF = 2816
NEG = -30000.0
DEBUG = False
PHASES = (1, 2, 3, 4)


class Buf:
    def __init__(self, name, t):
        self.name = name
        self.t = t
        self.st = {}
        self.dsem = None

    def __getitem__(self, k):
        return self.t[k]


class Sched:
    ENG = ["pe", "dve", "act", "pool", "sp"]

    def __init__(self, nc, stack):
        self.nc = nc
        self.stack = stack
        self.sems = {}
        self.count = {}
        self.known = {e: {} for e in self.ENG}
        self.lists = {e: [] for e in self.ENG}
        for e in self.ENG:
            self.sems[e] = stack.enter_context(nc.semaphore("s_" + e))
            self.count[e] = 0
        self.dsems = []
        for i in range(24):
            nm = "d%d" % i
            self.sems[nm] = stack.enter_context(nc.semaphore("s_" + nm))
            self.count[nm] = 0
            self.dsems.append(nm)
        self.dnext = 0
        self.nbuf = 0
        self.capture = None

    def replay(self, item):
        kind, eng, fn, reads, writes, _ = item
        if kind == "op":
            self.op(eng, fn, reads, writes)
        else:
            out_ap, in_ap, sbuf, kw = fn
            self.dma(eng, out_ap, in_ap, reads, writes, sbuf=sbuf, **kw)

    def sb(self, stack, shape, dt, name=None):
        self.nbuf += 1
        name = (name or "b") + "_%d" % self.nbuf
        return Buf(name, stack.enter_context(self.nc.sbuf_tensor(name, list(shape), dt)))

    def ps(self, stack, name=None):
        self.nbuf += 1
        name = (name or "p") + "_%d" % self.nbuf
        return Buf(name, stack.enter_context(self.nc.psum_tensor(name, [128, 512], F32)))

    def dsem_for(self, buf):
        if buf.dsem is None:
            buf.dsem = self.dsems[self.dnext % len(self.dsems)]
            self.dnext += 1
        return buf.dsem

    @staticmethod
    def _norm(x):
        return x if isinstance(x, tuple) else (x, None)

    def _deps(self, reads, writes):
        deps = {}

        def add(tok):
            if tok is None:
                return
            s, v = tok
            if deps.get(s, 0) < v:
                deps[s] = v

        for item in reads:
            b, p = self._norm(item)
            for q, st in b.st.items():
                if p is None or q is None or p == q:
                    add(st[0])
        for item in writes:
            b, p = self._norm(item)
            for q, st in b.st.items():
                if p is None or q is None or p == q:
                    add(st[0])
                    for s, v in st[1].items():
                        add((s, v))
        return deps

    def _commit(self, tok, reads, writes):
        for item in reads:
            b, p = self._norm(item)
            st = b.st.setdefault(p, [None, {}])
            if st[1].get(tok[0], 0) < tok[1]:
                st[1][tok[0]] = tok[1]
        for item in writes:
            b, p = self._norm(item)
            if p is None:
                b.st = {None: [tok, {}]}
            else:
                b.st[p] = [tok, {}]

    def _waits(self, eng, deps):
        w = []
        kn = self.known[eng]
        for s, v in deps.items():
            if s == eng and eng == "pe":
                continue
            if kn.get(s, 0) < v:
                kn[s] = v
                w.append((s, v))
        return w

    def op(self, eng, fn, reads=(), writes=()):
        if self.capture is not None:
            self.capture.append(("op", eng, fn, tuple(reads), tuple(writes), None))
            return
        deps = self._deps(reads, writes)
        w = self._waits(eng, deps)
        self.count[eng] += 1
        tok = (eng, self.count[eng])
        self.lists[eng].append((w, fn, eng, 1))
        self._commit(tok, reads, writes)

    def dma(self, q, out_ap, in_ap, reads=(), writes=(), sbuf=None, **kw):
        if self.capture is not None:
            self.capture.append(("dma", q, (out_ap, in_ap, sbuf, kw), tuple(reads), tuple(writes), None))
            return
        ds = self.dsem_for(sbuf)
        deps = self._deps(reads, writes)
        deps[ds] = max(deps.get(ds, 0), self.count[ds])
        w = self._waits(q, deps)
        self.count[ds] += 16
        tok = (ds, self.count[ds])
        self.lists[q].append((w, lambda e: e.dma_start(out=out_ap, in_=in_ap, **kw), ds, 16))
        self._commit(tok, reads, writes)

    def barrier(self):
        for e in self.ENG:
            deps = {s: c for s, c in self.count.items() if c > 0 and s != e}
            w = self._waits(e, deps)
            if w:
                self.lists[e].append((w, None, None, 0))

    def emit(self):
        self.barrier()
        nc = self.nc
        sems = self.sems
        lists = self.lists

        def run(e, items):
            for w, fn, s, inc in items:
                for ws, wv in w:
                    e.wait_ge(sems[ws], wv)
                if fn is not None:
                    fn(e).then_inc(sems[s], inc)

        with nc.Block() as block:
            @block.tensor
            def _(e):
                run(e, lists["pe"])

            @block.vector
            def _(e):
                run(e, lists["dve"])

            @block.scalar
            def _(e):
                run(e, lists["act"])

            @block.gpsimd
            def _(e):
                run(e, lists["pool"])

            @block.sync
            def _(e):
                run(e, lists["sp"])


class Rot:
    def __init__(self, bufs):
        self.bufs = bufs
        self.i = 0

    def next(self):
        b = self.bufs[self.i % len(self.bufs)]
        self.i += 1
        return b


def att_tiles(rows):
    sigs = []
    tiles = []
    for j in range(rows // 2):
        kb = min(max(2 * j - 4, 0), rows - 9)
        i0 = 2 * j
        r00 = min(max(i0 - 4, 0), rows - 8)
        r01 = min(max(i0 + 1 - 4, 0), rows - 8)
        sig = (i0 - kb, r00 - kb, r01 - kb)
        if sig not in sigs:
            sigs.append(sig)
        tiles.append((kb, sigs.index(sig)))
    return tiles, sigs


def build_bias(rpb_l, sigs):
    nv = len(sigs)
    out = np.full((nv, 5 * 128, 8, 128), NEG, np.float32)
    qc = np.arange(64)
    cs = np.clip(qc - 8, 0, GW - 16)
    for vi, (di, d0, d1) in enumerate(sigs):
        for ri in range(2):
            irel = di + ri
            r0rel = d0 if ri == 0 else d1
            for kr in range(r0rel, r0rel + 8):
                ro = kr - irel + 7
                for q in range(64):
                    kc = np.arange(cs[q], cs[q] + 16)
                    co = kc - q + 15
                    out[vi, kr * 64 + kc, :, ri * 64 + q] = rpb_l[:, ro, co].T
    return out.reshape(nv, 5, 128, 8, 128).transpose(0, 2, 1, 3, 4).copy()


def build_nc(S, NL, NV):
    from contextlib import ExitStack
    nc = bass.Bass("TRN2", target_bir_lowering=False)
    rows = S // GW
    NT = S // 128
    tiles, sigs = att_tiles(rows)
    assert len(sigs) == NV

    def din(name, shape, dt=F32):
        return nc.dram_tensor(name, list(shape), dt, kind="ExternalInput").ap()

    def dscr(name, shape, dt):
        if DEBUG:
            return nc.dram_tensor(name, list(shape), dt, kind="ExternalOutput").ap()
        return nc.dram_tensor(name, list(shape), dt).ap()

    xT = din("xT", [D, S])
    cT = din("cT", [128, 8])
    ada_w = din("ada_w", [NL, D, 6 * D])
    ada_bT = din("ada_bT", [NL, 128, 48])
    n1g = din("n1g", [NL, 128, 8])
    n2g = din("n2g", [NL, 128, 8])
    w_in = din("w_in", [NL, D, IN_COLS])
    muT = din("muT", [NL, 128, 2, 15])
    w0T = din("w0T", [NL, 128, 2, 4])
    a0T = din("a0T", [NL, 128, 2, 4])
    w2Z = din("w2Z", [NL, 128, 2, RW])
    a2Z = din("a2Z", [NL, 128, 2, RW])
    g2 = din("g2", [NL, 128, RW])
    vecT = din("vecT", [NL, 128, 5, 4])
    qkg = din("qkg", [NL, 128, 2])
    biasT = din("biasT", [NL, NV, 128, 5, 8, 128])
    w_out = din("w_out", [NL, D, D])
    f_in = din("f_in", [NL, D, 2 * DFF])
    f_out = din("f_out", [NL, DFF, D])
    outT = nc.dram_tensor("outT", [D, S], F32, kind="ExternalOutput").ap()

    hT = dscr("hT", [D, S], F32)
    zT = dscr("zT", [RWKV_COLS, S], BF16)
    zsT = dscr("zsT", [RWKV_COLS, S], BF16)
    QT = dscr("QT", [RW, S], BF16)
    KT = dscr("KT", [RW, S], BF16)
    Vtm = dscr("Vtm", [S, 520], BF16)
    ymT = dscr("ymT", [D, S], BF16)
    yfw = dscr("yfw", [S, RW], F32)
    class DB:
        pass
    dram = {n: Buf(n, None) for n in ["hT", "zT", "zsT", "QT", "KT", "Vtm", "ymT", "yfw", "outT"]}

    with ExitStack() as gs:
        Sx = Sched(nc, gs)
        psb = [Sx.ps(gs) for _ in range(8)]
        PS = Rot(psb)
        PS6 = Rot(psb[2:])
        identf = Sx.sb(gs, [128, 128], F32, "identf")
        identb = Sx.sb(gs, [128, 128], BF16, "identb")
        onesb = Sx.sb(gs, [128, 128], BF16, "onesb")
        blkf = Sx.sb(gs, [128, 128], F32, "blkf")
        blkb = Sx.sb(gs, [128, 128], BF16, "blkb")
        mLs = Sx.sb(gs, [128, 128], F32, "mLs")
        mLi = Sx.sb(gs, [128, 128], F32, "mLi")
        mUs = Sx.sb(gs, [128, 128], F32, "mUs")
        mUi = Sx.sb(gs, [128, 128], F32, "mUi")
        modT = Sx.sb(gs, [128, NL, 48], F32, "modT")
        cact = Sx.sb(gs, [128, 8], F32, "cact")
        lay = Sx.sb(gs, [128, 64], F32, "lay")
        tmpc = Sx.sb(gs, [128, 16], F32, "tmpc")

        Sx.op("pool", lambda e: e.memset(identf[:], 0.0), writes=[identf])
        Sx.op("pool", lambda e: e.affine_select(out=identf[:], in_=identf[:], pattern=[[-1, 128]],
                                                compare_op=ALU.not_equal, fill=1.0, base=0, channel_multiplier=1),
              reads=[identf], writes=[identf])
        Sx.op("pool", lambda e: e.tensor_copy(identb[:], identf[:]), reads=[identf], writes=[identb])
        Sx.op("pool", lambda e: e.memset(onesb[:], 1.0), writes=[onesb])
        Sx.op("pool", lambda e: e.memset(blkf[:], 0.0), writes=[blkf])
        Sx.op("pool", lambda e: e.memset(blkf[0:64, 0:64], 1.0), reads=[blkf], writes=[blkf])
        Sx.op("pool", lambda e: e.memset(blkf[64:128, 64:128], 1.0), reads=[blkf], writes=[blkf])
        Sx.op("pool", lambda e: e.tensor_copy(blkb[:], blkf[:]), reads=[blkf], writes=[blkb])
        for mb, cmp, stp, cm in [(mLs, ALU.is_gt, -1, 1), (mLi, ALU.is_ge, -1, 1), (mUs, ALU.is_gt, 1, -1), (mUi, ALU.is_ge, 1, -1)]:
            Sx.op("pool", lambda e, mb=mb: e.memset(mb[:], 1.0), writes=[mb])
            Sx.op("pool", lambda e, mb=mb, cmp=cmp, stp=stp, cm=cm: e.affine_select(
                out=mb[:], in_=mb[:], pattern=[[stp, 128]], compare_op=cmp, fill=0.0, base=0, channel_multiplier=cm),
                reads=[mb], writes=[mb])

        Sx.dma("sp", cact[:], cT[:, :], writes=[cact], sbuf=cact)
        Sx.op("act", lambda e: e.activation(out=cact[:], in_=cact[:], func=AF.Silu), reads=[cact], writes=[cact])
        with ExitStack() as ls:
            awp = Rot([Sx.sb(ls, [128, 8, 512], F32, "aw") for _ in range(2)])
            adb = Sx.sb(ls, [128, NL, 48], F32, "adb")
            Sx.dma("sp", adb[:], ada_bT.rearrange("l p c -> p l c"), writes=[adb], sbuf=adb)
            for l in range(NL):
                pm = PS.next()
                for cb in range(12):
                    aw = awp.next()
                    Sx.dma("sp", aw[:], ada_w[l, :, cb * 512:(cb + 1) * 512].rearrange("(kc p) f -> p kc f", p=128),
                           writes=[aw], sbuf=aw)
                    for f4 in range(4):
                        fc = cb * 4 + f4
                        for kc in range(8):
                            Sx.op("pe", lambda e, aw=aw, kc=kc, f4=f4, fc=fc, pm=pm: e.matmul(
                                pm[:, fc:fc + 1], aw[:, kc, f4 * 128:(f4 + 1) * 128], cact[:, kc:kc + 1],
                                start=(kc == 0), stop=(kc == 7)), reads=[aw, cact], writes=[pm])
                Sx.op("dve", lambda e, pm=pm, l=l: e.tensor_tensor(out=modT[:, l, :], in0=pm[:, 0:48], in1=adb[:, l, :],
                                                                   op=ALU.add), reads=[pm, adb], writes=[modT])
            Sx.barrier()

        def layer_cols(l):
            Sx.dma("sp", tmpc[:, 0:8], n1g[l], writes=[tmpc], sbuf=tmpc)
            Sx.dma("sp", tmpc[:, 8:16], n2g[l], writes=[tmpc], sbuf=tmpc)
            Sx.op("dve", lambda e: e.scalar_tensor_tensor(out=lay[:, 0:8], in0=modT[:, l, 8:16], scalar=1.0,
                                                          in1=tmpc[:, 0:8], op0=ALU.add, op1=ALU.mult),
                  reads=[modT, tmpc], writes=[lay])
            Sx.op("dve", lambda e: e.scalar_tensor_tensor(out=lay[:, 24:32], in0=modT[:, l, 32:40], scalar=1.0,
                                                          in1=tmpc[:, 8:16], op0=ALU.add, op1=ALU.mult),
                  reads=[modT, tmpc, lay], writes=[lay])
            for dst, src in [(8, 0), (16, 16), (32, 24), (40, 40)]:
                Sx.op("dve", lambda e, dst=dst, src=src: e.tensor_copy(lay[:, dst:dst + 8], modT[:, l, src:src + 8]),
                      reads=[modT, lay], writes=[lay])

        def rmsnorm(stk, hb, ub, n, gcol, shcol, sq, rstd, tmpr):
            Sx.op("act", lambda e: e.activation(out=sq[:], in_=hb[:], func=AF.Square), reads=[hb], writes=[sq])
            pm = PS.next()
            for kc in range(8):
                Sx.op("pe", lambda e, kc=kc: e.matmul(pm[:, 0:n], onesb[:], sq[:, kc, :], start=(kc == 0), stop=(kc == 7)),
                      reads=[sq, onesb], writes=[pm])
            Sx.op("act", lambda e: e.activation(out=rstd[:], in_=pm[:, 0:n], func=AF.Ln, scale=1.0 / D, bias=epsc[:, 0:1]),
                  reads=[pm, epsc], writes=[rstd])
            Sx.op("act", lambda e: e.activation(out=rstd[:], in_=rstd[:], func=AF.Exp, scale=-0.5), reads=[rstd], writes=[rstd])
            for kc in range(8):
                tm = tmpr.next()
                Sx.op("dve", lambda e, kc=kc, tm=tm: e.scalar_tensor_tensor(
                    out=tm[:], in0=hb[:, kc, :], scalar=lay[:, gcol + kc:gcol + kc + 1], in1=rstd[:],
                    op0=ALU.mult, op1=ALU.mult), reads=[hb, lay, rstd], writes=[tm])
                Sx.op("act", lambda e, kc=kc, tm=tm: e.activation(out=ub[:, kc, :], in_=tm[:], func=AF.Identity,
                                                                   bias=lay[:, shcol + kc:shcol + kc + 1]),
                      reads=[tm, lay], writes=[(ub, kc)])

        epsc = Sx.sb(gs, [128, 4], F32, "epsc")
        Sx.op("pool", lambda e: e.memset(epsc[:, 0:1], 1e-6), writes=[epsc])
        Sx.op("pool", lambda e: e.memset(epsc[:, 1:2], 64e-5), reads=[epsc], writes=[epsc])
        Sx.op("pool", lambda e: e.memset(epsc[:, 2:3], 1e-19), reads=[epsc], writes=[epsc])

        evac_flip = [0]

        def evac(out_ap, in_ap, reads, writes, eng=None):
            evac_flip[0] ^= 1
            if eng == "dve" or (eng is None and evac_flip[0]):
                Sx.op("dve", lambda e: e.tensor_copy(out_ap, in_ap), reads=reads, writes=writes)
            else:
                Sx.op("act", lambda e: e.activation(out=out_ap, in_=in_ap, func=AF.Copy), reads=reads, writes=writes)

        def P1(l):
            hsrc, hbuf = (xT, None) if l == 0 else (hT, dram["hT"])
            with ExitStack() as ls:
                win = Sx.sb(ls, [128, 8, IN_COLS], BF16, "win")
                hp = Rot([Sx.sb(ls, [128, 8, 512], F32, "h") for _ in range(2)])
                sq = Sx.sb(ls, [128, 8, 512], BF16, "sq")
                ub = Sx.sb(ls, [128, 8, 512], BF16, "u")
                rstd = Sx.sb(ls, [128, 512], F32, "rstd")
                tmpr = Rot([Sx.sb(ls, [128, 512], F32, "tm") for _ in range(2)])
                zo = Rot([Sx.sb(ls, [128, 512], BF16, "zo") for _ in range(4)])
                sqz = Rot([Sx.sb(ls, [128, 512], BF16, "sqz") for _ in range(2)])
                rsq = Rot([Sx.sb(ls, [128, 512], F32, "rsq") for _ in range(2)])
                gq = Sx.sb(ls, [128, 2], F32, "gq")
                vop = Rot([Sx.sb(ls, [128, 8, 65], BF16, "vo") for _ in range(2)])
                for vo in vop.bufs:
                    Sx.op("pool", lambda e, vo=vo: e.memset(vo[:], 1.0), writes=[vo])
                Sx.dma("sp", gq[:], qkg[l], writes=[gq], sbuf=gq)
                for kc in range(8):
                    Sx.dma("pool", win[:, kc, :], w_in[l, kc * 128:(kc + 1) * 128, :], writes=[(win, kc)], sbuf=win,
                           max_dma_last_dim=4096)
                nb = S // 512

                def load(b):
                    hb = hp.next()
                    Sx.dma("sp", hb[:], hsrc[:, b * 512:(b + 1) * 512].rearrange("(kc p) t -> p kc t", p=128),
                           reads=[hbuf] if hbuf else [], writes=[hb], sbuf=hb)
                    return hb
                nxt = load(0)
                for b in range(nb):
                    hb = nxt
                    if b + 1 < nb:
                        nxt = load(b + 1)
                    if l == 0:
                        Sx.dma("pool", hT[:, b * 512:(b + 1) * 512].rearrange("(kc p) t -> p kc t", p=128), hb[:],
                               reads=[hb], writes=[(dram["hT"], b)], sbuf=hb)
                    rmsnorm(ls, hb, ub, 512, 0, 8, sq, rstd, tmpr)
                    tsl = slice(b * 512, (b + 1) * 512)
                    for fc in range(23):
                        pm = PS.next()
                        for kc in range(8):
                            Sx.op("pe", lambda e, kc=kc, fc=fc, pm=pm: e.matmul(
                                pm[:, :], win[:, kc, fc * 128:(fc + 1) * 128], ub[:, kc, :], start=(kc == 0), stop=(kc == 7)),
                                reads=[win, ub], writes=[pm])
                        z = zo.next()
                        if fc < 15:
                            evac(z[:], pm[:, :], [pm], [z])
                            Sx.dma("pool", zT[fc * 128:(fc + 1) * 128, tsl], z[:], reads=[z], writes=[(dram["zT"], (fc, b))], sbuf=z)
                        else:
                            isq = fc < 19
                            s2 = sqz.next()
                            r2 = rsq.next()
                            Sx.op("act", lambda e, s2=s2, pm=pm: e.activation(out=s2[:], in_=pm[:, :], func=AF.Square),
                                  reads=[pm], writes=[s2])
                            p2 = PS.next()
                            Sx.op("pe", lambda e, p2=p2, s2=s2: e.matmul(p2[:, :], blkb[:], s2[:], start=True, stop=True),
                                  reads=[s2, blkb], writes=[p2])
                            Sx.op("act", lambda e, p2=p2, r2=r2: e.activation(out=r2[:], in_=p2[:, :], func=AF.Ln,
                                                                             scale=1.0 / 64, bias=epsc[:, 0:1]),
                                  reads=[p2, epsc], writes=[r2])
                            Sx.op("act", lambda e, r2=r2: e.activation(out=r2[:], in_=r2[:], func=AF.Exp, scale=-0.5), reads=[r2], writes=[r2])
                            gi = 0 if isq else 1
                            Sx.op("dve", lambda e, z=z, pm=pm, r2=r2, gi=gi: e.scalar_tensor_tensor(
                                out=z[:], in0=pm[:, :], scalar=gq[:, gi:gi + 1], in1=r2[:], op0=ALU.mult, op1=ALU.mult),
                                reads=[pm, gq, r2], writes=[z])
                            if isq:
                                Sx.dma("pool", QT[(fc - 15) * 128:(fc - 14) * 128, tsl], z[:], reads=[z],
                                       writes=[(dram["QT"], (fc, b))], sbuf=z)
                            else:
                                Sx.dma("pool", KT[(fc - 19) * 128:(fc - 18) * 128, tsl], z[:], reads=[z],
                                       writes=[(dram["KT"], (fc, b))], sbuf=z)
                    for sub in range(4):
                        pm = PS.next()
                        for kc in range(8):
                            Sx.op("pe", lambda e, kc=kc, sub=sub, pm=pm: e.matmul(
                                pm[:, :], ub[:, kc, sub * 128:(sub + 1) * 128], win[:, kc, 2944:3456],
                                start=(kc == 0), stop=(kc == 7)), reads=[win, ub], writes=[pm])
                        z = vop.next()
                        evac(z[:, :, 0:64], pm[:, :].rearrange("p (h d) -> p h d", d=64), [pm], [z])
                        Sx.dma("pool", Vtm[b * 512 + sub * 128: b * 512 + (sub + 1) * 128, :], z[:].rearrange("p h d -> p (h d)"), reads=[z],
                               writes=[(dram["Vtm"], (b, sub))], sbuf=z)
                Sx.barrier()

        def P2(l):
            with ExitStack() as ls:
                ktp = Rot([Sx.sb(ls, [128, 4, 576], BF16, "kt") for _ in range(2)])
                qzp = Rot([Sx.sb(ls, [128, 8, 128], BF16, "qz") for _ in range(2)])
                vwp = Rot([Sx.sb(ls, [128, 5, 8, 65], BF16, "vw") for _ in range(2)])
                bias = Sx.sb(ls, [128, 5, 8, 128], F32, "bias")
                sTp = Rot([Sx.sb(ls, [128, 5, 128], F32, "sT") for _ in range(3)])
                pTp = Rot([Sx.sb(ls, [128, 5, 128], BF16, "pT") for _ in range(3)])
                yap = Rot([Sx.sb(ls, [128, 512], BF16, "ya") for _ in range(2)])
                yTp = Rot([Sx.sb(ls, [128, 4, 128], BF16, "yT") for _ in range(2)])
                rs = Sx.sb(ls, [128, 8], F32, "rs")
                for qz in qzp.bufs:
                    Sx.op("pool", lambda e, qz=qz: e.memset(qz[:], 0.0), writes=[qz])
                QTv = QT.rearrange("(pr par d) t -> par d pr t", par=2, d=64)

                def load(j):
                    kb, var = tiles[j]
                    kt, qz, vw = ktp.next(), qzp.next(), vwp.next()
                    Sx.dma("sp", kt[:], KT[:, kb * 64:kb * 64 + 576].rearrange("(pr p) t -> p pr t", p=128),
                           reads=[dram["KT"]], writes=[kt], sbuf=kt)
                    qzv = qz[:].rearrange("p (pr par) q -> p par pr q", par=2)
                    for par in range(2):
                        Sx.dma("sp", qzv[par * 64:(par + 1) * 64, par, :, :], QTv[par, :, :, j * 128:(j + 1) * 128],
                               reads=[dram["QT"]], writes=[qz], sbuf=qz)
                    Sx.dma("sp", vw[:, 0:4, :, :].rearrange("p c h d -> p c (h d)"),
                           Vtm[kb * 64:kb * 64 + 512, :].rearrange("(c p) f -> p c f", p=128),
                           reads=[dram["Vtm"]], writes=[vw], sbuf=vw)
                    Sx.dma("sp", vw[0:64, 4, :, :].rearrange("p h d -> p (h d)"),
                           Vtm[kb * 64 + 512:kb * 64 + 576, :],
                           reads=[dram["Vtm"]], writes=[vw], sbuf=vw)
                    return kt, qz, vw
                cur_var = -1
                nxt = load(0)
                for j in range(NT):
                    kt, qz, vw = nxt
                    if j + 1 < NT:
                        nxt = load(j + 1)
                    kb, var = tiles[j]
                    if var != cur_var:
                        cur_var = var
                        Sx.dma("sp", bias[:], biasT[l, var], writes=[bias], sbuf=bias)
                    poA, poB = psb[0], psb[1]

                    def stage1(h):
                        pr = h // 2
                        pA, pB = PS6.next(), PS6.next()
                        for c in range(4):
                            Sx.op("pe", lambda e, c=c, pA=pA, kt=kt, qz=qz, pr=pr, h=h: e.matmul(
                                pA[:, c * 128:(c + 1) * 128], kt[:, pr, c * 128:(c + 1) * 128], qz[:, h, :],
                                start=True, stop=True), reads=[kt, qz], writes=[pA])
                        Sx.op("pe", lambda e, pB=pB, kt=kt, qz=qz, pr=pr, h=h: e.matmul(
                            pB[0:64, 0:128], kt[:, pr, 512:576], qz[:, h, :], start=True, stop=True),
                            reads=[kt, qz], writes=[pB])
                        sT, pT = sTp.next(), pTp.next()
                        Sx.op("dve", lambda e, sT=sT, pA=pA, h=h: e.scalar_tensor_tensor(
                            out=sT[:, 0:4, :], in0=pA[:, :].rearrange("p (c q) -> p c q", c=4), scalar=0.125,
                            in1=bias[:, 0:4, h, :], op0=ALU.mult, op1=ALU.add), reads=[pA, bias], writes=[(sT, 0)])
                        Sx.op("dve", lambda e, sT=sT, pB=pB, h=h: e.scalar_tensor_tensor(
                            out=sT[0:64, 4, :], in0=pB[0:64, 0:128], scalar=0.125,
                            in1=bias[0:64, 4, h, :], op0=ALU.mult, op1=ALU.add), reads=[pB, bias], writes=[(sT, 1)])
                        Sx.op("act", lambda e, sT=sT, pT=pT: e.activation(out=pT[:, 0:4, :], in_=sT[:, 0:4, :], func=AF.Exp),
                              reads=[(sT, 0)], writes=[(pT, 0)])
                        Sx.op("act", lambda e, sT=sT, pT=pT: e.activation(out=pT[0:64, 4, :], in_=sT[0:64, 4, :], func=AF.Exp),
                              reads=[(sT, 1)], writes=[(pT, 1)])
                        return pT

                    def stage2(h, pT):
                        po = poA if h < 4 else poB
                        hh = h % 4
                        for c in range(4):
                            Sx.op("pe", lambda e, c=c, po=po, pT=pT, vw=vw, h=h, hh=hh: e.matmul(
                                po[:, hh * 65:(hh + 1) * 65], pT[:, c, :], vw[:, c, h, :], start=(c == 0), stop=False),
                                reads=[(pT, 0), vw], writes=[po])
                        Sx.op("pe", lambda e, po=po, pT=pT, vw=vw, h=h, hh=hh: e.matmul(
                            po[:, hh * 65:(hh + 1) * 65], pT[0:64, 4, :], vw[0:64, 4, h, :], start=False, stop=True),
                            reads=[(pT, 1), vw], writes=[po])
                    pTs = {}
                    pTs[0] = stage1(0)
                    for h in range(8):
                        if h + 1 < 8:
                            pTs[h + 1] = stage1(h + 1)
                        stage2(h, pTs.pop(h))
                    ya = yap.next()
                    for hf, po in enumerate([poA, poB]):
                        pv = po[:, 0:260].rearrange("p (h d) -> p h d", d=65)
                        Sx.op("dve", lambda e, pv=pv, hf=hf: e.reciprocal(rs[:, hf * 4:(hf + 1) * 4].unsqueeze(2), pv[:, :, 64:65]),
                              reads=[po], writes=[rs])
                        Sx.op("dve", lambda e, pv=pv, hf=hf, ya=ya: e.tensor_tensor(
                            out=ya[:, hf * 256:(hf + 1) * 256].rearrange("p (h d) -> p h d", d=64), in0=pv[:, :, 0:64],
                            in1=rs[:, hf * 4:(hf + 1) * 4].unsqueeze(2).to_broadcast([128, 4, 64]), op=ALU.mult),
                            reads=[po, rs], writes=[ya])
                    pt = PS6.next()
                    ptb = pt[:].bitcast(BF16)
                    for pr in range(4):
                        Sx.op("pe", lambda e, pr=pr, ptb=ptb, ya=ya: e.transpose(ptb[:, pr * 128:(pr + 1) * 128],
                                                                                ya[:, pr * 128:(pr + 1) * 128], identb[:]),
                              reads=[ya, identb], writes=[pt])
                    yT = yTp.next()
                    evac(yT[:].rearrange("p a q -> p (a q)"), ptb[:, 0:512], [pt], [yT])
                    Sx.dma("pool", ymT[512:1024, j * 128:(j + 1) * 128].rearrange("(pr p) q -> p pr q", p=128), yT[:],
                           reads=[yT], writes=[(dram["ymT"], ("a", j))], sbuf=yT)
                Sx.barrier()

        def P4(l, last):
            NB = 256
            hdst, hdb = (outT, dram["outT"]) if last else (hT, dram["hT"])
            with ExitStack() as ls:
                wo = Sx.sb(ls, [128, 8, D], BF16, "wo")
                wf1 = Sx.sb(ls, [128, 8, 2 * DFF], BF16, "wf1")
                wf2 = Sx.sb(ls, [128, 22, D], BF16, "wf2")
                hp = Rot([Sx.sb(ls, [128, 8, NB], F32, "h") for _ in range(2)])
                ymp = Rot([Sx.sb(ls, [128, 8, NB], BF16, "ym") for _ in range(2)])
                sq = Sx.sb(ls, [128, 8, NB], BF16, "sq")
                ub = Sx.sb(ls, [128, 8, NB], BF16, "u")
                hid = Sx.sb(ls, [128, 22, NB], BF16, "hid")
                rstd = Sx.sb(ls, [128, NB], F32, "rstd")
                tmpr = Rot([Sx.sb(ls, [128, NB], F32, "tm") for _ in range(2)])
                silp = Rot([Sx.sb(ls, [128, NB], F32, "sil") for _ in range(2)])
                for kc in range(8):
                    Sx.dma("pool", wo[:, kc, :], w_out[l, kc * 128:(kc + 1) * 128, :], writes=[(wo, kc)], sbuf=wo)
                for kc in range(8):
                    Sx.dma("pool", wf1[:, kc, :], f_in[l, kc * 128:(kc + 1) * 128, :], writes=[(wf1, kc)], sbuf=wf1,
                           max_dma_last_dim=4096)
                for j in range(22):
                    Sx.dma("pool", wf2[:, j, :], f_out[l, j * 128:(j + 1) * 128, :], writes=[(wf2, j)], sbuf=wf2)
                nb = S // NB

                def load(b):
                    hb, ym = hp.next(), ymp.next()
                    Sx.dma("sp", hb[:], hT[:, b * NB:(b + 1) * NB].rearrange("(kc p) t -> p kc t", p=128),
                           reads=[(dram["hT"], b)], writes=[hb], sbuf=hb)
                    Sx.dma("sp", ym[:], ymT[:, b * NB:(b + 1) * NB].rearrange("(kc p) t -> p kc t", p=128),
                           reads=[dram["ymT"]], writes=[ym], sbuf=ym)
                    return hb, ym
                nxt = load(0)
                for b in range(nb):
                    hb, ym = nxt
                    if b + 1 < nb:
                        nxt = load(b + 1)
                    for fc in range(8):
                        pm = PS.next()
                        for kc in range(8):
                            Sx.op("pe", lambda e, kc=kc, fc=fc, pm=pm, ym=ym: e.matmul(
                                pm[:, 0:NB], wo[:, kc, fc * 128:(fc + 1) * 128], ym[:, kc, :], start=(kc == 0), stop=(kc == 7)),
                                reads=[wo, ym], writes=[pm])
                        Sx.op("dve", lambda e, fc=fc, pm=pm, hb=hb: e.scalar_tensor_tensor(
                            out=hb[:, fc, :], in0=pm[:, 0:NB], scalar=lay[:, 16 + fc:17 + fc], in1=hb[:, fc, :],
                            op0=ALU.mult, op1=ALU.add), reads=[pm, lay, hb], writes=[hb])
                    rmsnorm(ls, hb, ub, NB, 24, 32, sq, rstd, tmpr)
                    for j in range(22):
                        pg, pu = PS.next(), PS.next()
                        for kc in range(8):
                            Sx.op("pe", lambda e, kc=kc, j=j, pg=pg: e.matmul(
                                pg[:, 0:NB], wf1[:, kc, j * 128:(j + 1) * 128], ub[:, kc, :], start=(kc == 0), stop=(kc == 7)),
                                reads=[wf1, ub], writes=[pg])
                        for kc in range(8):
                            Sx.op("pe", lambda e, kc=kc, j=j, pu=pu: e.matmul(
                                pu[:, 0:NB], wf1[:, kc, DFF + j * 128:DFF + (j + 1) * 128], ub[:, kc, :],
                                start=(kc == 0), stop=(kc == 7)), reads=[wf1, ub], writes=[pu])
                        sl = silp.next()
                        Sx.op("act", lambda e, sl=sl, pg=pg: e.activation(out=sl[:], in_=pg[:, 0:NB], func=AF.Silu),
                              reads=[pg], writes=[sl])
                        Sx.op("dve", lambda e, sl=sl, pu=pu, j=j: e.tensor_tensor(out=hid[:, j, :], in0=sl[:], in1=pu[:, 0:NB],
                                                                              op=ALU.mult), reads=[sl, pu], writes=[(hid, j)])
                    for fc in range(8):
                        pm = PS.next()
                        for j in range(22):
                            Sx.op("pe", lambda e, j=j, fc=fc, pm=pm: e.matmul(
                                pm[:, 0:NB], wf2[:, j, fc * 128:(fc + 1) * 128], hid[:, j, :], start=(j == 0), stop=(j == 21)),
                                reads=[wf2, hid], writes=[pm])
                        Sx.op("dve", lambda e, fc=fc, pm=pm, hb=hb: e.scalar_tensor_tensor(
                            out=hb[:, fc, :], in0=pm[:, 0:NB], scalar=lay[:, 40 + fc:41 + fc], in1=hb[:, fc, :],
                            op0=ALU.mult, op1=ALU.add), reads=[pm, lay, hb], writes=[hb])
                    Sx.dma("pool", hdst[:, b * NB:(b + 1) * NB].rearrange("(kc p) t -> p kc t", p=128), hb[:],
                           reads=[hb], writes=[(hdb, b)], sbuf=hb)
                Sx.barrier()

        P3 = make_P3(locals())

        for l in range(NL):
            layer_cols(l)
            if 1 in PHASES:
                P1(l)
            if 2 in PHASES:
                P2(l)
            if 3 in PHASES:
                P3(l)
            if 4 in PHASES:
                P4(l, l == NL - 1)
        Sx.emit()
    return nc


def make_P3(env):
    Sx = env["Sx"]; PS = env["PS"]; nc = env["nc"]; S = env["S"]; NT = env["NT"]; dram = env["dram"]
    zT = env["zT"]; zsT = env["zsT"]; Vtm = env["Vtm"]; ymT = env["ymT"]; yfw = env["yfw"]
    muT = env["muT"]; w0T = env["w0T"]; a0T = env["a0T"]; w2Z = env["w2Z"]; a2Z = env["a2Z"]; g2 = env["g2"]
    vecT = env["vecT"]
    identf = env["identf"]; identb = env["identb"]; blkf = env["blkf"]; blkb = env["blkb"]
    mLs = env["mLs"]; mLi = env["mLi"]; mUs = env["mUs"]; mUi = env["mUi"]; epsc = env["epsc"]
    evac = env["evac"]; psb = env["psb"]
    from contextlib import ExitStack
    MID = 63
    C1 = float(np.exp(-0.5))

    def P3pre(l):
        with ExitStack() as ls:
            PW = min(2048, S)
            mu = Sx.sb(ls, [128, 3, 15], F32, "mu")
            Sx.dma("sp", mu[:, 0:2, :], muT[l], writes=[mu], sbuf=mu)
            Sx.op("dve", lambda e: e.tensor_tensor(out=mu[:, 2, :], in0=mu[:, 0, :], in1=mu[:, 1, :], op=ALU.add),
                  reads=[mu], writes=[mu])
            Sx.op("dve", lambda e: e.tensor_scalar(out=mu[:, 2, :], in0=mu[:, 2, :], scalar1=-1.0, scalar2=1.0,
                                                   op0=ALU.mult, op1=ALU.add), reads=[mu], writes=[mu])
            zrp = Rot([Sx.sb(ls, [128, S + 2], BF16, "zr") for _ in range(2)])
            tmp = Rot([Sx.sb(ls, [128, PW], F32, "ztmp") for _ in range(2)])
            zop = Rot([Sx.sb(ls, [128, PW], BF16, "zso") for _ in range(3)])
            for zr in zrp.bufs:
                Sx.op("pool", lambda e, zr=zr: e.memset(zr[:, 0:1], 0.0), writes=[zr])
                Sx.op("pool", lambda e, zr=zr: e.memset(zr[:, S + 1:S + 2], 0.0), reads=[zr], writes=[zr])

            def loadrow(j):
                zr = zrp.next()
                Sx.dma("sp", zr[:, 1:S + 1], zT[j * 128:(j + 1) * 128, :], reads=[dram["zT"]], writes=[zr], sbuf=zr)
                return zr
            nxt = loadrow(0)
            for j in range(15):
                zr = nxt
                if j + 1 < 15:
                    nxt = loadrow(j + 1)
                for pc in range(S // PW):
                    a = pc * PW
                    tm, zo = tmp.next(), zop.next()
                    Sx.op("act", lambda e, tm=tm, zr=zr, a=a, j=j: e.activation(out=tm[:], in_=zr[:, 1 + a:1 + a + PW], func=AF.Identity,
                                                                              scale=mu[:, 2, j:j + 1]), reads=[zr, mu], writes=[tm])
                    Sx.op("dve", lambda e, tm=tm, zr=zr, a=a, j=j: e.scalar_tensor_tensor(out=tm[:], in0=zr[:, a:a + PW], scalar=mu[:, 0, j:j + 1],
                                                                                        in1=tm[:], op0=ALU.mult, op1=ALU.add),
                          reads=[zr, mu, tm], writes=[tm])
                    Sx.op("dve", lambda e, tm=tm, zr=zr, a=a, j=j, zo=zo: e.scalar_tensor_tensor(out=zo[:], in0=zr[:, a + 2:a + 2 + PW],
                                                                                               scalar=mu[:, 1, j:j + 1], in1=tm[:],
                                                                                               op0=ALU.mult, op1=ALU.add),
                          reads=[zr, mu, tm], writes=[zo])
                    Sx.dma("pool", zsT[j * 128:(j + 1) * 128, a:a + PW], zo[:], reads=[zo], writes=[(dram["zsT"], (j, pc))], sbuf=zo)
            Sx.barrier()

    def P3(l):
        P3pre(l)
        with ExitStack() as ls:
            sb = lambda shape, dt, name: Sx.sb(ls, shape, dt, name)
            w0c = sb([128, 2, 4], F32, "w0c"); a0c = sb([128, 2, 4], F32, "a0c")
            w2s = sb([128, 2, RW], BF16, "w2s"); a2s = sb([128, 2, RW], BF16, "a2s"); g2s = sb([128, RW], BF16, "g2s")
            vec = sb([128, 5, 4], F32, "vec")
            oneka = sb([128, 4], F32, "oneka")
            ones128 = sb([128, 128], F32, "ones128")
            Sx.dma("sp", w0c[:], w0T[l], writes=[w0c], sbuf=w0c)
            Sx.dma("sp", a0c[:], a0T[l], writes=[a0c], sbuf=a0c)
            Sx.dma("sp", vec[:], vecT[l], writes=[vec], sbuf=vec)
            Sx.dma("pool", w2s[:], w2Z[l], writes=[w2s], sbuf=w2s)
            Sx.dma("pool", a2s[:], a2Z[l], writes=[a2s], sbuf=a2s)
            Sx.dma("pool", g2s[:], g2[l], writes=[g2s], sbuf=g2s)
            Sx.op("pool", lambda e: e.memset(ones128[:], 1.0), writes=[ones128])
            Sx.op("dve", lambda e: e.tensor_scalar(out=oneka[:], in0=vec[:, 1, :], scalar1=-1.0, scalar2=1.0,
                                                   op0=ALU.mult, op1=ALU.add), reads=[vec], writes=[oneka])
            kark = sb([128, 8], F32, "kark")
            Sx.op("dve", lambda e: e.tensor_tensor(out=kark[:, 0:4], in0=vec[:, 1, :], in1=vec[:, 2, :], op=ALU.mult),
                  reads=[vec], writes=[kark])
            Sx.op("dve", lambda e: e.scalar_tensor_tensor(out=kark[:, 4:8], in0=oneka[:], scalar=2.0, in1=vec[:, 2, :],
                                                          op0=ALU.mult, op1=ALU.mult), reads=[vec, oneka, kark], writes=[kark])
            tot = sb([128, 4], F32, "tot")
            bm32 = sb([128, 128], F32, "bm32")
            mDL = sb([128, 128], F32, "mDL"); mNL = sb([128, 128], F32, "mNL")
            mDU = sb([128, 128], F32, "mDU"); mNU = sb([128, 128], F32, "mNU")
            Sx.op("pool", lambda e: e.memset(bm32[:], 0.0), writes=[bm32])
            for q4 in range(4):
                Sx.op("pool", lambda e, q4=q4: e.memset(bm32[q4 * 32:(q4 + 1) * 32, q4 * 32:(q4 + 1) * 32], 1.0), reads=[bm32], writes=[bm32])
            for (mm_, mD_, mN_) in [(mLs, mDL, mNL), (mUs, mDU, mNU)]:
                Sx.op("pool", lambda e, mm_=mm_, mD_=mD_: e.tensor_tensor(out=mD_[:], in0=mm_[:], in1=bm32[:], op=ALU.mult),
                      reads=[mm_, bm32], writes=[mD_])
                Sx.op("pool", lambda e, mm_=mm_, mD_=mD_, mN_=mN_: e.tensor_tensor(out=mN_[:], in0=mm_[:], in1=mD_[:], op=ALU.subtract),
                      reads=[mm_, mD_], writes=[mN_])
            zcp = Rot([sb([128, 15, 128], BF16, "zc") for _ in range(2)])
            actb = sb([128, 3, 128], BF16, "actb")
            sg = sb([128, 4, 128], F32, "sg")
            aa = sb([128, 4, 128], F32, "aa")
            aa2 = sb([128, 4, 128], F32, "aa2")
            kk = sb([128, 4, 128], F32, "kk")
            t1 = sb([128, 4, 128], F32, "t1")
            t2 = sb([128, 4, 128], F32, "t2")
            kd = sb([128, 4, 128], F32, "kd")
            bb = sb([128, 4, 128], F32, "bb")
            Lc = sb([128, 4, 128], F32, "Lc")
            Lm = sb([128, 4, 128], F32, "Lm")
            eR = sb([128, 4, 128], F32, "eR"); eA = sb([128, 4, 128], F32, "eA")
            eB = sb([128, 4, 128], F32, "eB"); eE = sb([128, 4, 128], F32, "eE")
            ARp = Rot([sb([128, 4, 2, 128], BF16, "AR") for _ in range(4)])
            BTup = Rot([sb([128, 4, 128], BF16, "BTu") for _ in range(2)])
            KTup = Rot([sb([128, 4, 128], BF16, "KTu") for _ in range(2)])
            AZ = sb([128, 4, 2, 128], BF16, "AZ"); BZ = sb([128, 4, 2, 128], BF16, "BZ"); KZ = sb([128, 4, 2, 128], BF16, "KZ")
            bpf = sb([128, 4, 128], BF16, "bpf"); kpf = sb([128, 4, 128], BF16, "kpf"); vTf = sb([128, 4, 128], BF16, "vTf")
            Bpp = Rot([sb([128, 4, 128], BF16, "Bp") for _ in range(3)])
            Kpp = Rot([sb([128, 4, 128], BF16, "Kp") for _ in range(3)])
            Vp = Rot([sb([128, 512], BF16, "V") for _ in range(3)])
            PCf = Rot([sb([128, 4, 128], F32, "PCf") for _ in range(4)])
            PMf = Rot([sb([128, 4, 128], F32, "PMf") for _ in range(4)])
            pcol = sb([128, 8], F32, "pcol")
            Dk = [sb([128, 8, 128], BF16, "Dk%d" % i) for i in range(2)]
            Dtk = [sb([128, 8, 128], BF16, "Dtk%d" % i) for i in range(2)]
            Sk = [sb([128, 8, 128], BF16, "Sk%d" % i) for i in range(2)]
            Stk = [sb([128, 8, 128], BF16, "Stk%d" % i) for i in range(2)]
            Nn = sb([128, 8, 128], BF16, "Nn"); Eb = sb([128, 8, 128], BF16, "Eb"); Etb = sb([128, 8, 128], BF16, "Etb")
            Gp = sb([128, 8, 128], BF16, "Gp"); Fp = sb([128, 8, 128], BF16, "Fp")
            Tp = Rot([sb([128, 8, 128], BF16, "T") for _ in range(2)])
            Akp = Rot([sb([128, 8, 128], BF16, "Ak") for _ in range(3)])
            Arbp = Rot([sb([128, 8, 128], BF16, "Arb") for _ in range(3)])
            Arkp = Rot([sb([128, 8, 128], BF16, "Ark") for _ in range(3)])
            H = sb([128, 4, 128], F32, "H")
            Hs = Rot([sb([128, 4, 128], BF16, "Hs") for _ in range(2)])
            Xb = sb([128, 512], BF16, "Xb"); Ub = sb([128, 512], BF16, "Ub")
            Yp = Rot([sb([128, 512], F32, "Y") for _ in range(2)])
            gTp = Rot([sb([128, 4, 128], F32, "gT") for _ in range(4)])
            bonp = Rot([sb([128, 4, 128], F32, "bon") for _ in range(4)])
            PSA, PSB, PSS = Rot(psb[0:2]), Rot(psb[2:6]), Rot(psb[6:8])
            yn = sb([128, 8, 64], F32, "yn"); ynb = sb([128, 512], BF16, "ynb")
            st8 = sb([128, 32], F32, "st8")
            oT = Rot([sb([128, 4, 128], BF16, "oT") for _ in range(2)])
            if DEBUG:
                print("P3 sbuf bytes remaining", nc.sbuf_bytes_remaining)
            for zb in (AZ, BZ, KZ):
                Sx.op("pool", lambda e, zb=zb: e.memset(zb[:], 0.0), writes=[zb])

            def pair_ops(eng, fn_name, out_b, out_ap, in_b, in_ap, col_b, col_ap, op):
                pass

            def loadz(c):
                zc = zcp.next()
                Sx.dma("sp", zc[:], zsT[:, c * 128:(c + 1) * 128].rearrange("(c p) t -> p c t", p=128),
                       reads=[dram["zsT"]], writes=[zc], sbuf=zc)
                return zc

            def prepA(c, d, zc, out):
                post = (d == 1)
                gT, bon = (gTp.next(), bonp.next()) if post else (None, None)
                zs = zc
                r_ = lambda: zs[:, 0:4, :]
                k_ = lambda: zs[:, 4:8, :]
                v_ = lambda: zs[:, 8:12, :]
                Sx.op("act", lambda e: e.activation(out=actb[:, 0, :], in_=zs[:, 12, :], func=AF.Tanh), reads=[zs], writes=[(actb, 0)])
                Sx.op("act", lambda e: e.activation(out=actb[:, 1, :], in_=zs[:, 13, :], func=AF.Copy), reads=[zs], writes=[(actb, 1)])
                if post:
                    Sx.op("act", lambda e: e.activation(out=actb[:, 2, :], in_=zs[:, 14, :], func=AF.Sigmoid), reads=[zs], writes=[(actb, 2)])

                def lora(wz, dd, idx, outb, bias_b, scale_out=None):
                    pm = PSA.next()
                    for pr in range(4):
                        Sx.op("pe", lambda e, pr=pr, pm=pm: e.matmul(pm[:, pr * 128:(pr + 1) * 128], wz[:, dd, pr * 128:(pr + 1) * 128],
                                                                    actb[:, idx, :], start=True, stop=True),
                              reads=[wz, (actb, idx)], writes=[pm])
                    for pr in range(4):
                        Sx.op("act", lambda e, pr=pr, pm=pm: e.activation(out=outb[:, pr, :], in_=pm[:, pr * 128:(pr + 1) * 128],
                                                                         func=AF.Sigmoid, bias=bias_b[:, dd, pr:pr + 1]),
                              reads=[pm, bias_b], writes=[outb])
                yield
                lora(w2s, d, 0, sg, w0c)
                yield
                lora(a2s, d, 1, aa, a0c)
                yield
                if post:
                    lora(a2s, 0, 1, aa2, a0c)
                    pm = PSA.next()
                    for pr in range(4):
                        Sx.op("pe", lambda e, pr=pr, pm=pm: e.matmul(pm[:, pr * 128:(pr + 1) * 128], g2s[:, pr * 128:(pr + 1) * 128],
                                                                    actb[:, 2, :], start=True, stop=True),
                              reads=[g2s, (actb, 2)], writes=[pm])
                    evac(gT[:].rearrange("p a t -> p (a t)"), pm[:, :], [pm], [gT])
                yield
                for pr in range(4):
                    Sx.op("dve", lambda e, pr=pr: e.tensor_scalar(out=kk[:, pr, :], in0=zs[:, 4 + pr, :], scalar1=vec[:, 0, pr:pr + 1],
                                                                scalar2=None, op0=ALU.mult), reads=[zs, vec], writes=[kk])
                Sx.op("pool", lambda e: e.tensor_tensor(out=t1[:], in0=kk[:], in1=kk[:], op=ALU.mult), reads=[kk], writes=[t1])
                pm = PSA.next()
                for pr in range(4):
                    Sx.op("pe", lambda e, pr=pr, pm=pm: e.matmul(pm[:, pr * 128:(pr + 1) * 128], blkf[:], t1[:, pr, :], start=True, stop=True),
                          reads=[blkf, t1], writes=[pm])
                Sx.op("act", lambda e, pm=pm: e.activation(out=t2[:].rearrange("p a t -> p (a t)"), in_=pm[:, :], func=AF.Ln,
                                                          bias=epsc[:, 2:3]), reads=[pm, epsc], writes=[t2])
                Sx.op("act", lambda e: e.activation(out=t2[:], in_=t2[:], func=AF.Exp, scale=-0.5), reads=[t2], writes=[t2])
                Sx.op("dve", lambda e: e.tensor_tensor(out=kk[:], in0=kk[:], in1=t2[:], op=ALU.mult), reads=[kk, t2], writes=[kk])
                yield
                for pr in range(4):
                    Sx.op("dve", lambda e, pr=pr: e.tensor_scalar(out=t1[:, pr, :], in0=aa[:, pr, :], scalar1=vec[:, 1, pr:pr + 1],
                                                                scalar2=oneka[:, pr:pr + 1], op0=ALU.mult, op1=ALU.add),
                          reads=[aa, vec, oneka], writes=[t1])
                Sx.op("pool", lambda e: e.tensor_tensor(out=kd[:], in0=zs[:, 4:8, :], in1=t1[:], op=ALU.mult), reads=[zs, t1], writes=[kd])
                Sx.op("pool", lambda e: e.tensor_tensor(out=bb[:], in0=kk[:], in1=aa[:], op=ALU.mult), reads=[kk, aa], writes=[bb])
                yield
                if post:
                    Sx.op("dve", lambda e: e.tensor_tensor(out=t2[:], in0=aa[:], in1=aa2[:], op=ALU.add), reads=[aa, aa2], writes=[t2])
                    for pr in range(4):
                        Sx.op("dve", lambda e, pr=pr: e.tensor_scalar(out=t2[:, pr, :], in0=t2[:, pr, :], scalar1=kark[:, pr:pr + 1],
                                                                    scalar2=kark[:, 4 + pr:5 + pr], op0=ALU.mult, op1=ALU.add),
                              reads=[t2, kark], writes=[t2])
                    Sx.op("dve", lambda e: e.tensor_tensor(out=t2[:], in0=t2[:], in1=zs[:, 4:8, :], op=ALU.mult), reads=[t2, zs], writes=[t2])
                    Sx.op("dve", lambda e: e.tensor_tensor(out=t2[:], in0=t2[:], in1=zs[:, 0:4, :], op=ALU.mult), reads=[t2, zs], writes=[t2])
                    pm = PSA.next()
                    for pr in range(4):
                        Sx.op("pe", lambda e, pr=pr, pm=pm: e.matmul(pm[:, pr * 128:(pr + 1) * 128], blkf[:], t2[:, pr, :], start=True, stop=True),
                              reads=[blkf, t2], writes=[pm])
                    Sx.op("dve", lambda e, pm=pm: e.tensor_tensor(out=bon[:].rearrange("p a t -> p (a t)"), in0=pm[:, :],
                                                                 in1=zs[:, 8:12, :].rearrange("p a t -> p (a t)"), op=ALU.mult),
                          reads=[pm, zs], writes=[bon])
                yield
                for pr in range(4):
                    Sx.op("dve", lambda e, pr=pr: e.tensor_tensor_scan(out=Lc[:, pr, :], data0=ones128[:], data1=sg[:, pr, :], initial=0.0,
                                                                      op0=ALU.mult, op1=ALU.add), reads=[ones128, sg], writes=[Lc])
                last = 127
                if d == 1:
                    Sx.op("dve", lambda e: e.tensor_tensor(out=t1[:], in0=sg[:], in1=Lc[:], op=ALU.subtract), reads=[sg, Lc], writes=[t1])
                    Sx.op("dve", lambda e: e.tensor_copy(tot[:].unsqueeze(2), Lc[:, :, 127:128]), reads=[Lc], writes=[tot])
                    Sx.op("dve", lambda e: e.tensor_tensor(out=Lc[:], in0=t1[:], in1=tot[:].unsqueeze(2).to_broadcast([128, 4, 128]),
                                                           op=ALU.add), reads=[t1, tot], writes=[Lc])
                    last = 0
                yield
                Sx.op("dve", lambda e: e.tensor_tensor(out=Lm[:], in0=Lc[:], in1=Lc[:, :, MID:MID + 1].to_broadcast([128, 4, 128]),
                                                       op=ALU.subtract), reads=[Lc], writes=[Lm])
                Sx.op("act", lambda e: e.activation(out=eR[:], in_=Lm[:], func=AF.Exp, scale=-C1), reads=[Lm], writes=[eR])
                Sx.op("act", lambda e: e.activation(out=eB[:], in_=Lm[:], func=AF.Exp, scale=C1), reads=[Lm], writes=[eB])
                Sx.op("pool", lambda e: e.tensor_tensor(out=t1[:], in0=Lm[:], in1=sg[:], op=ALU.subtract), reads=[Lm, sg], writes=[t1])
                Sx.op("act", lambda e: e.activation(out=eA[:], in_=t1[:], func=AF.Exp, scale=-C1), reads=[t1], writes=[eA])
                Sx.op("dve", lambda e: e.tensor_tensor(out=t2[:], in0=Lc[:], in1=Lc[:, :, last:last + 1].to_broadcast([128, 4, 128]),
                                                       op=ALU.subtract), reads=[Lc], writes=[t2])
                Sx.op("act", lambda e: e.activation(out=eE[:], in_=t2[:], func=AF.Exp, scale=C1), reads=[t2], writes=[eE])
                yield
                PC, PM = PCf.next(), PMf.next()
                Sx.op("act", lambda e: e.activation(out=pcol[:, 0:4].unsqueeze(2), in_=Lc[:, :, last:last + 1], func=AF.Exp, scale=-C1),
                      reads=[Lc], writes=[pcol])
                Sx.op("act", lambda e: e.activation(out=pcol[:, 4:8].unsqueeze(2), in_=Lc[:, :, MID:MID + 1], func=AF.Exp, scale=-C1),
                      reads=[Lc, pcol], writes=[pcol])
                Sx.op("dve", lambda e, PC=PC: e.tensor_copy(PC[:], pcol[:, 0:4].unsqueeze(2).to_broadcast([128, 4, 128])),
                      reads=[pcol], writes=[PC])
                for pr in range(4):
                    Sx.op("act", lambda e, PM=PM, pr=pr: e.activation(out=PM[:, pr, :], in_=blkf[:], func=AF.Identity,
                                                                   scale=pcol[:, 4 + pr:5 + pr]), reads=[pcol, blkf], writes=[PM])
                yield
                AR = ARp.next()
                Sx.op("dve", lambda e, AR=AR: e.scalar_tensor_tensor(out=AR[:, :, 0, :], in0=kk[:], scalar=-1.0, in1=eA[:],
                                                                   op0=ALU.mult, op1=ALU.mult), reads=[kk, eA], writes=[AR])
                Sx.op("pool", lambda e, AR=AR: e.tensor_tensor(out=AR[:, :, 1, :], in0=zs[:, 0:4, :], in1=eR[:], op=ALU.mult),
                      reads=[zs, eR, AR], writes=[AR])
                BTu, KTu = BTup.next(), KTup.next()
                Sx.op("dve", lambda e: e.tensor_tensor(out=BTu[:], in0=bb[:], in1=eB[:], op=ALU.mult), reads=[bb, eB], writes=[BTu])
                Sx.op("dve", lambda e: e.tensor_tensor(out=KTu[:], in0=kd[:], in1=eB[:], op=ALU.mult), reads=[kd, eB], writes=[KTu])
                Sx.op("pool", lambda e: e.tensor_tensor(out=bpf[:], in0=bb[:], in1=eE[:], op=ALU.mult), reads=[bb, eE], writes=[bpf])
                Sx.op("pool", lambda e: e.tensor_tensor(out=kpf[:], in0=kd[:], in1=eE[:], op=ALU.mult), reads=[kd, eE], writes=[kpf])
                Sx.op("act", lambda e: e.activation(out=vTf[:], in_=zs[:, 8:12, :], func=AF.Copy), reads=[zs], writes=[vTf])
                out.update(dict(AR=AR, PC=PC, PM=PM, gT=gT, bon=bon, BTu=BTu))
                yield "SPLIT"
                for par in range(2):
                    ps_ = slice(par * 64, (par + 1) * 64)
                    Sx.op("act", lambda e, ps_=ps_, par=par, AR=AR: e.activation(out=AZ[ps_, :, par, :], in_=AR[ps_, :, 0, :], func=AF.Copy), reads=[AR], writes=[AZ])
                    Sx.op("act", lambda e, ps_=ps_, par=par: e.activation(out=BZ[ps_, :, par, :], in_=BTu[ps_, :, :], func=AF.Copy), reads=[BTu], writes=[BZ])
                    Sx.op("act", lambda e, ps_=ps_, par=par: e.activation(out=KZ[ps_, :, par, :], in_=KTu[ps_, :, :], func=AF.Copy), reads=[KTu], writes=[KZ])
                yield
                Bp, Kp, V = Bpp.next(), Kpp.next(), Vp.next()
                for src, dst in [(bpf, Bp), (kpf, Kp), (vTf, V)]:
                    pt = PSA.next()
                    ptb = pt[:].bitcast(BF16)
                    for pr in range(4):
                        Sx.op("pe", lambda e, pr=pr, ptb=ptb, src=src: e.transpose(ptb[:, pr * 128:(pr + 1) * 128], src[:, pr, :], identb[:]),
                              reads=[src, identb], writes=[pt])
                    dap = dst[:] if dst is V else dst[:].rearrange("p a t -> p (a t)")
                    evac(dap, ptb[:, 0:512], [pt], [dst])
                    yield
                out.update(dict(Bp=Bp, Kp=Kp, V=V))

            def prepB(d, P):
                AR = P["AR"]; BTu = P["BTu"]
                m_abT = mUs if d == 0 else mLs
                mD_ab, mN_ab = (mDL, mNL) if d == 0 else (mDU, mNU)
                mD_abT = mDU if d == 0 else mDL
                m_inT = mUi if d == 0 else mLi
                Ak, Arb, Ark = Akp.next(), Arbp.next(), Arkp.next()
                for hg in range(2):
                    p1 = PSB.next()
                    for hh in range(4):
                        h = hg * 4 + hh
                        pr, par = h // 2, h % 2
                        Sx.op("pe", lambda e, p1=p1, hh=hh, pr=pr, par=par: e.matmul(p1[:, hh * 128:(hh + 1) * 128], AZ[:, pr, par, :],
                                                                                      BTu[:, pr, :], start=True, stop=True),
                              reads=[AZ, BTu], writes=[p1])
                    Sx.op("dve", lambda e, p1=p1, hg=hg: e.tensor_tensor(out=Dk[0][:, hg * 4:(hg + 1) * 4, :],
                                                                        in0=p1[:, :].rearrange("p (a t) -> p a t", a=4),
                                                                        in1=mD_ab[:].unsqueeze(1).to_broadcast([128, 4, 128]), op=ALU.mult),
                          reads=[p1, mD_ab], writes=[(Dk[0], hg)])
                    Sx.op("dve", lambda e, p1=p1, hg=hg: e.tensor_tensor(out=Nn[:, hg * 4:(hg + 1) * 4, :],
                                                                        in0=p1[:, :].rearrange("p (a t) -> p a t", a=4),
                                                                        in1=mN_ab[:].unsqueeze(1).to_broadcast([128, 4, 128]), op=ALU.mult),
                          reads=[p1, mN_ab], writes=[(Nn, hg)])
                    for (LZ, o1, m1, o2, m2) in [(BZ, Dtk[0], mD_abT, Arb, m_inT), (KZ, Ak, m_abT, Ark, m_inT)]:
                        for h2 in range(2):
                            pass
                        for half in range(2):
                            p2 = PSB.next()
                            for q in range(2):
                                h = hg * 4 + half * 2 + q
                                pr, par = h // 2, h % 2
                                Sx.op("pe", lambda e, p2=p2, q=q, pr=pr, par=par, LZ=LZ, AR=AR: e.matmul(
                                    p2[:, q * 256:(q + 1) * 256], LZ[:, pr, par, :], AR[:, pr, :, :].rearrange("p a t -> p (a t)"),
                                    start=True, stop=True), reads=[LZ, AR], writes=[p2])
                            h0 = hg * 4 + half * 2
                            pv = p2[:, :].rearrange("p (q a t) -> p q a t", q=2, a=2)
                            Sx.op("dve", lambda e, pv=pv, o1=o1, m1=m1, h0=h0: e.tensor_tensor(
                                out=o1[:, h0:h0 + 2, :], in0=pv[:, :, 0, :], in1=m1[:].unsqueeze(1).to_broadcast([128, 2, 128]), op=ALU.mult),
                                reads=[p2, m1], writes=[(o1, h0)])
                            Sx.op("dve", lambda e, pv=pv, o2=o2, m2=m2, h0=h0: e.tensor_tensor(
                                out=o2[:, h0:h0 + 2, :], in0=pv[:, :, 1, :], in1=m2[:].unsqueeze(1).to_broadcast([128, 2, 128]), op=ALU.mult),
                                reads=[p2, m2], writes=[(o2, h0)])
                            yield
                idb3 = lambda n: identb[:].unsqueeze(1).to_broadcast([128, n, 128])
                Sx.op("dve", lambda e: e.tensor_tensor(out=Sk[0][:], in0=Dk[0][:], in1=idb3(8), op=ALU.add), reads=[Dk[0], identb], writes=[Sk[0]])
                Sx.op("pool", lambda e: e.tensor_tensor(out=Stk[0][:], in0=Dtk[0][:], in1=idb3(8), op=ALU.add), reads=[Dtk[0], identb], writes=[Stk[0]])
                yield

                def mm4(ps_, lhsb, rhsb, hg, rd):
                    for hh in range(4):
                        h = hg * 4 + hh
                        cs = slice(hh * 128, (hh + 1) * 128)
                        Sx.op("pe", lambda e, ps_=ps_, cs=cs, h=h: e.matmul(ps_[:, cs], lhsb[:, h, :], rhsb[:, h, :], start=True, stop=True),
                              reads=rd, writes=[ps_])

                def hv(b, hg):
                    return b[:, hg * 4:(hg + 1) * 4, :].rearrange("p a t -> p (a t)")
                cur = 0
                for lv in range(4):
                    nx = 1 - cur
                    for hg in range(2):
                        pD = PSB.next()
                        mm4(pD, Dtk[cur], Dk[cur], hg, [Dtk[cur], Dk[cur]])
                        evac(hv(Dk[nx], hg), pD[:, :], [pD], [(Dk[nx], hg)], eng="act")
                        pDt = PSB.next()
                        mm4(pDt, Dk[cur], Dtk[cur], hg, [Dtk[cur], Dk[cur]])
                        evac(hv(Dtk[nx], hg), pDt[:, :], [pDt], [(Dtk[nx], hg)], eng="act")
                        yield
                    for hg in range(2):
                        pS = PSB.next()
                        mm4(pS, Dtk[nx], Sk[cur], hg, [(Dtk[nx], hg), Sk[cur]])
                        Sx.op("dve", lambda e, pS=pS, hg=hg, cur=cur, nx=nx: e.tensor_tensor(out=hv(Sk[nx], hg), in0=pS[:, :], in1=hv(Sk[cur], hg), op=ALU.add),
                              reads=[pS, Sk[cur]], writes=[(Sk[nx], hg)])
                        pS2 = PSB.next()
                        mm4(pS2, Dk[nx], Stk[cur], hg, [(Dk[nx], hg), Stk[cur]])
                        Sx.op("dve", lambda e, pS2=pS2, hg=hg, cur=cur, nx=nx: e.tensor_tensor(out=hv(Stk[nx], hg), in0=pS2[:, :], in1=hv(Stk[cur], hg), op=ALU.add),
                              reads=[pS2, Stk[cur]], writes=[(Stk[nx], hg)])
                        yield
                    cur = nx
                    if lv == 1:
                        yield "SPLIT"
                Dinv, Dip = Sk[cur], Stk[cur]
                for hg in range(2):
                    pE = PSB.next()
                    mm4(pE, Nn, Dip, hg, [Nn, Dip])
                    evac(hv(Etb, hg), pE[:, :], [pE], [(Etb, hg)], eng="act")
                    pE2 = PSB.next()
                    mm4(pE2, Dip, Nn, hg, [Nn, Dip])
                    evac(hv(Eb, hg), pE2[:, :], [pE2], [(Eb, hg)], eng="act")
                    yield
                for hg in range(2):
                    pG = PSB.next()
                    mm4(pG, Eb, Etb, hg, [(Eb, hg), (Etb, hg)])
                    Sx.op("dve", lambda e, pG=pG, hg=hg: e.tensor_tensor(out=Gp[:, hg * 4:(hg + 1) * 4, :], in0=pG[:, :].rearrange("p (a t) -> p a t", a=4),
                                                                        in1=idb3(4), op=ALU.add), reads=[pG, identb], writes=[(Gp, hg)])
                    yield
                for hg in range(2):
                    pF = PSB.next()
                    mm4(pF, Eb, Gp, hg, [(Eb, hg), (Gp, hg)])
                    Sx.op("dve", lambda e, pF=pF, hg=hg: e.tensor_tensor(out=hv(Fp, hg), in0=pF[:, :], in1=hv(Gp, hg), op=ALU.add),
                          reads=[pF, (Gp, hg)], writes=[(Fp, hg)])
                    yield
                T = Tp.next()
                for hg in range(2):
                    pT = PSB.next()
                    mm4(pT, Dinv, Fp, hg, [Dinv, (Fp, hg)])
                    evac(hv(T, hg), pT[:, :], [pT], [(T, hg)], eng="act")
                    yield
                P.update(dict(T=T, Ak=Ak, Arb=Arb, Ark=Ark))

            def serial(c, d, P, Hs_cur, PMnext, Hn):
                AR, Bp, Kp, V, PC, T, Ak, Arb, Ark = (P[k] for k in ["AR", "Bp", "Kp", "V", "PC", "T", "Ak", "Arb", "Ark"])
                pX = PSS.next()
                for h in range(8):
                    pr, par = h // 2, h % 2
                    cs = slice(h * 64, (h + 1) * 64)
                    Sx.op("pe", lambda e, pX=pX, cs=cs, pr=pr, par=par: e.matmul(pX[:, cs], AR[:, pr, 0, :], Hs_cur[:, pr, par * 64:(par + 1) * 64],
                                                                                  start=True, stop=False), reads=[AR, Hs_cur], writes=[pX])
                    Sx.op("pe", lambda e, pX=pX, cs=cs, h=h: e.matmul(pX[:, cs], Ak[:, h, :], V[:, cs], start=False, stop=True),
                          reads=[Ak, V], writes=[pX])
                Sx.op("dve", lambda e, pX=pX: e.tensor_copy(Xb[:], pX[:, :]), reads=[pX], writes=[Xb])
                yield
                pU = PSS.next()
                for h in range(8):
                    cs = slice(h * 64, (h + 1) * 64)
                    Sx.op("pe", lambda e, pU=pU, cs=cs, h=h: e.matmul(pU[:, cs], T[:, h, :], Xb[:, cs], start=True, stop=True),
                          reads=[T, Xb], writes=[pU])
                Sx.op("act", lambda e, pU=pU: e.activation(out=Ub[:], in_=pU[:, :], func=AF.Copy), reads=[pU], writes=[Ub])
                yield
                pY = PSS.next()
                for h in range(8):
                    pr, par = h // 2, h % 2
                    cs = slice(h * 64, (h + 1) * 64)
                    Sx.op("pe", lambda e, pY=pY, cs=cs, pr=pr, par=par: e.matmul(pY[:, cs], AR[:, pr, 1, :], Hs_cur[:, pr, par * 64:(par + 1) * 64],
                                                                                  start=True, stop=False), reads=[AR, Hs_cur], writes=[pY])
                    Sx.op("pe", lambda e, pY=pY, cs=cs, h=h: e.matmul(pY[:, cs], Arb[:, h, :], Ub[:, cs], start=False, stop=False),
                          reads=[Arb, Ub], writes=[pY])
                    Sx.op("pe", lambda e, pY=pY, cs=cs, h=h: e.matmul(pY[:, cs], Ark[:, h, :], V[:, cs], start=False, stop=True),
                          reads=[Ark, V], writes=[pY])
                pH = PSS.next()
                for pr in range(4):
                    cs = slice(pr * 128, (pr + 1) * 128)
                    Sx.op("pe", lambda e, pH=pH, cs=cs, pr=pr: e.matmul(pH[:, cs], Bp[:, pr, :], Ub[:, cs], start=True, stop=False),
                          reads=[Bp, Ub], writes=[pH])
                    Sx.op("pe", lambda e, pH=pH, cs=cs, pr=pr: e.matmul(pH[:, cs], Kp[:, pr, :], V[:, cs], start=False, stop=True),
                          reads=[Kp, V], writes=[pH])
                Sx.op("dve", lambda e: e.tensor_tensor(out=H[:], in0=H[:], in1=PC[:], op=ALU.mult), reads=[H, PC], writes=[H])
                Sx.op("dve", lambda e, pH=pH: e.tensor_tensor(out=H[:].rearrange("p a t -> p (a t)"), in0=H[:].rearrange("p a t -> p (a t)"),
                                                             in1=pH[:, :], op=ALU.add), reads=[H, pH], writes=[H])
                if PMnext is not None:
                    Sx.op("dve", lambda e, Hn=Hn: e.tensor_tensor(out=Hn[:], in0=H[:], in1=PMnext[:], op=ALU.mult),
                          reads=[H, PMnext], writes=[Hn])
                yield
                if d == 0:
                    Y = Yp.next()
                    Sx.op("act", lambda e, Y=Y, pY=pY: e.activation(out=Y[:], in_=pY[:, :], func=AF.Copy), reads=[pY], writes=[Y])
                    Sx.dma("sp", yfw[c * 128:(c + 1) * 128, :], Y[:], reads=[Y], writes=[(dram["yfw"], c)], sbuf=Y)
                else:
                    post(c, pY, P["gT"], P["bon"])

            def post(c, pY, gT, bon):
                Y = Yp.next()
                Sx.dma("sp", Y[:], yfw[c * 128:(c + 1) * 128, :], reads=[(dram["yfw"], c)], writes=[Y], sbuf=Y)
                Yv = Y[:].rearrange("p (h d) -> p h d", d=64)
                Sx.op("dve", lambda e: e.tensor_tensor(out=Y[:], in0=Y[:], in1=pY[:, :], op=ALU.add), reads=[Y, pY], writes=[Y])
                Sx.op("dve", lambda e: e.tensor_reduce(out=st8[:, 0:8], in_=Yv, axis=AX.X, op=ALU.add), reads=[Y], writes=[st8])
                Sx.op("dve", lambda e: e.tensor_scalar(out=st8[:, 0:8], in0=st8[:, 0:8], scalar1=1.0 / 64, scalar2=None, op0=ALU.mult),
                      reads=[st8], writes=[st8])
                Sx.op("dve", lambda e: e.tensor_tensor(out=yn[:], in0=Yv, in1=st8[:, 0:8].unsqueeze(2).to_broadcast([128, 8, 64]),
                                                       op=ALU.subtract), reads=[Y, st8], writes=[yn])
                Sx.op("pool", lambda e: e.tensor_tensor(out=Yv, in0=yn[:], in1=yn[:], op=ALU.mult), reads=[yn, Y], writes=[Y])
                Sx.op("dve", lambda e: e.tensor_reduce(out=st8[:, 8:16], in_=Yv, axis=AX.X, op=ALU.add), reads=[Y, st8], writes=[st8])
                Sx.op("act", lambda e: e.activation(out=st8[:, 16:24], in_=st8[:, 8:16], func=AF.Sqrt, scale=1.0 / 64, bias=epsc[:, 1:2]),
                      reads=[st8, epsc], writes=[st8])
                Sx.op("dve", lambda e: e.reciprocal(st8[:, 24:32], st8[:, 16:24]), reads=[st8], writes=[st8])
                Sx.op("dve", lambda e: e.tensor_tensor(out=ynb[:].rearrange("p (h d) -> p h d", d=64), in0=yn[:],
                                                       in1=st8[:, 24:32].unsqueeze(2).to_broadcast([128, 8, 64]), op=ALU.mult),
                      reads=[yn, st8], writes=[ynb])
                pt = PSS.next()
                ptb = pt[:].bitcast(BF16)
                for pr in range(4):
                    Sx.op("pe", lambda e, pr=pr, ptb=ptb: e.transpose(ptb[:, pr * 128:(pr + 1) * 128], ynb[:, pr * 128:(pr + 1) * 128], identb[:]),
                          reads=[ynb, identb], writes=[pt])
                o = oT.next()
                for pr in range(4):
                    Sx.op("dve", lambda e, pr=pr, ptb=ptb: e.tensor_scalar(out=t1[:, pr, :], in0=ptb[:, pr * 128:(pr + 1) * 128],
                                                                        scalar1=vec[:, 3, pr:pr + 1], scalar2=vec[:, 4, pr:pr + 1],
                                                                        op0=ALU.mult, op1=ALU.add), reads=[pt, vec], writes=[t1])
                Sx.op("pool", lambda e: e.tensor_tensor(out=t1[:], in0=t1[:], in1=bon[:], op=ALU.add), reads=[t1, bon], writes=[t1])
                Sx.op("pool", lambda e, o=o: e.tensor_tensor(out=o[:], in0=t1[:], in1=gT[:], op=ALU.mult), reads=[t1, gT], writes=[o])
                Sx.dma("sp", ymT[0:512, c * 128:(c + 1) * 128].rearrange("(pr p) t -> p pr t", p=128), o[:], reads=[o],
                       writes=[(dram["ymT"], ("r", c))], sbuf=o)

            def interleave(gens):
                st = [[g, n, 0] for g, n in gens]
                while st:
                    it = min(st, key=lambda x: (x[2] + 1.0) / x[1])
                    try:
                        next(it[0])
                        it[2] += 1
                    except StopIteration:
                        st.remove(it)

            for d in range(2):
                order = list(range(NT)) if d == 0 else list(range(NT - 1, -1, -1))
                n = len(order)
                Sx.op("pool", lambda e: e.memset(H[:], 0.0), writes=[H])
                Hlist = [Hs.next()]
                Sx.op("pool", lambda e, Hc=Hlist[0]: e.memset(Hc[:], 0.0), writes=[Hlist[0]])
                Ps = {}
                gA, gB, doneA = {}, {}, set()

                def startA(i, d=d, order=order, Ps=Ps, gA=gA):
                    zc = loadz(order[i])
                    Ps[i] = {}
                    gA[i] = prepA(order[i], d, zc, Ps[i])

                def until_split(g):
                    for v in g:
                        if v == "SPLIT":
                            return
                        yield

                def rest(g):
                    for v in g:
                        yield

                def streamA(r, gA=gA, doneA=doneA, n=n):
                    if r in gA:
                        yield from rest(gA[r])
                        doneA.add(r)
                        del gA[r]
                    if r + 1 < n:
                        startA(r + 1)
                        yield from until_split(gA[r + 1])

                def streamB(r, d=d, gB=gB, doneA=doneA, Ps=Ps, n=n):
                    if (r - 1) in gB:
                        yield from rest(gB[r - 1])
                        del gB[r - 1]
                    if r < n:
                        while r not in doneA:
                            yield
                        gB[r] = prepB(d, Ps[r])
                        yield from until_split(gB[r])

                def genS(i, d=d, order=order, Ps=Ps, Hlist=Hlist, n=n):
                    Hn = Hs.next() if i + 1 < n else None
                    Hc = Hlist[0]
                    Hlist[0] = Hn
                    yield from serial(order[i], d, Ps[i], Hc, Ps[i + 1]["PM"] if i + 1 < n else None, Hn)
                    del Ps[i]

                def capture(gen):
                    lst = []
                    Sx.capture = lst
                    for _ in gen:
                        pass
                    Sx.capture = None
                    return lst

                def merge(streams):
                    done = set()
                    stall = [0]
                    st = []
                    for segs in streams:
                        tot = sum(len(o) for o, _, _ in segs)
                        if tot:
                            st.append([segs, 0, 0, tot, 0])
                    while st:
                        cand = []
                        for x in st:
                            segs, si, oi, tot, em = x
                            while si < len(segs) and oi >= len(segs[si][0]):
                                if segs[si][2] is not None:
                                    done.add(segs[si][2])
                                si += 1
                                oi = 0
                            x[1], x[2] = si, oi
                            if si >= len(segs):
                                continue
                            if segs[si][1] is not None and segs[si][1] not in done:
                                continue
                            cand.append(x)
                        st = [x for x in st if x[1] < len(x[0])]
                        if not st:
                            break
                        if not cand:
                            stall[0] += 1
                            assert stall[0] < 3, "merge deadlock"
                            continue
                        stall[0] = 0
                        x = min(cand, key=lambda y: (y[4] + 1.0) / y[3])
                        Sx.replay(x[0][x[1]][0][x[2]])
                        x[2] += 1
                        x[4] += 1

                startA(0)
                for it in capture(until_split(gA[0])):
                    Sx.replay(it)
                for r in range(n + 2):
                    segS = [(capture(genS(r - 2)), None, None)] if 2 <= r <= n + 1 else []
                    segA = []
                    if r in gA:
                        segA.append((capture(rest(gA[r])), None, ("A", r)))
                        doneA.add(r)
                        del gA[r]
                    if r + 1 < n:
                        startA(r + 1)
                        segA.append((capture(until_split(gA[r + 1])), None, None))
                    segB = []
                    if (r - 1) in gB:
                        segB.append((capture(rest(gB[r - 1])), None, None))
                        del gB[r - 1]
                    if r < n:
                        gB[r] = prepB(d, Ps[r])
                        segB.append((capture(until_split(gB[r])), ("A", r), None))
                    merge([segS, segB, segA])
            Sx.barrier()
    return P3


_NC_CACHE = {}


def prep_inputs(inputs, S, NL):
    f = lambda a: np.ascontiguousarray(a, dtype=np.float32)
    rows = S // GW
    tiles, sigs = att_tiles(rows)
    common = {}
    common["ada_w"] = f(inputs["ada_w"][:NL])
    common["ada_bT"] = f(inputs["ada_b"][:NL].reshape(NL, 48, 128).transpose(0, 2, 1))
    common["n1g"] = f(inputs["norm1_g"][:NL].reshape(NL, 8, 128).transpose(0, 2, 1))
    common["n2g"] = f(inputs["norm2_g"][:NL].reshape(NL, 8, 128).transpose(0, 2, 1))
    common["w_in"] = f(inputs["w_in"][:NL])
    common["muT"] = f(inputs["shift_mu"][:NL].reshape(NL, 2, 15, 128).transpose(0, 3, 1, 2))
    common["w0T"] = f(inputs["w0"][:NL].reshape(NL, 2, 4, 128).transpose(0, 3, 1, 2))
    common["a0T"] = f(inputs["a0"][:NL].reshape(NL, 2, 4, 128).transpose(0, 3, 1, 2))
    w2Z = np.zeros((NL, 128, 2, RW), np.float32)
    a2Z = np.zeros((NL, 128, 2, RW), np.float32)
    for d in range(2):
        w2Z[:, d * 64:(d + 1) * 64, d, :] = inputs["w2"][:NL, d]
        a2Z[:, d * 64:(d + 1) * 64, d, :] = inputs["a2"][:NL, d]
    common["w2Z"] = w2Z
    common["a2Z"] = a2Z
    common["g2"] = f(inputs["g2"][:NL])
    vec = np.stack([inputs[k][:NL].reshape(NL, 4, 128).transpose(0, 2, 1) for k in ["k_k", "k_a", "r_k", "lnx_g", "lnx_b"]], axis=2)
    common["vecT"] = f(vec)
    qk = np.stack([np.tile(inputs["q_norm_g"][:NL], (1, 2)), np.tile(inputs["k_norm_g"][:NL], (1, 2))], axis=2)
    common["qkg"] = f(qk)
    common["biasT"] = f(np.stack([build_bias(np.asarray(inputs["rpb"][l]), sigs) for l in range(NL)]))
    common["w_out"] = f(inputs["w_out"][:NL])
    common["f_in"] = f(inputs["ffn_w_in"][:NL])
    common["f_out"] = f(inputs["ffn_w_out"][:NL])
    return common, len(sigs)


def run(inputs, S, NL, ncores=8, trace=False):
    inputs = {k: np.asarray(v) for k, v in inputs.items()}
    B = inputs["x"].shape[0]
    common, NV = prep_inputs(inputs, S, NL)
    key = (S, NL, NV)
    if key not in _NC_CACHE:
        _NC_CACHE[key] = build_nc(S, NL, NV)
    nc = _NC_CACHE[key]
    in_maps = []
    for cidx in range(ncores):
        b = cidx % B
        m = dict(common)
        m["xT"] = np.ascontiguousarray(inputs["x"][b, :S].T, dtype=np.float32)
        m["cT"] = np.ascontiguousarray(inputs["c"][b].reshape(8, 128).T, dtype=np.float32)
        in_maps.append(m)
    res = run_bass_kernel_spmd(nc, in_maps, core_ids=list(range(ncores)), trace=trace)
    if trace:
        print("EXEC_TIME_NS", res.exec_time_ns)
    nb = min(B, ncores)
    out = np.stack([np.ascontiguousarray(res.results[b]["outT"].T) for b in range(nb)], axis=0)
    if DEBUG:
        return out.astype(np.float32), res.results
    return out.astype(np.float32)


def kernel(**inputs):
    return run(inputs, 8192, 4)
```

```python
import numpy as np
import concourse.bass as bass
import concourse.mybir as mybir
from concourse.bass_utils import run_bass_kernel_spmd

F32 = mybir.dt.float32
BF16 = mybir.dt.bfloat16
AF = mybir.ActivationFunctionType
ALU = mybir.AluOpType
AX = mybir.AxisListType

D = 1024
GW = 64
RW = 512
RWKV_COLS = 1920
IN_COLS = 3456
DFF = 2816
NEG = -30000.0
DEBUG = False
PHASES = (1, 2, 3, 4)


class Buf:
    def __init__(self, name, t):
        self.name = name
        self.t = t
        self.st = {}
        self.dsem = None

    def __getitem__(self, k):
        return self.t[k]


class Sched:
    ENG = ["pe", "dve", "act", "pool", "sp"]

    def __init__(self, nc, stack):
        self.nc = nc
        self.stack = stack
        self.sems = {}
        self.count = {}
        self.known = {e: {} for e in self.ENG}
        self.lists = {e: [] for e in self.ENG}
        for e in self.ENG:
            self.sems[e] = stack.enter_context(nc.semaphore("s_" + e))
            self.count[e] = 0
        self.dsems = []
        for i in range(24):
            nm = "d%d" % i
            self.sems[nm] = stack.enter_context(nc.semaphore("s_" + nm))
            self.count[nm] = 0
            self.dsems.append(nm)
        self.dnext = 0
        self.nbuf = 0
        self.capture = None

    def merge_lists(self, lists):
        st = [[l, 0] for l in lists if l]
        while st:
            x = min(st, key=lambda y: (y[1] + 1.0) / len(y[0]))
            self.replay(x[0][x[1]])
            x[1] += 1
            if x[1] >= len(x[0]):
                st.remove(x)

    def replay(self, item):
        kind, eng, fn, reads, writes, _ = item
        if kind == "op":
            self.op(eng, fn, reads, writes)
        else:
            out_ap, in_ap, sbuf, kw = fn
            self.dma(eng, out_ap, in_ap, reads, writes, sbuf=sbuf, **kw)

    def sb(self, stack, shape, dt, name=None):
        self.nbuf += 1
        name = (name or "b") + "_%d" % self.nbuf
        return Buf(name, stack.enter_context(self.nc.sbuf_tensor(name, list(shape), dt)))

    def ps(self, stack, name=None):
        self.nbuf += 1
        name = (name or "p") + "_%d" % self.nbuf
        return Buf(name, stack.enter_context(self.nc.psum_tensor(name, [128, 512], F32)))

    def dsem_for(self, buf):
        if buf.dsem is None:
            buf.dsem = self.dsems[self.dnext % len(self.dsems)]
            self.dnext += 1
        return buf.dsem

    @staticmethod
    def _norm(x):
        return x if isinstance(x, tuple) else (x, None)

    def _deps(self, reads, writes):
        deps = {}

        def add(tok):
            if tok is None:
                return
            s, v = tok
            if deps.get(s, 0) < v:
                deps[s] = v

        for item in reads:
            b, p = self._norm(item)
            for q, st in b.st.items():
                if p is None or q is None or p == q:
                    add(st[0])
        for item in writes:
            b, p = self._norm(item)
            for q, st in b.st.items():
                if p is None or q is None or p == q:
                    add(st[0])
                    for s, v in st[1].items():
                        add((s, v))
        return deps

    def _commit(self, tok, reads, writes):
        for item in reads:
            b, p = self._norm(item)
            st = b.st.setdefault(p, [None, {}])
            if st[1].get(tok[0], 0) < tok[1]:
                st[1][tok[0]] = tok[1]
        for item in writes:
            b, p = self._norm(item)
            if p is None:
                b.st = {None: [tok, {}]}
            else:
                b.st[p] = [tok, {}]

    def _waits(self, eng, deps):
        w = []
        kn = self.known[eng]
        for s, v in deps.items():
            if s == eng and eng == "pe":
                continue
            if kn.get(s, 0) < v:
                kn[s] = v
                w.append((s, v))
        return w

    def op(self, eng, fn, reads=(), writes=()):
        if self.capture is not None:
            self.capture.append(("op", eng, fn, tuple(reads), tuple(writes), None))
            return
        deps = self._deps(reads, writes)
        w = self._waits(eng, deps)
        self.count[eng] += 1
        tok = (eng, self.count[eng])
        self.lists[eng].append((w, fn, eng, 1))
        self._commit(tok, reads, writes)

    def dma(self, q, out_ap, in_ap, reads=(), writes=(), sbuf=None, **kw):
        if self.capture is not None:
            self.capture.append(("dma", q, (out_ap, in_ap, sbuf, kw), tuple(reads), tuple(writes), None))
            return
        ds = self.dsem_for(sbuf)
        deps = self._deps(reads, writes)
        deps[ds] = max(deps.get(ds, 0), self.count[ds])
        w = self._waits(q, deps)
        self.count[ds] += 16
        tok = (ds, self.count[ds])
        self.lists[q].append((w, lambda e: e.dma_start(out=out_ap, in_=in_ap, **kw), ds, 16))
        self._commit(tok, reads, writes)

    def barrier(self):
        for e in self.ENG:
            deps = {s: c for s, c in self.count.items() if c > 0 and s != e}
            w = self._waits(e, deps)
            if w:
                self.lists[e].append((w, None, None, 0))

    def emit(self):
        self.barrier()
        nc = self.nc
        sems = self.sems
        lists = self.lists

        def run(e, items):
            for w, fn, s, inc in items:
                for ws, wv in w:
                    e.wait_ge(sems[ws], wv)
                if fn is not None:
                    fn(e).then_inc(sems[s], inc)

        with nc.Block() as block:
            @block.tensor
            def _(e):
                run(e, lists["pe"])

            @block.vector
            def _(e):
                run(e, lists["dve"])

            @block.scalar
            def _(e):
                run(e, lists["act"])

            @block.gpsimd
            def _(e):
                run(e, lists["pool"])

            @block.sync
            def _(e):
                run(e, lists["sp"])


class Rot:
    def __init__(self, bufs):
        self.bufs = bufs
        self.i = 0

    def next(self):
        b = self.bufs[self.i % len(self.bufs)]
        self.i += 1
        return b


def att_tiles(rows):
    sigs = []
    tiles = []
    for j in range(rows // 2):
        kb = min(max(2 * j - 4, 0), rows - 9)
        i0 = 2 * j
        r00 = min(max(i0 - 4, 0), rows - 8)
        r01 = min(max(i0 + 1 - 4, 0), rows - 8)
        sig = (i0 - kb, r00 - kb, r01 - kb)
        if sig not in sigs:
            sigs.append(sig)
        tiles.append((kb, sigs.index(sig)))
    return tiles, sigs


def build_bias(rpb_l, sigs):
    nv = len(sigs)
    out = np.full((nv, 5 * 128, 8, 128), NEG, np.float32)
    qc = np.arange(64)
    cs = np.clip(qc - 8, 0, GW - 16)
    for vi, (di, d0, d1) in enumerate(sigs):
        for ri in range(2):
            irel = di + ri
            r0rel = d0 if ri == 0 else d1
            for kr in range(r0rel, r0rel + 8):
                ro = kr - irel + 7
                for q in range(64):
                    kc = np.arange(cs[q], cs[q] + 16)
                    co = kc - q + 15
                    out[vi, kr * 64 + kc, :, ri * 64 + q] = rpb_l[:, ro, co].T
    return out.reshape(nv, 5, 128, 8, 128).transpose(0, 2, 1, 3, 4).copy()


def build_nc(S, NL, NV):
    from contextlib import ExitStack
    nc = bass.Bass("TRN2", target_bir_lowering=False)
    rows = S // GW
    NT = S // 128
    tiles, sigs = att_tiles(rows)
    assert len(sigs) == NV

    def din(name, shape, dt=F32):
        return nc.dram_tensor(name, list(shape), dt, kind="ExternalInput").ap()

    def dscr(name, shape, dt):
        if DEBUG:
            return nc.dram_tensor(name, list(shape), dt, kind="ExternalOutput").ap()
        return nc.dram_tensor(name, list(shape), dt).ap()

    xT = din("xT", [D, S])
    cT = din("cT", [128, 8])
    ada_w = din("ada_w", [NL, D, 6 * D])
    ada_bT = din("ada_bT", [NL, 128, 48])
    n1g = din("n1g", [NL, 128, 8])
    n2g = din("n2g", [NL, 128, 8])
    w_in = din("w_in", [NL, D, IN_COLS])
    muT = din("muT", [NL, 128, 2, 15])
    w0T = din("w0T", [NL, 128, 2, 4])
    a0T = din("a0T", [NL, 128, 2, 4])
    w2Z = din("w2Z", [NL, 128, 2, RW])
    a2Z = din("a2Z", [NL, 128, 2, RW])
    g2 = din("g2", [NL, 128, RW])
    vecT = din("vecT", [NL, 128, 5, 4])
    qkg = din("qkg", [NL, 128, 2])
    biasT = din("biasT", [NL, NV, 128, 5, 8, 128])
    w_out = din("w_out", [NL, D, D])
    f_in = din("f_in", [NL, D, 2 * DFF])
    f_out = din("f_out", [NL, DFF, D])
    outT = nc.dram_tensor("outT", [D, S], F32, kind="ExternalOutput").ap()

    hT = dscr("hT", [D, S], F32)
    zT = dscr("zT", [RWKV_COLS, S], BF16)
    zsT = dscr("zsT", [RWKV_COLS, S], BF16)
    QT = dscr("QT", [RW, S], BF16)
    KT = dscr("KT", [RW, S], BF16)
    Vtm = dscr("Vtm", [S, 520], BF16)
    ymT = dscr("ymT", [D, S], BF16)
    yfw = dscr("yfw", [S, RW], F32)
    class DB:
        pass
    dram = {n: Buf(n, None) for n in ["hT", "zT", "zsT", "QT", "KT", "Vtm", "ymT", "yfw", "outT"]}

    with ExitStack() as gs:
        Sx = Sched(nc, gs)
        psb = [Sx.ps(gs) for _ in range(8)]
        PS = Rot(psb)
        PS6 = Rot(psb[2:])
        identf = Sx.sb(gs, [128, 128], F32, "identf")
        identb = Sx.sb(gs, [128, 128], BF16, "identb")
        onesb = Sx.sb(gs, [128, 128], BF16, "onesb")
        blkf = Sx.sb(gs, [128, 128], F32, "blkf")
        blkb = Sx.sb(gs, [128, 128], BF16, "blkb")
        mLs = Sx.sb(gs, [128, 128], F32, "mLs")
        mLi = Sx.sb(gs, [128, 128], F32, "mLi")
        mUs = Sx.sb(gs, [128, 128], F32, "mUs")
        mUi = Sx.sb(gs, [128, 128], F32, "mUi")
        modT = Sx.sb(gs, [128, NL, 48], F32, "modT")
        cact = Sx.sb(gs, [128, 8], F32, "cact")
        lay = Sx.sb(gs, [128, 64], F32, "lay")
        tmpc = Sx.sb(gs, [128, 16], F32, "tmpc")

        Sx.op("pool", lambda e: e.memset(identf[:], 0.0), writes=[identf])
        Sx.op("pool", lambda e: e.affine_select(out=identf[:], in_=identf[:], pattern=[[-1, 128]],
                                                compare_op=ALU.not_equal, fill=1.0, base=0, channel_multiplier=1),
              reads=[identf], writes=[identf])
        Sx.op("pool", lambda e: e.tensor_copy(identb[:], identf[:]), reads=[identf], writes=[identb])
        Sx.op("pool", lambda e: e.memset(onesb[:], 1.0), writes=[onesb])
        Sx.op("pool", lambda e: e.memset(blkf[:], 0.0), writes=[blkf])
        Sx.op("pool", lambda e: e.memset(blkf[0:64, 0:64], 1.0), reads=[blkf], writes=[blkf])
        Sx.op("pool", lambda e: e.memset(blkf[64:128, 64:128], 1.0), reads=[blkf], writes=[blkf])
        Sx.op("pool", lambda e: e.tensor_copy(blkb[:], blkf[:]), reads=[blkf], writes=[blkb])
        for mb, cmp, stp, cm in [(mLs, ALU.is_gt, -1, 1), (mLi, ALU.is_ge, -1, 1), (mUs, ALU.is_gt, 1, -1), (mUi, ALU.is_ge, 1, -1)]:
            Sx.op("pool", lambda e, mb=mb: e.memset(mb[:], 1.0), writes=[mb])
            Sx.op("pool", lambda e, mb=mb, cmp=cmp, stp=stp, cm=cm: e.affine_select(
                out=mb[:], in_=mb[:], pattern=[[stp, 128]], compare_op=cmp, fill=0.0, base=0, channel_multiplier=cm),
                reads=[mb], writes=[mb])

        Sx.dma("sp", cact[:], cT[:, :], writes=[cact], sbuf=cact)
        Sx.op("act", lambda e: e.activation(out=cact[:], in_=cact[:], func=AF.Silu), reads=[cact], writes=[cact])
        adb = Sx.sb(gs, [128, NL, 48], F32, "adb")
        Sx.dma("sp", adb[:], ada_bT.rearrange("l p c -> p l c"), writes=[adb], sbuf=adb)
        PS7 = Rot(psb[0:7])

        def mod_body(l, ls, pm, CW):
            awp = Rot([Sx.sb(ls, [128, 8, CW], F32, "aw") for _ in range(2)])
            for cb in range(6 * D // CW):
                aw = awp.next()
                Sx.dma("sp", aw[:], ada_w[l, :, cb * CW:(cb + 1) * CW].rearrange("(kc p) f -> p kc f", p=128),
                       writes=[aw], sbuf=aw)
                for f4 in range(CW // 128):
                    fc = cb * (CW // 128) + f4
                    for kc in range(8):
                        Sx.op("pe", lambda e, aw=aw, kc=kc, f4=f4, fc=fc: e.matmul(
                            pm[:, fc:fc + 1], aw[:, kc, f4 * 128:(f4 + 1) * 128], cact[:, kc:kc + 1],
                            start=(kc == 0), stop=(kc == 7)), reads=[aw, cact], writes=[pm])
            Sx.op("dve", lambda e: e.tensor_tensor(out=modT[:, l, :], in0=pm[:, 0:48], in1=adb[:, l, :],
                                                   op=ALU.add), reads=[pm, adb], writes=[(modT, l)])

        with ExitStack() as ls:
            mod_body(0, ls, psb[7], 512)
            Sx.barrier()

        def layer_cols(l):
            Sx.dma("sp", tmpc[:, 0:8], n1g[l], writes=[tmpc], sbuf=tmpc)
            Sx.dma("sp", tmpc[:, 8:16], n2g[l], writes=[tmpc], sbuf=tmpc)
            Sx.op("dve", lambda e: e.scalar_tensor_tensor(out=lay[:, 0:8], in0=modT[:, l, 8:16], scalar=1.0,
                                                          in1=tmpc[:, 0:8], op0=ALU.add, op1=ALU.mult),
                  reads=[modT, tmpc], writes=[lay])
            Sx.op("dve", lambda e: e.scalar_tensor_tensor(out=lay[:, 24:32], in0=modT[:, l, 32:40], scalar=1.0,
                                                          in1=tmpc[:, 8:16], op0=ALU.add, op1=ALU.mult),
                  reads=[modT, tmpc, lay], writes=[lay])
            for dst, src in [(8, 0), (16, 16), (32, 24), (40, 40)]:
                Sx.op("dve", lambda e, dst=dst, src=src: e.tensor_copy(lay[:, dst:dst + 8], modT[:, l, src:src + 8]),
                      reads=[modT, lay], writes=[lay])

        def rmsnorm(stk, hb, ub, n, gcol, shcol, sq, rstd, tmpr, ps=None):
            ps = ps or PS
            Sx.op("act", lambda e: e.activation(out=sq[:], in_=hb[:], func=AF.Square), reads=[hb], writes=[sq])
            pm = ps.next()
            for kc in range(8):
                Sx.op("pe", lambda e, kc=kc: e.matmul(pm[:, 0:n], onesb[:], sq[:, kc, :], start=(kc == 0), stop=(kc == 7)),
                      reads=[sq, onesb], writes=[pm])
            Sx.op("act", lambda e: e.activation(out=rstd[:], in_=pm[:, 0:n], func=AF.Ln, scale=1.0 / D, bias=epsc[:, 0:1]),
                  reads=[pm, epsc], writes=[rstd])
            Sx.op("act", lambda e: e.activation(out=rstd[:], in_=rstd[:], func=AF.Exp, scale=-0.5), reads=[rstd], writes=[rstd])
            for kc in range(8):
                tm = tmpr.next()
                Sx.op("dve", lambda e, kc=kc, tm=tm: e.scalar_tensor_tensor(
                    out=tm[:], in0=hb[:, kc, :], scalar=lay[:, gcol + kc:gcol + kc + 1], in1=rstd[:],
                    op0=ALU.mult, op1=ALU.mult), reads=[hb, lay, rstd], writes=[tm])
                Sx.op("act", lambda e, kc=kc, tm=tm: e.activation(out=ub[:, kc, :], in_=tm[:], func=AF.Identity,
                                                                   bias=lay[:, shcol + kc:shcol + kc + 1]),
                      reads=[tm, lay], writes=[(ub, kc)])

        epsc = Sx.sb(gs, [128, 4], F32, "epsc")
        Sx.op("pool", lambda e: e.memset(epsc[:, 0:1], 1e-6), writes=[epsc])
        Sx.op("pool", lambda e: e.memset(epsc[:, 1:2], 64e-5), reads=[epsc], writes=[epsc])
        Sx.op("pool", lambda e: e.memset(epsc[:, 2:3], 1e-19), reads=[epsc], writes=[epsc])

        evac_flip = [0]

        def evac(out_ap, in_ap, reads, writes, eng=None):
            evac_flip[0] ^= 1
            if eng == "dve" or (eng is None and evac_flip[0]):
                Sx.op("dve", lambda e: e.tensor_copy(out_ap, in_ap), reads=reads, writes=writes)
            else:
                Sx.op("act", lambda e: e.activation(out=out_ap, in_=in_ap, func=AF.Copy), reads=reads, writes=writes)

        def P1(l, ls):
            hsrc, hbuf = (xT, None) if l == 0 else (hT, dram["hT"])
            if True:
                win = Sx.sb(ls, [128, 8, IN_COLS], BF16, "win")
                hp = Rot([Sx.sb(ls, [128, 8, 512], F32, "h") for _ in range(2)])
                sq = Sx.sb(ls, [128, 8, 512], BF16, "sq")
                ub = Sx.sb(ls, [128, 8, 512], BF16, "u")
                rstd = Sx.sb(ls, [128, 512], F32, "rstd")
                tmpr = Rot([Sx.sb(ls, [128, 512], F32, "tm") for _ in range(2)])
                zo = Rot([Sx.sb(ls, [128, 512], BF16, "zo") for _ in range(4)])
                sqz = Rot([Sx.sb(ls, [128, 512], BF16, "sqz") for _ in range(2)])
                rsq = Rot([Sx.sb(ls, [128, 512], F32, "rsq") for _ in range(2)])
                gq = Sx.sb(ls, [128, 2], F32, "gq")
                vop = Rot([Sx.sb(ls, [128, 8, 65], BF16, "vo") for _ in range(2)])
                for vo in vop.bufs:
                    Sx.op("pool", lambda e, vo=vo: e.memset(vo[:], 1.0), writes=[vo])
                Sx.dma("sp", gq[:], qkg[l], writes=[gq], sbuf=gq)
                for kc in range(8):
                    Sx.dma("pool", win[:, kc, :], w_in[l, kc * 128:(kc + 1) * 128, :], writes=[(win, kc)], sbuf=win,
                           max_dma_last_dim=4096)
                nb = S // 512

                def load(b):
                    hb = hp.next()
                    Sx.dma("sp", hb[:], hsrc[:, b * 512:(b + 1) * 512].rearrange("(kc p) t -> p kc t", p=128),
                           reads=[hbuf] if hbuf else [], writes=[hb], sbuf=hb)
                    return hb
                nxt = load(0)
                for b in range(nb):
                    hb = nxt
                    if b + 1 < nb:
                        nxt = load(b + 1)
                    if l == 0:
                        Sx.dma("pool", hT[:, b * 512:(b + 1) * 512].rearrange("(kc p) t -> p kc t", p=128), hb[:],
                               reads=[hb], writes=[(dram["hT"], b)], sbuf=hb)
                    rmsnorm(ls, hb, ub, 512, 0, 8, sq, rstd, tmpr, ps=PS7)
                    tsl = slice(b * 512, (b + 1) * 512)
                    for fc in range(23):
                        pm = PS7.next()
                        for kc in range(8):
                            Sx.op("pe", lambda e, kc=kc, fc=fc, pm=pm: e.matmul(
                                pm[:, :], win[:, kc, fc * 128:(fc + 1) * 128], ub[:, kc, :], start=(kc == 0), stop=(kc == 7)),
                                reads=[win, ub], writes=[pm])
                        z = zo.next()
                        if fc < 15:
                            evac(z[:], pm[:, :], [pm], [z])
                            Sx.dma("pool", zT[fc * 128:(fc + 1) * 128, tsl], z[:], reads=[z], writes=[(dram["zT"], (fc, b))], sbuf=z)
                        else:
                            isq = fc < 19
                            s2 = sqz.next()
                            r2 = rsq.next()
                            Sx.op("act", lambda e, s2=s2, pm=pm: e.activation(out=s2[:], in_=pm[:, :], func=AF.Square),
                                  reads=[pm], writes=[s2])
                            p2 = PS7.next()
                            Sx.op("pe", lambda e, p2=p2, s2=s2: e.matmul(p2[:, :], blkb[:], s2[:], start=True, stop=True),
                                  reads=[s2, blkb], writes=[p2])
                            Sx.op("act", lambda e, p2=p2, r2=r2: e.activation(out=r2[:], in_=p2[:, :], func=AF.Ln,
                                                                             scale=1.0 / 64, bias=epsc[:, 0:1]),
                                  reads=[p2, epsc], writes=[r2])
                            Sx.op("act", lambda e, r2=r2: e.activation(out=r2[:], in_=r2[:], func=AF.Exp, scale=-0.5), reads=[r2], writes=[r2])
                            gi = 0 if isq else 1
                            Sx.op("dve", lambda e, z=z, pm=pm, r2=r2, gi=gi: e.scalar_tensor_tensor(
                                out=z[:], in0=pm[:, :], scalar=gq[:, gi:gi + 1], in1=r2[:], op0=ALU.mult, op1=ALU.mult),
                                reads=[pm, gq, r2], writes=[z])
                            if isq:
                                Sx.dma("pool", QT[(fc - 15) * 128:(fc - 14) * 128, tsl], z[:], reads=[z],
                                       writes=[(dram["QT"], (fc, b))], sbuf=z)
                            else:
                                Sx.dma("pool", KT[(fc - 19) * 128:(fc - 18) * 128, tsl], z[:], reads=[z],
                                       writes=[(dram["KT"], (fc, b))], sbuf=z)
                    for sub in range(4):
                        pm = PS7.next()
                        for kc in range(8):
                            Sx.op("pe", lambda e, kc=kc, sub=sub, pm=pm: e.matmul(
                                pm[:, :], ub[:, kc, sub * 128:(sub + 1) * 128], win[:, kc, 2944:3456],
                                start=(kc == 0), stop=(kc == 7)), reads=[win, ub], writes=[pm])
                        z = vop.next()
                        evac(z[:, :, 0:64], pm[:, :].rearrange("p (h d) -> p h d", d=64), [pm], [z])
                        Sx.dma("pool", Vtm[b * 512 + sub * 128: b * 512 + (sub + 1) * 128, :], z[:].rearrange("p h d -> p (h d)"), reads=[z],
                               writes=[(dram["Vtm"], (b, sub))], sbuf=z)

        def P2(l):
            with ExitStack() as ls:
                ktp = Rot([Sx.sb(ls, [128, 4, 576], BF16, "kt") for _ in range(2)])
                qzp = Rot([Sx.sb(ls, [128, 8, 128], BF16, "qz") for _ in range(2)])
                vwp = Rot([Sx.sb(ls, [128, 5, 8, 65], BF16, "vw") for _ in range(2)])
                bias = Sx.sb(ls, [128, 5, 8, 128], F32, "bias")
                sTp = Rot([Sx.sb(ls, [128, 5, 128], F32, "sT") for _ in range(3)])
                pTp = Rot([Sx.sb(ls, [128, 5, 128], BF16, "pT") for _ in range(3)])
                yap = Rot([Sx.sb(ls, [128, 512], BF16, "ya") for _ in range(2)])
                yTp = Rot([Sx.sb(ls, [128, 4, 128], BF16, "yT") for _ in range(2)])
                rs = Sx.sb(ls, [128, 8], F32, "rs")
                for qz in qzp.bufs:
                    Sx.op("pool", lambda e, qz=qz: e.memset(qz[:], 0.0), writes=[qz])
                QTv = QT.rearrange("(pr par d) t -> par d pr t", par=2, d=64)

                def load(j):
                    kb, var = tiles[j]
                    kt, qz, vw = ktp.next(), qzp.next(), vwp.next()
                    Sx.dma("sp", kt[:], KT[:, kb * 64:kb * 64 + 576].rearrange("(pr p) t -> p pr t", p=128),
                           reads=[dram["KT"]], writes=[kt], sbuf=kt)
                    qzv = qz[:].rearrange("p (pr par) q -> p par pr q", par=2)
                    for par in range(2):
                        Sx.dma("sp", qzv[par * 64:(par + 1) * 64, par, :, :], QTv[par, :, :, j * 128:(j + 1) * 128],
                               reads=[dram["QT"]], writes=[qz], sbuf=qz)
                    Sx.dma("sp", vw[:, 0:4, :, :].rearrange("p c h d -> p c (h d)"),
                           Vtm[kb * 64:kb * 64 + 512, :].rearrange("(c p) f -> p c f", p=128),
                           reads=[dram["Vtm"]], writes=[vw], sbuf=vw)
                    Sx.dma("sp", vw[0:64, 4, :, :].rearrange("p h d -> p (h d)"),
                           Vtm[kb * 64 + 512:kb * 64 + 576, :],
                           reads=[dram["Vtm"]], writes=[vw], sbuf=vw)
                    return kt, qz, vw
                cur_var = -1
                nxt = load(0)
                for j in range(NT):
                    kt, qz, vw = nxt
                    if j + 1 < NT:
                        nxt = load(j + 1)
                    kb, var = tiles[j]
                    if var != cur_var:
                        cur_var = var
                        Sx.dma("sp", bias[:], biasT[l, var], writes=[bias], sbuf=bias)
                    poA, poB = psb[0], psb[1]

                    def stage1(h):
                        pr = h // 2
                        pA, pB = PS6.next(), PS6.next()
                        for c in range(4):
                            Sx.op("pe", lambda e, c=c, pA=pA, kt=kt, qz=qz, pr=pr, h=h: e.matmul(
                                pA[:, c * 128:(c + 1) * 128], kt[:, pr, c * 128:(c + 1) * 128], qz[:, h, :],
                                start=True, stop=True), reads=[kt, qz], writes=[pA])
                        Sx.op("pe", lambda e, pB=pB, kt=kt, qz=qz, pr=pr, h=h: e.matmul(
                            pB[0:64, 0:128], kt[:, pr, 512:576], qz[:, h, :], start=True, stop=True),
                            reads=[kt, qz], writes=[pB])
                        sT, pT = sTp.next(), pTp.next()
                        Sx.op("dve", lambda e, sT=sT, pA=pA, h=h: e.scalar_tensor_tensor(
                            out=sT[:, 0:4, :], in0=pA[:, :].rearrange("p (c q) -> p c q", c=4), scalar=0.125,
                            in1=bias[:, 0:4, h, :], op0=ALU.mult, op1=ALU.add), reads=[pA, bias], writes=[(sT, 0)])
                        Sx.op("dve", lambda e, sT=sT, pB=pB, h=h: e.scalar_tensor_tensor(
                            out=sT[0:64, 4, :], in0=pB[0:64, 0:128], scalar=0.125,
                            in1=bias[0:64, 4, h, :], op0=ALU.mult, op1=ALU.add), reads=[pB, bias], writes=[(sT, 1)])
                        Sx.op("act", lambda e, sT=sT, pT=pT: e.activation(out=pT[:, 0:4, :], in_=sT[:, 0:4, :], func=AF.Exp),
                              reads=[(sT, 0)], writes=[(pT, 0)])
                        Sx.op("act", lambda e, sT=sT, pT=pT: e.activation(out=pT[0:64, 4, :], in_=sT[0:64, 4, :], func=AF.Exp),
                              reads=[(sT, 1)], writes=[(pT, 1)])
                        return pT

                    def stage2(h, pT):
                        po = poA if h < 4 else poB
                        hh = h % 4
                        for c in range(4):
                            Sx.op("pe", lambda e, c=c, po=po, pT=pT, vw=vw, h=h, hh=hh: e.matmul(
                                po[:, hh * 65:(hh + 1) * 65], pT[:, c, :], vw[:, c, h, :], start=(c == 0), stop=False),
                                reads=[(pT, 0), vw], writes=[po])
                        Sx.op("pe", lambda e, po=po, pT=pT, vw=vw, h=h, hh=hh: e.matmul(
                            po[:, hh * 65:(hh + 1) * 65], pT[0:64, 4, :], vw[0:64, 4, h, :], start=False, stop=True),
                            reads=[(pT, 1), vw], writes=[po])
                    pTs = {}
                    pTs[0] = stage1(0)
                    for h in range(8):
                        if h + 1 < 8:
                            pTs[h + 1] = stage1(h + 1)
                        stage2(h, pTs.pop(h))
                    ya = yap.next()
                    for hf, po in enumerate([poA, poB]):
                        pv = po[:, 0:260].rearrange("p (h d) -> p h d", d=65)
                        Sx.op("dve", lambda e, pv=pv, hf=hf: e.reciprocal(rs[:, hf * 4:(hf + 1) * 4].unsqueeze(2), pv[:, :, 64:65]),
                              reads=[po], writes=[rs])
                        Sx.op("dve", lambda e, pv=pv, hf=hf, ya=ya: e.tensor_tensor(
                            out=ya[:, hf * 256:(hf + 1) * 256].rearrange("p (h d) -> p h d", d=64), in0=pv[:, :, 0:64],
                            in1=rs[:, hf * 4:(hf + 1) * 4].unsqueeze(2).to_broadcast([128, 4, 64]), op=ALU.mult),
                            reads=[po, rs], writes=[ya])
                    pt = PS6.next()
                    ptb = pt[:].bitcast(BF16)
                    for pr in range(4):
                        Sx.op("pe", lambda e, pr=pr, ptb=ptb, ya=ya: e.transpose(ptb[:, pr * 128:(pr + 1) * 128],
                                                                                ya[:, pr * 128:(pr + 1) * 128], identb[:]),
                              reads=[ya, identb], writes=[pt])
                    yT = yTp.next()
                    evac(yT[:].rearrange("p a q -> p (a q)"), ptb[:, 0:512], [pt], [yT])
                    Sx.dma("pool", ymT[512:1024, j * 128:(j + 1) * 128].rearrange("(pr p) q -> p pr q", p=128), yT[:],
                           reads=[yT], writes=[(dram["ymT"], ("a", j))], sbuf=yT)
                Sx.barrier()

        def P4(l, last):
            NB = 256
            hdst, hdb = (outT, dram["outT"]) if last else (hT, dram["hT"])
            with ExitStack() as ls:
                wo = Sx.sb(ls, [128, 8, D], BF16, "wo")
                wf1 = Sx.sb(ls, [128, 8, 2 * DFF], BF16, "wf1")
                wf2 = Sx.sb(ls, [128, 22, D], BF16, "wf2")
                hp = Rot([Sx.sb(ls, [128, 8, NB], F32, "h") for _ in range(2)])
                ymp = Rot([Sx.sb(ls, [128, 8, NB], BF16, "ym") for _ in range(2)])
                sq = Sx.sb(ls, [128, 8, NB], BF16, "sq")
                ub = Sx.sb(ls, [128, 8, NB], BF16, "u")
                hid = Sx.sb(ls, [128, 22, NB], BF16, "hid")
                rstd = Sx.sb(ls, [128, NB], F32, "rstd")
                tmpr = Rot([Sx.sb(ls, [128, NB], F32, "tm") for _ in range(2)])
                silp = Rot([Sx.sb(ls, [128, NB], F32, "sil") for _ in range(2)])
                for kc in range(8):
                    Sx.dma("pool", wo[:, kc, :], w_out[l, kc * 128:(kc + 1) * 128, :], writes=[(wo, kc)], sbuf=wo)
                for kc in range(8):
                    Sx.dma("pool", wf1[:, kc, :], f_in[l, kc * 128:(kc + 1) * 128, :], writes=[(wf1, kc)], sbuf=wf1,
                           max_dma_last_dim=4096)
                for j in range(22):
                    Sx.dma("pool", wf2[:, j, :], f_out[l, j * 128:(j + 1) * 128, :], writes=[(wf2, j)], sbuf=wf2)
                nb = S // NB

                def load(b):
                    hb, ym = hp.next(), ymp.next()
                    Sx.dma("sp", hb[:], hT[:, b * NB:(b + 1) * NB].rearrange("(kc p) t -> p kc t", p=128),
                           reads=[(dram["hT"], b)], writes=[hb], sbuf=hb)
                    Sx.dma("sp", ym[:], ymT[:, b * NB:(b + 1) * NB].rearrange("(kc p) t -> p kc t", p=128),
                           reads=[dram["ymT"]], writes=[ym], sbuf=ym)
                    return hb, ym
                nxt = load(0)
                for b in range(nb):
                    hb, ym = nxt
                    if b + 1 < nb:
                        nxt = load(b + 1)
                    for fc in range(8):
                        pm = PS.next()
                        for kc in range(8):
                            Sx.op("pe", lambda e, kc=kc, fc=fc, pm=pm, ym=ym: e.matmul(
                                pm[:, 0:NB], wo[:, kc, fc * 128:(fc + 1) * 128], ym[:, kc, :], start=(kc == 0), stop=(kc == 7)),
                                reads=[wo, ym], writes=[pm])
                        Sx.op("dve", lambda e, fc=fc, pm=pm, hb=hb: e.scalar_tensor_tensor(
                            out=hb[:, fc, :], in0=pm[:, 0:NB], scalar=lay[:, 16 + fc:17 + fc], in1=hb[:, fc, :],
                            op0=ALU.mult, op1=ALU.add), reads=[pm, lay, hb], writes=[hb])
                    rmsnorm(ls, hb, ub, NB, 24, 32, sq, rstd, tmpr)
                    for j in range(22):
                        pg, pu = PS.next(), PS.next()
                        for kc in range(8):
                            Sx.op("pe", lambda e, kc=kc, j=j, pg=pg: e.matmul(
                                pg[:, 0:NB], wf1[:, kc, j * 128:(j + 1) * 128], ub[:, kc, :], start=(kc == 0), stop=(kc == 7)),
                                reads=[wf1, ub], writes=[pg])
                        for kc in range(8):
                            Sx.op("pe", lambda e, kc=kc, j=j, pu=pu: e.matmul(
                                pu[:, 0:NB], wf1[:, kc, DFF + j * 128:DFF + (j + 1) * 128], ub[:, kc, :],
                                start=(kc == 0), stop=(kc == 7)), reads=[wf1, ub], writes=[pu])
                        sl = silp.next()
                        Sx.op("act", lambda e, sl=sl, pg=pg: e.activation(out=sl[:], in_=pg[:, 0:NB], func=AF.Silu),
                              reads=[pg], writes=[sl])
                        Sx.op("dve", lambda e, sl=sl, pu=pu, j=j: e.tensor_tensor(out=hid[:, j, :], in0=sl[:], in1=pu[:, 0:NB],
                                                                              op=ALU.mult), reads=[sl, pu], writes=[(hid, j)])
                    for fc in range(8):
                        pm = PS.next()
                        for j in range(22):
                            Sx.op("pe", lambda e, j=j, fc=fc, pm=pm: e.matmul(
                                pm[:, 0:NB], wf2[:, j, fc * 128:(fc + 1) * 128], hid[:, j, :], start=(j == 0), stop=(j == 21)),
                                reads=[wf2, hid], writes=[pm])
                        Sx.op("dve", lambda e, fc=fc, pm=pm, hb=hb: e.scalar_tensor_tensor(
                            out=hb[:, fc, :], in0=pm[:, 0:NB], scalar=lay[:, 40 + fc:41 + fc], in1=hb[:, fc, :],
                            op0=ALU.mult, op1=ALU.add), reads=[pm, lay, hb], writes=[hb])
                    Sx.dma("pool", hdst[:, b * NB:(b + 1) * NB].rearrange("(kc p) t -> p kc t", p=128), hb[:],
                           reads=[hb], writes=[(hdb, b)], sbuf=hb)
                Sx.barrier()

        P3 = make_P3(locals())

        for l in range(NL):
            layer_cols(l)
            if 1 in PHASES:
                with ExitStack() as ls1:
                    la, lb = [], []
                    Sx.capture = la
                    P1(l, ls1)
                    if l + 1 < NL:
                        Sx.capture = lb
                        mod_body(l + 1, ls1, psb[7], 256)
                    Sx.capture = None
                    Sx.merge_lists([la, lb])
                    Sx.barrier()
            if 2 in PHASES:
                P2(l)
            if 3 in PHASES:
                P3(l)
            if 4 in PHASES:
                P4(l, l == NL - 1)
        Sx.emit()
    return nc


def make_P3(env):
    Sx = env["Sx"]; PS = env["PS"]; nc = env["nc"]; S = env["S"]; NT = env["NT"]; dram = env["dram"]
    zT = env["zT"]; zsT = env["zsT"]; Vtm = env["Vtm"]; ymT = env["ymT"]; yfw = env["yfw"]
    muT = env["muT"]; w0T = env["w0T"]; a0T = env["a0T"]; w2Z = env["w2Z"]; a2Z = env["a2Z"]; g2 = env["g2"]
    vecT = env["vecT"]
    identf = env["identf"]; identb = env["identb"]; blkf = env["blkf"]; blkb = env["blkb"]
    mLs = env["mLs"]; mLi = env["mLi"]; mUs = env["mUs"]; mUi = env["mUi"]; epsc = env["epsc"]
    evac = env["evac"]; psb = env["psb"]
    from contextlib import ExitStack
    MID = 63
    C1 = float(np.exp(-0.5))

    def P3pre(l):
        with ExitStack() as ls:
            PW = min(2048, S)
            mu = Sx.sb(ls, [128, 3, 15], F32, "mu")
            Sx.dma("sp", mu[:, 0:2, :], muT[l], writes=[mu], sbuf=mu)
            Sx.op("dve", lambda e: e.tensor_tensor(out=mu[:, 2, :], in0=mu[:, 0, :], in1=mu[:, 1, :], op=ALU.add),
                  reads=[mu], writes=[mu])
            Sx.op("dve", lambda e: e.tensor_scalar(out=mu[:, 2, :], in0=mu[:, 2, :], scalar1=-1.0, scalar2=1.0,
                                                   op0=ALU.mult, op1=ALU.add), reads=[mu], writes=[mu])
            zrp = Rot([Sx.sb(ls, [128, S + 2], BF16, "zr") for _ in range(2)])
            tmp = Rot([Sx.sb(ls, [128, PW], F32, "ztmp") for _ in range(2)])
            zop = Rot([Sx.sb(ls, [128, PW], BF16, "zso") for _ in range(3)])
            for zr in zrp.bufs:
                Sx.op("pool", lambda e, zr=zr: e.memset(zr[:, 0:1], 0.0), writes=[zr])
                Sx.op("pool", lambda e, zr=zr: e.memset(zr[:, S + 1:S + 2], 0.0), reads=[zr], writes=[zr])

            def loadrow(j):
                zr = zrp.next()
                Sx.dma("sp", zr[:, 1:S + 1], zT[j * 128:(j + 1) * 128, :], reads=[dram["zT"]], writes=[zr], sbuf=zr)
                return zr
            nxt = loadrow(0)
            for j in range(15):
                zr = nxt
                if j + 1 < 15:
                    nxt = loadrow(j + 1)
                for pc in range(S // PW):
                    a = pc * PW
                    tm, zo = tmp.next(), zop.next()
                    Sx.op("act", lambda e, tm=tm, zr=zr, a=a, j=j: e.activation(out=tm[:], in_=zr[:, 1 + a:1 + a + PW], func=AF.Identity,
                                                                              scale=mu[:, 2, j:j + 1]), reads=[zr, mu], writes=[tm])
                    Sx.op("dve", lambda e, tm=tm, zr=zr, a=a, j=j: e.scalar_tensor_tensor(out=tm[:], in0=zr[:, a:a + PW], scalar=mu[:, 0, j:j + 1],
                                                                                        in1=tm[:], op0=ALU.mult, op1=ALU.add),
                          reads=[zr, mu, tm], writes=[tm])
                    Sx.op("dve", lambda e, tm=tm, zr=zr, a=a, j=j, zo=zo: e.scalar_tensor_tensor(out=zo[:], in0=zr[:, a + 2:a + 2 + PW],
                                                                                               scalar=mu[:, 1, j:j + 1], in1=tm[:],
                                                                                               op0=ALU.mult, op1=ALU.add),
                          reads=[zr, mu, tm], writes=[zo])
                    Sx.dma("pool", zsT[j * 128:(j + 1) * 128, a:a + PW], zo[:], reads=[zo], writes=[(dram["zsT"], (j, pc))], sbuf=zo)
            Sx.barrier()

    def P3(l):
        P3pre(l)
        with ExitStack() as ls:
            sb = lambda shape, dt, name: Sx.sb(ls, shape, dt, name)
            w0c = sb([128, 2, 4], F32, "w0c"); a0c = sb([128, 2, 4], F32, "a0c")
            w2s = sb([128, 2, RW], BF16, "w2s"); a2s = sb([128, 2, RW], BF16, "a2s"); g2s = sb([128, RW], BF16, "g2s")
            vec = sb([128, 5, 4], F32, "vec")
            oneka = sb([128, 4], F32, "oneka")
            ones128 = sb([128, 128], F32, "ones128")
            Sx.dma("sp", w0c[:], w0T[l], writes=[w0c], sbuf=w0c)
            Sx.dma("sp", a0c[:], a0T[l], writes=[a0c], sbuf=a0c)
            Sx.dma("sp", vec[:], vecT[l], writes=[vec], sbuf=vec)
            Sx.dma("pool", w2s[:], w2Z[l], writes=[w2s], sbuf=w2s)
            Sx.dma("pool", a2s[:], a2Z[l], writes=[a2s], sbuf=a2s)
            Sx.dma("pool", g2s[:], g2[l], writes=[g2s], sbuf=g2s)
            Sx.op("pool", lambda e: e.memset(ones128[:], 1.0), writes=[ones128])
            Sx.op("dve", lambda e: e.tensor_scalar(out=oneka[:], in0=vec[:, 1, :], scalar1=-1.0, scalar2=1.0,
                                                   op0=ALU.mult, op1=ALU.add), reads=[vec], writes=[oneka])
            kark = sb([128, 8], F32, "kark")
            Sx.op("dve", lambda e: e.tensor_tensor(out=kark[:, 0:4], in0=vec[:, 1, :], in1=vec[:, 2, :], op=ALU.mult),
                  reads=[vec], writes=[kark])
            Sx.op("dve", lambda e: e.scalar_tensor_tensor(out=kark[:, 4:8], in0=oneka[:], scalar=2.0, in1=vec[:, 2, :],
                                                          op0=ALU.mult, op1=ALU.mult), reads=[vec, oneka, kark], writes=[kark])
            tot = sb([128, 4], F32, "tot")
            bm32 = sb([128, 128], F32, "bm32")
            mDL = sb([128, 128], F32, "mDL"); mNL = sb([128, 128], F32, "mNL")
            mDU = sb([128, 128], F32, "mDU"); mNU = sb([128, 128], F32, "mNU")
            Sx.op("pool", lambda e: e.memset(bm32[:], 0.0), writes=[bm32])
            for q4 in range(4):
                Sx.op("pool", lambda e, q4=q4: e.memset(bm32[q4 * 32:(q4 + 1) * 32, q4 * 32:(q4 + 1) * 32], 1.0), reads=[bm32], writes=[bm32])
            for (mm_, mD_, mN_) in [(mLs, mDL, mNL), (mUs, mDU, mNU)]:
                Sx.op("pool", lambda e, mm_=mm_, mD_=mD_: e.tensor_tensor(out=mD_[:], in0=mm_[:], in1=bm32[:], op=ALU.mult),
                      reads=[mm_, bm32], writes=[mD_])
                Sx.op("pool", lambda e, mm_=mm_, mD_=mD_, mN_=mN_: e.tensor_tensor(out=mN_[:], in0=mm_[:], in1=mD_[:], op=ALU.subtract),
                      reads=[mm_, mD_], writes=[mN_])
            zcp = Rot([sb([128, 15, 128], BF16, "zc") for _ in range(2)])
            actb = sb([128, 3, 128], BF16, "actb")
            sg = sb([128, 4, 128], F32, "sg")
            aa = sb([128, 4, 128], F32, "aa")
            aa2 = sb([128, 4, 128], F32, "aa2")
            kk = sb([128, 4, 128], F32, "kk")
            t1 = sb([128, 4, 128], F32, "t1")
            t2 = sb([128, 4, 128], F32, "t2")
            kd = sb([128, 4, 128], F32, "kd")
            bb = sb([128, 4, 128], F32, "bb")
            Lc = sb([128, 4, 128], F32, "Lc")
            Lm = sb([128, 4, 128], F32, "Lm")
            eR = sb([128, 4, 128], F32, "eR"); eA = sb([128, 4, 128], F32, "eA")
            eB = sb([128, 4, 128], F32, "eB"); eE = sb([128, 4, 128], F32, "eE")
            ARp = Rot([sb([128, 4, 2, 128], BF16, "AR") for _ in range(4)])
            BTup = Rot([sb([128, 4, 128], BF16, "BTu") for _ in range(2)])
            KTup = Rot([sb([128, 4, 128], BF16, "KTu") for _ in range(2)])
            AZ = sb([128, 4, 2, 128], BF16, "AZ"); BZ = sb([128, 4, 2, 128], BF16, "BZ"); KZ = sb([128, 4, 2, 128], BF16, "KZ")
            bpf = sb([128, 4, 128], BF16, "bpf"); kpf = sb([128, 4, 128], BF16, "kpf"); vTf = sb([128, 4, 128], BF16, "vTf")
            Bpp = Rot([sb([128, 4, 128], BF16, "Bp") for _ in range(3)])
            Kpp = Rot([sb([128, 4, 128], BF16, "Kp") for _ in range(3)])
            Vp = Rot([sb([128, 512], BF16, "V") for _ in range(3)])
            PCf = Rot([sb([128, 4, 128], F32, "PCf") for _ in range(4)])
            PMf = Rot([sb([128, 4, 128], F32, "PMf") for _ in range(4)])
            pcol = sb([128, 8], F32, "pcol")
            Dk = [sb([128, 8, 128], BF16, "Dk%d" % i) for i in range(2)]
            Dtk = [sb([128, 8, 128], BF16, "Dtk%d" % i) for i in range(2)]
            Sk = [sb([128, 8, 128], BF16, "Sk%d" % i) for i in range(2)]
            Stk = [sb([128, 8, 128], BF16, "Stk%d" % i) for i in range(2)]
            Nn = sb([128, 8, 128], BF16, "Nn"); Eb = sb([128, 8, 128], BF16, "Eb"); Etb = sb([128, 8, 128], BF16, "Etb")
            Gp = sb([128, 8, 128], BF16, "Gp"); Fp = sb([128, 8, 128], BF16, "Fp")
            Tp = Rot([sb([128, 8, 128], BF16, "T") for _ in range(2)])
            Akp = Rot([sb([128, 8, 128], BF16, "Ak") for _ in range(3)])
            Arbp = Rot([sb([128, 8, 128], BF16, "Arb") for _ in range(3)])
            Arkp = Rot([sb([128, 8, 128], BF16, "Ark") for _ in range(3)])
            H = sb([128, 4, 128], F32, "H")
            Hs = Rot([sb([128, 4, 128], BF16, "Hs") for _ in range(2)])
            Xb = sb([128, 512], BF16, "Xb"); Ub = sb([128, 512], BF16, "Ub")
            Yp = Rot([sb([128, 512], F32, "Y") for _ in range(2)])
            gTp = Rot([sb([128, 4, 128], F32, "gT") for _ in range(4)])
            bonp = Rot([sb([128, 4, 128], F32, "bon") for _ in range(4)])
            PSA, PSB, PSS = Rot(psb[0:2]), Rot(psb[2:6]), Rot(psb[6:8])
            yn = sb([128, 8, 64], F32, "yn"); ynb = sb([128, 512], BF16, "ynb")
            st8 = sb([128, 32], F32, "st8")
            oT = Rot([sb([128, 4, 128], BF16, "oT") for _ in range(2)])
            if DEBUG:
                print("P3 sbuf bytes remaining", nc.sbuf_bytes_remaining)
            for zb in (AZ, BZ, KZ):
                Sx.op("pool", lambda e, zb=zb: e.memset(zb[:], 0.0), writes=[zb])

            def pair_ops(eng, fn_name, out_b, out_ap, in_b, in_ap, col_b, col_ap, op):
                pass

            def loadz(c):
                zc = zcp.next()
                Sx.dma("sp", zc[:], zsT[:, c * 128:(c + 1) * 128].rearrange("(c p) t -> p c t", p=128),
                       reads=[dram["zsT"]], writes=[zc], sbuf=zc)
                return zc

            def prepA(c, d, zc, out):
                post = (d == 1)
                gT, bon = (gTp.next(), bonp.next()) if post else (None, None)
                zs = zc
                r_ = lambda: zs[:, 0:4, :]
                k_ = lambda: zs[:, 4:8, :]
                v_ = lambda: zs[:, 8:12, :]
                Sx.op("act", lambda e: e.activation(out=actb[:, 0, :], in_=zs[:, 12, :], func=AF.Tanh), reads=[zs], writes=[(actb, 0)])
                Sx.op("act", lambda e: e.activation(out=actb[:, 1, :], in_=zs[:, 13, :], func=AF.Copy), reads=[zs], writes=[(actb, 1)])
                if post:
                    Sx.op("act", lambda e: e.activation(out=actb[:, 2, :], in_=zs[:, 14, :], func=AF.Sigmoid), reads=[zs], writes=[(actb, 2)])

                def lora(wz, dd, idx, outb, bias_b, scale_out=None):
                    pm = PSA.next()
                    for pr in range(4):
                        Sx.op("pe", lambda e, pr=pr, pm=pm: e.matmul(pm[:, pr * 128:(pr + 1) * 128], wz[:, dd, pr * 128:(pr + 1) * 128],
                                                                    actb[:, idx, :], start=True, stop=True),
                              reads=[wz, (actb, idx)], writes=[pm])
                    for pr in range(4):
                        Sx.op("act", lambda e, pr=pr, pm=pm: e.activation(out=outb[:, pr, :], in_=pm[:, pr * 128:(pr + 1) * 128],
                                                                         func=AF.Sigmoid, bias=bias_b[:, dd, pr:pr + 1]),
                              reads=[pm, bias_b], writes=[outb])
                yield
                lora(w2s, d, 0, sg, w0c)
                yield
                lora(a2s, d, 1, aa, a0c)
                yield
                if post:
                    lora(a2s, 0, 1, aa2, a0c)
                    pm = PSA.next()
                    for pr in range(4):
                        Sx.op("pe", lambda e, pr=pr, pm=pm: e.matmul(pm[:, pr * 128:(pr + 1) * 128], g2s[:, pr * 128:(pr + 1) * 128],
                                                                    actb[:, 2, :], start=True, stop=True),
                              reads=[g2s, (actb, 2)], writes=[pm])
                    evac(gT[:].rearrange("p a t -> p (a t)"), pm[:, :], [pm], [gT])
                yield
                for pr in range(4):
                    Sx.op("dve", lambda e, pr=pr: e.tensor_scalar(out=kk[:, pr, :], in0=zs[:, 4 + pr, :], scalar1=vec[:, 0, pr:pr + 1],
                                                                scalar2=None, op0=ALU.mult), reads=[zs, vec], writes=[kk])
                Sx.op("pool", lambda e: e.tensor_tensor(out=t1[:], in0=kk[:], in1=kk[:], op=ALU.mult), reads=[kk], writes=[t1])
                pm = PSA.next()
                for pr in range(4):
                    Sx.op("pe", lambda e, pr=pr, pm=pm: e.matmul(pm[:, pr * 128:(pr + 1) * 128], blkf[:], t1[:, pr, :], start=True, stop=True),
                          reads=[blkf, t1], writes=[pm])
                Sx.op("act", lambda e, pm=pm: e.activation(out=t2[:].rearrange("p a t -> p (a t)"), in_=pm[:, :], func=AF.Ln,
                                                          bias=epsc[:, 2:3]), reads=[pm, epsc], writes=[t2])
                Sx.op("act", lambda e: e.activation(out=t2[:], in_=t2[:], func=AF.Exp, scale=-0.5), reads=[t2], writes=[t2])
                Sx.op("dve", lambda e: e.tensor_tensor(out=kk[:], in0=kk[:], in1=t2[:], op=ALU.mult), reads=[kk, t2], writes=[kk])
                yield
                for pr in range(4):
                    Sx.op("dve", lambda e, pr=pr: e.tensor_scalar(out=t1[:, pr, :], in0=aa[:, pr, :], scalar1=vec[:, 1, pr:pr + 1],
                                                                scalar2=oneka[:, pr:pr + 1], op0=ALU.mult, op1=ALU.add),
                          reads=[aa, vec, oneka], writes=[t1])
                Sx.op("pool", lambda e: e.tensor_tensor(out=kd[:], in0=zs[:, 4:8, :], in1=t1[:], op=ALU.mult), reads=[zs, t1], writes=[kd])
                Sx.op("pool", lambda e: e.tensor_tensor(out=bb[:], in0=kk[:], in1=aa[:], op=ALU.mult), reads=[kk, aa], writes=[bb])
                yield
                if post:
                    Sx.op("dve", lambda e: e.tensor_tensor(out=t2[:], in0=aa[:], in1=aa2[:], op=ALU.add), reads=[aa, aa2], writes=[t2])
                    for pr in range(4):
                        Sx.op("dve", lambda e, pr=pr: e.tensor_scalar(out=t2[:, pr, :], in0=t2[:, pr, :], scalar1=kark[:, pr:pr + 1],
                                                                    scalar2=kark[:, 4 + pr:5 + pr], op0=ALU.mult, op1=ALU.add),
                              reads=[t2, kark], writes=[t2])
                    Sx.op("dve", lambda e: e.tensor_tensor(out=t2[:], in0=t2[:], in1=zs[:, 4:8, :], op=ALU.mult), reads=[t2, zs], writes=[t2])
                    Sx.op("dve", lambda e: e.tensor_tensor(out=t2[:], in0=t2[:], in1=zs[:, 0:4, :], op=ALU.mult), reads=[t2, zs], writes=[t2])
                    pm = PSA.next()
                    for pr in range(4):
                        Sx.op("pe", lambda e, pr=pr, pm=pm: e.matmul(pm[:, pr * 128:(pr + 1) * 128], blkf[:], t2[:, pr, :], start=True, stop=True),
                              reads=[blkf, t2], writes=[pm])
                    Sx.op("dve", lambda e, pm=pm: e.tensor_tensor(out=bon[:].rearrange("p a t -> p (a t)"), in0=pm[:, :],
                                                                 in1=zs[:, 8:12, :].rearrange("p a t -> p (a t)"), op=ALU.mult),
                          reads=[pm, zs], writes=[bon])
                yield
                for pr in range(4):
                    Sx.op("dve", lambda e, pr=pr: e.tensor_tensor_scan(out=Lc[:, pr, :], data0=ones128[:], data1=sg[:, pr, :], initial=0.0,
                                                                      op0=ALU.mult, op1=ALU.add), reads=[ones128, sg], writes=[Lc])
                last = 127
                if d == 1:
                    Sx.op("dve", lambda e: e.tensor_tensor(out=t1[:], in0=sg[:], in1=Lc[:], op=ALU.subtract), reads=[sg, Lc], writes=[t1])
                    Sx.op("dve", lambda e: e.tensor_copy(tot[:].unsqueeze(2), Lc[:, :, 127:128]), reads=[Lc], writes=[tot])
                    Sx.op("dve", lambda e: e.tensor_tensor(out=Lc[:], in0=t1[:], in1=tot[:].unsqueeze(2).to_broadcast([128, 4, 128]),
                                                           op=ALU.add), reads=[t1, tot], writes=[Lc])
                    last = 0
                yield
                Sx.op("dve", lambda e: e.tensor_tensor(out=Lm[:], in0=Lc[:], in1=Lc[:, :, MID:MID + 1].to_broadcast([128, 4, 128]),
                                                       op=ALU.subtract), reads=[Lc], writes=[Lm])
                Sx.op("act", lambda e: e.activation(out=eR[:], in_=Lm[:], func=AF.Exp, scale=-C1), reads=[Lm], writes=[eR])
                Sx.op("act", lambda e: e.activation(out=eB[:], in_=Lm[:], func=AF.Exp, scale=C1), reads=[Lm], writes=[eB])
                Sx.op("pool", lambda e: e.tensor_tensor(out=t1[:], in0=Lm[:], in1=sg[:], op=ALU.subtract), reads=[Lm, sg], writes=[t1])
                Sx.op("act", lambda e: e.activation(out=eA[:], in_=t1[:], func=AF.Exp, scale=-C1), reads=[t1], writes=[eA])
                Sx.op("dve", lambda e: e.tensor_tensor(out=t2[:], in0=Lc[:], in1=Lc[:, :, last:last + 1].to_broadcast([128, 4, 128]),
                                                       op=ALU.subtract), reads=[Lc], writes=[t2])
                Sx.op("act", lambda e: e.activation(out=eE[:], in_=t2[:], func=AF.Exp, scale=C1), reads=[t2], writes=[eE])
                yield
                PC, PM = PCf.next(), PMf.next()
                Sx.op("act", lambda e: e.activation(out=pcol[:, 0:4].unsqueeze(2), in_=Lc[:, :, last:last + 1], func=AF.Exp, scale=-C1),
                      reads=[Lc], writes=[pcol])
                Sx.op("act", lambda e: e.activation(out=pcol[:, 4:8].unsqueeze(2), in_=Lc[:, :, MID:MID + 1], func=AF.Exp, scale=-C1),
                      reads=[Lc, pcol], writes=[pcol])
                Sx.op("dve", lambda e, PC=PC: e.tensor_copy(PC[:], pcol[:, 0:4].unsqueeze(2).to_broadcast([128, 4, 128])),
                      reads=[pcol], writes=[PC])
                for pr in range(4):
                    Sx.op("act", lambda e, PM=PM, pr=pr: e.activation(out=PM[:, pr, :], in_=blkf[:], func=AF.Identity,
                                                                   scale=pcol[:, 4 + pr:5 + pr]), reads=[pcol, blkf], writes=[PM])
                yield
                AR = ARp.next()
                Sx.op("dve", lambda e, AR=AR: e.scalar_tensor_tensor(out=AR[:, :, 0, :], in0=kk[:], scalar=-1.0, in1=eA[:],
                                                                   op0=ALU.mult, op1=ALU.mult), reads=[kk, eA], writes=[AR])
                Sx.op("pool", lambda e, AR=AR: e.tensor_tensor(out=AR[:, :, 1, :], in0=zs[:, 0:4, :], in1=eR[:], op=ALU.mult),
                      reads=[zs, eR, AR], writes=[AR])
                BTu, KTu = BTup.next(), KTup.next()
                Sx.op("dve", lambda e: e.tensor_tensor(out=BTu[:], in0=bb[:], in1=eB[:], op=ALU.mult), reads=[bb, eB], writes=[BTu])
                Sx.op("dve", lambda e: e.tensor_tensor(out=KTu[:], in0=kd[:], in1=eB[:], op=ALU.mult), reads=[kd, eB], writes=[KTu])
                Sx.op("pool", lambda e: e.tensor_tensor(out=bpf[:], in0=bb[:], in1=eE[:], op=ALU.mult), reads=[bb, eE], writes=[bpf])
                Sx.op("pool", lambda e: e.tensor_tensor(out=kpf[:], in0=kd[:], in1=eE[:], op=ALU.mult), reads=[kd, eE], writes=[kpf])
                Sx.op("act", lambda e: e.activation(out=vTf[:], in_=zs[:, 8:12, :], func=AF.Copy), reads=[zs], writes=[vTf])
                out.update(dict(AR=AR, PC=PC, PM=PM, gT=gT, bon=bon, BTu=BTu))
                yield "SPLIT"
                for par in range(2):
                    ps_ = slice(par * 64, (par + 1) * 64)
                    Sx.op("act", lambda e, ps_=ps_, par=par, AR=AR: e.activation(out=AZ[ps_, :, par, :], in_=AR[ps_, :, 0, :], func=AF.Copy), reads=[AR], writes=[AZ])
                    Sx.op("act", lambda e, ps_=ps_, par=par: e.activation(out=BZ[ps_, :, par, :], in_=BTu[ps_, :, :], func=AF.Copy), reads=[BTu], writes=[BZ])
                    Sx.op("act", lambda e, ps_=ps_, par=par: e.activation(out=KZ[ps_, :, par, :], in_=KTu[ps_, :, :], func=AF.Copy), reads=[KTu], writes=[KZ])
                yield
                Bp, Kp, V = Bpp.next(), Kpp.next(), Vp.next()
                for src, dst in [(bpf, Bp), (kpf, Kp), (vTf, V)]:
                    pt = PSA.next()
                    ptb = pt[:].bitcast(BF16)
                    for pr in range(4):
                        Sx.op("pe", lambda e, pr=pr, ptb=ptb, src=src: e.transpose(ptb[:, pr * 128:(pr + 1) * 128], src[:, pr, :], identb[:]),
                              reads=[src, identb], writes=[pt])
                    dap = dst[:] if dst is V else dst[:].rearrange("p a t -> p (a t)")
                    evac(dap, ptb[:, 0:512], [pt], [dst])
                    yield
                out.update(dict(Bp=Bp, Kp=Kp, V=V))

            def prepB(d, P):
                AR = P["AR"]; BTu = P["BTu"]
                m_abT = mUs if d == 0 else mLs
                mD_ab, mN_ab = (mDL, mNL) if d == 0 else (mDU, mNU)
                mD_abT = mDU if d == 0 else mDL
                m_inT = mUi if d == 0 else mLi
                Ak, Arb, Ark = Akp.next(), Arbp.next(), Arkp.next()
                for hg in range(2):
                    p1 = PSB.next()
                    for hh in range(4):
                        h = hg * 4 + hh
                        pr, par = h // 2, h % 2
                        Sx.op("pe", lambda e, p1=p1, hh=hh, pr=pr, par=par: e.matmul(p1[:, hh * 128:(hh + 1) * 128], AZ[:, pr, par, :],
                                                                                      BTu[:, pr, :], start=True, stop=True),
                              reads=[AZ, BTu], writes=[p1])
                    Sx.op("dve", lambda e, p1=p1, hg=hg: e.tensor_tensor(out=Dk[0][:, hg * 4:(hg + 1) * 4, :],
                                                                        in0=p1[:, :].rearrange("p (a t) -> p a t", a=4),
                                                                        in1=mD_ab[:].unsqueeze(1).to_broadcast([128, 4, 128]), op=ALU.mult),
                          reads=[p1, mD_ab], writes=[(Dk[0], hg)])
                    Sx.op("dve", lambda e, p1=p1, hg=hg: e.tensor_tensor(out=Nn[:, hg * 4:(hg + 1) * 4, :],
                                                                        in0=p1[:, :].rearrange("p (a t) -> p a t", a=4),
                                                                        in1=mN_ab[:].unsqueeze(1).to_broadcast([128, 4, 128]), op=ALU.mult),
                          reads=[p1, mN_ab], writes=[(Nn, hg)])
                    for (LZ, o1, m1, o2, m2) in [(BZ, Dtk[0], mD_abT, Arb, m_inT), (KZ, Ak, m_abT, Ark, m_inT)]:
                        for h2 in range(2):
                            pass
                        for half in range(2):
                            p2 = PSB.next()
                            for q in range(2):
                                h = hg * 4 + half * 2 + q
                                pr, par = h // 2, h % 2
                                Sx.op("pe", lambda e, p2=p2, q=q, pr=pr, par=par, LZ=LZ, AR=AR: e.matmul(
                                    p2[:, q * 256:(q + 1) * 256], LZ[:, pr, par, :], AR[:, pr, :, :].rearrange("p a t -> p (a t)"),
                                    start=True, stop=True), reads=[LZ, AR], writes=[p2])
                            h0 = hg * 4 + half * 2
                            pv = p2[:, :].rearrange("p (q a t) -> p q a t", q=2, a=2)
                            Sx.op("dve", lambda e, pv=pv, o1=o1, m1=m1, h0=h0: e.tensor_tensor(
                                out=o1[:, h0:h0 + 2, :], in0=pv[:, :, 0, :], in1=m1[:].unsqueeze(1).to_broadcast([128, 2, 128]), op=ALU.mult),
                                reads=[p2, m1], writes=[(o1, h0)])
                            Sx.op("dve", lambda e, pv=pv, o2=o2, m2=m2, h0=h0: e.tensor_tensor(
                                out=o2[:, h0:h0 + 2, :], in0=pv[:, :, 1, :], in1=m2[:].unsqueeze(1).to_broadcast([128, 2, 128]), op=ALU.mult),
                                reads=[p2, m2], writes=[(o2, h0)])
                            yield
                idb3 = lambda n: identb[:].unsqueeze(1).to_broadcast([128, n, 128])
                Sx.op("dve", lambda e: e.tensor_tensor(out=Sk[0][:], in0=Dk[0][:], in1=idb3(8), op=ALU.add), reads=[Dk[0], identb], writes=[Sk[0]])
                Sx.op("pool", lambda e: e.tensor_tensor(out=Stk[0][:], in0=Dtk[0][:], in1=idb3(8), op=ALU.add), reads=[Dtk[0], identb], writes=[Stk[0]])
                yield

                def mm4(ps_, lhsb, rhsb, hg, rd):
                    for hh in range(4):
                        h = hg * 4 + hh
                        cs = slice(hh * 128, (hh + 1) * 128)
                        Sx.op("pe", lambda e, ps_=ps_, cs=cs, h=h: e.matmul(ps_[:, cs], lhsb[:, h, :], rhsb[:, h, :], start=True, stop=True),
                              reads=rd, writes=[ps_])

                def hv(b, hg):
                    return b[:, hg * 4:(hg + 1) * 4, :].rearrange("p a t -> p (a t)")
                cur = 0
                for lv in range(4):
                    nx = 1 - cur
                    for hg in range(2):
                        pD = PSB.next()
                        mm4(pD, Dtk[cur], Dk[cur], hg, [Dtk[cur], Dk[cur]])
                        evac(hv(Dk[nx], hg), pD[:, :], [pD], [(Dk[nx], hg)], eng="act")
                        pDt = PSB.next()
                        mm4(pDt, Dk[cur], Dtk[cur], hg, [Dtk[cur], Dk[cur]])
                        evac(hv(Dtk[nx], hg), pDt[:, :], [pDt], [(Dtk[nx], hg)], eng="act")
                        yield
                    for hg in range(2):
                        pS = PSB.next()
                        mm4(pS, Dtk[nx], Sk[cur], hg, [(Dtk[nx], hg), Sk[cur]])
                        Sx.op("dve", lambda e, pS=pS, hg=hg, cur=cur, nx=nx: e.tensor_tensor(out=hv(Sk[nx], hg), in0=pS[:, :], in1=hv(Sk[cur], hg), op=ALU.add),
                              reads=[pS, Sk[cur]], writes=[(Sk[nx], hg)])
                        pS2 = PSB.next()
                        mm4(pS2, Dk[nx], Stk[cur], hg, [(Dk[nx], hg), Stk[cur]])
                        Sx.op("dve", lambda e, pS2=pS2, hg=hg, cur=cur, nx=nx: e.tensor_tensor(out=hv(Stk[nx], hg), in0=pS2[:, :], in1=hv(Stk[cur], hg), op=ALU.add),
                              reads=[pS2, Stk[cur]], writes=[(Stk[nx], hg)])
                        yield
                    cur = nx
                    if lv == 1:
                        yield "SPLIT"
                Dinv, Dip = Sk[cur], Stk[cur]
                for hg in range(2):
                    pE = PSB.next()
                    mm4(pE, Nn, Dip, hg, [Nn, Dip])
                    evac(hv(Etb, hg), pE[:, :], [pE], [(Etb, hg)], eng="act")
                    pE2 = PSB.next()
                    mm4(pE2, Dip, Nn, hg, [Nn, Dip])
                    evac(hv(Eb, hg), pE2[:, :], [pE2], [(Eb, hg)], eng="act")
                    yield
                for hg in range(2):
                    pG = PSB.next()
                    mm4(pG, Eb, Etb, hg, [(Eb, hg), (Etb, hg)])
                    Sx.op("dve", lambda e, pG=pG, hg=hg: e.tensor_tensor(out=Gp[:, hg * 4:(hg + 1) * 4, :], in0=pG[:, :].rearrange("p (a t) -> p a t", a=4),
                                                                        in1=idb3(4), op=ALU.add), reads=[pG, identb], writes=[(Gp, hg)])
                    yield
                for hg in range(2):
                    pF = PSB.next()
                    mm4(pF, Eb, Gp, hg, [(Eb, hg), (Gp, hg)])
                    Sx.op("dve", lambda e, pF=pF, hg=hg: e.tensor_tensor(out=hv(Fp, hg), in0=pF[:, :], in1=hv(Gp, hg), op=ALU.add),
                          reads=[pF, (Gp, hg)], writes=[(Fp, hg)])
                    yield
                T = Tp.next()
                for hg in range(2):
                    pT = PSB.next()
                    mm4(pT, Dinv, Fp, hg, [Dinv, (Fp, hg)])
                    evac(hv(T, hg), pT[:, :], [pT], [(T, hg)], eng="act")
                    yield
                P.update(dict(T=T, Ak=Ak, Arb=Arb, Ark=Ark))

            def serial(c, d, P, Hs_cur, PMnext, Hn):
                AR, Bp, Kp, V, PC, T, Ak, Arb, Ark = (P[k] for k in ["AR", "Bp", "Kp", "V", "PC", "T", "Ak", "Arb", "Ark"])
                pX = PSS.next()
                for h in range(8):
                    pr, par = h // 2, h % 2
                    cs = slice(h * 64, (h + 1) * 64)
                    Sx.op("pe", lambda e, pX=pX, cs=cs, pr=pr, par=par: e.matmul(pX[:, cs], AR[:, pr, 0, :], Hs_cur[:, pr, par * 64:(par + 1) * 64],
                                                                                  start=True, stop=False), reads=[AR, Hs_cur], writes=[pX])
                    Sx.op("pe", lambda e, pX=pX, cs=cs, h=h: e.matmul(pX[:, cs], Ak[:, h, :], V[:, cs], start=False, stop=True),
                          reads=[Ak, V], writes=[pX])
                Sx.op("dve", lambda e, pX=pX: e.tensor_copy(Xb[:], pX[:, :]), reads=[pX], writes=[Xb])
                yield
                pU = PSS.next()
                for h in range(8):
                    cs = slice(h * 64, (h + 1) * 64)
                    Sx.op("pe", lambda e, pU=pU, cs=cs, h=h: e.matmul(pU[:, cs], T[:, h, :], Xb[:, cs], start=True, stop=True),
                          reads=[T, Xb], writes=[pU])
                Sx.op("act", lambda e, pU=pU: e.activation(out=Ub[:], in_=pU[:, :], func=AF.Copy), reads=[pU], writes=[Ub])
                yield
                pY = PSS.next()
                for h in range(8):
                    pr, par = h // 2, h % 2
                    cs = slice(h * 64, (h + 1) * 64)
                    Sx.op("pe", lambda e, pY=pY, cs=cs, pr=pr, par=par: e.matmul(pY[:, cs], AR[:, pr, 1, :], Hs_cur[:, pr, par * 64:(par + 1) * 64],
                                                                                  start=True, stop=False), reads=[AR, Hs_cur], writes=[pY])
                    Sx.op("pe", lambda e, pY=pY, cs=cs, h=h: e.matmul(pY[:, cs], Arb[:, h, :], Ub[:, cs], start=False, stop=False),
                          reads=[Arb, Ub], writes=[pY])
                    Sx.op("pe", lambda e, pY=pY, cs=cs, h=h: e.matmul(pY[:, cs], Ark[:, h, :], V[:, cs], start=False, stop=True),
                          reads=[Ark, V], writes=[pY])
                pH = PSS.next()
                for pr in range(4):
                    cs = slice(pr * 128, (pr + 1) * 128)
                    Sx.op("pe", lambda e, pH=pH, cs=cs, pr=pr: e.matmul(pH[:, cs], Bp[:, pr, :], Ub[:, cs], start=True, stop=False),
                          reads=[Bp, Ub], writes=[pH])
                    Sx.op("pe", lambda e, pH=pH, cs=cs, pr=pr: e.matmul(pH[:, cs], Kp[:, pr, :], V[:, cs], start=False, stop=True),
                          reads=[Kp, V], writes=[pH])
                Sx.op("dve", lambda e: e.tensor_tensor(out=H[:], in0=H[:], in1=PC[:], op=ALU.mult), reads=[H, PC], writes=[H])
                Sx.op("dve", lambda e, pH=pH: e.tensor_tensor(out=H[:].rearrange("p a t -> p (a t)"), in0=H[:].rearrange("p a t -> p (a t)"),
                                                             in1=pH[:, :], op=ALU.add), reads=[H, pH], writes=[H])
                if PMnext is not None:
                    Sx.op("dve", lambda e, Hn=Hn: e.tensor_tensor(out=Hn[:], in0=H[:], in1=PMnext[:], op=ALU.mult),
                          reads=[H, PMnext], writes=[Hn])
                yield
                if d == 0:
                    Y = Yp.next()
                    Sx.op("act", lambda e, Y=Y, pY=pY: e.activation(out=Y[:], in_=pY[:, :], func=AF.Copy), reads=[pY], writes=[Y])
                    Sx.dma("sp", yfw[c * 128:(c + 1) * 128, :], Y[:], reads=[Y], writes=[(dram["yfw"], c)], sbuf=Y)
                else:
                    post(c, pY, P["gT"], P["bon"])

            def post(c, pY, gT, bon):
                Y = Yp.next()
                Sx.dma("sp", Y[:], yfw[c * 128:(c + 1) * 128, :], reads=[(dram["yfw"], c)], writes=[Y], sbuf=Y)
                Yv = Y[:].rearrange("p (h d) -> p h d", d=64)
                Sx.op("dve", lambda e: e.tensor_tensor(out=Y[:], in0=Y[:], in1=pY[:, :], op=ALU.add), reads=[Y, pY], writes=[Y])
                Sx.op("dve", lambda e: e.tensor_reduce(out=st8[:, 0:8], in_=Yv, axis=AX.X, op=ALU.add), reads=[Y], writes=[st8])
                Sx.op("dve", lambda e: e.tensor_scalar(out=st8[:, 0:8], in0=st8[:, 0:8], scalar1=1.0 / 64, scalar2=None, op0=ALU.mult),
                      reads=[st8], writes=[st8])
                Sx.op("dve", lambda e: e.tensor_tensor(out=yn[:], in0=Yv, in1=st8[:, 0:8].unsqueeze(2).to_broadcast([128, 8, 64]),
                                                       op=ALU.subtract), reads=[Y, st8], writes=[yn])
                Sx.op("pool", lambda e: e.tensor_tensor(out=Yv, in0=yn[:], in1=yn[:], op=ALU.mult), reads=[yn, Y], writes=[Y])
                Sx.op("dve", lambda e: e.tensor_reduce(out=st8[:, 8:16], in_=Yv, axis=AX.X, op=ALU.add), reads=[Y, st8], writes=[st8])
                Sx.op("act", lambda e: e.activation(out=st8[:, 16:24], in_=st8[:, 8:16], func=AF.Sqrt, scale=1.0 / 64, bias=epsc[:, 1:2]),
                      reads=[st8, epsc], writes=[st8])
                Sx.op("dve", lambda e: e.reciprocal(st8[:, 24:32], st8[:, 16:24]), reads=[st8], writes=[st8])
                Sx.op("dve", lambda e: e.tensor_tensor(out=ynb[:].rearrange("p (h d) -> p h d", d=64), in0=yn[:],
                                                       in1=st8[:, 24:32].unsqueeze(2).to_broadcast([128, 8, 64]), op=ALU.mult),
                      reads=[yn, st8], writes=[ynb])
                pt = PSS.next()
                ptb = pt[:].bitcast(BF16)
                for pr in range(4):
                    Sx.op("pe", lambda e, pr=pr, ptb=ptb: e.transpose(ptb[:, pr * 128:(pr + 1) * 128], ynb[:, pr * 128:(pr + 1) * 128], identb[:]),
                          reads=[ynb, identb], writes=[pt])
                o = oT.next()
                for pr in range(4):
                    Sx.op("dve", lambda e, pr=pr, ptb=ptb: e.tensor_scalar(out=t1[:, pr, :], in0=ptb[:, pr * 128:(pr + 1) * 128],
                                                                        scalar1=vec[:, 3, pr:pr + 1], scalar2=vec[:, 4, pr:pr + 1],
                                                                        op0=ALU.mult, op1=ALU.add), reads=[pt, vec], writes=[t1])
                Sx.op("pool", lambda e: e.tensor_tensor(out=t1[:], in0=t1[:], in1=bon[:], op=ALU.add), reads=[t1, bon], writes=[t1])
                Sx.op("pool", lambda e, o=o: e.tensor_tensor(out=o[:], in0=t1[:], in1=gT[:], op=ALU.mult), reads=[t1, gT], writes=[o])
                Sx.dma("sp", ymT[0:512, c * 128:(c + 1) * 128].rearrange("(pr p) t -> p pr t", p=128), o[:], reads=[o],
                       writes=[(dram["ymT"], ("r", c))], sbuf=o)

            def interleave(gens):
                st = [[g, n, 0] for g, n in gens]
                while st:
                    it = min(st, key=lambda x: (x[2] + 1.0) / x[1])
                    try:
                        next(it[0])
                        it[2] += 1
                    except StopIteration:
                        st.remove(it)

            for d in range(2):
                order = list(range(NT)) if d == 0 else list(range(NT - 1, -1, -1))
                n = len(order)
                Sx.op("pool", lambda e: e.memset(H[:], 0.0), writes=[H])
                Hlist = [Hs.next()]
                Sx.op("pool", lambda e, Hc=Hlist[0]: e.memset(Hc[:], 0.0), writes=[Hlist[0]])
                Ps = {}
                gA, gB, doneA = {}, {}, set()

                def startA(i, d=d, order=order, Ps=Ps, gA=gA):
                    zc = loadz(order[i])
                    Ps[i] = {}
                    gA[i] = prepA(order[i], d, zc, Ps[i])

                def until_split(g):
                    for v in g:
                        if v == "SPLIT":
                            return
                        yield

                def rest(g):
                    for v in g:
                        yield

                def streamA(r, gA=gA, doneA=doneA, n=n):
                    if r in gA:
                        yield from rest(gA[r])
                        doneA.add(r)
                        del gA[r]
                    if r + 1 < n:
                        startA(r + 1)
                        yield from until_split(gA[r + 1])

                def streamB(r, d=d, gB=gB, doneA=doneA, Ps=Ps, n=n):
                    if (r - 1) in gB:
                        yield from rest(gB[r - 1])
                        del gB[r - 1]
                    if r < n:
                        while r not in doneA:
                            yield
                        gB[r] = prepB(d, Ps[r])
                        yield from until_split(gB[r])

                def genS(i, d=d, order=order, Ps=Ps, Hlist=Hlist, n=n):
                    Hn = Hs.next() if i + 1 < n else None
                    Hc = Hlist[0]
                    Hlist[0] = Hn
                    yield from serial(order[i], d, Ps[i], Hc, Ps[i + 1]["PM"] if i + 1 < n else None, Hn)
                    del Ps[i]

                def capture(gen):
                    lst = []
                    Sx.capture = lst
                    for _ in gen:
                        pass
                    Sx.capture = None
                    return lst

                def merge(streams):
                    done = set()
                    stall = [0]
                    st = []
                    for segs in streams:
                        tot = sum(len(o) for o, _, _ in segs)
                        if tot:
                            st.append([segs, 0, 0, tot, 0])
                    while st:
                        cand = []
                        for x in st:
                            segs, si, oi, tot, em = x
                            while si < len(segs) and oi >= len(segs[si][0]):
                                if segs[si][2] is not None:
                                    done.add(segs[si][2])
                                si += 1
                                oi = 0
                            x[1], x[2] = si, oi
                            if si >= len(segs):
                                continue
                            if segs[si][1] is not None and segs[si][1] not in done:
                                continue
                            cand.append(x)
                        st = [x for x in st if x[1] < len(x[0])]
                        if not st:
                            break
                        if not cand:
                            stall[0] += 1
                            assert stall[0] < 3, "merge deadlock"
                            continue
                        stall[0] = 0
                        x = min(cand, key=lambda y: (y[4] + 1.0) / y[3])
                        Sx.replay(x[0][x[1]][0][x[2]])
                        x[2] += 1
                        x[4] += 1

                startA(0)
                for it in capture(until_split(gA[0])):
                    Sx.replay(it)
                for r in range(n + 2):
                    segS = [(capture(genS(r - 2)), None, None)] if 2 <= r <= n + 1 else []
                    segA = []
                    if r in gA:
                        segA.append((capture(rest(gA[r])), None, ("A", r)))
                        doneA.add(r)
                        del gA[r]
                    if r + 1 < n:
                        startA(r + 1)
                        segA.append((capture(until_split(gA[r + 1])), None, None))
                    segB = []
                    if (r - 1) in gB:
                        segB.append((capture(rest(gB[r - 1])), None, None))
                        del gB[r - 1]
                    if r < n:
                        gB[r] = prepB(d, Ps[r])
                        segB.append((capture(until_split(gB[r])), ("A", r), None))
                    merge([segS, segB, segA])
            Sx.barrier()
    return P3


_NC_CACHE = {}


def prep_inputs(inputs, S, NL):
    f = lambda a: np.ascontiguousarray(a, dtype=np.float32)
    rows = S // GW
    tiles, sigs = att_tiles(rows)
    common = {}
    common["ada_w"] = f(inputs["ada_w"][:NL])
    common["ada_bT"] = f(inputs["ada_b"][:NL].reshape(NL, 48, 128).transpose(0, 2, 1))
    common["n1g"] = f(inputs["norm1_g"][:NL].reshape(NL, 8, 128).transpose(0, 2, 1))
    common["n2g"] = f(inputs["norm2_g"][:NL].reshape(NL, 8, 128).transpose(0, 2, 1))
    common["w_in"] = f(inputs["w_in"][:NL])
    common["muT"] = f(inputs["shift_mu"][:NL].reshape(NL, 2, 15, 128).transpose(0, 3, 1, 2))
    common["w0T"] = f(inputs["w0"][:NL].reshape(NL, 2, 4, 128).transpose(0, 3, 1, 2))
    common["a0T"] = f(inputs["a0"][:NL].reshape(NL, 2, 4, 128).transpose(0, 3, 1, 2))
    w2Z = np.zeros((NL, 128, 2, RW), np.float32)
    a2Z = np.zeros((NL, 128, 2, RW), np.float32)
    for d in range(2):
        w2Z[:, d * 64:(d + 1) * 64, d, :] = inputs["w2"][:NL, d]
        a2Z[:, d * 64:(d + 1) * 64, d, :] = inputs["a2"][:NL, d]
    common["w2Z"] = w2Z
    common["a2Z"] = a2Z
    common["g2"] = f(inputs["g2"][:NL])
    vec = np.stack([inputs[k][:NL].reshape(NL, 4, 128).transpose(0, 2, 1) for k in ["k_k", "k_a", "r_k", "lnx_g", "lnx_b"]], axis=2)
    common["vecT"] = f(vec)
    qk = np.stack([np.tile(inputs["q_norm_g"][:NL], (1, 2)), np.tile(inputs["k_norm_g"][:NL], (1, 2))], axis=2)
    common["qkg"] = f(qk)
    common["biasT"] = f(np.stack([build_bias(np.asarray(inputs["rpb"][l]), sigs) for l in range(NL)]))
    common["w_out"] = f(inputs["w_out"][:NL])
    common["f_in"] = f(inputs["ffn_w_in"][:NL])
    common["f_out"] = f(inputs["ffn_w_out"][:NL])
    return common, len(sigs)


def run(inputs, S, NL, ncores=8, trace=False):
    inputs = {k: np.asarray(v) for k, v in inputs.items()}
    B = inputs["x"].shape[0]
    common, NV = prep_inputs(inputs, S, NL)
    key = (S, NL, NV)
    if key not in _NC_CACHE:
        _NC_CACHE[key] = build_nc(S, NL, NV)
    nc = _NC_CACHE[key]
    in_maps = []
    for cidx in range(ncores):
        b = cidx % B
        m = dict(common)
        m["xT"] = np.ascontiguousarray(inputs["x"][b, :S].T, dtype=np.float32)
        m["cT"] = np.ascontiguousarray(inputs["c"][b].reshape(8, 128).T, dtype=np.float32)
        in_maps.append(m)
    res = run_bass_kernel_spmd(nc, in_maps, core_ids=list(range(ncores)), trace=trace)
    if trace:
        print("EXEC_TIME_NS", res.exec_time_ns)
    nb = min(B, ncores)
    out = np.stack([np.ascontiguousarray(res.results[b]["outT"].T) for b in range(nb)], axis=0)
    if DEBUG:
        return out.astype(np.float32), res.results
    return out.astype(np.float32)


def kernel(**inputs):
    return run(inputs, 8192, 4)
```
